# Optimizing a Trainium2 kernel written in Bass

```python
import math
import jax, jax.numpy as jnp
from jax import lax
import numpy as np

D_MODEL = 1024
BATCH = 2
SEQ = 16384
DEPTH = 1

HEAD_DIM = 64
SWA_Q_HEADS = 8
SWA_KV_HEADS = 2
SWA_GROUP = SWA_Q_HEADS // SWA_KV_HEADS
WINDOW = 128
BLOCK = 128
DIFF_HEADS = 4
DIFF_V_DIM = 2 * HEAD_DIM
N_HEADS_TOTAL = SWA_Q_HEADS + DIFF_HEADS
N_BUCKETS = 32
MAX_DISTANCE = 128
D_FF = 2816
CONV_WIDTH = 3
N_BRANCH = 2
EPS = 1e-6

SWA_Q_W = SWA_Q_HEADS * HEAD_DIM
SWA_KV_W = SWA_KV_HEADS * HEAD_DIM
DIFF_QK_W = DIFF_HEADS * 2 * HEAD_DIM
DIFF_V_W = DIFF_HEADS * DIFF_V_DIM
IN_SPLITS = (SWA_Q_W, SWA_KV_W, SWA_KV_W, DIFF_QK_W, DIFF_QK_W, DIFF_V_W, N_BRANCH * D_MODEL)
IN_WIDTH = sum(IN_SPLITS)

kernel_name = "hybrid_swa_sink_diffattn_convffn"


def rms_norm(x, g):
    xf = x.astype(jnp.float32)
    y = xf * lax.rsqrt(jnp.mean(xf * xf, axis=-1, keepdims=True) + EPS)
    return (y * g.astype(jnp.float32)).astype(x.dtype)


def t5_bucket(rel):
    n = jnp.maximum(rel, 0)
    max_exact = N_BUCKETS // 2
    nf = jnp.maximum(n, 1).astype(jnp.float32)
    large = max_exact + (jnp.log(nf / max_exact) / math.log(MAX_DISTANCE / max_exact)
                         * (N_BUCKETS - max_exact)).astype(jnp.int32)
    large = jnp.minimum(large, N_BUCKETS - 1)
    return jnp.where(n < max_exact, n, large)


def swa_sink_attention(q, k, v, sinks, bias_tab):
    B, S = q.shape[0], q.shape[1]
    nb = S // BLOCK
    qb = q.reshape(B, nb, BLOCK, SWA_KV_HEADS, SWA_GROUP, HEAD_DIM)
    kb = k.reshape(B, nb, BLOCK, SWA_KV_HEADS, HEAD_DIM)
    vb = v.reshape(B, nb, BLOCK, SWA_KV_HEADS, HEAD_DIM)
    pad = ((0, 0), (1, 0), (0, 0), (0, 0), (0, 0))
    kk = jnp.concatenate([jnp.pad(kb, pad)[:, :-1], kb], axis=2)
    vv = jnp.concatenate([jnp.pad(vb, pad)[:, :-1], vb], axis=2)
    s = jnp.einsum('bnqhgd,bnkhd->bnhgqk', qb, kk).astype(jnp.float32) * (HEAD_DIM ** -0.5)
    qi = jnp.arange(BLOCK)[:, None]
    kj = jnp.arange(2 * BLOCK)[None, :]
    rel = BLOCK + qi - kj
    band = (rel >= 0) & (rel < WINDOW)
    valid = band[None] & ((jnp.arange(nb)[:, None, None] > 0) | (kj[None] >= BLOCK))
    bias = jnp.take(bias_tab.astype(jnp.float32), t5_bucket(rel), axis=0)
    bias = bias.transpose(2, 0, 1).reshape(SWA_KV_HEADS, SWA_GROUP, BLOCK, 2 * BLOCK)
    s = jnp.where(valid[None, :, None, None], s + bias[None, None], -jnp.inf)
    sink = jnp.broadcast_to(sinks.astype(jnp.float32).reshape(1, 1, SWA_KV_HEADS, SWA_GROUP, 1, 1),
                            s.shape[:-1] + (1,))
    p = jax.nn.softmax(jnp.concatenate([s, sink], axis=-1), axis=-1)[..., :-1]
    o = jnp.einsum('bnhgqk,bnkhd->bnqhgd', p.astype(v.dtype), vv)
    return o.reshape(B, S, SWA_Q_HEADS * HEAD_DIM)


def diff_attention(q, k, v, lam, bias_tab):
    B, S = q.shape[0], q.shape[1]
    nb = S // BLOCK
    qblocks = jnp.moveaxis(q.reshape(B, nb, BLOCK, DIFF_HEADS, 2, HEAD_DIM), 1, 0)
    k_pos = jnp.arange(S)
    tab = bias_tab.astype(jnp.float32)

    def one_block(args):
        qi, n = args
        s = jnp.einsum('bqhcd,bkhcd->bhcqk', qi, k).astype(jnp.float32) * (HEAD_DIM ** -0.5)
        rel = (n * BLOCK + jnp.arange(BLOCK))[:, None] - k_pos[None, :]
        bias = jnp.take(tab, t5_bucket(rel), axis=0).transpose(2, 0, 1)
        s = jnp.where((rel >= 0)[None, None, None], s + bias[None, :, None], -jnp.inf)
        p = jax.nn.softmax(s, axis=-1)
        a = (p[:, :, 0] - lam * p[:, :, 1]).astype(v.dtype)
        return jnp.einsum('bhqk,bkhe->bqhe', a, v)

    out = lax.map(one_block, (qblocks, jnp.arange(nb)))
    return jnp.moveaxis(out, 0, 1).reshape(B, S, DIFF_HEADS, DIFF_V_DIM)


def causal_dwconv(u, w, b):
    C = u.shape[-1]
    y = lax.conv_general_dilated(u, w.reshape(CONV_WIDTH, 1, C).astype(u.dtype),
                                 window_strides=(1,), padding=[(CONV_WIDTH - 1, 0)],
                                 dimension_numbers=('NWC', 'WIO', 'NWC'),
                                 feature_group_count=C)
    return y + b.astype(u.dtype)


def split_cols(z, sizes):
    outs, off = [], 0
    for sz in sizes:
        outs.append(z[..., off:off + sz])
        off += sz
    return outs


def setup_inputs(seed: int = 0) -> dict:
    key = jax.random.key(seed)
    ks = jax.random.split(key, 24)
    f32 = jnp.float32

    def nrm(k, shape, scale):
        return jax.random.normal(k, shape, f32) * scale

    return {
        "x": nrm(ks[0], (BATCH, SEQ, D_MODEL), 1.0),
        "rel_bias": nrm(ks[1], (N_BUCKETS, N_HEADS_TOTAL), 0.5),
        "g_mix": 1.0 + nrm(ks[2], (DEPTH, D_MODEL), 0.02),
        "w_in": nrm(ks[3], (DEPTH, D_MODEL, IN_WIDTH), D_MODEL ** -0.5),
        "qn_a": 1.0 + nrm(ks[4], (DEPTH, HEAD_DIM), 0.02),
        "kn_a": 1.0 + nrm(ks[5], (DEPTH, HEAD_DIM), 0.02),
        "sinks": nrm(ks[6], (DEPTH, SWA_Q_HEADS), 0.5),
        "qn_b": 1.0 + nrm(ks[7], (DEPTH, HEAD_DIM), 0.02),
        "kn_b": 1.0 + nrm(ks[8], (DEPTH, HEAD_DIM), 0.02),
        "lam_q1": nrm(ks[9], (DEPTH, HEAD_DIM), 0.1),
        "lam_k1": nrm(ks[10], (DEPTH, HEAD_DIM), 0.1),
        "lam_q2": nrm(ks[11], (DEPTH, HEAD_DIM), 0.1),
        "lam_k2": nrm(ks[12], (DEPTH, HEAD_DIM), 0.1),
        "subln_b": 1.0 + nrm(ks[13], (DEPTH, DIFF_V_DIM), 0.02),
        "w_br_a": nrm(ks[14], (DEPTH, SWA_Q_W, D_MODEL), SWA_Q_W ** -0.5),
        "w_br_b": nrm(ks[15], (DEPTH, DIFF_V_W, D_MODEL), DIFF_V_W ** -0.5),
        "w_o": nrm(ks[16], (DEPTH, D_MODEL, D_MODEL), D_MODEL ** -0.5),
        "g_ffn": 1.0 + nrm(ks[17], (DEPTH, D_MODEL), 0.02),
        "w_up": nrm(ks[18], (DEPTH, D_MODEL, 2 * D_FF), D_MODEL ** -0.5),
        "conv_w": nrm(ks[19], (DEPTH, CONV_WIDTH, 2 * D_FF), CONV_WIDTH ** -0.5),
        "conv_b": nrm(ks[20], (DEPTH, 2 * D_FF), 0.02),
        "w_down": nrm(ks[21], (DEPTH, D_FF, D_MODEL), D_FF ** -0.5),
    }


def reference(x, rel_bias, g_mix, w_in, qn_a, kn_a, sinks, qn_b, kn_b, lam_q1, lam_k1,
              lam_q2, lam_k2, subln_b, w_br_a, w_br_b, w_o, g_ffn, w_up, conv_w, conv_b, w_down):
    B, S = x.shape[0], x.shape[1]
    bias_a = rel_bias[:, :SWA_Q_HEADS]
    bias_b = rel_bias[:, SWA_Q_HEADS:]
    for l in range(DEPTH):
        lam_init = 0.8 - 0.6 * math.exp(-0.3 * l)
        h = rms_norm(x, g_mix[l])
        z = h @ w_in[l]
        qa, ka, va, qb, kb, vb, gl = split_cols(z, IN_SPLITS)
        qa = rms_norm(qa.reshape(B, S, SWA_Q_HEADS, HEAD_DIM), qn_a[l])
        ka = rms_norm(ka.reshape(B, S, SWA_KV_HEADS, HEAD_DIM), kn_a[l])
        va = va.reshape(B, S, SWA_KV_HEADS, HEAD_DIM)
        ya = swa_sink_attention(qa, ka, va, sinks[l], bias_a)

        qb = rms_norm(qb.reshape(B, S, DIFF_HEADS, 2, HEAD_DIM), qn_b[l])
        kb = rms_norm(kb.reshape(B, S, DIFF_HEADS, 2, HEAD_DIM), kn_b[l])
        vb = vb.reshape(B, S, DIFF_HEADS, DIFF_V_DIM)
        lam = (jnp.exp(jnp.sum(lam_q1[l].astype(jnp.float32) * lam_k1[l].astype(jnp.float32)))
               - jnp.exp(jnp.sum(lam_q2[l].astype(jnp.float32) * lam_k2[l].astype(jnp.float32)))
               + lam_init)
        ob = diff_attention(qb, kb, vb, lam, bias_b)
        yb = (rms_norm(ob, subln_b[l]) * (1.0 - lam_init)).reshape(B, S, DIFF_V_W)

        gates = jax.nn.sigmoid(gl.reshape(B, S, N_BRANCH, D_MODEL).astype(jnp.float32)).astype(x.dtype)
        mixed = gates[:, :, 0] * (ya @ w_br_a[l]) + gates[:, :, 1] * (yb @ w_br_b[l])
        x = x + mixed @ w_o[l]
        h = rms_norm(x, g_ffn[l])
        u = causal_dwconv(h @ w_up[l], conv_w[l], conv_b[l])
        ug, uv = u[..., :D_FF], u[..., D_FF:]
        x = x + (jax.nn.silu(ug) * uv) @ w_down[l]
    return x
```

```python
import contextlib
import math
import numpy as np
import concourse.bass as bass
import concourse.mybir as mybir
from concourse.bass_utils import run_bass_kernel_spmd

F32 = mybir.dt.float32
BF16 = mybir.dt.bfloat16
AF = mybir.ActivationFunctionType
ALU = mybir.AluOpType

ENGS = ("pe", "act", "dve", "pool", "sp")
SEM_ROT = 3000


class Buf:
    __slots__ = ("name", "w", "rs")

    def __init__(self, name):
        self.name = name
        self.w = None
        self.rs = []


class Op:
    __slots__ = ("eng", "fn", "waits", "signal", "dma_key", "dma_sem", "dma_val", "sig_sem", "sig_val")

    def __init__(self, eng, fn, dma_key=None):
        self.eng = eng
        self.fn = fn
        self.waits = []
        self.signal = False
        self.dma_key = dma_key
        self.dma_sem = None
        self.dma_val = None
        self.sig_sem = None
        self.sig_val = None


class Prog:
    def __init__(self, nc):
        self.nc = nc
        self.ops = {e: [] for e in ENGS}
        self.dma_cnt = {}
        self.all_dma_last = {}
        self.stack = contextlib.ExitStack()
        self.nbufs = 0
        self.strict = False

    def buf(self, name=None):
        self.nbufs += 1
        return Buf(f"{name or 'b'}#{self.nbufs}")

    def bufs(self, n, name="b"):
        return [self.buf(f"{name}{i}") for i in range(n)]

    def _dep(self, op, y):
        if y is None or y is op:
            return
        if y.dma_key is None and y.eng == op.eng and op.dma_key is None and not self.strict:
            return
        if y.dma_key is None:
            y.signal = True
        if y not in op.waits:
            op.waits.append(y)

    def op(self, eng, fn, reads=(), writes=(), dma_key=None):
        o = Op(eng, fn, dma_key)
        for b in reads:
            self._dep(o, b.w)
        for b in writes:
            self._dep(o, b.w)
            for r in b.rs:
                self._dep(o, r)
        for b in reads:
            b.rs.append(o)
        for b in writes:
            b.w = o
            b.rs = []
        if dma_key is not None:
            st = self.dma_cnt.setdefault(dma_key, [0, 0])
            if st[1] + 16 > 4000:
                st[0] += 1
                st[1] = 0
            st[1] += 16
            o.dma_sem = (dma_key, st[0])
            o.dma_val = st[1]
            self.all_dma_last[dma_key] = o
        self.ops[eng].append(o)
        return o

    def dma(self, out, in_, reads=(), writes=(), key=None, eng="sp", **kw):
        prim = writes[0] if len(writes) else reads[0]
        key = prim.name
        return self.op(eng, lambda e: e.dma_start(out=out, in_=in_, **kw), reads, writes, dma_key=key)

    def barrier(self):
        lasts = [self.ops[e][-1] for e in ENGS if self.ops[e]]
        dmas = list(self.all_dma_last.values())
        news = []
        for e in ENGS:
            o = Op(e, None)
            for y in lasts:
                self._dep(o, y)
            for y in dmas:
                self._dep(o, y)
            news.append(o)
        for o in news:
            self.ops[o.eng].append(o)

    def emit(self):
        nc = self.nc
        semkeys = set()
        for e in ENGS:
            gen, cnt = 0, 0
            for o in self.ops[e]:
                if o.dma_key is not None:
                    semkeys.add(o.dma_sem)
                    continue
                if o.signal:
                    if cnt >= SEM_ROT:
                        gen += 1
                        cnt = 0
                    cnt += 1
                    o.sig_sem = ("eng", e, gen)
                    o.sig_val = cnt
                    semkeys.add(o.sig_sem)
        sems = {}
        for n, k in enumerate(sorted(semkeys, key=str)):
            sems[k] = self.stack.enter_context(nc.semaphore(f"sm{n}"))
        self.nsems = len(sems)
        block = self.stack.enter_context(nc.Block())
        engmap = {"pe": "tensor", "act": "scalar", "dve": "vector", "pool": "gpsimd", "sp": "sync"}

        def make(e):
            def body(eng):
                waited = {}
                for o in self.ops[e]:
                    for y in o.waits:
                        if y.dma_key is not None:
                            sk, v = y.dma_sem, y.dma_val
                        else:
                            sk, v = y.sig_sem, y.sig_val
                        if waited.get(sk, 0) >= v:
                            continue
                        waited[sk] = v
                        eng.wait_ge(sems[sk], v)
                    if o.fn is None:
                        if o.signal:
                            eng.nop().then_inc(sems[o.sig_sem], 1)
                        continue
                    ins = o.fn(eng)
                    if o.dma_key is not None:
                        ins.then_inc(sems[o.dma_sem], 16)
                    elif o.signal:
                        ins.then_inc(sems[o.sig_sem], 1)
            return body

        for e in ENGS:
            getattr(block, engmap[e])(make(e))
        self.stack.close()


class Arena:
    def __init__(self, prog, nbytes, name="arena"):
        nc = prog.nc
        self.t8 = prog.stack.enter_context(nc.sbuf_tensor(name, [128, nbytes], mybir.dt.uint8))
        self.views = {}
        self.nbytes = nbytes
        self.off = 0

    def view(self, dt):
        if dt not in self.views:
            self.views[dt] = self.t8.bitcast(dt)
        return self.views[dt]

    def alloc(self, nelem, dt):
        sz = mybir.dt.size(dt)
        self.off = (self.off + 63) // 64 * 64
        o = self.off
        self.off += nelem * sz
        assert self.off <= self.nbytes, f"arena overflow {self.off} > {self.nbytes}"
        return self.view(dt)[:, o // sz: o // sz + nelem]

    def alloc_raw(self, nbytes):
        self.off = (self.off + 63) // 64 * 64
        o = self.off
        self.off += nbytes
        assert self.off <= self.nbytes, f"arena overflow {self.off} > {self.nbytes}"
        return o

    def at(self, o, nelem, dt):
        sz = mybir.dt.size(dt)
        return self.view(dt)[:, o // sz: o // sz + nelem]

    def mark(self):
        return self.off

    def release(self, m):
        self.off = m


D = 1024
S = 16384
KC = 8
NSLOT = 8
SLOTW = 768
HW = 640
DFF = 2816
NFC = 22
EPS = 1e-6
LAM_INIT = 0.8 - 0.6 * math.exp(-0.3 * 0)
VECD = 2688
STRIPW = 2560
C_QA, C_KA, C_VA, C_QB, C_KB, C_VB, C_GL = 0, 512, 640, 768, 1280, 1792, 2304

DEBUG_SCRATCH = False
P2_STAGE = 9
P2_SLOTS = 8
SKIP_P1 = False
N_PHASES = 5


def _t5_bucket_np(rel):
    n = np.maximum(rel, 0)
    nf = np.maximum(n, 1).astype(np.float32)
    large = 16 + (np.log(nf / np.float32(16)) / np.float32(math.log(8.0)) * np.float32(16)).astype(np.int32)
    large = np.minimum(large, 31)
    return np.where(n < 16, n, large)


class K:
    def __init__(self):
        nc = bass.Bass("TRN2", target_bir_lowering=False)
        self.nc = nc
        self.P = Prog(nc)
        self.A = Arena(self.P, 209920)
        self.ps = self.P.stack.enter_context(nc.psum_tensor("ps", [128, 8, 512], F32))
        self.psB = self.P.bufs(8, "psb")

    def mm(self, out, lhsT, rhs, start, stop, R, W):
        self.P.op("pe", lambda e: e.matmul(out, lhsT=lhsT, rhs=rhs, start=start, stop=stop), R, W)

    def act(self, out, in_, func, R, W, bias=None, scale=1.0):
        b = self.zero1 if bias is None else bias
        npart = out.shape[0]
        if npart != 128 and b.shape[0] == 128:
            b = b[0:npart, :]
        self.P.op("act", lambda e: e.activation(out=out, in_=in_, func=func, bias=b, scale=scale), R, W)

    def tt(self, eng, out, a, b, op, R, W):
        self.P.op(eng, lambda e: e.tensor_tensor(out=out, in0=a, in1=b, op=op), R, W)

    def ts(self, eng, out, a, s1, s2, op0, op1, R, W):
        if op1 is None:
            self.P.op(eng, lambda e: e.tensor_scalar(out=out, in0=a, scalar1=s1, scalar2=None, op0=op0), R, W)
        else:
            self.P.op(eng, lambda e: e.tensor_scalar(out=out, in0=a, scalar1=s1, scalar2=s2, op0=op0, op1=op1), R, W)

    def stt(self, eng, out, a, s, b, op0, op1, R, W):
        self.P.op(eng, lambda e: e.scalar_tensor_tensor(out=out, in0=a, scalar=s, in1=b, op0=op0, op1=op1), R, W)

    def cp(self, eng, out, in_, R, W):
        if eng == "act":
            self.act(out, in_, AF.Identity, R, W)
        else:
            self.P.op(eng, lambda e: e.tensor_copy(out=out, in_=in_), R, W)

    def rcp(self, out, in_, R, W):
        self.P.op("dve", lambda e: e.reciprocal(out=out, in_=in_), R, W)

    def memset(self, eng, ap, val, W):
        self.P.op(eng, lambda e: e.memset(ap, val), (), W)

    def declare(self):
        nc = self.nc

        def din(name, shape, dt=F32):
            return nc.dram_tensor(name, list(shape), dt, kind="ExternalInput")

        def dscr(name, shape, dt):
            return nc.dram_tensor(name, list(shape), dt, kind="ExternalOutput" if DEBUG_SCRATCH else "Internal")

        self.d_xT = din("xT", [128, KC, S])
        self.d_xo = din("xo", [128, KC, NSLOT * SLOTW])
        self.d_win = din("w_in", [128, KC, 4352])
        self.d_wbra = din("w_br_a", [128, 4, D])
        self.d_wbrb = din("w_br_b", [128, 4, D])
        self.d_wo = din("w_o", [128, KC, D])
        self.d_wup = din("w_up", [128, KC, 2 * DFF])
        self.d_wdn = din("w_down", [128, NFC, D])
        self.d_gmix = din("g_mix", [128, KC])
        self.d_gffn = din("g_ffn", [128, KC])
        self.d_cw = din("conv_w", [128, 3, 44])
        self.d_cb = din("conv_b", [128, 44])
        self.d_relb = din("rel_bias", [32, 12])
        self.d_qna = din("qn_a", [1, 64])
        self.d_kna = din("kn_a", [1, 64])
        self.d_qnb = din("qn_b", [1, 64])
        self.d_knb = din("kn_b", [1, 64])
        self.d_sinks = din("sinks", [1, 8])
        self.d_lq1 = din("lam_q1", [1, 64])
        self.d_lk1 = din("lam_k1", [1, 64])
        self.d_lq2 = din("lam_q2", [1, 64])
        self.d_lk2 = din("lam_k2", [1, 64])
        self.d_subln = din("subln_b", [128, 1])
        self.d_oha = din("oh_a", [33, 384])
        self.d_ohd = din("oh_d", [33, VECD])
        self.d_m0 = din("m0", [128, 1])
        self.d_J = din("Jmat", [128, 128])
        self.d_bd = din("bdones", [128, 128])
        self.d_out = nc.dram_tensor("outT", [128, KC, NSLOT * 512], F32, kind="ExternalOutput")
        self.d_KT = dscr("s_KT", [4, 128, S], BF16)
        self.d_VS = dscr("s_VS", [4, 128, 128, 128], BF16)
        self.d_QB = dscr("s_QB", [NSLOT, 128, 4, HW], BF16)
        self.d_YA = dscr("s_YA", [NSLOT, 128, 4, HW], BF16)
        self.d_YB = dscr("s_YB", [NSLOT, 128, 4, HW], BF16)
        self.d_X2 = dscr("s_X2", [NSLOT, 128, KC, HW], F32)
        if DEBUG_SCRATCH:
            self.d_dstrip = dscr("s_strip", [128, 4 * STRIPW], BF16)
            self.d_dsa = dscr("s_sa", [128, 2 * 8 * 128], BF16)
            self.d_dsmall = dscr("s_small", [128, 8], F32)
        self.d_VA = dscr("s_VECA", [8, 384], F32)
        self.d_VD = dscr("s_VECD", [4, VECD], F32)

    def setup(self):
        P, A, nc = self.P, self.A, self.nc
        ps, psB = self.ps, self.psB
        cB = P.buf("consts")
        self.cB = cB
        self.zero1 = A.alloc(1, F32)
        self.eps1 = A.alloc(1, F32)
        self.tiny1 = A.alloc(1, F32)
        self.ones_bf = A.alloc(128, BF16)
        self.ones_f = A.alloc(128, F32)
        self.bd_bf = A.alloc(128, BF16)
        self.J = A.alloc(128, F32)
        bd_f = A.alloc(128, F32)
        self.memset("pool", self.zero1, 0.0, [cB])
        self.memset("pool", self.eps1, EPS, [cB])
        self.memset("pool", self.tiny1, 1e-30, [cB])
        self.memset("pool", self.ones_bf, 1.0, [cB])
        self.memset("pool", self.ones_f, 1.0, [cB])
        jB = P.buf("J")
        P.dma(self.J, self.d_J.ap(), writes=[jB], key="c0")
        P.dma(bd_f, self.d_bd.ap(), writes=[jB], key="c0")
        self.cp("dve", self.bd_bf, bd_f, [jB], [cB])
        sB = P.buf("small")
        self.gmix = A.alloc(KC, F32)
        self.gffn = A.alloc(KC, F32)
        self.cw = A.alloc(3 * 44, F32)
        self.cb = A.alloc(44, F32)
        self.m0 = A.alloc(1, F32)
        P.dma(self.gmix, self.d_gmix.ap(), writes=[sB], key="c1")
        P.dma(self.gffn, self.d_gffn.ap(), writes=[sB], key="c1")
        P.dma(self.cw, self.d_cw.ap().rearrange("p a b -> p (a b)"), writes=[sB], key="c1")
        P.dma(self.cb, self.d_cb.ap(), writes=[sB], key="c1")
        P.dma(self.m0, self.d_m0.ap(), writes=[sB], key="c1")
        self.gq_a = A.alloc(1, F32)
        self.gk_a = A.alloc(1, F32)
        self.gq_b = A.alloc(1, F32)
        self.gk_b = A.alloc(1, F32)
        for dst, src in ((self.gq_a, self.d_qna), (self.gk_a, self.d_kna), (self.gq_b, self.d_qnb), (self.gk_b, self.d_knb)):
            for hlf in range(2):
                P.dma(dst[64 * hlf:64 * hlf + 64, :], bass.AP(src, 0, [[1, 64], [1, 1]]), writes=[sB], key="c1")
        self.gsub = A.alloc(1, F32)
        P.dma(self.gsub, self.d_subln.ap(), writes=[sB], key="c1")
        self.ts("dve", self.gsub, self.gsub, 1.0 - LAM_INIT, None, ALU.mult, None, [sB], [sB])
        lam4 = A.alloc(4 * 64, F32)
        for n_, src in enumerate((self.d_lq1, self.d_lk1, self.d_lq2, self.d_lk2)):
            P.dma(lam4[:, n_ * 64:(n_ + 1) * 64], bass.AP(src, 0, [[0, 128], [1, 64]]), writes=[sB], key="c1")
        lp = A.alloc(2 * 64, F32)
        ls = A.alloc(2, F32)
        self.neglam = A.alloc(1, F32)
        self.tt("dve", lp[:, 0:64], lam4[:, 0:64], lam4[:, 64:128], ALU.mult, [sB], [sB])
        self.tt("dve", lp[:, 64:128], lam4[:, 128:192], lam4[:, 192:256], ALU.mult, [sB], [sB])
        P.op("dve", lambda e: e.reduce_sum(out=ls[:, 0:1], in_=lp[:, 0:64], axis=mybir.AxisListType.X), [sB], [sB])
        P.op("dve", lambda e: e.reduce_sum(out=ls[:, 1:2], in_=lp[:, 64:128], axis=mybir.AxisListType.X), [sB], [sB])
        self.act(ls, ls, AF.Exp, [sB, cB], [sB])
        self.tt("dve", self.neglam, ls[:, 1:2], ls[:, 0:1], ALU.subtract, [sB], [sB])
        self.ts("dve", self.neglam, self.neglam, -LAM_INIT, None, ALU.add, None, [sB], [sB])
        self.sB = sB
        self.esrow = A.alloc(2 * 512, F32)
        sk = A.alloc(8, F32)
        P.dma(sk[0:1, :], self.d_sinks.ap(), writes=[sB], key="c1")
        self.act(sk[0:1, :], sk[0:1, :], AF.Exp, [sB, cB], [sB])
        for hq in range(8):
            g_, r_ = hq // 4, hq % 4
            col = g_ * 512 + ((r_ % 2) * 2 + r_ // 2) * 128
            self.ts("dve", self.esrow[0:1, col:col + 128], self.ones_f[0:1, 0:128], sk[0:1, hq:hq + 1], None, ALU.mult, None, [sB, cB], [sB])

        tB = P.buf("tab")
        tabp = A.alloc(12, F32)
        tab31 = A.alloc(4, F32)
        self.memset("pool", tabp[32:33, :], -30000.0, [tB])
        P.dma(tabp[0:32, :], self.d_relb.ap(), writes=[tB], key="c2")
        P.dma(tab31[0:32, :], bass.AP(self.d_relb, 31 * 12 + 8, [[0, 32], [1, 4]]), writes=[tB], key="c2")
        self.tt("dve", tabp[0:32, 8:12], tabp[0:32, 8:12], tab31[0:32, :], ALU.subtract, [tB], [tB])
        m = A.mark()
        oha = A.alloc(384, F32)
        ohd = A.alloc(VECD, F32)
        veca = A.alloc(384, F32)
        vecd = A.alloc(VECD, F32)
        ohB = P.buf("oh")
        P.dma(oha[0:33, :], self.d_oha.ap(), writes=[ohB], key="c2")
        P.dma(ohd[0:33, :], self.d_ohd.ap(), writes=[ohB], key="c2")
        vB = P.buf("vec")
        self.mm(ps[0:8, 0, 0:384], tabp[0:33, 0:8], oha[0:33, :], True, True, [tB, ohB], [psB[0]])
        self.act(veca[0:8, :], ps[0:8, 0, 0:384], AF.Exp, [psB[0], cB], [vB])
        for pc in range(6):
            w = 512 if pc < 5 else VECD - 2560
            bk = 1 + pc % 2
            self.mm(ps[0:4, bk, 0:w], tabp[0:33, 8:12], ohd[0:33, pc * 512: pc * 512 + w], True, True, [tB, ohB], [psB[bk]])
            self.act(vecd[0:4, pc * 512: pc * 512 + w], ps[0:4, bk, 0:w], AF.Exp, [psB[bk], cB], [vB])
        dvB = P.buf("dvec")
        P.dma(self.d_VA.ap(), veca[0:8, :], reads=[vB], writes=[dvB], key="c3")
        P.dma(self.d_VD.ap(), vecd[0:4, :], reads=[vB], writes=[dvB], key="c3")
        A.release(m)
        self.pre_strip_mark = A.mark()
        self.strip = A.alloc(4 * STRIPW, BF16).rearrange("p (h u) -> p h u", h=4)
        self.sa = A.alloc(2 * 8 * 128, BF16).rearrange("p (t h q) -> p t h q", t=2, h=8)
        self.stripB = P.buf("strip")
        self.persist_mark = A.mark()
        m = A.mark()
        rev = A.alloc(STRIPW, F32)
        revB = P.bufs(2, "rev")
        P.dma(rev[:, 0:2048].rearrange("p (h u) -> p h u", h=8), bass.AP(self.d_VA, 0, [[1, 128], [384, 8], [1, 256]]),
              reads=[dvB], writes=[revB[0]], key="c4")
        for pi in range(4):
            bk = pi % 2
            self.mm(ps[:, bk, :], self.J, rev[:, pi * 512:(pi + 1) * 512], True, True, [jB, revB[0]], [psB[bk]])
            src = ps[:, bk, :].rearrange("p (h t q) -> p h t q", h=2, t=2)
            for ty in range(2):
                self.cp("dve" if ty == 0 else "act", self.sa[:, ty, 2 * pi:2 * pi + 2, :], src[:, :, ty, :], [psB[bk]], [self.stripB])
        for h in range(4):
            rb = revB[(h + 1) % 2]
            P.dma(rev, bass.AP(self.d_VD, h * VECD, [[1, 128], [1, STRIPW]]), reads=[dvB], writes=[revB[0], revB[1]], key="c4")
            for pc in range(5):
                bk = pc % 2
                self.mm(ps[:, bk, :], self.J, rev[:, pc * 512:(pc + 1) * 512], True, True, [jB, revB[0], revB[1]], [psB[bk]])
                self.cp("dve" if pc % 2 == 0 else "act", self.strip[:, h, pc * 512:(pc + 1) * 512], ps[:, bk, :], [psB[bk]], [self.stripB])
        A.release(m)
        if DEBUG_SCRATCH:
            P.dma(self.d_dstrip.ap(), self.strip.rearrange("p h u -> p (h u)"), reads=[self.stripB])
            P.dma(self.d_dsa.ap(), self.sa.rearrange("p t h q -> p (t h q)"), reads=[self.stripB])
            for n_, t_ in enumerate((self.neglam, self.gsub, self.gq_b, self.gk_b)):
                P.dma(self.d_dsmall.ap()[:, n_:n_ + 1], t_, reads=[self.sB], allow_slow_non_contiguous=True)

    def load_w(self, dst, src, kc, ncols, gain=None, dup=None, key="w"):
        P, A = self.P, self.A
        m = A.mark()
        pw = 512 if kc <= 8 else 128
        stg = [A.alloc(kc * pw, F32).rearrange("p (c n) -> p c n", c=kc) for _ in range(2)]
        sb = P.bufs(2, "stg")
        wB = P.buf("wdst")
        engs = ("dve", "pool", "act")
        n = 0
        for i, c0 in enumerate(range(0, ncols, pw)):
            w = min(pw, ncols - c0)
            s, b = stg[i % 2], sb[i % 2]
            P.dma(s[:, :, 0:w], src[:, :, c0:c0 + w], writes=[b], key=f"{key}{i % 2}")
            for c in range(kc):
                eng = engs[n % 3]
                n += 1
                if gain is None:
                    self.cp(eng, dst[:, c, c0:c0 + w], s[:, c, 0:w], [b], [wB])
                elif eng == "act":
                    self.act(dst[:, c, c0:c0 + w], s[:, c, 0:w], AF.Identity, [b, self.sB, self.cB], [wB], scale=gain[:, c:c + 1])
                else:
                    self.ts(eng, dst[:, c, c0:c0 + w], s[:, c, 0:w], gain[:, c:c + 1], None, ALU.mult, None, [b, self.sB], [wB])
        self.P.barrier()
        A.release(m)
        return wB

    def norm_tile(self, xt, xB, n, sq, sqB, hT, hB, rstd, rB, pieces):
        ps, psB = self.ps, self.psB
        for c in range(KC):
            self.act(sq[c % 2][:, 0:n], xt[:, c, :], AF.Square, [xB, self.cB], [sqB[c % 2]])
            for (o, w, bank) in pieces:
                self.mm(ps[:, bank, 0:w], self.ones_bf, sq[c % 2][:, o:o + w], c == 0, c == KC - 1, [sqB[c % 2], self.cB], [psB[bank]])
        for (o, w, bank) in pieces:
            self.act(rstd[:, o:o + w], ps[:, bank, 0:w], AF.Sqrt, [psB[bank], self.cB], [rB], bias=self.eps1, scale=1.0 / D)
        self.rcp(rstd[:, 0:n], rstd[:, 0:n], [rB], [rB])
        for c in range(KC):
            self.tt("dve" if c % 2 == 0 else "pool", hT[:, c, 0:n], xt[:, c, :], rstd[:, 0:n], ALU.mult, [xB, rB], [hB[c]])

    def proj_headnorm(self, wfn, hT, hB, o, w, out, outB, gain, bk_p, bk_n, tmp, tmpB, wB):
        ps, psB = self.ps, self.psB
        for c in range(KC):
            self.mm(ps[:, bk_p, 0:w], wfn(c), hT[:, c, o:o + w], c == 0, c == KC - 1, [wB, hB[c]], [psB[bk_p]])
        ksq, rk = tmp
        self.act(ksq[:, 0:w], ps[:, bk_p, 0:w], AF.Square, [psB[bk_p], self.cB], [tmpB[0]])
        self.mm(ps[:, bk_n, 0:w], self.bd_bf, ksq[:, 0:w], True, True, [tmpB[0], self.cB], [psB[bk_n]])
        self.act(rk[:, 0:w], ps[:, bk_n, 0:w], AF.Sqrt, [psB[bk_n], self.cB], [tmpB[1]], bias=self.eps1, scale=1.0 / 64)
        self.rcp(rk[:, 0:w], rk[:, 0:w], [tmpB[1]], [tmpB[1]])
        self.stt("dve", out, ps[:, bk_p, 0:w], gain, rk[:, 0:w], ALU.mult, ALU.mult, [psB[bk_p], tmpB[1], self.sB], [outB])

    def phase1(self):
        P, A = self.P, self.A
        ps, psB = self.ps, self.psB
        m = A.mark()
        wkb = A.alloc(KC * 512, BF16).rearrange("p (c n) -> p c n", c=KC)
        wvb = A.alloc(KC * 512, BF16).rearrange("p (c n) -> p c n", c=KC)
        win = self.d_win.ap()
        wB1 = self.load_w(wkb, win[:, :, C_KB:C_KB + 512], KC, 512, gain=self.gmix)
        wB2 = self.load_w(wvb, win[:, :, C_VB:C_VB + 512], KC, 512, gain=self.gmix)
        xt = [A.alloc(KC * 512, F32).rearrange("p (c n) -> p c n", c=KC) for _ in range(2)]
        xB = P.bufs(2, "x")
        sq = [A.alloc(512, BF16) for _ in range(2)]
        sqB = P.bufs(2, "sq")
        hT = [A.alloc(KC * 512, BF16).rearrange("p (c n) -> p c n", c=KC) for _ in range(2)]
        hB = [P.bufs(KC, "h") for _ in range(2)]
        rstd = [A.alloc(512, F32) for _ in range(2)]
        rB = P.bufs(2, "r")
        tmp = [(A.alloc(512, BF16), A.alloc(512, F32)) for _ in range(2)]
        tmpB = [P.bufs(2, "tmp") for _ in range(2)]
        kout = [A.alloc(4 * 512, BF16).rearrange("p (h n) -> p h n", h=4) for _ in range(2)]
        koB = [P.bufs(4, "ko") for _ in range(2)]
        vout = [A.alloc(4 * 512, BF16).rearrange("p (b n) -> p b n", b=4) for _ in range(2)]
        voB = [P.bufs(4, "vo") for _ in range(2)]
        xT = self.d_xT.ap()
        NT = S // 512

        def stats(T):
            pp = T % 2
            P.dma(xt[pp], xT[:, :, T * 512:(T + 1) * 512], writes=[xB[pp]], key=f"x{pp}")
            self.norm_tile(xt[pp], xB[pp], 512, sq, sqB, hT[pp], hB[pp], rstd[pp], rB[pp], [(0, 512, 0)])

        stats(0)
        for T in range(NT):
            pp = T % 2
            if T + 1 < NT:
                stats(T + 1)
            for hh in range(4):
                self.proj_headnorm(lambda c, hh=hh: wkb[:, c, hh * 128:(hh + 1) * 128], hT[pp], hB[pp], 0, 512,
                                   kout[pp][:, hh, :], koB[pp][hh], self.gk_b, 1 + hh % 2, 3, tmp[hh % 2], tmpB[hh % 2], wB1)
            for hh in range(4):
                P.dma(self.d_KT.ap()[hh, :, T * 512:(T + 1) * 512], kout[pp][:, hh, :], reads=[koB[pp][hh]], key=f"ko{pp}")
            for blk in range(4):
                bk = 4 + blk % 2
                for c in range(KC):
                    self.mm(ps[:, bk, :], hT[pp][:, c, blk * 128:(blk + 1) * 128], wvb[:, c, :], c == 0, c == KC - 1,
                            [hB[pp][c], wB2], [psB[bk]])
                self.cp("act" if blk % 2 == 0 else "dve", vout[pp][:, blk, :], ps[:, bk, :], [psB[bk]], [voB[pp][blk]])
            for hh in range(4):
                P.dma(self.d_VS.ap()[hh, :, 4 * T:4 * T + 4, :], vout[pp][:, :, hh * 128:(hh + 1) * 128],
                      reads=voB[pp], key=f"vo{pp}")
        P.barrier()
        A.release(m)

    def phase2(self):
        P, A = self.P, self.A
        ps, psB = self.ps, self.psB
        m = A.mark()
        win = self.d_win.ap()

        def walloc(n):
            return A.alloc(KC * n, BF16).rearrange("p (c n) -> p c n", c=KC)

        wqa, wka2, wva, wqb = walloc(512), walloc(256), walloc(128), walloc(512)
        wBq = self.load_w(wqa, win[:, :, C_QA:C_QA + 512], KC, 512, gain=self.gmix)
        for g in range(2):
            for hlf in range(2):
                self.load_w(wka2[:, :, g * 128 + hlf * 64: g * 128 + hlf * 64 + 64], win[:, :, C_KA + 64 * g:C_KA + 64 * g + 64], KC, 64,
                            gain=self.gmix)
        self.load_w(wva, win[:, :, C_VA:C_VA + 128], KC, 128, gain=self.gmix)
        self.load_w(wqb, win[:, :, C_QB:C_QB + 512], KC, 512, gain=self.gmix)
        wB = P.buf("w2")
        xt = A.alloc(KC * SLOTW, F32).rearrange("p (c n) -> p c n", c=KC)
        xB = P.buf("x")
        sq = [A.alloc(SLOTW, BF16) for _ in range(2)]
        sqB = P.bufs(2, "sq")
        hT = A.alloc(KC * SLOTW, BF16).rearrange("p (c n) -> p c n", c=KC)
        hB = P.bufs(KC, "h")
        rstd = A.alloc(SLOTW, F32)
        rB = P.buf("r")
        tmp = [(A.alloc(512, BF16), A.alloc(512, F32)) for _ in range(2)]
        tmpB = [P.bufs(2, "tmp") for _ in range(2)]
        qaT = A.alloc(4 * HW, BF16).rearrange("p (c n) -> p c n", c=4)
        qaB = P.bufs(4, "qa")
        kaT = A.alloc(2 * SLOTW, BF16).rearrange("p (g n) -> p g n", g=2)
        kaB = P.bufs(2, "ka")
        va = A.alloc(6 * 128, BF16).rearrange("p (b n) -> p b n", b=6)
        vaB = P.buf("va")
        qbT = A.alloc(4 * HW, BF16).rearrange("p (c n) -> p c n", c=4)
        qbB = P.buf("qb")
        yaT = A.alloc(4 * HW, BF16).rearrange("p (c n) -> p c n", c=4)
        yaB = P.buf("ya")
        pt = [A.alloc(512, BF16) for _ in range(4)]
        ptB = P.bufs(4, "pt")
        den = A.alloc(512, F32)
        denB = P.buf("den")
        xo = self.d_xo.ap()
        for i in range(min(NSLOT, P2_SLOTS)):
            P.dma(xt, xo[:, :, i * SLOTW:(i + 1) * SLOTW], writes=[xB], key="x2")
            self.norm_tile(xt, xB, SLOTW, sq, sqB, hT, hB, rstd, rB, [(0, 512, 0), (512, 256, 7)])
            cnt = 0
            for (o, w) in ((128, 512), (640, 128)):
                for cm in range(4):
                    self.proj_headnorm(lambda c, cm=cm: wqa[:, c, cm * 128:(cm + 1) * 128], hT, hB, o, w,
                                       qaT[:, cm, o - 128:o - 128 + w], qaB[cm], self.gq_a, 1 + cnt % 2, 3, tmp[cnt % 2], tmpB[cnt % 2], wB)
                    cnt += 1
                for cm in range(4):
                    self.proj_headnorm(lambda c, cm=cm: wqb[:, c, cm * 128:(cm + 1) * 128], hT, hB, o, w,
                                       qbT[:, cm, o - 128:o - 128 + w], qbB, self.gq_b, 1 + cnt % 2, 3, tmp[cnt % 2], tmpB[cnt % 2], wB)
                    cnt += 1
            for (o, w) in ((0, 512), (512, 256)):
                for g in range(2):
                    self.proj_headnorm(lambda c, g=g: wka2[:, c, g * 128:(g + 1) * 128], hT, hB, o, w,
                                       kaT[:, g, o:o + w], kaB[g], self.gk_a, 1 + cnt % 2, 3, tmp[cnt % 2], tmpB[cnt % 2], wB)
                    cnt += 1
            P.dma(self.d_QB.ap()[i], qbT, reads=[qbB], key="qbo")
            for half in range(2):
                bk = 4 + half
                for bl in range(3):
                    blk = half * 3 + bl
                    for c in range(KC):
                        self.mm(ps[:, bk, bl * 128:(bl + 1) * 128], hT[:, c, blk * 128:(blk + 1) * 128], wva[:, c, :], c == 0, c == KC - 1,
                                [hB[c], wB], [psB[bk]])
                self.cp("act", va[:, half * 3:half * 3 + 3, :], ps[:, bk, 0:384].rearrange("p (b n) -> p b n", b=3), [psB[bk]], [vaB])
            for n in range(1, 6 if P2_STAGE >= 1 else 0):
                qo = (n - 1) * 128
                for g in range(2):
                    for kk, kblk in enumerate((n - 1, n)):
                        idx = g * 2 + kk
                        b0 = (2, 6)[idx % 2]
                        for r in range(4):
                            par, rr = r % 2, r // 2
                            pb = 64 * par
                            self.mm(ps[:, b0 + par, rr * 128:(rr + 1) * 128], kaT[pb:pb + 64, g, kblk * 128:(kblk + 1) * 128],
                                    qaT[pb:pb + 64, 2 * g + rr, qo:qo + 128], True, True,
                                    [kaB[g], qaB[2 * g + rr]], [psB[b0 + par]])
                        self.act(pt[idx].rearrange("p (a n) -> p a n", a=2), ps[:, b0:b0 + 2, 0:256], AF.Exp,
                                 [psB[b0], psB[b0 + 1], self.cB], [ptB[idx]], scale=0.125)
                        ty = 1 if kk == 0 else 0
                        fa = self.sa[:, ty, 4 * g:4 * g + 4, :].rearrange("p (rr par) q -> p par rr q", par=2)
                        p4 = pt[idx].rearrange("p (par rr q) -> p par rr q", par=2, rr=2)
                        self.tt("pool", p4, p4, fa, ALU.mult, [ptB[idx], self.stripB], [ptB[idx]])
                        if i == 0 and n == 2 and kk == 0:
                            self.ts("pool", pt[idx], pt[idx], self.m0[:, 0:1], None, ALU.mult, None, [ptB[idx], self.sB], [ptB[idx]])
                if P2_STAGE < 2:
                    continue
                for g in range(2):
                    for kk, kblk in enumerate((n - 1, n)):
                        idx = g * 2 + kk
                        self.mm(ps[64 * g:64 * g + 64, 4, :], va[:, kblk, 64 * g:64 * g + 64], pt[idx], kk == 0, kk == 1,
                                [vaB, ptB[idx]], [psB[4]])
                    for kk in range(2):
                        idx = g * 2 + kk
                        self.mm(ps[64 * g:64 * g + 64, 5, :], self.ones_bf[:, 0:64], pt[idx], kk == 0, (kk == 1 and P2_STAGE < 3),
                                [ptB[idx], self.cB], [psB[5]])
                    if P2_STAGE >= 3:
                        self.mm(ps[64 * g:64 * g + 64, 5, :], self.ones_f[0:1, 0:64], self.esrow[0:1, g * 512:(g + 1) * 512], False, True,
                                [self.sB, self.cB], [psB[5]])
                self.rcp(den, ps[:, 5, :], [psB[5]], [denB])
                self.tt("dve", yaT[:, :, qo:qo + 128].rearrange("p (rr par) q -> p par rr q", par=2),
                        ps[:, 4, :].rearrange("p (par rr q) -> p par rr q", par=2, rr=2),
                        den.rearrange("p (par rr q) -> p par rr q", par=2, rr=2), ALU.mult, [psB[4], denB], [yaB])
            P.dma(self.d_YA.ap()[i], yaT, reads=[yaB], key="yao")
        P.barrier()
        A.release(m)

    def phase3(self):
        P, A = self.P, self.A
        ps, psB = self.ps, self.psB
        m = A.mark()
        KTs = [A.alloc(S, BF16) for _ in range(2)]
        VSs = [A.alloc(128 * 128, BF16).rearrange("p (b e) -> p b e", b=128) for _ in range(2)]
        kvB = [(P.bufs(4, "kt"), P.bufs(4, "vs")) for _ in range(2)]
        qt = [A.alloc(HW, BF16) for _ in range(2)]
        qB = P.bufs(2, "q")
        NPB = 3
        pt = [A.alloc(1024, BF16) for _ in range(NPB)]
        ptB = P.bufs(NPB, "pt")
        acc = A.alloc(1024, F32)
        accB = P.bufs(2, "acc")
        acch = A.alloc(1024, F32)
        acchB = P.bufs(2, "acch")
        r12 = A.alloc(1024, F32)
        r12B = P.buf("r12")
        dd = A.alloc(1024, F32)
        ddB = P.buf("dd")
        dsq = A.alloc(512, BF16)
        dsqB = P.buf("dsq")
        rr = A.alloc(512, F32)
        rrB = P.buf("rr")
        yb = [A.alloc(HW, BF16) for _ in range(2)]
        ybB = P.bufs(2, "yb")
        SB = [(0, 1), (2, 3)]

        def load_kv(h):
            pp = h % 2
            for q4 in range(4):
                P.dma(KTs[pp][:, q4 * 4096:(q4 + 1) * 4096], self.d_KT.ap()[h, :, q4 * 4096:(q4 + 1) * 4096], writes=[kvB[pp][0][q4]])
            for q4 in range(4):
                P.dma(VSs[pp][:, q4 * 32:(q4 + 1) * 32, :], self.d_VS.ap()[h, :, q4 * 32:(q4 + 1) * 32, :], writes=[kvB[pp][1][q4]])

        def finalize(ncol, o_ap, obufs, sum_ap, sum_bufs, acc_t, acc_bufs, nseg, ycols, ybt, ybb):
            for comp in range(2):
                for sg in range(nseg):
                    self.mm(sum_ap(comp), self.ones_f, acc_t[:, comp * 512 + sg * ncol: comp * 512 + (sg + 1) * ncol],
                            sg == 0, sg == nseg - 1, [acc_bufs[comp], self.cB], [sum_bufs[comp]])
            for comp in range(2):
                self.ts("dve", r12[:, comp * 512:comp * 512 + ncol], sum_ap(comp), self.tiny1[:, 0:1], None, ALU.add, None,
                        [sum_bufs[comp], self.cB], [r12B])
                self.rcp(r12[:, comp * 512:comp * 512 + ncol], r12[:, comp * 512:comp * 512 + ncol], [r12B], [r12B])
                self.tt("dve", dd[:, comp * 512: comp * 512 + ncol], o_ap(comp), r12[:, comp * 512:comp * 512 + ncol], ALU.mult,
                        [obufs[comp], r12B], [ddB])
            self.stt("dve", dd[:, 0:ncol], dd[:, 512:512 + ncol], self.neglam[:, 0:1], dd[:, 0:ncol], ALU.mult, ALU.add, [ddB, self.sB], [ddB])
            self.act(dsq[:, 0:ncol], dd[:, 0:ncol], AF.Square, [ddB, self.cB], [dsqB])
            self.mm(ps[:, 7, 0:ncol], self.ones_bf, dsq[:, 0:ncol], True, True, [dsqB, self.cB], [psB[7]])
            self.act(rr[:, 0:ncol], ps[:, 7, 0:ncol], AF.Sqrt, [psB[7], self.cB], [rrB], bias=self.eps1, scale=1.0 / 128)
            self.rcp(rr[:, 0:ncol], rr[:, 0:ncol], [rrB], [rrB])
            self.stt("dve", ybt[:, ycols:ycols + ncol], dd[:, 0:ncol], self.gsub[:, 0:1], rr[:, 0:ncol], ALU.mult, ALU.mult, [ddB, rrB, self.sB], [ybb])

        load_kv(0)
        cnt_q = 0
        for h in range(4):
            pp = h % 2
            KT, VS = KTs[pp], VSs[pp]
            kB, vB = kvB[pp]
            if h + 1 < 4:
                load_kv(h + 1)
            for i in range(NSLOT):
                qq = cnt_q % 2
                cnt_q += 1
                q, qb_ = qt[qq], qB[qq]
                P.dma(q, self.d_QB.ap()[i, :, h, :], writes=[qb_], key=f"q{qq}")
                ybt, ybb = yb[qq], ybB[qq]
                nkb = 16 * i + 16
                near0 = 16 * i - 1

                def qk(t):
                    sb = SB[t % 2]
                    for comp in range(2):
                        self.mm(ps[:, sb[comp], :], KT[64 * comp:64 * comp + 64, t * 128:(t + 1) * 128], q[64 * comp:64 * comp + 64, 128:640],
                                True, True, [kB[t // 32], qb_], [psB[sb[comp]]])

                qk(0)
                for t in range(nkb):
                    if t + 1 < nkb:
                        qk(t + 1)
                    sb = SB[t % 2]
                    p_, pB_ = pt[t % NPB], ptB[t % NPB]
                    self.act(p_.rearrange("p (c n) -> p c n", c=2), ps[:, sb[0]:sb[0] + 2, :], AF.Exp, [psB[sb[0]], psB[sb[1]], self.cB], [pB_], scale=0.125)
                    if t >= near0:
                        s_ = t - near0
                        off = 2048 - 128 * s_
                        for comp in range(2):
                            self.tt("dve" if comp == 0 else "pool", p_[:, comp * 512:(comp + 1) * 512], p_[:, comp * 512:(comp + 1) * 512],
                                    self.strip[:, h, off:off + 512], ALU.mult, [pB_, self.stripB], [pB_])
                    for comp in range(2):
                        eng = "dve" if comp == 0 else "pool"
                        a_ = acc[:, comp * 512:(comp + 1) * 512]
                        if t == 0:
                            self.cp(eng, a_, p_[:, comp * 512:(comp + 1) * 512], [pB_], [accB[comp]])
                        else:
                            self.tt(eng, a_, a_, p_[:, comp * 512:(comp + 1) * 512], ALU.add, [pB_, accB[comp]], [accB[comp]])
                    for comp in range(2):
                        self.mm(ps[:, 4 + comp, :], VS[:, t, :], p_[:, comp * 512:(comp + 1) * 512], t == 0, t == nkb - 1,
                                [vB[t // 32], pB_], [psB[4 + comp]])
                finalize(512, lambda comp: ps[:, 4 + comp, :], [psB[4], psB[5]], lambda comp: ps[:, comp, :], [psB[0], psB[1]], acc, accB, 1, 128, ybt, ybb)
                nb = (16 * i + 12) // 4

                def qkh(t):
                    sb = SB[t % 2]
                    for comp in range(2):
                        for u in range(4):
                            kb = 4 * t + 3 - u
                            self.mm(ps[:, sb[comp], u * 128:(u + 1) * 128], KT[64 * comp:64 * comp + 64, kb * 128:(kb + 1) * 128],
                                    q[64 * comp:64 * comp + 64, 0:128], True, True, [kB[kb // 32], qb_], [psB[sb[comp]]])

                qkh(0)
                for t in range(nb):
                    if t + 1 < nb:
                        qkh(t + 1)
                    sb = SB[t % 2]
                    p_, pB_ = pt[t % NPB], ptB[t % NPB]
                    self.act(p_.rearrange("p (c n) -> p c n", c=2), ps[:, sb[0]:sb[0] + 2, :], AF.Exp, [psB[sb[0]], psB[sb[1]], self.cB], [pB_], scale=0.125)
                    kb0 = 4 * t
                    if kb0 >= 16 * i - 4:
                        off = 2048 - 128 * (kb0 - 16 * i + 5)
                        for comp in range(2):
                            self.tt("dve" if comp == 0 else "pool", p_[:, comp * 512:(comp + 1) * 512], p_[:, comp * 512:(comp + 1) * 512],
                                    self.strip[:, h, off:off + 512], ALU.mult, [pB_, self.stripB], [pB_])
                    for comp in range(2):
                        eng = "dve" if comp == 0 else "pool"
                        a_ = acch[:, comp * 512:(comp + 1) * 512]
                        if t == 0:
                            self.cp(eng, a_, p_[:, comp * 512:(comp + 1) * 512], [pB_], [acchB[comp]])
                        else:
                            self.tt(eng, a_, a_, p_[:, comp * 512:(comp + 1) * 512], ALU.add, [pB_, acchB[comp]], [acchB[comp]])
                    for comp in range(2):
                        for u in range(4):
                            kb = 4 * t + 3 - u
                            self.mm(ps[:, 6, comp * 128:(comp + 1) * 128], VS[:, kb, :], p_[:, comp * 512 + u * 128: comp * 512 + (u + 1) * 128],
                                    t == 0 and u == 0 and comp == 0, t == nb - 1 and u == 3, [vB[kb // 32], pB_], [psB[6]])
                finalize(128, lambda comp: ps[:, 6, comp * 128:(comp + 1) * 128], [psB[6], psB[6]], lambda comp: ps[:, 7, 256 + comp * 128:256 + (comp + 1) * 128], [psB[7], psB[7]], acch, acchB, 4, 0, ybt, ybb)
                P.dma(self.d_YB.ap()[i, :, h, :], ybt, reads=[ybb], key=f"ybo{qq}")
        P.barrier()
        A.release(m)

    def phase4a(self):
        P, A = self.P, self.A
        ps, psB = self.ps, self.psB
        m = A.mark()
        win = self.d_win.ap()
        wgl = A.alloc(KC * 2048, BF16).rearrange("p (c n) -> p c n", c=KC)
        wbra = A.alloc(4 * D, BF16).rearrange("p (c n) -> p c n", c=4)
        wbrb = A.alloc(4 * D, BF16).rearrange("p (c n) -> p c n", c=4)
        wo = A.alloc(KC * D, BF16).rearrange("p (c n) -> p c n", c=KC)
        self.load_w(wgl, win[:, :, C_GL:C_GL + 2048], KC, 2048, gain=self.gmix)
        self.load_w(wbra, self.d_wbra.ap(), 4, D)
        self.load_w(wbrb, self.d_wbrb.ap(), 4, D)
        self.load_w(wo, self.d_wo.ap(), KC, D)
        wB = P.buf("w4a")
        xt = A.alloc(KC * HW, F32).rearrange("p (c n) -> p c n", c=KC)
        xB = P.buf("x")
        sq = [A.alloc(HW, BF16) for _ in range(2)]
        sqB = P.bufs(2, "sq")
        hT = A.alloc(KC * HW, BF16).rearrange("p (c n) -> p c n", c=KC)
        hB = P.bufs(KC, "h")
        rstd = A.alloc(HW, F32)
        rB = P.buf("r")
        gates = A.alloc(16 * HW, BF16).rearrange("p (c n) -> p c n", c=16)
        gB = P.bufs(16, "g")
        ya = A.alloc(4 * HW, BF16).rearrange("p (c n) -> p c n", c=4)
        yb = A.alloc(4 * HW, BF16).rearrange("p (c n) -> p c n", c=4)
        yB = P.bufs(2, "y")
        mixed = A.alloc(KC * HW, BF16).rearrange("p (c n) -> p c n", c=KC)
        mxB = P.bufs(KC, "mx")
        t1 = [A.alloc(512, BF16) for _ in range(2)]
        t2 = [A.alloc(512, BF16) for _ in range(2)]
        tB = [P.bufs(2, "t") for _ in range(2)]
        x2 = [A.alloc(512, F32) for _ in range(2)]
        x2B = P.bufs(2, "x2")
        xo = self.d_xo.ap()
        PIECES = ((0, 512), (512, 128))
        cnt = 0
        for i in range(NSLOT):
            P.dma(xt, xo[:, :, i * SLOTW + 128:(i + 1) * SLOTW], writes=[xB], key="x4")
            P.dma(ya, self.d_YA.ap()[i], writes=[yB[0]], key="ya")
            P.dma(yb, self.d_YB.ap()[i], writes=[yB[1]], key="yb")
            self.norm_tile(xt, xB, HW, sq, sqB, hT, hB, rstd, rB, [(0, 512, 0), (512, 128, 7)])
            for (o, w) in PIECES:
                for gc in range(16):
                    bk = 1 + gc % 2
                    for c in range(KC):
                        self.mm(ps[:, bk, 0:w], wgl[:, c, gc * 128:(gc + 1) * 128], hT[:, c, o:o + w], c == 0, c == KC - 1, [wB, hB[c]], [psB[bk]])
                    self.act(gates[:, gc, o:o + w], ps[:, bk, 0:w], AF.Sigmoid, [psB[bk], self.cB], [gB[gc]])
            for (o, w) in PIECES:
                for mc in range(KC):
                    k2 = cnt % 2
                    cnt += 1
                    ba, bb = 3 + 2 * k2, 4 + 2 * k2
                    for r in range(4):
                        self.mm(ps[:, ba, 0:w], wbra[:, r, mc * 128:(mc + 1) * 128], ya[:, r, o:o + w], r == 0, r == 3, [wB, yB[0]], [psB[ba]])
                    for r in range(4):
                        self.mm(ps[:, bb, 0:w], wbrb[:, r, mc * 128:(mc + 1) * 128], yb[:, r, o:o + w], r == 0, r == 3, [wB, yB[1]], [psB[bb]])
                    self.tt("dve", t1[k2][:, 0:w], ps[:, ba, 0:w], gates[:, mc, o:o + w], ALU.mult, [psB[ba], gB[mc]], [tB[k2][0]])
                    self.tt("dve", t2[k2][:, 0:w], ps[:, bb, 0:w], gates[:, 8 + mc, o:o + w], ALU.mult, [psB[bb], gB[8 + mc]], [tB[k2][1]])
                    self.tt("pool", mixed[:, mc, o:o + w], t1[k2][:, 0:w], t2[k2][:, 0:w], ALU.add, [tB[k2][0], tB[k2][1]], [mxB[mc]])
            for (o, w) in PIECES:
                for oc in range(KC):
                    k2 = cnt % 2
                    cnt += 1
                    bk = 1 + k2
                    for mc in range(KC):
                        self.mm(ps[:, bk, 0:w], wo[:, mc, oc * 128:(oc + 1) * 128], mixed[:, mc, o:o + w], mc == 0, mc == KC - 1, [wB, mxB[mc]], [psB[bk]])
                    self.tt("dve", x2[k2][:, 0:w], ps[:, bk, 0:w], xt[:, oc, o:o + w], ALU.add, [psB[bk], xB], [x2B[k2]])
                    P.dma(self.d_X2.ap()[i, :, oc, o:o + w], x2[k2][:, 0:w], reads=[x2B[k2]], key=f"x2o{k2}")
        P.barrier()
        A.release(m)

    def phase4b(self):
        P, A = self.P, self.A
        ps, psB = self.ps, self.psB
        A.release(self.pre_strip_mark)
        m = A.mark()
        wup = A.alloc(KC * 2 * DFF, BF16).rearrange("p (c n) -> p c n", c=KC)
        wdn = A.alloc(NFC * D, BF16).rearrange("p (c n) -> p c n", c=NFC)
        self.load_w(wup, self.d_wup.ap(), KC, 2 * DFF, gain=self.gffn)
        self.load_w(wdn, self.d_wdn.ap(), NFC, D)
        wB = P.buf("w4b")
        W2 = 514
        xa_off = A.alloc_raw(NFC * 512 * 2)
        xaB = P.buf("xa")
        sq = [A.alloc(W2, BF16) for _ in range(2)]
        sqB = P.bufs(2, "sq")
        hT = A.alloc(KC * W2, BF16).rearrange("p (c n) -> p c n", c=KC)
        hB = P.bufs(KC, "h")
        rstd = A.alloc(W2, F32)
        rB = P.buf("r")
        uh = A.alloc(44 * 2, F32).rearrange("p (c n) -> p c n", c=44)
        uhB = P.buf("uh")
        U = [A.alloc(W2, F32) for _ in range(4)]
        UB = P.bufs(4, "U")
        tg = [A.alloc(512, F32) for _ in range(2)]
        tv = [A.alloc(512, F32) for _ in range(2)]
        tgB = P.bufs(2, "tg")
        tvB = P.bufs(2, "tv")
        ot = [A.alloc(512, F32) for _ in range(2)]
        otB = P.bufs(2, "ot")
        xr = [A.alloc(512, F32) for _ in range(2)]
        xrB = P.bufs(2, "xr")
        cw = self.cw.rearrange("p (a b) -> p a b", a=3)
        outT = self.d_out.ap()
        xt = A.at(xa_off, KC * W2, F32).rearrange("p (c n) -> p c n", c=KC)
        aT = A.at(xa_off, NFC * 512, BF16).rearrange("p (c n) -> p c n", c=NFC)
        for i in range(NSLOT):
            P.dma(xt, self.d_X2.ap()[i, :, :, 126:640], writes=[xaB])
            self.norm_tile(xt, xaB, W2, sq, sqB, hT, hB, rstd, rB, [(0, 2, 7), (2, 512, 0)])
            for fc in range(44):
                for c in range(KC):
                    self.mm(ps[:, 7, fc * 2:fc * 2 + 2], wup[:, c, fc * 128:(fc + 1) * 128], hT[:, c, 0:2], c == 0, c == KC - 1, [wB, hB[c]], [psB[7]])
            self.cp("dve", uh, ps[:, 7, 0:88].rearrange("p (c n) -> p c n", c=44), [psB[7]], [uhB])
            for f in range(NFC):
                k2 = f % 2
                for half, fc in enumerate((f, NFC + f)):
                    bk = 1 + 2 * k2 + half
                    ui = 2 * k2 + half
                    for c in range(KC):
                        self.mm(ps[:, bk, :], wup[:, c, fc * 128:(fc + 1) * 128], hT[:, c, 2:W2], c == 0, c == KC - 1, [wB, hB[c]], [psB[bk]])
                    self.cp("pool", U[ui][:, 0:2], uh[:, fc, :], [uhB], [UB[ui]])
                    self.cp("act", U[ui][:, 2:W2], ps[:, bk, :], [psB[bk]], [UB[ui]])
                    t_, tb_ = (tg[k2], tgB[k2]) if half == 0 else (tv[k2], tvB[k2])
                    eng = "dve"
                    self.ts(eng, t_, U[ui][:, 0:512], cw[:, 0, fc:fc + 1], self.cb[:, fc:fc + 1], ALU.mult, ALU.add, [UB[ui], self.sB], [tb_])
                    self.stt(eng, t_, U[ui][:, 1:513], cw[:, 1, fc:fc + 1], t_, ALU.mult, ALU.add, [UB[ui], self.sB, tb_], [tb_])
                    self.stt(eng, t_, U[ui][:, 2:514], cw[:, 2, fc:fc + 1], t_, ALU.mult, ALU.add, [UB[ui], self.sB, tb_], [tb_])
                self.act(tg[k2], tg[k2], AF.Silu, [tgB[k2], self.cB], [tgB[k2]])
                self.tt("pool", aT[:, f, :], tg[k2], tv[k2], ALU.mult, [tgB[k2], tvB[k2]], [xaB])
            for oc in range(KC):
                k2 = oc % 2
                bk = 5 + k2
                P.dma(xr[k2], self.d_X2.ap()[i, :, oc, 128:640], writes=[xrB[k2]])
                for f in range(NFC):
                    self.mm(ps[:, bk, :], wdn[:, f, oc * 128:(oc + 1) * 128], aT[:, f, :], f == 0, f == NFC - 1, [wB, xaB], [psB[bk]])
                self.tt("dve", ot[k2], ps[:, bk, :], xr[k2], ALU.add, [psB[bk], xrB[k2]], [otB[k2]])
                P.dma(outT[:, oc, i * 512:(i + 1) * 512], ot[k2], reads=[otB[k2]])
        P.barrier()
        A.release(m)

    def build(self, nph=N_PHASES):
        self.declare()
        self.P.strict = True
        self.setup()
        self.P.strict = False
        self.P.barrier()
        phases = [self.phase1, self.phase2, self.phase3, self.phase4a, self.phase4b]
        for n_, ph in enumerate(phases[:nph]):
            if n_ == 0 and SKIP_P1:
                continue
            ph()
        self.P.barrier()
        self.P.emit()
        return self.nc


def _host_inputs(inputs):
    f = np.float32
    x = np.asarray(inputs["x"], dtype=f)

    def pc(w, kc):
        n = w.shape[1]
        return np.ascontiguousarray(w.reshape(kc, 128, n).transpose(1, 0, 2))

    w_br_a = np.asarray(inputs["w_br_a"][0], dtype=f)
    wa = w_br_a.reshape(2, 4, 64, D)
    wbra = np.ascontiguousarray(wa.transpose(0, 2, 1, 3).reshape(128, 4, D))
    common = {
        "w_in": pc(np.asarray(inputs["w_in"][0], dtype=f), KC),
        "w_br_a": wbra,
        "w_br_b": pc(np.asarray(inputs["w_br_b"][0], dtype=f), 4),
        "w_o": pc(np.asarray(inputs["w_o"][0], dtype=f), KC),
        "w_up": pc(np.asarray(inputs["w_up"][0], dtype=f), KC),
        "w_down": pc(np.asarray(inputs["w_down"][0], dtype=f), NFC),
        "g_mix": np.ascontiguousarray(np.asarray(inputs["g_mix"][0], dtype=f).reshape(KC, 128).T),
        "g_ffn": np.ascontiguousarray(np.asarray(inputs["g_ffn"][0], dtype=f).reshape(KC, 128).T),
        "conv_w": np.ascontiguousarray(np.asarray(inputs["conv_w"][0], dtype=f).reshape(3, 44, 128).transpose(2, 0, 1)),
        "conv_b": np.ascontiguousarray(np.asarray(inputs["conv_b"][0], dtype=f).reshape(44, 128).T),
        "rel_bias": np.ascontiguousarray(np.asarray(inputs["rel_bias"], dtype=f)),
        "qn_a": np.asarray(inputs["qn_a"], dtype=f).reshape(1, 64),
        "kn_a": np.asarray(inputs["kn_a"], dtype=f).reshape(1, 64),
        "qn_b": np.asarray(inputs["qn_b"], dtype=f).reshape(1, 64),
        "kn_b": np.asarray(inputs["kn_b"], dtype=f).reshape(1, 64),
        "sinks": np.asarray(inputs["sinks"], dtype=f).reshape(1, 8),
        "lam_q1": np.asarray(inputs["lam_q1"], dtype=f).reshape(1, 64),
        "lam_k1": np.asarray(inputs["lam_k1"], dtype=f).reshape(1, 64),
        "lam_q2": np.asarray(inputs["lam_q2"], dtype=f).reshape(1, 64),
        "lam_k2": np.asarray(inputs["lam_k2"], dtype=f).reshape(1, 64),
        "subln_b": np.asarray(inputs["subln_b"], dtype=f).reshape(128, 1),
        "Jmat": np.ascontiguousarray(np.eye(128, dtype=f)[::-1]),
        "bdones": np.kron(np.eye(2, dtype=f), np.ones((64, 64), dtype=f)),
    }
    oha = np.zeros((33, 384), dtype=f)
    for mm_ in range(384):
        d = mm_ - 127
        if 0 <= d < 128:
            oha[int(_t5_bucket_np(np.array(d))), mm_] = 1
        else:
            oha[32, mm_] = 1
    common["oh_a"] = oha
    in_maps = []
    for core in range(8):
        b, j = core // 4, core % 4
        xTb = np.ascontiguousarray(x[b].T.reshape(KC, 128, S).transpose(1, 0, 2))
        xo = np.zeros((128, KC, NSLOT, SLOTW), dtype=f)
        for i in range(NSLOT):
            G = 4 * i + j
            t0 = 512 * G - 256
            lo = max(t0, 0)
            xo[:, :, i, lo - t0:] = xTb[:, :, lo:t0 + SLOTW]
        ohd = np.zeros((33, VECD), dtype=f)
        d = np.arange(VECD) + 512 * j - 2047
        bk = _t5_bucket_np(d)
        for mm_ in range(VECD):
            if d[mm_] >= 0:
                ohd[bk[mm_], mm_] = 1
            else:
                ohd[32, mm_] = 1
        mp = dict(common)
        mp["xT"] = xTb
        mp["xo"] = xo.reshape(128, KC, NSLOT * SLOTW)
        mp["oh_d"] = ohd
        mp["m0"] = np.full((128, 1), 0.0 if j == 0 else 1.0, dtype=f)
        in_maps.append(mp)
    return in_maps


_NC_CACHE = {}


def kernel(**inputs):
    in_maps = _host_inputs(inputs)
    if "nc" not in _NC_CACHE:
        _NC_CACHE["nc"] = K().build()
    nc = _NC_CACHE["nc"]
    res = run_bass_kernel_spmd(nc, in_maps, core_ids=list(range(8)))
    out = np.zeros((2, S, D), dtype=np.float32)
    for core in range(8):
        b, j = core // 4, core % 4
        o = res.results[core]["outT"].reshape(128, KC, NSLOT, 512)
        for i in range(NSLOT):
            G = 4 * i + j
            out[b, 512 * G:512 * (G + 1), :] = o[:, :, i, :].transpose(2, 1, 0).reshape(512, D)
    return out
```

```python
import contextlib
import math
import numpy as np
import concourse.bass as bass
import concourse.mybir as mybir
from concourse.bass_utils import run_bass_kernel_spmd

F32 = mybir.dt.float32
BF16 = mybir.dt.bfloat16
AF = mybir.ActivationFunctionType
ALU = mybir.AluOpType

ENGS = ("pe", "act", "dve", "pool", "sp")
SEM_ROT = 3000


class Buf:
    __slots__ = ("name", "w", "rs")

    def __init__(self, name):
        self.name = name
        self.w = None
        self.rs = []


class Op:
    __slots__ = ("eng", "fn", "waits", "signal", "dma_key", "dma_sem", "dma_val", "sig_sem", "sig_val")

    def __init__(self, eng, fn, dma_key=None):
        self.eng = eng
        self.fn = fn
        self.waits = []
        self.signal = False
        self.dma_key = dma_key
        self.dma_sem = None
        self.dma_val = None
        self.sig_sem = None
        self.sig_val = None


class Prog:
    def __init__(self, nc):
        self.nc = nc
        self.ops = {e: [] for e in ENGS}
        self.dma_cnt = {}
        self.all_dma_last = {}
        self.stack = contextlib.ExitStack()
        self.nbufs = 0
        self.strict = False

    def buf(self, name=None):
        self.nbufs += 1
        return Buf(f"{name or 'b'}#{self.nbufs}")

    def bufs(self, n, name="b"):
        return [self.buf(f"{name}{i}") for i in range(n)]

    def _dep(self, op, y):
        if y is None or y is op:
            return
        if y.dma_key is None and y.eng == op.eng and op.dma_key is None and not self.strict:
            return
        if y.dma_key is None:
            y.signal = True
        if y not in op.waits:
            op.waits.append(y)

    def op(self, eng, fn, reads=(), writes=(), dma_key=None):
        o = Op(eng, fn, dma_key)
        for b in reads:
            self._dep(o, b.w)
        for b in writes:
            self._dep(o, b.w)
            for r in b.rs:
                self._dep(o, r)
        for b in reads:
            b.rs.append(o)
        for b in writes:
            b.w = o
            b.rs = []
        if dma_key is not None:
            st = self.dma_cnt.setdefault(dma_key, [0, 0])
            if st[1] + 16 > 4000:
                st[0] += 1
                st[1] = 0
            st[1] += 16
            o.dma_sem = (dma_key, st[0])
            o.dma_val = st[1]
            self.all_dma_last[dma_key] = o
        self.ops[eng].append(o)
        return o

    def dma(self, out, in_, reads=(), writes=(), key=None, eng="sp", **kw):
        prim = writes[0] if len(writes) else reads[0]
        key = prim.name
        return self.op(eng, lambda e: e.dma_start(out=out, in_=in_, **kw), reads, writes, dma_key=key)

    def barrier(self):
        lasts = [self.ops[e][-1] for e in ENGS if self.ops[e]]
        dmas = list(self.all_dma_last.values())
        news = []
        for e in ENGS:
            o = Op(e, None)
            for y in lasts:
                self._dep(o, y)
            for y in dmas:
                self._dep(o, y)
            news.append(o)
        for o in news:
            self.ops[o.eng].append(o)

    def emit(self):
        nc = self.nc
        semkeys = set()
        for e in ENGS:
            gen, cnt = 0, 0
            for o in self.ops[e]:
                if o.dma_key is not None:
                    semkeys.add(o.dma_sem)
                    continue
                if o.signal:
                    if cnt >= SEM_ROT:
                        gen += 1
                        cnt = 0
                    cnt += 1
                    o.sig_sem = ("eng", e, gen)
                    o.sig_val = cnt
                    semkeys.add(o.sig_sem)
        sems = {}
        for n, k in enumerate(sorted(semkeys, key=str)):
            sems[k] = self.stack.enter_context(nc.semaphore(f"sm{n}"))
        self.nsems = len(sems)
        block = self.stack.enter_context(nc.Block())
        engmap = {"pe": "tensor", "act": "scalar", "dve": "vector", "pool": "gpsimd", "sp": "sync"}

        def make(e):
            def body(eng):
                waited = {}
                for o in self.ops[e]:
                    for y in o.waits:
                        if y.dma_key is not None:
                            sk, v = y.dma_sem, y.dma_val
                        else:
                            sk, v = y.sig_sem, y.sig_val
                        if waited.get(sk, 0) >= v:
                            continue
                        waited[sk] = v
                        eng.wait_ge(sems[sk], v)
                    if o.fn is None:
                        if o.signal:
                            eng.nop().then_inc(sems[o.sig_sem], 1)
                        continue
                    ins = o.fn(eng)
                    if o.dma_key is not None:
                        ins.then_inc(sems[o.dma_sem], 16)
                    elif o.signal:
                        ins.then_inc(sems[o.sig_sem], 1)
            return body

        for e in ENGS:
            getattr(block, engmap[e])(make(e))
        self.stack.close()


class Arena:
    def __init__(self, prog, nbytes, name="arena"):
        nc = prog.nc
        self.t8 = prog.stack.enter_context(nc.sbuf_tensor(name, [128, nbytes], mybir.dt.uint8))
        self.views = {}
        self.nbytes = nbytes
        self.off = 0

    def view(self, dt):
        if dt not in self.views:
            self.views[dt] = self.t8.bitcast(dt)
        return self.views[dt]

    def alloc(self, nelem, dt):
        sz = mybir.dt.size(dt)
        self.off = (self.off + 63) // 64 * 64
        o = self.off
        self.off += nelem * sz
        assert self.off <= self.nbytes, f"arena overflow {self.off} > {self.nbytes}"
        return self.view(dt)[:, o // sz: o // sz + nelem]

    def alloc_raw(self, nbytes):
        self.off = (self.off + 63) // 64 * 64
        o = self.off
        self.off += nbytes
        assert self.off <= self.nbytes, f"arena overflow {self.off} > {self.nbytes}"
        return o

    def at(self, o, nelem, dt):
        sz = mybir.dt.size(dt)
        return self.view(dt)[:, o // sz: o // sz + nelem]

    def mark(self):
        return self.off

    def release(self, m):
        self.off = m


D = 1024
S = 16384
KC = 8
NSLOT = 8
SLOTW = 768
HW = 640
DFF = 2816
NFC = 22
EPS = 1e-6
LAM_INIT = 0.8 - 0.6 * math.exp(-0.3 * 0)
VECD = 2688
STRIPW = 2560
C_QA, C_KA, C_VA, C_QB, C_KB, C_VB, C_GL = 0, 512, 640, 768, 1280, 1792, 2304

DEBUG_SCRATCH = False
P2_STAGE = 9
P2_SLOTS = 8
SKIP_P1 = False
N_PHASES = 5


def _t5_bucket_np(rel):
    n = np.maximum(rel, 0)
    nf = np.maximum(n, 1).astype(np.float32)
    large = 16 + (np.log(nf / np.float32(16)) / np.float32(math.log(8.0)) * np.float32(16)).astype(np.int32)
    large = np.minimum(large, 31)
    return np.where(n < 16, n, large)


class K:
    def __init__(self):
        nc = bass.Bass("TRN2", target_bir_lowering=False)
        self.nc = nc
        self.P = Prog(nc)
        self.A = Arena(self.P, 209920)
        self.ps = self.P.stack.enter_context(nc.psum_tensor("ps", [128, 8, 512], F32))
        self.psB = self.P.bufs(8, "psb")

    def mm(self, out, lhsT, rhs, start, stop, R, W):
        self.P.op("pe", lambda e: e.matmul(out, lhsT=lhsT, rhs=rhs, start=start, stop=stop), R, W)

    def act(self, out, in_, func, R, W, bias=None, scale=1.0):
        b = self.zero1 if bias is None else bias
        npart = out.shape[0]
        if npart != 128 and b.shape[0] == 128:
            b = b[0:npart, :]
        self.P.op("act", lambda e: e.activation(out=out, in_=in_, func=func, bias=b, scale=scale), R, W)

    def tt(self, eng, out, a, b, op, R, W):
        self.P.op(eng, lambda e: e.tensor_tensor(out=out, in0=a, in1=b, op=op), R, W)

    def ts(self, eng, out, a, s1, s2, op0, op1, R, W):
        if op1 is None:
            self.P.op(eng, lambda e: e.tensor_scalar(out=out, in0=a, scalar1=s1, scalar2=None, op0=op0), R, W)
        else:
            self.P.op(eng, lambda e: e.tensor_scalar(out=out, in0=a, scalar1=s1, scalar2=s2, op0=op0, op1=op1), R, W)

    def stt(self, eng, out, a, s, b, op0, op1, R, W):
        self.P.op(eng, lambda e: e.scalar_tensor_tensor(out=out, in0=a, scalar=s, in1=b, op0=op0, op1=op1), R, W)

    def cp(self, eng, out, in_, R, W):
        if eng == "act":
            self.act(out, in_, AF.Identity, R, W)
        else:
            self.P.op(eng, lambda e: e.tensor_copy(out=out, in_=in_), R, W)

    def rcp(self, out, in_, R, W):
        self.P.op("dve", lambda e: e.reciprocal(out=out, in_=in_), R, W)

    def memset(self, eng, ap, val, W):
        self.P.op(eng, lambda e: e.memset(ap, val), (), W)

    def declare(self):
        nc = self.nc

        def din(name, shape, dt=F32):
            return nc.dram_tensor(name, list(shape), dt, kind="ExternalInput")

        def dscr(name, shape, dt):
            return nc.dram_tensor(name, list(shape), dt, kind="ExternalOutput" if DEBUG_SCRATCH else "Internal")

        self.d_xT = din("xT", [128, KC, S])
        self.d_xo = din("xo", [128, KC, NSLOT * SLOTW])
        self.d_win = din("w_in", [128, KC, 4352])
        self.d_wbra = din("w_br_a", [128, 4, D])
        self.d_wbrb = din("w_br_b", [128, 4, D])
        self.d_wo = din("w_o", [128, KC, D])
        self.d_wup = din("w_up", [128, KC, 2 * DFF])
        self.d_wdn = din("w_down", [128, NFC, D])
        self.d_gmix = din("g_mix", [128, KC])
        self.d_gffn = din("g_ffn", [128, KC])
        self.d_cw = din("conv_w", [128, 3, 44])
        self.d_cb = din("conv_b", [128, 44])
        self.d_relb = din("rel_bias", [32, 12])
        self.d_qna = din("qn_a", [1, 64])
        self.d_kna = din("kn_a", [1, 64])
        self.d_qnb = din("qn_b", [1, 64])
        self.d_knb = din("kn_b", [1, 64])
        self.d_sinks = din("sinks", [1, 8])
        self.d_lq1 = din("lam_q1", [1, 64])
        self.d_lk1 = din("lam_k1", [1, 64])
        self.d_lq2 = din("lam_q2", [1, 64])
        self.d_lk2 = din("lam_k2", [1, 64])
        self.d_subln = din("subln_b", [128, 1])
        self.d_oha = din("oh_a", [33, 384])
        self.d_ohd = din("oh_d", [33, VECD])
        self.d_m0 = din("m0", [128, 1])
        self.d_J = din("Jmat", [128, 128])
        self.d_bd = din("bdones", [128, 128])
        self.d_out = nc.dram_tensor("outT", [128, KC, NSLOT * 512], F32, kind="ExternalOutput")
        self.d_KT = dscr("s_KT", [4, 128, S], BF16)
        self.d_VS = dscr("s_VS", [4, 128, 128, 128], BF16)
        self.d_QB = dscr("s_QB", [NSLOT, 128, 4, HW], BF16)
        self.d_YA = dscr("s_YA", [NSLOT, 128, 4, HW], BF16)
        self.d_YB = dscr("s_YB", [NSLOT, 128, 4, HW], BF16)
        self.d_X2 = dscr("s_X2", [NSLOT, 128, KC, HW], F32)
        if DEBUG_SCRATCH:
            self.d_dstrip = dscr("s_strip", [128, 4 * STRIPW], BF16)
            self.d_dsa = dscr("s_sa", [128, 2 * 8 * 128], BF16)
            self.d_dsmall = dscr("s_small", [128, 8], F32)
        self.d_bc = nc.dram_tensor("s_bc", [2, 3, 512], F32, kind="Internal")
        self.d_VA = dscr("s_VECA", [8, 384], F32)
        self.d_VD = dscr("s_VECD", [4, VECD], F32)

    def setup(self):
        P, A, nc = self.P, self.A, self.nc
        ps, psB = self.ps, self.psB
        cB = P.buf("consts")
        self.cB = cB
        self.zero1 = A.alloc(1, F32)
        self.eps1 = A.alloc(1, F32)
        self.tiny1 = A.alloc(1, F32)
        self.ones_bf = A.alloc(128, BF16)
        self.ones_f = A.alloc(128, F32)
        self.bd_bf = A.alloc(128, BF16)
        self.J = A.alloc(128, F32)
        bd_f = A.alloc(128, F32)
        self.memset("pool", self.zero1, 0.0, [cB])
        self.memset("pool", self.eps1, EPS, [cB])
        self.memset("pool", self.tiny1, 1e-30, [cB])
        self.memset("pool", self.ones_bf, 1.0, [cB])
        self.memset("pool", self.ones_f, 1.0, [cB])
        jB = P.buf("J")
        P.dma(self.J, self.d_J.ap(), writes=[jB], key="c0")
        P.dma(bd_f, self.d_bd.ap(), writes=[jB], key="c0")
        self.cp("dve", self.bd_bf, bd_f, [jB], [cB])
        sB = P.buf("small")
        self.gmix = A.alloc(KC, F32)
        self.gffn = A.alloc(KC, F32)
        self.cw = A.alloc(3 * 44, F32)
        self.cb = A.alloc(44, F32)
        self.m0 = A.alloc(1, F32)
        P.dma(self.gmix, self.d_gmix.ap(), writes=[sB], key="c1")
        P.dma(self.gffn, self.d_gffn.ap(), writes=[sB], key="c1")
        P.dma(self.cw, self.d_cw.ap().rearrange("p a b -> p (a b)"), writes=[sB], key="c1")
        P.dma(self.cb, self.d_cb.ap(), writes=[sB], key="c1")
        P.dma(self.m0, self.d_m0.ap(), writes=[sB], key="c1")
        self.gq_a = A.alloc(1, F32)
        self.gk_a = A.alloc(1, F32)
        self.gq_b = A.alloc(1, F32)
        self.gk_b = A.alloc(1, F32)
        for dst, src in ((self.gq_a, self.d_qna), (self.gk_a, self.d_kna), (self.gq_b, self.d_qnb), (self.gk_b, self.d_knb)):
            for hlf in range(2):
                P.dma(dst[64 * hlf:64 * hlf + 64, :], bass.AP(src, 0, [[1, 64], [1, 1]]), writes=[sB], key="c1")
        self.gsub = A.alloc(1, F32)
        P.dma(self.gsub, self.d_subln.ap(), writes=[sB], key="c1")
        self.ts("dve", self.gsub, self.gsub, 1.0 - LAM_INIT, None, ALU.mult, None, [sB], [sB])
        lam4 = A.alloc(4 * 64, F32)
        for n_, src in enumerate((self.d_lq1, self.d_lk1, self.d_lq2, self.d_lk2)):
            P.dma(lam4[:, n_ * 64:(n_ + 1) * 64], bass.AP(src, 0, [[0, 128], [1, 64]]), writes=[sB], key="c1")
        lp = A.alloc(2 * 64, F32)
        ls = A.alloc(2, F32)
        self.neglam = A.alloc(1, F32)
        self.tt("dve", lp[:, 0:64], lam4[:, 0:64], lam4[:, 64:128], ALU.mult, [sB], [sB])
        self.tt("dve", lp[:, 64:128], lam4[:, 128:192], lam4[:, 192:256], ALU.mult, [sB], [sB])
        P.op("dve", lambda e: e.reduce_sum(out=ls[:, 0:1], in_=lp[:, 0:64], axis=mybir.AxisListType.X), [sB], [sB])
        P.op("dve", lambda e: e.reduce_sum(out=ls[:, 1:2], in_=lp[:, 64:128], axis=mybir.AxisListType.X), [sB], [sB])
        self.act(ls, ls, AF.Exp, [sB, cB], [sB])
        self.tt("dve", self.neglam, ls[:, 1:2], ls[:, 0:1], ALU.subtract, [sB], [sB])
        self.ts("dve", self.neglam, self.neglam, -LAM_INIT, None, ALU.add, None, [sB], [sB])
        self.sB = sB
        self.esrow = A.alloc(2 * 512, F32)
        sk = A.alloc(8, F32)
        P.dma(sk[0:1, :], self.d_sinks.ap(), writes=[sB], key="c1")
        self.act(sk[0:1, :], sk[0:1, :], AF.Exp, [sB, cB], [sB])
        for hq in range(8):
            g_, r_ = hq // 4, hq % 4
            col = g_ * 512 + ((r_ % 2) * 2 + r_ // 2) * 128
            self.ts("dve", self.esrow[0:1, col:col + 128], self.ones_f[0:1, 0:128], sk[0:1, hq:hq + 1], None, ALU.mult, None, [sB, cB], [sB])

        tB = P.buf("tab")
        tabp = A.alloc(12, F32)
        tab31 = A.alloc(4, F32)
        self.memset("pool", tabp[32:33, :], -30000.0, [tB])
        P.dma(tabp[0:32, :], self.d_relb.ap(), writes=[tB], key="c2")
        P.dma(tab31[0:32, :], bass.AP(self.d_relb, 31 * 12 + 8, [[0, 32], [1, 4]]), writes=[tB], key="c2")
        self.tt("dve", tabp[0:32, 8:12], tabp[0:32, 8:12], tab31[0:32, :], ALU.subtract, [tB], [tB])
        m = A.mark()
        oha = A.alloc(384, F32)
        ohd = A.alloc(VECD, F32)
        veca = A.alloc(384, F32)
        vecd = A.alloc(VECD, F32)
        ohB = P.buf("oh")
        P.dma(oha[0:33, :], self.d_oha.ap(), writes=[ohB], key="c2")
        P.dma(ohd[0:33, :], self.d_ohd.ap(), writes=[ohB], key="c2")
        vB = P.buf("vec")
        self.mm(ps[0:8, 0, 0:384], tabp[0:33, 0:8], oha[0:33, :], True, True, [tB, ohB], [psB[0]])
        self.act(veca[0:8, :], ps[0:8, 0, 0:384], AF.Exp, [psB[0], cB], [vB])
        for pc in range(6):
            w = 512 if pc < 5 else VECD - 2560
            bk = 1 + pc % 2
            self.mm(ps[0:4, bk, 0:w], tabp[0:33, 8:12], ohd[0:33, pc * 512: pc * 512 + w], True, True, [tB, ohB], [psB[bk]])
            self.act(vecd[0:4, pc * 512: pc * 512 + w], ps[0:4, bk, 0:w], AF.Exp, [psB[bk], cB], [vB])
        dvB = P.buf("dvec")
        P.dma(self.d_VA.ap(), veca[0:8, :], reads=[vB], writes=[dvB], key="c3")
        P.dma(self.d_VD.ap(), vecd[0:4, :], reads=[vB], writes=[dvB], key="c3")
        A.release(m)
        self.pre_strip_mark = A.mark()
        self.strip = A.alloc(4 * STRIPW, BF16).rearrange("p (h u) -> p h u", h=4)
        self.sa = A.alloc(2 * 8 * 128, BF16).rearrange("p (t h q) -> p t h q", t=2, h=8)
        self.stripB = P.buf("strip")
        self.persist_mark = A.mark()
        m = A.mark()
        rev = A.alloc(STRIPW, F32)
        revB = P.bufs(2, "rev")
        P.dma(rev[:, 0:2048].rearrange("p (h u) -> p h u", h=8), bass.AP(self.d_VA, 0, [[1, 128], [384, 8], [1, 256]]),
              reads=[dvB], writes=[revB[0]], key="c4")
        for pi in range(4):
            bk = pi % 2
            self.mm(ps[:, bk, :], self.J, rev[:, pi * 512:(pi + 1) * 512], True, True, [jB, revB[0]], [psB[bk]])
            src = ps[:, bk, :].rearrange("p (h t q) -> p h t q", h=2, t=2)
            for ty in range(2):
                self.cp("dve" if ty == 0 else "act", self.sa[:, ty, 2 * pi:2 * pi + 2, :], src[:, :, ty, :], [psB[bk]], [self.stripB])
        for h in range(4):
            rb = revB[(h + 1) % 2]
            P.dma(rev, bass.AP(self.d_VD, h * VECD, [[1, 128], [1, STRIPW]]), reads=[dvB], writes=[revB[0], revB[1]], key="c4")
            for pc in range(5):
                bk = pc % 2
                self.mm(ps[:, bk, :], self.J, rev[:, pc * 512:(pc + 1) * 512], True, True, [jB, revB[0], revB[1]], [psB[bk]])
                self.cp("dve" if pc % 2 == 0 else "act", self.strip[:, h, pc * 512:(pc + 1) * 512], ps[:, bk, :], [psB[bk]], [self.stripB])
        A.release(m)
        if DEBUG_SCRATCH:
            P.dma(self.d_dstrip.ap(), self.strip.rearrange("p h u -> p (h u)"), reads=[self.stripB])
            P.dma(self.d_dsa.ap(), self.sa.rearrange("p t h q -> p (t h q)"), reads=[self.stripB])
            for n_, t_ in enumerate((self.neglam, self.gsub, self.gq_b, self.gk_b)):
                P.dma(self.d_dsmall.ap()[:, n_:n_ + 1], t_, reads=[self.sB], allow_slow_non_contiguous=True)

    def load_w(self, dst, src, kc, ncols, gain=None, dup=None, key="w"):
        P, A = self.P, self.A
        m = A.mark()
        pw = 512 if kc <= 8 else 128
        stg = [A.alloc(kc * pw, F32).rearrange("p (c n) -> p c n", c=kc) for _ in range(2)]
        sb = P.bufs(2, "stg")
        wB = P.buf("wdst")
        engs = ("dve", "pool", "act")
        n = 0
        for i, c0 in enumerate(range(0, ncols, pw)):
            w = min(pw, ncols - c0)
            s, b = stg[i % 2], sb[i % 2]
            P.dma(s[:, :, 0:w], src[:, :, c0:c0 + w], writes=[b], key=f"{key}{i % 2}")
            for c in range(kc):
                eng = engs[n % 3]
                n += 1
                if gain is None:
                    self.cp(eng, dst[:, c, c0:c0 + w], s[:, c, 0:w], [b], [wB])
                elif eng == "act":
                    self.act(dst[:, c, c0:c0 + w], s[:, c, 0:w], AF.Identity, [b, self.sB, self.cB], [wB], scale=gain[:, c:c + 1])
                else:
                    self.ts(eng, dst[:, c, c0:c0 + w], s[:, c, 0:w], gain[:, c:c + 1], None, ALU.mult, None, [b, self.sB], [wB])
        self.P.barrier()
        A.release(m)
        return wB

    def norm_tile(self, xt, xB, n, sq, sqB, hT, hB, rstd, rB, pieces):
        ps, psB = self.ps, self.psB
        for c in range(KC):
            self.act(sq[c % 2][:, 0:n], xt[:, c, :], AF.Square, [xB, self.cB], [sqB[c % 2]])
            for (o, w, bank) in pieces:
                self.mm(ps[:, bank, 0:w], self.ones_bf, sq[c % 2][:, o:o + w], c == 0, c == KC - 1, [sqB[c % 2], self.cB], [psB[bank]])
        for (o, w, bank) in pieces:
            self.act(rstd[:, o:o + w], ps[:, bank, 0:w], AF.Sqrt, [psB[bank], self.cB], [rB], bias=self.eps1, scale=1.0 / D)
        self.rcp(rstd[:, 0:n], rstd[:, 0:n], [rB], [rB])
        for c in range(KC):
            self.tt("dve" if c % 2 == 0 else "pool", hT[:, c, 0:n], xt[:, c, :], rstd[:, 0:n], ALU.mult, [xB, rB], [hB[c]])

    def proj_headnorm(self, wfn, hT, hB, o, w, out, outB, gain, bk_p, bk_n, tmp, tmpB, wB):
        ps, psB = self.ps, self.psB
        for c in range(KC):
            self.mm(ps[:, bk_p, 0:w], wfn(c), hT[:, c, o:o + w], c == 0, c == KC - 1, [wB, hB[c]], [psB[bk_p]])
        ksq, rk = tmp
        self.act(ksq[:, 0:w], ps[:, bk_p, 0:w], AF.Square, [psB[bk_p], self.cB], [tmpB[0]])
        self.mm(ps[:, bk_n, 0:w], self.bd_bf, ksq[:, 0:w], True, True, [tmpB[0], self.cB], [psB[bk_n]])
        self.act(rk[:, 0:w], ps[:, bk_n, 0:w], AF.Sqrt, [psB[bk_n], self.cB], [tmpB[1]], bias=self.eps1, scale=1.0 / 64)
        self.rcp(rk[:, 0:w], rk[:, 0:w], [tmpB[1]], [tmpB[1]])
        self.stt("dve", out, ps[:, bk_p, 0:w], gain, rk[:, 0:w], ALU.mult, ALU.mult, [psB[bk_p], tmpB[1], self.sB], [outB])

    def phase1(self):
        P, A = self.P, self.A
        ps, psB = self.ps, self.psB
        m = A.mark()
        wkb = A.alloc(KC * 512, BF16).rearrange("p (c n) -> p c n", c=KC)
        wvb = A.alloc(KC * 512, BF16).rearrange("p (c n) -> p c n", c=KC)
        win = self.d_win.ap()
        wB1 = self.load_w(wkb, win[:, :, C_KB:C_KB + 512], KC, 512, gain=self.gmix)
        wB2 = self.load_w(wvb, win[:, :, C_VB:C_VB + 512], KC, 512, gain=self.gmix)
        xt = [A.alloc(KC * 512, F32).rearrange("p (c n) -> p c n", c=KC) for _ in range(2)]
        xB = P.bufs(2, "x")
        sq = [A.alloc(512, BF16) for _ in range(2)]
        sqB = P.bufs(2, "sq")
        hT = [A.alloc(KC * 512, BF16).rearrange("p (c n) -> p c n", c=KC) for _ in range(2)]
        hB = [P.bufs(KC, "h") for _ in range(2)]
        rstd = [A.alloc(512, F32) for _ in range(2)]
        rB = P.bufs(2, "r")
        tmp = [(A.alloc(512, BF16), A.alloc(512, F32)) for _ in range(2)]
        tmpB = [P.bufs(2, "tmp") for _ in range(2)]
        kout = [A.alloc(4 * 512, BF16).rearrange("p (h n) -> p h n", h=4) for _ in range(2)]
        koB = [P.bufs(4, "ko") for _ in range(2)]
        vout = [A.alloc(4 * 512, BF16).rearrange("p (b n) -> p b n", b=4) for _ in range(2)]
        voB = [P.bufs(4, "vo") for _ in range(2)]
        xT = self.d_xT.ap()
        NT = S // 512

        def stats(T):
            pp = T % 2
            P.dma(xt[pp], xT[:, :, T * 512:(T + 1) * 512], writes=[xB[pp]], key=f"x{pp}")
            self.norm_tile(xt[pp], xB[pp], 512, sq, sqB, hT[pp], hB[pp], rstd[pp], rB[pp], [(0, 512, 0)])

        stats(0)
        for T in range(NT):
            pp = T % 2
            if T + 1 < NT:
                stats(T + 1)
            for hh in range(4):
                self.proj_headnorm(lambda c, hh=hh: wkb[:, c, hh * 128:(hh + 1) * 128], hT[pp], hB[pp], 0, 512,
                                   kout[pp][:, hh, :], koB[pp][hh], self.gk_b, 1 + hh % 2, 3, tmp[hh % 2], tmpB[hh % 2], wB1)
            for hh in range(4):
                P.dma(self.d_KT.ap()[hh, :, T * 512:(T + 1) * 512], kout[pp][:, hh, :], reads=[koB[pp][hh]], key=f"ko{pp}")
            for blk in range(4):
                bk = 4 + blk % 2
                for c in range(KC):
                    self.mm(ps[:, bk, :], hT[pp][:, c, blk * 128:(blk + 1) * 128], wvb[:, c, :], c == 0, c == KC - 1,
                            [hB[pp][c], wB2], [psB[bk]])
                self.cp("act" if blk % 2 == 0 else "dve", vout[pp][:, blk, :], ps[:, bk, :], [psB[bk]], [voB[pp][blk]])
            for hh in range(4):
                P.dma(self.d_VS.ap()[hh, :, 4 * T:4 * T + 4, :], vout[pp][:, :, hh * 128:(hh + 1) * 128],
                      reads=voB[pp], key=f"vo{pp}")
        P.barrier()
        A.release(m)

    def phase2(self):
        P, A = self.P, self.A
        ps, psB = self.ps, self.psB
        m = A.mark()
        win = self.d_win.ap()

        def walloc(n):
            return A.alloc(KC * n, BF16).rearrange("p (c n) -> p c n", c=KC)

        wqa, wka2, wva, wqb = walloc(512), walloc(256), walloc(128), walloc(512)
        wBq = self.load_w(wqa, win[:, :, C_QA:C_QA + 512], KC, 512, gain=self.gmix)
        for g in range(2):
            for hlf in range(2):
                self.load_w(wka2[:, :, g * 128 + hlf * 64: g * 128 + hlf * 64 + 64], win[:, :, C_KA + 64 * g:C_KA + 64 * g + 64], KC, 64,
                            gain=self.gmix)
        self.load_w(wva, win[:, :, C_VA:C_VA + 128], KC, 128, gain=self.gmix)
        self.load_w(wqb, win[:, :, C_QB:C_QB + 512], KC, 512, gain=self.gmix)
        wB = P.buf("w2")
        xt = A.alloc(KC * SLOTW, F32).rearrange("p (c n) -> p c n", c=KC)
        xB = P.buf("x")
        sq = [A.alloc(SLOTW, BF16) for _ in range(2)]
        sqB = P.bufs(2, "sq")
        hT = A.alloc(KC * SLOTW, BF16).rearrange("p (c n) -> p c n", c=KC)
        hB = P.bufs(KC, "h")
        rstd = A.alloc(SLOTW, F32)
        rB = P.buf("r")
        tmp = [(A.alloc(512, BF16), A.alloc(512, F32)) for _ in range(2)]
        tmpB = [P.bufs(2, "tmp") for _ in range(2)]
        qaT = A.alloc(4 * HW, BF16).rearrange("p (c n) -> p c n", c=4)
        qaB = P.bufs(4, "qa")
        kaT = A.alloc(2 * SLOTW, BF16).rearrange("p (g n) -> p g n", g=2)
        kaB = P.bufs(2, "ka")
        va = A.alloc(6 * 128, BF16).rearrange("p (b n) -> p b n", b=6)
        vaB = P.buf("va")
        qbT = A.alloc(4 * HW, BF16).rearrange("p (c n) -> p c n", c=4)
        qbB = P.buf("qb")
        yaT = A.alloc(4 * HW, BF16).rearrange("p (c n) -> p c n", c=4)
        yaB = P.buf("ya")
        pt = [A.alloc(512, BF16) for _ in range(4)]
        ptB = P.bufs(4, "pt")
        den = A.alloc(512, F32)
        denB = P.buf("den")
        xo = self.d_xo.ap()
        for i in range(min(NSLOT, P2_SLOTS)):
            P.dma(xt, xo[:, :, i * SLOTW:(i + 1) * SLOTW], writes=[xB], key="x2")
            self.norm_tile(xt, xB, SLOTW, sq, sqB, hT, hB, rstd, rB, [(0, 512, 0), (512, 256, 7)])
            cnt = 0
            for (o, w) in ((128, 512), (640, 128)):
                for cm in range(4):
                    self.proj_headnorm(lambda c, cm=cm: wqa[:, c, cm * 128:(cm + 1) * 128], hT, hB, o, w,
                                       qaT[:, cm, o - 128:o - 128 + w], qaB[cm], self.gq_a, 1 + cnt % 2, 3, tmp[cnt % 2], tmpB[cnt % 2], wB)
                    cnt += 1
                for cm in range(4):
                    self.proj_headnorm(lambda c, cm=cm: wqb[:, c, cm * 128:(cm + 1) * 128], hT, hB, o, w,
                                       qbT[:, cm, o - 128:o - 128 + w], qbB, self.gq_b, 1 + cnt % 2, 3, tmp[cnt % 2], tmpB[cnt % 2], wB)
                    cnt += 1
            for (o, w) in ((0, 512), (512, 256)):
                for g in range(2):
                    self.proj_headnorm(lambda c, g=g: wka2[:, c, g * 128:(g + 1) * 128], hT, hB, o, w,
                                       kaT[:, g, o:o + w], kaB[g], self.gk_a, 1 + cnt % 2, 3, tmp[cnt % 2], tmpB[cnt % 2], wB)
                    cnt += 1
            P.dma(self.d_QB.ap()[i], qbT, reads=[qbB], key="qbo")
            for half in range(2):
                bk = 4 + half
                for bl in range(3):
                    blk = half * 3 + bl
                    for c in range(KC):
                        self.mm(ps[:, bk, bl * 128:(bl + 1) * 128], hT[:, c, blk * 128:(blk + 1) * 128], wva[:, c, :], c == 0, c == KC - 1,
                                [hB[c], wB], [psB[bk]])
                self.cp("act", va[:, half * 3:half * 3 + 3, :], ps[:, bk, 0:384].rearrange("p (b n) -> p b n", b=3), [psB[bk]], [vaB])
            for n in range(1, 6 if P2_STAGE >= 1 else 0):
                qo = (n - 1) * 128
                for g in range(2):
                    for kk, kblk in enumerate((n - 1, n)):
                        idx = g * 2 + kk
                        b0 = (2, 6)[idx % 2]
                        for r in range(4):
                            par, rr = r % 2, r // 2
                            pb = 64 * par
                            self.mm(ps[:, b0 + par, rr * 128:(rr + 1) * 128], kaT[pb:pb + 64, g, kblk * 128:(kblk + 1) * 128],
                                    qaT[pb:pb + 64, 2 * g + rr, qo:qo + 128], True, True,
                                    [kaB[g], qaB[2 * g + rr]], [psB[b0 + par]])
                        self.act(pt[idx].rearrange("p (a n) -> p a n", a=2), ps[:, b0:b0 + 2, 0:256], AF.Exp,
                                 [psB[b0], psB[b0 + 1], self.cB], [ptB[idx]], scale=0.125)
                        ty = 1 if kk == 0 else 0
                        fa = self.sa[:, ty, 4 * g:4 * g + 4, :].rearrange("p (rr par) q -> p par rr q", par=2)
                        p4 = pt[idx].rearrange("p (par rr q) -> p par rr q", par=2, rr=2)
                        self.tt("pool", p4, p4, fa, ALU.mult, [ptB[idx], self.stripB], [ptB[idx]])
                        if i == 0 and n == 2 and kk == 0:
                            self.ts("pool", pt[idx], pt[idx], self.m0[:, 0:1], None, ALU.mult, None, [ptB[idx], self.sB], [ptB[idx]])
                if P2_STAGE < 2:
                    continue
                for g in range(2):
                    for kk, kblk in enumerate((n - 1, n)):
                        idx = g * 2 + kk
                        self.mm(ps[64 * g:64 * g + 64, 4, :], va[:, kblk, 64 * g:64 * g + 64], pt[idx], kk == 0, kk == 1,
                                [vaB, ptB[idx]], [psB[4]])
                    for kk in range(2):
                        idx = g * 2 + kk
                        self.mm(ps[64 * g:64 * g + 64, 5, :], self.ones_bf[:, 0:64], pt[idx], kk == 0, (kk == 1 and P2_STAGE < 3),
                                [ptB[idx], self.cB], [psB[5]])
                    if P2_STAGE >= 3:
                        self.mm(ps[64 * g:64 * g + 64, 5, :], self.ones_f[0:1, 0:64], self.esrow[0:1, g * 512:(g + 1) * 512], False, True,
                                [self.sB, self.cB], [psB[5]])
                self.rcp(den, ps[:, 5, :], [psB[5]], [denB])
                self.tt("dve", yaT[:, :, qo:qo + 128].rearrange("p (rr par) q -> p par rr q", par=2),
                        ps[:, 4, :].rearrange("p (par rr q) -> p par rr q", par=2, rr=2),
                        den.rearrange("p (par rr q) -> p par rr q", par=2, rr=2), ALU.mult, [psB[4], denB], [yaB])
            P.dma(self.d_YA.ap()[i], yaT, reads=[yaB], key="yao")
        P.barrier()
        A.release(m)

    def phase3(self):
        P, A = self.P, self.A
        ps, psB = self.ps, self.psB
        m = A.mark()
        KTs = [A.alloc(S, BF16) for _ in range(2)]
        VSs = [A.alloc(128 * 128, BF16).rearrange("p (b e) -> p b e", b=128) for _ in range(2)]
        kvB = [(P.bufs(4, "kt"), P.bufs(4, "vs")) for _ in range(2)]
        qt = [A.alloc(HW, BF16) for _ in range(2)]
        qB = P.bufs(2, "q")
        NPB = 3
        pt = [A.alloc(1024, BF16) for _ in range(NPB)]
        ptB = P.bufs(NPB, "pt")
        ssum = [A.alloc(512, F32) for _ in range(2)]
        ssumB = P.bufs(2, "ssum")
        rbc = [A.alloc(1024, F32) for _ in range(2)]
        rbcB = P.bufs(2, "rbc")
        dd = [A.alloc(1024, F32) for _ in range(2)]
        ddB = P.bufs(2, "dd")
        dsq = [A.alloc(512, BF16) for _ in range(2)]
        dsqB = P.bufs(2, "dsq")
        rrow = [A.alloc(512, F32) for _ in range(2)]
        rrowB = P.bufs(2, "rrow")
        rrbc = [A.alloc(512, F32) for _ in range(2)]
        rrbcB = P.bufs(2, "rrbc")
        yb = [A.alloc(HW, BF16) for _ in range(2)]
        ybB = P.bufs(2, "yb")
        dbcB = [P.bufs(2, "dbc") for _ in range(2)]
        q7B = P.bufs(4, "ps7q")
        ones32 = self.ones_bf[:, 0:32]
        SB = [(0, 1), (2, 3)]
        dbc = self.d_bc

        def load_kv(h):
            pp = h % 2
            for q4 in range(4):
                P.dma(KTs[pp][:, q4 * 4096:(q4 + 1) * 4096], self.d_KT.ap()[h, :, q4 * 4096:(q4 + 1) * 4096], writes=[kvB[pp][0][q4]])
            for q4 in range(4):
                P.dma(VSs[pp][:, q4 * 32:(q4 + 1) * 32, :], self.d_VS.ap()[h, :, q4 * 32:(q4 + 1) * 32, :], writes=[kvB[pp][1][q4]])

        pending = []
        gcount = [0]

        def flush():
            while pending:
                pending.pop(0)()

        def finalize(ncol, o_ap, obufs, ycols, ybt, ybb, out_dma):
            gp = gcount[0] % 2
            gcount[0] += 1
            ss_, rb_, dd_, dq_, rw_, rrb_ = ssum[gp], rbc[gp], dd[gp], dsq[gp], rrow[gp], rrbc[gp]
            self.cp("dve", ss_[0:64, 0:ncol], ps[0:64, 7, 0:ncol], [q7B[0], q7B[1]], [ssumB[gp]])
            for comp in range(2):
                P.dma(dbc.ap()[gp, comp:comp + 1, 0:ncol], ss_[32 * comp:32 * comp + 1, 0:ncol], reads=[ssumB[gp]], writes=[dbcB[gp][0]])
            P.dma(rb_.rearrange("p (c n) -> p c n", c=2)[:, :, 0:ncol], bass.AP(dbc, gp * 3 * 512, [[0, 128], [512, 2], [1, ncol]]),
                  reads=[dbcB[gp][0]], writes=[rbcB[gp]])
            for comp in range(2):
                r_ = rb_[:, comp * 512:comp * 512 + ncol]
                self.ts("dve", r_, r_, self.tiny1[:, 0:1], None, ALU.add, None, [rbcB[gp], self.cB], [rbcB[gp]])
                self.rcp(r_, r_, [rbcB[gp]], [rbcB[gp]])
                self.tt("dve", dd_[:, comp * 512: comp * 512 + ncol], o_ap(comp), r_, ALU.mult, [obufs[comp], rbcB[gp]], [ddB[gp]])
            self.stt("dve", dd_[:, 0:ncol], dd_[:, 512:512 + ncol], self.neglam[:, 0:1], dd_[:, 0:ncol], ALU.mult, ALU.add, [ddB[gp], self.sB], [ddB[gp]])
            self.tt("pool", dq_[:, 0:ncol], dd_[:, 0:ncol], dd_[:, 0:ncol], ALU.mult, [ddB[gp]], [dsqB[gp]])

            def stage1():
                self.mm(ps[64:96, 7, 0:ncol], ones32, dq_[:, 0:ncol], True, True, [dsqB[gp], self.cB], [q7B[2]])
                self.act(rw_[64:96, 0:ncol], ps[64:96, 7, 0:ncol], AF.Ln, [q7B[2], self.cB], [rrowB[gp]], bias=self.eps1[64:96, :], scale=1.0 / 128)
                P.strict = True
                self.act(rw_[64:96, 0:ncol], rw_[64:96, 0:ncol], AF.Exp, [rrowB[gp], self.cB], [rrowB[gp]], bias=self.zero1[64:96, :], scale=-0.5)
                P.strict = False
                P.dma(dbc.ap()[gp, 2:3, 0:ncol], rw_[64:65, 0:ncol], reads=[rrowB[gp]], writes=[dbcB[gp][1]])
                P.dma(rrb_[:, 0:ncol], bass.AP(dbc, (gp * 3 + 2) * 512, [[0, 128], [1, ncol]]), reads=[dbcB[gp][1]], writes=[rrbcB[gp]])
                self.stt("dve", ybt[:, ycols:ycols + ncol], dd_[:, 0:ncol], self.gsub[:, 0:1], rrb_[:, 0:ncol], ALU.mult, ALU.mult,
                         [ddB[gp], rrbcB[gp], self.sB], [ybb])
                if out_dma is not None:
                    out_dma()
            pending.append(stage1)

        def group(nsteps, qk, av, near_off, ncol, ncols_sum, sum_rhs):
            qk(0)
            for t in range(nsteps):
                if t + 1 < nsteps:
                    qk(t + 1)
                sb = SB[t % 2]
                p_, pB_ = pt[t % NPB], ptB[t % NPB]
                self.act(p_.rearrange("p (c n) -> p c n", c=2), ps[:, sb[0]:sb[0] + 2, :], AF.Exp, [psB[sb[0]], psB[sb[1]], self.cB], [pB_], scale=0.125)
                off = near_off(t)
                if off is not None:
                    for comp in range(2):
                        self.tt("dve" if comp == 0 else "pool", p_[:, comp * 512:(comp + 1) * 512], p_[:, comp * 512:(comp + 1) * 512],
                                self.strip_h[:, off:off + 512], ALU.mult, [pB_, self.stripB], [pB_])
                for comp in range(2):
                    rl = sum_rhs(p_, comp)
                    for n_, r_ in enumerate(rl):
                        self.mm(ps[32 * comp:32 * comp + 32, 7, 0:ncol], ones32, r_, t == 0 and n_ == 0, t == nsteps - 1 and n_ == len(rl) - 1,
                                [pB_, self.cB], [q7B[comp]])
                av(t, p_, pB_)
                if t == 3:
                    flush()
            flush()

        load_kv(0)
        cnt_q = 0
        for h in range(4):
            pp = h % 2
            KT, VS = KTs[pp], VSs[pp]
            kB, vB = kvB[pp]
            self.strip_h = self.strip[:, h, :]
            if h + 1 < 4:
                load_kv(h + 1)
            for i in range(NSLOT):
                qq = cnt_q % 2
                cnt_q += 1
                q, qb_ = qt[qq], qB[qq]
                P.dma(q, self.d_QB.ap()[i, :, h, :], writes=[qb_])
                ybt, ybb = yb[qq], ybB[qq]
                nkb = 16 * i + 16
                near0 = 16 * i - 1

                def qk(t, KT=KT, q=q, kB=kB, qb_=qb_):
                    sb = SB[t % 2]
                    for comp in range(2):
                        self.mm(ps[:, sb[comp], :], KT[64 * comp:64 * comp + 64, t * 128:(t + 1) * 128], q[64 * comp:64 * comp + 64, 128:640],
                                True, True, [kB[t // 32], qb_], [psB[sb[comp]]])

                def av(t, p_, pB_, VS=VS, vB=vB, nkb=nkb):
                    for comp in range(2):
                        self.mm(ps[:, 4 + comp, :], VS[:, t, :], p_[:, comp * 512:(comp + 1) * 512], t == 0, t == nkb - 1,
                                [vB[t // 32], pB_], [psB[4 + comp]])

                group(nkb, qk, av, lambda t, near0=near0: (2048 - 128 * (t - near0)) if t >= near0 else None, 512, 512,
                      lambda p_, comp: [p_[:, comp * 512:(comp + 1) * 512]])
                finalize(512, lambda comp: ps[:, 4 + comp, :], [psB[4], psB[5]], 128, ybt, ybb, None)
                nb = (16 * i + 12) // 4

                def qkh(t, KT=KT, q=q, kB=kB, qb_=qb_):
                    sb = SB[t % 2]
                    for comp in range(2):
                        for u in range(4):
                            kb = 4 * t + 3 - u
                            self.mm(ps[:, sb[comp], u * 128:(u + 1) * 128], KT[64 * comp:64 * comp + 64, kb * 128:(kb + 1) * 128],
                                    q[64 * comp:64 * comp + 64, 0:128], True, True, [kB[kb // 32], qb_], [psB[sb[comp]]])

                def avh(t, p_, pB_, VS=VS, vB=vB, nb=nb):
                    for comp in range(2):
                        for u in range(4):
                            kb = 4 * t + 3 - u
                            self.mm(ps[:, 6, comp * 128:(comp + 1) * 128], VS[:, kb, :], p_[:, comp * 512 + u * 128: comp * 512 + (u + 1) * 128],
                                    t == 0 and u == 0 and comp == 0, t == nb - 1 and u == 3, [vB[kb // 32], pB_], [psB[6]])

                def odma(i=i, h=h, ybt=ybt, ybb=ybb):
                    P.dma(self.d_YB.ap()[i, :, h, :], ybt, reads=[ybb])

                group(nb, qkh, avh, lambda t, i=i: (2048 - 128 * (4 * t - 16 * i + 5)) if 4 * t >= 16 * i - 4 else None, 128, 128,
                      lambda p_, comp: [p_[:, comp * 512 + u * 128: comp * 512 + (u + 1) * 128] for u in range(4)])
                finalize(128, lambda comp: ps[:, 6, comp * 128:(comp + 1) * 128], [psB[6], psB[6]], 0, ybt, ybb, odma)
        flush()
        P.barrier()
        A.release(m)

    def phase4a(self):
        P, A = self.P, self.A
        ps, psB = self.ps, self.psB
        m = A.mark()
        win = self.d_win.ap()
        wgl = A.alloc(KC * 2048, BF16).rearrange("p (c n) -> p c n", c=KC)
        wbra = A.alloc(4 * D, BF16).rearrange("p (c n) -> p c n", c=4)
        wbrb = A.alloc(4 * D, BF16).rearrange("p (c n) -> p c n", c=4)
        wo = A.alloc(KC * D, BF16).rearrange("p (c n) -> p c n", c=KC)
        self.load_w(wgl, win[:, :, C_GL:C_GL + 2048], KC, 2048, gain=self.gmix)
        self.load_w(wbra, self.d_wbra.ap(), 4, D)
        self.load_w(wbrb, self.d_wbrb.ap(), 4, D)
        self.load_w(wo, self.d_wo.ap(), KC, D)
        wB = P.buf("w4a")
        xt = A.alloc(KC * HW, F32).rearrange("p (c n) -> p c n", c=KC)
        xB = P.buf("x")
        sq = [A.alloc(HW, BF16) for _ in range(2)]
        sqB = P.bufs(2, "sq")
        hT = A.alloc(KC * HW, BF16).rearrange("p (c n) -> p c n", c=KC)
        hB = P.bufs(KC, "h")
        rstd = A.alloc(HW, F32)
        rB = P.buf("r")
        gates = A.alloc(16 * HW, BF16).rearrange("p (c n) -> p c n", c=16)
        gB = P.bufs(16, "g")
        ya = A.alloc(4 * HW, BF16).rearrange("p (c n) -> p c n", c=4)
        yb = A.alloc(4 * HW, BF16).rearrange("p (c n) -> p c n", c=4)
        yB = P.bufs(2, "y")
        mixed = A.alloc(KC * HW, BF16).rearrange("p (c n) -> p c n", c=KC)
        mxB = P.bufs(KC, "mx")
        t1 = [A.alloc(512, BF16) for _ in range(2)]
        t2 = [A.alloc(512, BF16) for _ in range(2)]
        tB = [P.bufs(2, "t") for _ in range(2)]
        x2 = [A.alloc(512, F32) for _ in range(2)]
        x2B = P.bufs(2, "x2")
        xo = self.d_xo.ap()
        PIECES = ((0, 512), (512, 128))
        cnt = 0
        for i in range(NSLOT):
            P.dma(xt, xo[:, :, i * SLOTW + 128:(i + 1) * SLOTW], writes=[xB], key="x4")
            P.dma(ya, self.d_YA.ap()[i], writes=[yB[0]], key="ya")
            P.dma(yb, self.d_YB.ap()[i], writes=[yB[1]], key="yb")
            self.norm_tile(xt, xB, HW, sq, sqB, hT, hB, rstd, rB, [(0, 512, 0), (512, 128, 7)])
            for (o, w) in PIECES:
                for gc in range(16):
                    bk = 1 + gc % 2
                    for c in range(KC):
                        self.mm(ps[:, bk, 0:w], wgl[:, c, gc * 128:(gc + 1) * 128], hT[:, c, o:o + w], c == 0, c == KC - 1, [wB, hB[c]], [psB[bk]])
                    self.act(gates[:, gc, o:o + w], ps[:, bk, 0:w], AF.Sigmoid, [psB[bk], self.cB], [gB[gc]])
            for (o, w) in PIECES:
                for mc in range(KC):
                    k2 = cnt % 2
                    cnt += 1
                    ba, bb = 3 + 2 * k2, 4 + 2 * k2
                    for r in range(4):
                        self.mm(ps[:, ba, 0:w], wbra[:, r, mc * 128:(mc + 1) * 128], ya[:, r, o:o + w], r == 0, r == 3, [wB, yB[0]], [psB[ba]])
                    for r in range(4):
                        self.mm(ps[:, bb, 0:w], wbrb[:, r, mc * 128:(mc + 1) * 128], yb[:, r, o:o + w], r == 0, r == 3, [wB, yB[1]], [psB[bb]])
                    self.tt("dve", t1[k2][:, 0:w], ps[:, ba, 0:w], gates[:, mc, o:o + w], ALU.mult, [psB[ba], gB[mc]], [tB[k2][0]])
                    self.tt("dve", t2[k2][:, 0:w], ps[:, bb, 0:w], gates[:, 8 + mc, o:o + w], ALU.mult, [psB[bb], gB[8 + mc]], [tB[k2][1]])
                    self.tt("pool", mixed[:, mc, o:o + w], t1[k2][:, 0:w], t2[k2][:, 0:w], ALU.add, [tB[k2][0], tB[k2][1]], [mxB[mc]])
            for (o, w) in PIECES:
                for oc in range(KC):
                    k2 = cnt % 2
                    cnt += 1
                    bk = 1 + k2
                    for mc in range(KC):
                        self.mm(ps[:, bk, 0:w], wo[:, mc, oc * 128:(oc + 1) * 128], mixed[:, mc, o:o + w], mc == 0, mc == KC - 1, [wB, mxB[mc]], [psB[bk]])
                    self.tt("dve", x2[k2][:, 0:w], ps[:, bk, 0:w], xt[:, oc, o:o + w], ALU.add, [psB[bk], xB], [x2B[k2]])
                    P.dma(self.d_X2.ap()[i, :, oc, o:o + w], x2[k2][:, 0:w], reads=[x2B[k2]], key=f"x2o{k2}")
        P.barrier()
        A.release(m)

    def phase4b(self):
        P, A = self.P, self.A
        ps, psB = self.ps, self.psB
        A.release(self.pre_strip_mark)
        m = A.mark()
        wup = A.alloc(KC * 2 * DFF, BF16).rearrange("p (c n) -> p c n", c=KC)
        wdn = A.alloc(NFC * D, BF16).rearrange("p (c n) -> p c n", c=NFC)
        self.load_w(wup, self.d_wup.ap(), KC, 2 * DFF, gain=self.gffn)
        self.load_w(wdn, self.d_wdn.ap(), NFC, D)
        wB = P.buf("w4b")
        W2 = 514
        xa_off = A.alloc_raw(NFC * 512 * 2)
        xaB = P.buf("xa")
        sq = [A.alloc(W2, BF16) for _ in range(2)]
        sqB = P.bufs(2, "sq")
        hT = A.alloc(KC * W2, BF16).rearrange("p (c n) -> p c n", c=KC)
        hB = P.bufs(KC, "h")
        rstd = A.alloc(W2, F32)
        rB = P.buf("r")
        uh = A.alloc(44 * 2, F32).rearrange("p (c n) -> p c n", c=44)
        uhB = P.buf("uh")
        U = [A.alloc(W2, F32) for _ in range(4)]
        UB = P.bufs(4, "U")
        tg = [A.alloc(512, F32) for _ in range(2)]
        tv = [A.alloc(512, F32) for _ in range(2)]
        tgB = P.bufs(2, "tg")
        tvB = P.bufs(2, "tv")
        ot = [A.alloc(512, F32) for _ in range(2)]
        otB = P.bufs(2, "ot")
        xr = [A.alloc(512, F32) for _ in range(2)]
        xrB = P.bufs(2, "xr")
        cw = self.cw.rearrange("p (a b) -> p a b", a=3)
        outT = self.d_out.ap()
        xt = A.at(xa_off, KC * W2, F32).rearrange("p (c n) -> p c n", c=KC)
        aT = A.at(xa_off, NFC * 512, BF16).rearrange("p (c n) -> p c n", c=NFC)
        for i in range(NSLOT):
            P.dma(xt, self.d_X2.ap()[i, :, :, 126:640], writes=[xaB])
            self.norm_tile(xt, xaB, W2, sq, sqB, hT, hB, rstd, rB, [(0, 2, 7), (2, 512, 0)])
            for fc in range(44):
                for c in range(KC):
                    self.mm(ps[:, 7, fc * 2:fc * 2 + 2], wup[:, c, fc * 128:(fc + 1) * 128], hT[:, c, 0:2], c == 0, c == KC - 1, [wB, hB[c]], [psB[7]])
            self.cp("dve", uh, ps[:, 7, 0:88].rearrange("p (c n) -> p c n", c=44), [psB[7]], [uhB])
            for f in range(NFC):
                k2 = f % 2
                for half, fc in enumerate((f, NFC + f)):
                    bk = 1 + 2 * k2 + half
                    ui = 2 * k2 + half
                    for c in range(KC):
                        self.mm(ps[:, bk, :], wup[:, c, fc * 128:(fc + 1) * 128], hT[:, c, 2:W2], c == 0, c == KC - 1, [wB, hB[c]], [psB[bk]])
                    self.cp("pool", U[ui][:, 0:2], uh[:, fc, :], [uhB], [UB[ui]])
                    self.cp("act", U[ui][:, 2:W2], ps[:, bk, :], [psB[bk]], [UB[ui]])
                    t_, tb_ = (tg[k2], tgB[k2]) if half == 0 else (tv[k2], tvB[k2])
                    eng = "dve"
                    self.ts(eng, t_, U[ui][:, 0:512], cw[:, 0, fc:fc + 1], self.cb[:, fc:fc + 1], ALU.mult, ALU.add, [UB[ui], self.sB], [tb_])
                    self.stt(eng, t_, U[ui][:, 1:513], cw[:, 1, fc:fc + 1], t_, ALU.mult, ALU.add, [UB[ui], self.sB, tb_], [tb_])
                    self.stt(eng, t_, U[ui][:, 2:514], cw[:, 2, fc:fc + 1], t_, ALU.mult, ALU.add, [UB[ui], self.sB, tb_], [tb_])
                self.act(tg[k2], tg[k2], AF.Silu, [tgB[k2], self.cB], [tgB[k2]])
                self.tt("pool", aT[:, f, :], tg[k2], tv[k2], ALU.mult, [tgB[k2], tvB[k2]], [xaB])
            for oc in range(KC):
                k2 = oc % 2
                bk = 5 + k2
                P.dma(xr[k2], self.d_X2.ap()[i, :, oc, 128:640], writes=[xrB[k2]])
                for f in range(NFC):
                    self.mm(ps[:, bk, :], wdn[:, f, oc * 128:(oc + 1) * 128], aT[:, f, :], f == 0, f == NFC - 1, [wB, xaB], [psB[bk]])
                self.tt("dve", ot[k2], ps[:, bk, :], xr[k2], ALU.add, [psB[bk], xrB[k2]], [otB[k2]])
                P.dma(outT[:, oc, i * 512:(i + 1) * 512], ot[k2], reads=[otB[k2]])
        P.barrier()
        A.release(m)

    def build(self, nph=N_PHASES):
        self.declare()
        self.P.strict = True
        self.setup()
        self.P.strict = False
        self.P.barrier()
        phases = [self.phase1, self.phase2, self.phase3, self.phase4a, self.phase4b]
        for n_, ph in enumerate(phases[:nph]):
            if n_ == 0 and SKIP_P1:
                continue
            ph()
        self.P.barrier()
        self.P.emit()
        return self.nc


def _host_inputs(inputs):
    f = np.float32
    x = np.asarray(inputs["x"], dtype=f)

    def pc(w, kc):
        n = w.shape[1]
        return np.ascontiguousarray(w.reshape(kc, 128, n).transpose(1, 0, 2))

    w_br_a = np.asarray(inputs["w_br_a"][0], dtype=f)
    wa = w_br_a.reshape(2, 4, 64, D)
    wbra = np.ascontiguousarray(wa.transpose(0, 2, 1, 3).reshape(128, 4, D))
    common = {
        "w_in": pc(np.asarray(inputs["w_in"][0], dtype=f), KC),
        "w_br_a": wbra,
        "w_br_b": pc(np.asarray(inputs["w_br_b"][0], dtype=f), 4),
        "w_o": pc(np.asarray(inputs["w_o"][0], dtype=f), KC),
        "w_up": pc(np.asarray(inputs["w_up"][0], dtype=f), KC),
        "w_down": pc(np.asarray(inputs["w_down"][0], dtype=f), NFC),
        "g_mix": np.ascontiguousarray(np.asarray(inputs["g_mix"][0], dtype=f).reshape(KC, 128).T),
        "g_ffn": np.ascontiguousarray(np.asarray(inputs["g_ffn"][0], dtype=f).reshape(KC, 128).T),
        "conv_w": np.ascontiguousarray(np.asarray(inputs["conv_w"][0], dtype=f).reshape(3, 44, 128).transpose(2, 0, 1)),
        "conv_b": np.ascontiguousarray(np.asarray(inputs["conv_b"][0], dtype=f).reshape(44, 128).T),
        "rel_bias": np.ascontiguousarray(np.asarray(inputs["rel_bias"], dtype=f)),
        "qn_a": np.asarray(inputs["qn_a"], dtype=f).reshape(1, 64),
        "kn_a": np.asarray(inputs["kn_a"], dtype=f).reshape(1, 64),
        "qn_b": np.asarray(inputs["qn_b"], dtype=f).reshape(1, 64),
        "kn_b": np.asarray(inputs["kn_b"], dtype=f).reshape(1, 64),
        "sinks": np.asarray(inputs["sinks"], dtype=f).reshape(1, 8),
        "lam_q1": np.asarray(inputs["lam_q1"], dtype=f).reshape(1, 64),
        "lam_k1": np.asarray(inputs["lam_k1"], dtype=f).reshape(1, 64),
        "lam_q2": np.asarray(inputs["lam_q2"], dtype=f).reshape(1, 64),
        "lam_k2": np.asarray(inputs["lam_k2"], dtype=f).reshape(1, 64),
        "subln_b": np.asarray(inputs["subln_b"], dtype=f).reshape(128, 1),
        "Jmat": np.ascontiguousarray(np.eye(128, dtype=f)[::-1]),
        "bdones": np.kron(np.eye(2, dtype=f), np.ones((64, 64), dtype=f)),
    }
    oha = np.zeros((33, 384), dtype=f)
    for mm_ in range(384):
        d = mm_ - 127
        if 0 <= d < 128:
            oha[int(_t5_bucket_np(np.array(d))), mm_] = 1
        else:
            oha[32, mm_] = 1
    common["oh_a"] = oha
    in_maps = []
    for core in range(8):
        b, j = core // 4, core % 4
        xTb = np.ascontiguousarray(x[b].T.reshape(KC, 128, S).transpose(1, 0, 2))
        xo = np.zeros((128, KC, NSLOT, SLOTW), dtype=f)
        for i in range(NSLOT):
            G = 4 * i + j
            t0 = 512 * G - 256
            lo = max(t0, 0)
            xo[:, :, i, lo - t0:] = xTb[:, :, lo:t0 + SLOTW]
        ohd = np.zeros((33, VECD), dtype=f)
        d = np.arange(VECD) + 512 * j - 2047
        bk = _t5_bucket_np(d)
        for mm_ in range(VECD):
            if d[mm_] >= 0:
                ohd[bk[mm_], mm_] = 1
            else:
                ohd[32, mm_] = 1
        mp = dict(common)
        mp["xT"] = xTb
        mp["xo"] = xo.reshape(128, KC, NSLOT * SLOTW)
        mp["oh_d"] = ohd
        mp["m0"] = np.full((128, 1), 0.0 if j == 0 else 1.0, dtype=f)
        in_maps.append(mp)
    return in_maps


_NC_CACHE = {}


def kernel(**inputs):
    in_maps = _host_inputs(inputs)
    if "nc" not in _NC_CACHE:
        _NC_CACHE["nc"] = K().build()
    nc = _NC_CACHE["nc"]
    res = run_bass_kernel_spmd(nc, in_maps, core_ids=list(range(8)))
    out = np.zeros((2, S, D), dtype=np.float32)
    for core in range(8):
        b, j = core // 4, core % 4
        o = res.results[core]["outT"].reshape(128, KC, NSLOT, 512)
        for i in range(NSLOT):
            G = 4 * i + j
            out[b, 512 * G:512 * (G + 1), :] = o[:, :, i, :].transpose(2, 1, 0).reshape(512, D)
    return out
```

```python
import contextlib
import math
import numpy as np
import concourse.bass as bass
import concourse.mybir as mybir
from concourse.bass_utils import run_bass_kernel_spmd

F32 = mybir.dt.float32
BF16 = mybir.dt.bfloat16
AF = mybir.ActivationFunctionType
ALU = mybir.AluOpType

ENGS = ("pe", "act", "dve", "pool", "sp")
SEM_ROT = 3000


class Buf:
    __slots__ = ("name", "w", "rs")

    def __init__(self, name):
        self.name = name
        self.w = None
        self.rs = []


class Op:
    __slots__ = ("eng", "fn", "waits", "signal", "dma_key", "dma_sem", "dma_val", "sig_sem", "sig_val")

    def __init__(self, eng, fn, dma_key=None):
        self.eng = eng
        self.fn = fn
        self.waits = []
        self.signal = False
        self.dma_key = dma_key
        self.dma_sem = None
        self.dma_val = None
        self.sig_sem = None
        self.sig_val = None


class Prog:
    def __init__(self, nc):
        self.nc = nc
        self.ops = {e: [] for e in ENGS}
        self.dma_cnt = {}
        self.all_dma_last = {}
        self.stack = contextlib.ExitStack()
        self.nbufs = 0
        self.strict = False

    def buf(self, name=None):
        self.nbufs += 1
        return Buf(f"{name or 'b'}#{self.nbufs}")

    def bufs(self, n, name="b"):
        return [self.buf(f"{name}{i}") for i in range(n)]

    def _dep(self, op, y):
        if y is None or y is op:
            return
        if y.dma_key is None and y.eng == op.eng and op.dma_key is None and not self.strict:
            return
        if y.dma_key is None:
            y.signal = True
        if y not in op.waits:
            op.waits.append(y)

    def op(self, eng, fn, reads=(), writes=(), dma_key=None):
        o = Op(eng, fn, dma_key)
        for b in reads:
            self._dep(o, b.w)
        for b in writes:
            self._dep(o, b.w)
            for r in b.rs:
                self._dep(o, r)
        for b in reads:
            b.rs.append(o)
        for b in writes:
            b.w = o
            b.rs = []
        if dma_key is not None:
            st = self.dma_cnt.setdefault(dma_key, [0, 0])
            if st[1] + 16 > 4000:
                st[0] += 1
                st[1] = 0
            st[1] += 16
            o.dma_sem = (dma_key, st[0])
            o.dma_val = st[1]
            self.all_dma_last[dma_key] = o
        self.ops[eng].append(o)
        return o

    def dma(self, out, in_, reads=(), writes=(), key=None, eng="sp", **kw):
        prim = writes[0] if len(writes) else reads[0]
        key = prim.name
        return self.op(eng, lambda e: e.dma_start(out=out, in_=in_, **kw), reads, writes, dma_key=key)

    def barrier(self):
        lasts = [self.ops[e][-1] for e in ENGS if self.ops[e]]
        dmas = list(self.all_dma_last.values())
        news = []
        for e in ENGS:
            o = Op(e, None)
            for y in lasts:
                self._dep(o, y)
            for y in dmas:
                self._dep(o, y)
            news.append(o)
        for o in news:
            self.ops[o.eng].append(o)

    def emit(self):
        nc = self.nc
        semkeys = set()
        for e in ENGS:
            gen, cnt = 0, 0
            for o in self.ops[e]:
                if o.dma_key is not None:
                    semkeys.add(o.dma_sem)
                    continue
                if o.signal:
                    if cnt >= SEM_ROT:
                        gen += 1
                        cnt = 0
                    cnt += 1
                    o.sig_sem = ("eng", e, gen)
                    o.sig_val = cnt
                    semkeys.add(o.sig_sem)
        sems = {}
        for n, k in enumerate(sorted(semkeys, key=str)):
            sems[k] = self.stack.enter_context(nc.semaphore(f"sm{n}"))
        self.nsems = len(sems)
        block = self.stack.enter_context(nc.Block())
        engmap = {"pe": "tensor", "act": "scalar", "dve": "vector", "pool": "gpsimd", "sp": "sync"}

        def make(e):
            def body(eng):
                waited = {}
                for o in self.ops[e]:
                    for y in o.waits:
                        if y.dma_key is not None:
                            sk, v = y.dma_sem, y.dma_val
                        else:
                            sk, v = y.sig_sem, y.sig_val
                        if waited.get(sk, 0) >= v:
                            continue
                        waited[sk] = v
                        eng.wait_ge(sems[sk], v)
                    if o.fn is None:
                        if o.signal:
                            eng.nop().then_inc(sems[o.sig_sem], 1)
                        continue
                    ins = o.fn(eng)
                    if o.dma_key is not None:
                        ins.then_inc(sems[o.dma_sem], 16)
                    elif o.signal:
                        ins.then_inc(sems[o.sig_sem], 1)
            return body

        for e in ENGS:
            getattr(block, engmap[e])(make(e))
        self.stack.close()


class Arena:
    def __init__(self, prog, nbytes, name="arena"):
        nc = prog.nc
        self.t8 = prog.stack.enter_context(nc.sbuf_tensor(name, [128, nbytes], mybir.dt.uint8))
        self.views = {}
        self.nbytes = nbytes
        self.off = 0

    def view(self, dt):
        if dt not in self.views:
            self.views[dt] = self.t8.bitcast(dt)
        return self.views[dt]

    def alloc(self, nelem, dt):
        sz = mybir.dt.size(dt)
        self.off = (self.off + 63) // 64 * 64
        o = self.off
        self.off += nelem * sz
        assert self.off <= self.nbytes, f"arena overflow {self.off} > {self.nbytes}"
        return self.view(dt)[:, o // sz: o // sz + nelem]

    def alloc_raw(self, nbytes):
        self.off = (self.off + 63) // 64 * 64
        o = self.off
        self.off += nbytes
        assert self.off <= self.nbytes, f"arena overflow {self.off} > {self.nbytes}"
        return o

    def at(self, o, nelem, dt):
        sz = mybir.dt.size(dt)
        return self.view(dt)[:, o // sz: o // sz + nelem]

    def mark(self):
        return self.off

    def release(self, m):
        self.off = m


D = 1024
S = 16384
KC = 8
NSLOT = 8
SLOTW = 768
HW = 640
DFF = 2816
NFC = 22
EPS = 1e-6
LAM_INIT = 0.8 - 0.6 * math.exp(-0.3 * 0)
VECD = 2688
STRIPW = 2560
C_QA, C_KA, C_VA, C_QB, C_KB, C_VB, C_GL = 0, 512, 640, 768, 1280, 1792, 2304

DEBUG_SCRATCH = False
P2_STAGE = 9
P2_SLOTS = 8
SKIP_P1 = False
N_PHASES = 5


def _t5_bucket_np(rel):
    n = np.maximum(rel, 0)
    nf = np.maximum(n, 1).astype(np.float32)
    large = 16 + (np.log(nf / np.float32(16)) / np.float32(math.log(8.0)) * np.float32(16)).astype(np.int32)
    large = np.minimum(large, 31)
    return np.where(n < 16, n, large)


class K:
    def __init__(self):
        nc = bass.Bass("TRN2", target_bir_lowering=False)
        self.nc = nc
        self.P = Prog(nc)
        self.A = Arena(self.P, 209920)
        self.ps = self.P.stack.enter_context(nc.psum_tensor("ps", [128, 8, 512], F32))
        self.psB = self.P.bufs(8, "psb")

    def mm(self, out, lhsT, rhs, start, stop, R, W):
        self.P.op("pe", lambda e: e.matmul(out, lhsT=lhsT, rhs=rhs, start=start, stop=stop), R, W)

    def act(self, out, in_, func, R, W, bias=None, scale=1.0):
        b = self.zero1 if bias is None else bias
        npart = out.shape[0]
        if npart != 128 and b.shape[0] == 128:
            b = b[0:npart, :]
        self.P.op("act", lambda e: e.activation(out=out, in_=in_, func=func, bias=b, scale=scale), R, W)

    def tt(self, eng, out, a, b, op, R, W):
        self.P.op(eng, lambda e: e.tensor_tensor(out=out, in0=a, in1=b, op=op), R, W)

    def ts(self, eng, out, a, s1, s2, op0, op1, R, W):
        if op1 is None:
            self.P.op(eng, lambda e: e.tensor_scalar(out=out, in0=a, scalar1=s1, scalar2=None, op0=op0), R, W)
        else:
            self.P.op(eng, lambda e: e.tensor_scalar(out=out, in0=a, scalar1=s1, scalar2=s2, op0=op0, op1=op1), R, W)

    def stt(self, eng, out, a, s, b, op0, op1, R, W):
        self.P.op(eng, lambda e: e.scalar_tensor_tensor(out=out, in0=a, scalar=s, in1=b, op0=op0, op1=op1), R, W)

    def cp(self, eng, out, in_, R, W):
        if eng == "act":
            self.act(out, in_, AF.Identity, R, W)
        else:
            self.P.op(eng, lambda e: e.tensor_copy(out=out, in_=in_), R, W)

    def rcp(self, out, in_, R, W):
        self.P.op("dve", lambda e: e.reciprocal(out=out, in_=in_), R, W)

    def memset(self, eng, ap, val, W):
        self.P.op(eng, lambda e: e.memset(ap, val), (), W)

    def declare(self):
        nc = self.nc

        def din(name, shape, dt=F32):
            return nc.dram_tensor(name, list(shape), dt, kind="ExternalInput")

        def dscr(name, shape, dt):
            return nc.dram_tensor(name, list(shape), dt, kind="ExternalOutput" if DEBUG_SCRATCH else "Internal")

        self.d_xT = din("xT", [128, KC, S])
        self.d_xo = din("xo", [128, KC, NSLOT * SLOTW])
        self.d_win = din("w_in", [128, KC, 4352])
        self.d_wbra = din("w_br_a", [128, 4, D])
        self.d_wbrb = din("w_br_b", [128, 4, D])
        self.d_wo = din("w_o", [128, KC, D])
        self.d_wup = din("w_up", [128, KC, 2 * DFF])
        self.d_wdn = din("w_down", [128, NFC, D])
        self.d_gmix = din("g_mix", [128, KC])
        self.d_gffn = din("g_ffn", [128, KC])
        self.d_cw = din("conv_w", [128, 3, 44])
        self.d_cb = din("conv_b", [128, 44])
        self.d_relb = din("rel_bias", [32, 12])
        self.d_qna = din("qn_a", [1, 64])
        self.d_kna = din("kn_a", [1, 64])
        self.d_qnb = din("qn_b", [1, 64])
        self.d_knb = din("kn_b", [1, 64])
        self.d_sinks = din("sinks", [1, 8])
        self.d_lq1 = din("lam_q1", [1, 64])
        self.d_lk1 = din("lam_k1", [1, 64])
        self.d_lq2 = din("lam_q2", [1, 64])
        self.d_lk2 = din("lam_k2", [1, 64])
        self.d_subln = din("subln_b", [128, 1])
        self.d_oha = din("oh_a", [33, 384])
        self.d_ohd = din("oh_d", [33, VECD])
        self.d_m0 = din("m0", [128, 1])
        self.d_J = din("Jmat", [128, 128])
        self.d_bd = din("bdones", [128, 128])
        self.d_out = nc.dram_tensor("outT", [128, KC, NSLOT * 512], F32, kind="ExternalOutput")
        self.d_KT = dscr("s_KT", [4, 128, S], BF16)
        self.d_VS = dscr("s_VS", [4, 128, 128, 128], BF16)
        self.d_QB = dscr("s_QB", [NSLOT, 128, 4, HW], BF16)
        self.d_YA = dscr("s_YA", [NSLOT, 128, 4, HW], BF16)
        self.d_YB = dscr("s_YB", [NSLOT, 128, 4, HW], BF16)
        self.d_X2 = dscr("s_X2", [NSLOT, 128, KC, HW], F32)
        if DEBUG_SCRATCH:
            self.d_dstrip = dscr("s_strip", [128, 4 * STRIPW], BF16)
            self.d_dsa = dscr("s_sa", [128, 2 * 8 * 128], BF16)
            self.d_dsmall = dscr("s_small", [128, 8], F32)
        self.d_bc = nc.dram_tensor("s_bc", [2, 3, 512], F32, kind="Internal")
        self.d_VA = dscr("s_VECA", [8, 384], F32)
        self.d_VD = dscr("s_VECD", [4, VECD], F32)

    def setup(self):
        P, A, nc = self.P, self.A, self.nc
        ps, psB = self.ps, self.psB
        cB = P.buf("consts")
        self.cB = cB
        self.zero1 = A.alloc(1, F32)
        self.eps1 = A.alloc(1, F32)
        self.tiny1 = A.alloc(1, F32)
        self.ones_bf = A.alloc(128, BF16)
        self.ones_f = A.alloc(128, F32)
        self.bd_bf = A.alloc(128, BF16)
        self.J = A.alloc(128, F32)
        bd_f = A.alloc(128, F32)
        self.memset("pool", self.zero1, 0.0, [cB])
        self.memset("pool", self.eps1, EPS, [cB])
        self.memset("pool", self.tiny1, 1e-30, [cB])
        self.memset("pool", self.ones_bf, 1.0, [cB])
        self.memset("pool", self.ones_f, 1.0, [cB])
        jB = P.buf("J")
        P.dma(self.J, self.d_J.ap(), writes=[jB], key="c0")
        P.dma(bd_f, self.d_bd.ap(), writes=[jB], key="c0")
        self.cp("dve", self.bd_bf, bd_f, [jB], [cB])
        sB = P.buf("small")
        self.gmix = A.alloc(KC, F32)
        self.gffn = A.alloc(KC, F32)
        self.cw = A.alloc(3 * 44, F32)
        self.cb = A.alloc(44, F32)
        self.m0 = A.alloc(1, F32)
        P.dma(self.gmix, self.d_gmix.ap(), writes=[sB], key="c1")
        P.dma(self.gffn, self.d_gffn.ap(), writes=[sB], key="c1")
        P.dma(self.cw, self.d_cw.ap().rearrange("p a b -> p (a b)"), writes=[sB], key="c1")
        P.dma(self.cb, self.d_cb.ap(), writes=[sB], key="c1")
        P.dma(self.m0, self.d_m0.ap(), writes=[sB], key="c1")
        self.gq_a = A.alloc(1, F32)
        self.gk_a = A.alloc(1, F32)
        self.gq_b = A.alloc(1, F32)
        self.gk_b = A.alloc(1, F32)
        for dst, src in ((self.gq_a, self.d_qna), (self.gk_a, self.d_kna), (self.gq_b, self.d_qnb), (self.gk_b, self.d_knb)):
            for hlf in range(2):
                P.dma(dst[64 * hlf:64 * hlf + 64, :], bass.AP(src, 0, [[1, 64], [1, 1]]), writes=[sB], key="c1")
        self.gsub = A.alloc(1, F32)
        P.dma(self.gsub, self.d_subln.ap(), writes=[sB], key="c1")
        self.ts("dve", self.gsub, self.gsub, 1.0 - LAM_INIT, None, ALU.mult, None, [sB], [sB])
        lam4 = A.alloc(4 * 64, F32)
        for n_, src in enumerate((self.d_lq1, self.d_lk1, self.d_lq2, self.d_lk2)):
            P.dma(lam4[:, n_ * 64:(n_ + 1) * 64], bass.AP(src, 0, [[0, 128], [1, 64]]), writes=[sB], key="c1")
        lp = A.alloc(2 * 64, F32)
        ls = A.alloc(2, F32)
        self.neglam = A.alloc(1, F32)
        self.tt("dve", lp[:, 0:64], lam4[:, 0:64], lam4[:, 64:128], ALU.mult, [sB], [sB])
        self.tt("dve", lp[:, 64:128], lam4[:, 128:192], lam4[:, 192:256], ALU.mult, [sB], [sB])
        P.op("dve", lambda e: e.reduce_sum(out=ls[:, 0:1], in_=lp[:, 0:64], axis=mybir.AxisListType.X), [sB], [sB])
        P.op("dve", lambda e: e.reduce_sum(out=ls[:, 1:2], in_=lp[:, 64:128], axis=mybir.AxisListType.X), [sB], [sB])
        self.act(ls, ls, AF.Exp, [sB, cB], [sB])
        self.tt("dve", self.neglam, ls[:, 1:2], ls[:, 0:1], ALU.subtract, [sB], [sB])
        self.ts("dve", self.neglam, self.neglam, -LAM_INIT, None, ALU.add, None, [sB], [sB])
        self.sB = sB
        self.esrow = A.alloc(2 * 512, F32)
        sk = A.alloc(8, F32)
        P.dma(sk[0:1, :], self.d_sinks.ap(), writes=[sB], key="c1")
        self.act(sk[0:1, :], sk[0:1, :], AF.Exp, [sB, cB], [sB])
        for hq in range(8):
            g_, r_ = hq // 4, hq % 4
            col = g_ * 512 + ((r_ % 2) * 2 + r_ // 2) * 128
            self.ts("dve", self.esrow[0:1, col:col + 128], self.ones_f[0:1, 0:128], sk[0:1, hq:hq + 1], None, ALU.mult, None, [sB, cB], [sB])

        tB = P.buf("tab")
        tabp = A.alloc(12, F32)
        tab31 = A.alloc(4, F32)
        self.memset("pool", tabp[32:33, :], -30000.0, [tB])
        P.dma(tabp[0:32, :], self.d_relb.ap(), writes=[tB], key="c2")
        P.dma(tab31[0:32, :], bass.AP(self.d_relb, 31 * 12 + 8, [[0, 32], [1, 4]]), writes=[tB], key="c2")
        self.tt("dve", tabp[0:32, 8:12], tabp[0:32, 8:12], tab31[0:32, :], ALU.subtract, [tB], [tB])
        m = A.mark()
        oha = A.alloc(384, F32)
        ohd = A.alloc(VECD, F32)
        veca = A.alloc(384, F32)
        vecd = A.alloc(VECD, F32)
        ohB = P.buf("oh")
        P.dma(oha[0:33, :], self.d_oha.ap(), writes=[ohB], key="c2")
        P.dma(ohd[0:33, :], self.d_ohd.ap(), writes=[ohB], key="c2")
        vB = P.buf("vec")
        self.mm(ps[0:8, 0, 0:384], tabp[0:33, 0:8], oha[0:33, :], True, True, [tB, ohB], [psB[0]])
        self.act(veca[0:8, :], ps[0:8, 0, 0:384], AF.Exp, [psB[0], cB], [vB])
        for pc in range(6):
            w = 512 if pc < 5 else VECD - 2560
            bk = 1 + pc % 2
            self.mm(ps[0:4, bk, 0:w], tabp[0:33, 8:12], ohd[0:33, pc * 512: pc * 512 + w], True, True, [tB, ohB], [psB[bk]])
            self.act(vecd[0:4, pc * 512: pc * 512 + w], ps[0:4, bk, 0:w], AF.Exp, [psB[bk], cB], [vB])
        dvB = P.buf("dvec")
        P.dma(self.d_VA.ap(), veca[0:8, :], reads=[vB], writes=[dvB], key="c3")
        P.dma(self.d_VD.ap(), vecd[0:4, :], reads=[vB], writes=[dvB], key="c3")
        A.release(m)
        self.pre_strip_mark = A.mark()
        self.strip = A.alloc(4 * STRIPW, BF16).rearrange("p (h u) -> p h u", h=4)
        self.sa = A.alloc(2 * 8 * 128, BF16).rearrange("p (t h q) -> p t h q", t=2, h=8)
        self.stripB = P.buf("strip")
        self.persist_mark = A.mark()
        m = A.mark()
        rev = A.alloc(STRIPW, F32)
        revB = P.bufs(2, "rev")
        P.dma(rev[:, 0:2048].rearrange("p (h u) -> p h u", h=8), bass.AP(self.d_VA, 0, [[1, 128], [384, 8], [1, 256]]),
              reads=[dvB], writes=[revB[0]], key="c4")
        for pi in range(4):
            bk = pi % 2
            self.mm(ps[:, bk, :], self.J, rev[:, pi * 512:(pi + 1) * 512], True, True, [jB, revB[0]], [psB[bk]])
            src = ps[:, bk, :].rearrange("p (h t q) -> p h t q", h=2, t=2)
            for ty in range(2):
                self.cp("dve" if ty == 0 else "act", self.sa[:, ty, 2 * pi:2 * pi + 2, :], src[:, :, ty, :], [psB[bk]], [self.stripB])
        for h in range(4):
            rb = revB[(h + 1) % 2]
            P.dma(rev, bass.AP(self.d_VD, h * VECD, [[1, 128], [1, STRIPW]]), reads=[dvB], writes=[revB[0], revB[1]], key="c4")
            for pc in range(5):
                bk = pc % 2
                self.mm(ps[:, bk, :], self.J, rev[:, pc * 512:(pc + 1) * 512], True, True, [jB, revB[0], revB[1]], [psB[bk]])
                self.cp("dve" if pc % 2 == 0 else "act", self.strip[:, h, pc * 512:(pc + 1) * 512], ps[:, bk, :], [psB[bk]], [self.stripB])
        A.release(m)
        if DEBUG_SCRATCH:
            P.dma(self.d_dstrip.ap(), self.strip.rearrange("p h u -> p (h u)"), reads=[self.stripB])
            P.dma(self.d_dsa.ap(), self.sa.rearrange("p t h q -> p (t h q)"), reads=[self.stripB])
            for n_, t_ in enumerate((self.neglam, self.gsub, self.gq_b, self.gk_b)):
                P.dma(self.d_dsmall.ap()[:, n_:n_ + 1], t_, reads=[self.sB], allow_slow_non_contiguous=True)

    def load_w(self, dst, src, kc, ncols, gain=None, dup=None, key="w"):
        P, A = self.P, self.A
        m = A.mark()
        pw = 512 if kc <= 8 else 128
        stg = [A.alloc(kc * pw, F32).rearrange("p (c n) -> p c n", c=kc) for _ in range(2)]
        sb = P.bufs(2, "stg")
        wB = P.buf("wdst")
        engs = ("dve", "pool", "act")
        n = 0
        for i, c0 in enumerate(range(0, ncols, pw)):
            w = min(pw, ncols - c0)
            s, b = stg[i % 2], sb[i % 2]
            P.dma(s[:, :, 0:w], src[:, :, c0:c0 + w], writes=[b], key=f"{key}{i % 2}")
            for c in range(kc):
                eng = engs[n % 3]
                n += 1
                if gain is None:
                    self.cp(eng, dst[:, c, c0:c0 + w], s[:, c, 0:w], [b], [wB])
                elif eng == "act":
                    self.act(dst[:, c, c0:c0 + w], s[:, c, 0:w], AF.Identity, [b, self.sB, self.cB], [wB], scale=gain[:, c:c + 1])
                else:
                    self.ts(eng, dst[:, c, c0:c0 + w], s[:, c, 0:w], gain[:, c:c + 1], None, ALU.mult, None, [b, self.sB], [wB])
        self.P.barrier()
        A.release(m)
        return wB

    def norm_tile(self, xt, xB, n, sq, sqB, hT, hB, rstd, rB, pieces):
        ps, psB = self.ps, self.psB
        for c in range(KC):
            self.act(sq[c % 2][:, 0:n], xt[:, c, :], AF.Square, [xB, self.cB], [sqB[c % 2]])
            for (o, w, bank) in pieces:
                self.mm(ps[:, bank, 0:w], self.ones_bf, sq[c % 2][:, o:o + w], c == 0, c == KC - 1, [sqB[c % 2], self.cB], [psB[bank]])
        for (o, w, bank) in pieces:
            self.act(rstd[:, o:o + w], ps[:, bank, 0:w], AF.Sqrt, [psB[bank], self.cB], [rB], bias=self.eps1, scale=1.0 / D)
        self.rcp(rstd[:, 0:n], rstd[:, 0:n], [rB], [rB])
        for c in range(KC):
            self.tt("dve" if c % 2 == 0 else "pool", hT[:, c, 0:n], xt[:, c, :], rstd[:, 0:n], ALU.mult, [xB, rB], [hB[c]])

    def ph_tasks(self, tasks, hT, hB, wB, tmp, tmpB, pbanks, nbanks, mid=None):
        ps, psB = self.ps, self.psB
        n = len(tasks)

        def proj(k):
            wfn, o, w, out, outB, gain = tasks[k]
            bk = pbanks[k % len(pbanks)]
            ksq, _ = tmp[k % len(tmp)]
            for c in range(KC):
                self.mm(ps[:, bk, 0:w], wfn(c), hT[:, c, o:o + w], c == 0, c == KC - 1, [wB, hB[c]], [psB[bk]])
            self.act(ksq[:, 0:w], ps[:, bk, 0:w], AF.Square, [psB[bk], self.cB], [tmpB[k % len(tmp)][0]])

        def norm(k):
            wfn, o, w, out, outB, gain = tasks[k]
            bk = pbanks[k % len(pbanks)]
            bn = nbanks[k % len(nbanks)]
            ksq, rk = tmp[k % len(tmp)]
            tb = tmpB[k % len(tmp)]
            self.mm(ps[:, bn, 0:w], self.bd_bf, ksq[:, 0:w], True, True, [tb[0], self.cB], [psB[bn]])
            self.act(rk[:, 0:w], ps[:, bn, 0:w], AF.Sqrt, [psB[bn], self.cB], [tb[1]], bias=self.eps1, scale=1.0 / 64)
            self.rcp(rk[:, 0:w], rk[:, 0:w], [tb[1]], [tb[1]])
            self.stt("dve", out, ps[:, bk, 0:w], gain, rk[:, 0:w], ALU.mult, ALU.mult, [psB[bk], tb[1], self.sB], [outB])

        for k in range(n + 1):
            if k < n:
                proj(k)
            if k == n and mid is not None:
                mid()
            if k >= 1:
                norm(k - 1)

    def phase1(self):
        P, A = self.P, self.A
        ps, psB = self.ps, self.psB
        m = A.mark()
        wkb = A.alloc(KC * 512, BF16).rearrange("p (c n) -> p c n", c=KC)
        wvb = A.alloc(KC * 512, BF16).rearrange("p (c n) -> p c n", c=KC)
        win = self.d_win.ap()
        wB1 = self.load_w(wkb, win[:, :, C_KB:C_KB + 512], KC, 512, gain=self.gmix)
        wB2 = self.load_w(wvb, win[:, :, C_VB:C_VB + 512], KC, 512, gain=self.gmix)
        xt = [A.alloc(KC * 512, F32).rearrange("p (c n) -> p c n", c=KC) for _ in range(2)]
        xB = P.bufs(2, "x")
        sq = [A.alloc(512, BF16) for _ in range(2)]
        sqB = P.bufs(2, "sq")
        hT = [A.alloc(KC * 512, BF16).rearrange("p (c n) -> p c n", c=KC) for _ in range(2)]
        hB = [P.bufs(KC, "h") for _ in range(2)]
        rstd = [A.alloc(512, F32) for _ in range(2)]
        rB = P.bufs(2, "r")
        tmp = [(A.alloc(512, BF16), A.alloc(512, F32)) for _ in range(3)]
        tmpB = [P.bufs(2, "tmp") for _ in range(3)]
        kout = [A.alloc(4 * 512, BF16).rearrange("p (h n) -> p h n", h=4) for _ in range(2)]
        koB = [P.bufs(4, "ko") for _ in range(2)]
        vout = [A.alloc(4 * 512, BF16).rearrange("p (b n) -> p b n", b=4) for _ in range(2)]
        voB = [P.bufs(4, "vo") for _ in range(2)]
        xT = self.d_xT.ap()
        NT = S // 512

        def stats(T):
            pp = T % 2
            P.dma(xt[pp], xT[:, :, T * 512:(T + 1) * 512], writes=[xB[pp]], key=f"x{pp}")
            self.norm_tile(xt[pp], xB[pp], 512, sq, sqB, hT[pp], hB[pp], rstd[pp], rB[pp], [(0, 512, 0)])

        stats(0)
        for T in range(NT):
            pp = T % 2
            if T + 1 < NT:
                stats(T + 1)
            tasks = [((lambda c, hh=hh: wkb[:, c, hh * 128:(hh + 1) * 128]), 0, 512, kout[pp][:, hh, :], koB[pp][hh], self.gk_b) for hh in range(4)]
            self.ph_tasks(tasks, hT[pp], hB[pp], wB1, tmp, tmpB, (1, 2, 3), (6, 7))
            for hh in range(4):
                P.dma(self.d_KT.ap()[hh, :, T * 512:(T + 1) * 512], kout[pp][:, hh, :], reads=[koB[pp][hh]], key=f"ko{pp}")
            for blk in range(4):
                bk = 4 + blk % 2
                for c in range(KC):
                    self.mm(ps[:, bk, :], hT[pp][:, c, blk * 128:(blk + 1) * 128], wvb[:, c, :], c == 0, c == KC - 1,
                            [hB[pp][c], wB2], [psB[bk]])
                self.cp("act" if blk % 2 == 0 else "dve", vout[pp][:, blk, :], ps[:, bk, :], [psB[bk]], [voB[pp][blk]])
            for hh in range(4):
                P.dma(self.d_VS.ap()[hh, :, 4 * T:4 * T + 4, :], vout[pp][:, :, hh * 128:(hh + 1) * 128],
                      reads=voB[pp], key=f"vo{pp}")
        P.barrier()
        A.release(m)

    def phase2(self):
        P, A = self.P, self.A
        ps, psB = self.ps, self.psB
        m = A.mark()
        win = self.d_win.ap()

        def walloc(n):
            return A.alloc(KC * n, BF16).rearrange("p (c n) -> p c n", c=KC)

        wqa, wka2, wva, wqb = walloc(512), walloc(256), walloc(128), walloc(512)
        wBq = self.load_w(wqa, win[:, :, C_QA:C_QA + 512], KC, 512, gain=self.gmix)
        for g in range(2):
            for hlf in range(2):
                self.load_w(wka2[:, :, g * 128 + hlf * 64: g * 128 + hlf * 64 + 64], win[:, :, C_KA + 64 * g:C_KA + 64 * g + 64], KC, 64,
                            gain=self.gmix)
        self.load_w(wva, win[:, :, C_VA:C_VA + 128], KC, 128, gain=self.gmix)
        self.load_w(wqb, win[:, :, C_QB:C_QB + 512], KC, 512, gain=self.gmix)
        wB = P.buf("w2")
        xts = [A.alloc(KC * SLOTW, F32).rearrange("p (c n) -> p c n", c=KC) for _ in range(2)]
        xBs = P.bufs(2, "x")
        sq = [A.alloc(SLOTW, BF16) for _ in range(2)]
        sqB = P.bufs(2, "sq")
        hTs = [A.alloc(KC * SLOTW, BF16).rearrange("p (c n) -> p c n", c=KC) for _ in range(2)]
        hBs = [P.bufs(KC, "h") for _ in range(2)]
        rstds = [A.alloc(SLOTW, F32) for _ in range(2)]
        rBs = P.bufs(2, "r")
        tmp = [(A.alloc(512, BF16), A.alloc(512, F32)) for _ in range(3)]
        tmpB = [P.bufs(2, "tmp") for _ in range(3)]
        qaT = A.alloc(4 * HW, BF16).rearrange("p (c n) -> p c n", c=4)
        qaB = P.bufs(4, "qa")
        kaT = A.alloc(2 * SLOTW, BF16).rearrange("p (g n) -> p g n", g=2)
        kaB = P.bufs(2, "ka")
        va = A.alloc(6 * 128, BF16).rearrange("p (b n) -> p b n", b=6)
        vaB = P.buf("va")
        qbT = A.alloc(4 * HW, BF16).rearrange("p (c n) -> p c n", c=4)
        qbB = P.buf("qb")
        yaT = A.alloc(4 * HW, BF16).rearrange("p (c n) -> p c n", c=4)
        yaB = P.buf("ya")
        pt = [A.alloc(512, BF16) for _ in range(4)]
        ptB = P.bufs(4, "pt")
        den = A.alloc(512, F32)
        denB = P.buf("den")
        xo = self.d_xo.ap()
        nsl = min(NSLOT, P2_SLOTS)

        def stats(i):
            pp = i % 2
            P.dma(xts[pp], xo[:, :, i * SLOTW:(i + 1) * SLOTW], writes=[xBs[pp]], key="x2")
            self.norm_tile(xts[pp], xBs[pp], SLOTW, sq, sqB, hTs[pp], hBs[pp], rstds[pp], rBs[pp], [(0, 512, 0), (512, 256, 7)])

        stats(0)
        for i in range(nsl):
            hT, hB = hTs[i % 2], hBs[i % 2]
            tasks = []
            for (o, w) in ((128, 512), (640, 128)):
                for cm in range(4):
                    tasks.append(((lambda c, cm=cm: wqa[:, c, cm * 128:(cm + 1) * 128]), o, w, qaT[:, cm, o - 128:o - 128 + w], qaB[cm], self.gq_a))
                for cm in range(4):
                    tasks.append(((lambda c, cm=cm: wqb[:, c, cm * 128:(cm + 1) * 128]), o, w, qbT[:, cm, o - 128:o - 128 + w], qbB, self.gq_b))
            for (o, w) in ((0, 512), (512, 256)):
                for g in range(2):
                    tasks.append(((lambda c, g=g: wka2[:, c, g * 128:(g + 1) * 128]), o, w, kaT[:, g, o:o + w], kaB[g], self.gk_a))
            self.ph_tasks(tasks, hT, hB, wB, tmp, tmpB, (1, 2, 3), (5, 6))
            P.dma(self.d_QB.ap()[i], qbT, reads=[qbB], key="qbo")
            for half in range(2):
                bk = 4 + half
                for bl in range(3):
                    blk = half * 3 + bl
                    for c in range(KC):
                        self.mm(ps[:, bk, bl * 128:(bl + 1) * 128], hT[:, c, blk * 128:(blk + 1) * 128], wva[:, c, :], c == 0, c == KC - 1,
                                [hB[c], wB], [psB[bk]])
                self.cp("act", va[:, half * 3:half * 3 + 3, :], ps[:, bk, 0:384].rearrange("p (b n) -> p b n", b=3), [psB[bk]], [vaB])
            if i + 1 < nsl:
                stats(i + 1)
            for n in range(1, 6 if P2_STAGE >= 1 else 0):
                qo = (n - 1) * 128
                for g in range(2):
                    for kk, kblk in enumerate((n - 1, n)):
                        idx = g * 2 + kk
                        b0 = (2, 6)[idx % 2]
                        for r in range(4):
                            par, rr = r % 2, r // 2
                            pb = 64 * par
                            self.mm(ps[:, b0 + par, rr * 128:(rr + 1) * 128], kaT[pb:pb + 64, g, kblk * 128:(kblk + 1) * 128],
                                    qaT[pb:pb + 64, 2 * g + rr, qo:qo + 128], True, True,
                                    [kaB[g], qaB[2 * g + rr]], [psB[b0 + par]])
                        self.act(pt[idx].rearrange("p (a n) -> p a n", a=2), ps[:, b0:b0 + 2, 0:256], AF.Exp,
                                 [psB[b0], psB[b0 + 1], self.cB], [ptB[idx]], scale=0.125)
                        ty = 1 if kk == 0 else 0
                        fa = self.sa[:, ty, 4 * g:4 * g + 4, :].rearrange("p (rr par) q -> p par rr q", par=2)
                        p4 = pt[idx].rearrange("p (par rr q) -> p par rr q", par=2, rr=2)
                        self.tt("pool", p4, p4, fa, ALU.mult, [ptB[idx], self.stripB], [ptB[idx]])
                        if i == 0 and n == 2 and kk == 0:
                            self.ts("pool", pt[idx], pt[idx], self.m0[:, 0:1], None, ALU.mult, None, [ptB[idx], self.sB], [ptB[idx]])
                if P2_STAGE < 2:
                    continue
                for g in range(2):
                    for kk, kblk in enumerate((n - 1, n)):
                        idx = g * 2 + kk
                        self.mm(ps[64 * g:64 * g + 64, 4, :], va[:, kblk, 64 * g:64 * g + 64], pt[idx], kk == 0, kk == 1,
                                [vaB, ptB[idx]], [psB[4]])
                    for kk in range(2):
                        idx = g * 2 + kk
                        self.mm(ps[64 * g:64 * g + 64, 5, :], self.ones_bf[:, 0:64], pt[idx], kk == 0, (kk == 1 and P2_STAGE < 3),
                                [ptB[idx], self.cB], [psB[5]])
                    if P2_STAGE >= 3:
                        self.mm(ps[64 * g:64 * g + 64, 5, :], self.ones_f[0:1, 0:64], self.esrow[0:1, g * 512:(g + 1) * 512], False, True,
                                [self.sB, self.cB], [psB[5]])
                self.rcp(den, ps[:, 5, :], [psB[5]], [denB])
                self.tt("dve", yaT[:, :, qo:qo + 128].rearrange("p (rr par) q -> p par rr q", par=2),
                        ps[:, 4, :].rearrange("p (par rr q) -> p par rr q", par=2, rr=2),
                        den.rearrange("p (par rr q) -> p par rr q", par=2, rr=2), ALU.mult, [psB[4], denB], [yaB])
            P.dma(self.d_YA.ap()[i], yaT, reads=[yaB], key="yao")
        P.barrier()
        A.release(m)

    def phase3(self):
        P, A = self.P, self.A
        ps, psB = self.ps, self.psB
        m = A.mark()
        KTs = [A.alloc(S, BF16) for _ in range(2)]
        VSs = [A.alloc(128 * 128, BF16).rearrange("p (b e) -> p b e", b=128) for _ in range(2)]
        kvB = [(P.bufs(4, "kt"), P.bufs(4, "vs")) for _ in range(2)]
        qt = [A.alloc(HW, BF16) for _ in range(2)]
        qB = P.bufs(2, "q")
        NPB = 3
        pt = [A.alloc(1024, BF16) for _ in range(NPB)]
        ptB = P.bufs(NPB, "pt")
        ssum = [A.alloc(512, F32) for _ in range(2)]
        ssumB = P.bufs(2, "ssum")
        rbc = [A.alloc(1024, F32) for _ in range(2)]
        rbcB = P.bufs(2, "rbc")
        dd = [A.alloc(1024, F32) for _ in range(2)]
        ddB = P.bufs(2, "dd")
        dsq = [A.alloc(512, BF16) for _ in range(2)]
        dsqB = P.bufs(2, "dsq")
        rrow = [A.alloc(512, F32) for _ in range(2)]
        rrowB = P.bufs(2, "rrow")
        rrbc = [A.alloc(512, F32) for _ in range(2)]
        rrbcB = P.bufs(2, "rrbc")
        yb = [A.alloc(HW, BF16) for _ in range(2)]
        ybB = P.bufs(2, "yb")
        dbcB = [P.bufs(2, "dbc") for _ in range(2)]
        q7B = P.bufs(4, "ps7q")
        ones32 = self.ones_bf[:, 0:32]
        SB = [(0, 1), (2, 3)]
        dbc = self.d_bc

        def load_kv(h):
            pp = h % 2
            for q4 in range(4):
                P.dma(KTs[pp][:, q4 * 4096:(q4 + 1) * 4096], self.d_KT.ap()[h, :, q4 * 4096:(q4 + 1) * 4096], writes=[kvB[pp][0][q4]])
            for q4 in range(4):
                P.dma(VSs[pp][:, q4 * 32:(q4 + 1) * 32, :], self.d_VS.ap()[h, :, q4 * 32:(q4 + 1) * 32, :], writes=[kvB[pp][1][q4]])

        pending = []
        gcount = [0]

        def flush():
            while pending:
                pending.pop(0)()

        def finalize(ncol, o_ap, obufs, ycols, ybt, ybb, out_dma, nred=1):
            gp = gcount[0] % 2
            gcount[0] += 1
            small = ncol < 128
            P.strict = small
            ss_, rb_, dd_, dq_, rw_, rrb_ = ssum[gp], rbc[gp], dd[gp], dsq[gp], rrow[gp], rrbc[gp]
            if nred == 1:
                self.cp("dve", ss_[0:64, 0:ncol], ps[0:64, 7, 0:ncol], [q7B[0], q7B[1]], [ssumB[gp]])
            else:
                wtot = nred * ncol
                self.cp("dve", ss_[0:64, 0:wtot], ps[0:64, 7, 0:wtot], [q7B[0], q7B[1]], [ssumB[gp]])
                P.strict = True
                if nred == 12:
                    steps = [(4 * ncol, 8 * ncol, 4 * ncol), (4 * ncol, 4 * ncol, 4 * ncol), (2 * ncol, 2 * ncol, 2 * ncol), (ncol, ncol, ncol)]
                else:
                    steps = [(8 * ncol, 8 * ncol, 8 * ncol), (4 * ncol, 4 * ncol, 4 * ncol), (2 * ncol, 2 * ncol, 2 * ncol), (ncol, ncol, ncol)]
                for (wd_, src_, _) in steps:
                    self.tt("dve", ss_[0:64, 0:wd_], ss_[0:64, 0:wd_], ss_[0:64, src_:src_ + wd_], ALU.add, [ssumB[gp]], [ssumB[gp]])
                P.strict = small
            for comp in range(2):
                P.dma(dbc.ap()[gp, comp:comp + 1, 0:ncol], ss_[32 * comp:32 * comp + 1, 0:ncol], reads=[ssumB[gp]], writes=[dbcB[gp][0]])
            P.dma(rb_.rearrange("p (c n) -> p c n", c=2)[:, :, 0:ncol], bass.AP(dbc, gp * 3 * 512, [[0, 128], [512, 2], [1, ncol]]),
                  reads=[dbcB[gp][0]], writes=[rbcB[gp]])
            for comp in range(2):
                r_ = rb_[:, comp * 512:comp * 512 + ncol]
                self.ts("dve", r_, r_, self.tiny1[:, 0:1], None, ALU.add, None, [rbcB[gp], self.cB], [rbcB[gp]])
                self.rcp(r_, r_, [rbcB[gp]], [rbcB[gp]])
                self.tt("dve", dd_[:, comp * 512: comp * 512 + ncol], o_ap(comp), r_, ALU.mult, [obufs[comp], rbcB[gp]], [ddB[gp]])
            self.stt("dve", dd_[:, 0:ncol], dd_[:, 512:512 + ncol], self.neglam[:, 0:1], dd_[:, 0:ncol], ALU.mult, ALU.add, [ddB[gp], self.sB], [ddB[gp]])
            self.tt("pool", dq_[:, 0:ncol], dd_[:, 0:ncol], dd_[:, 0:ncol], ALU.mult, [ddB[gp]], [dsqB[gp]])
            P.strict = False

            def stage1():
                P.strict = small
                self.mm(ps[64:96, 7, 0:ncol], ones32, dq_[:, 0:ncol], True, True, [dsqB[gp], self.cB], [q7B[2]])
                self.act(rw_[64:96, 0:ncol], ps[64:96, 7, 0:ncol], AF.Ln, [q7B[2], self.cB], [rrowB[gp]], bias=self.eps1[64:96, :], scale=1.0 / 128)
                P.strict = True
                self.act(rw_[64:96, 0:ncol], rw_[64:96, 0:ncol], AF.Exp, [rrowB[gp], self.cB], [rrowB[gp]], bias=self.zero1[64:96, :], scale=-0.5)
                P.strict = small
                P.dma(dbc.ap()[gp, 2:3, 0:ncol], rw_[64:65, 0:ncol], reads=[rrowB[gp]], writes=[dbcB[gp][1]])
                P.dma(rrb_[:, 0:ncol], bass.AP(dbc, (gp * 3 + 2) * 512, [[0, 128], [1, ncol]]), reads=[dbcB[gp][1]], writes=[rrbcB[gp]])
                self.stt("dve", ybt[:, ycols:ycols + ncol], dd_[:, 0:ncol], self.gsub[:, 0:1], rrb_[:, 0:ncol], ALU.mult, ALU.mult,
                         [ddB[gp], rrbcB[gp], self.sB], [ybb])
                P.strict = False
                if out_dma is not None:
                    out_dma()
            pending.append(stage1)

        def group(nsteps, qk, av, near_off, ncol, ncols_sum, sum_rhs):
            qk(0)
            for t in range(nsteps):
                if t + 1 < nsteps:
                    qk(t + 1)
                sb = SB[t % 2]
                p_, pB_ = pt[t % NPB], ptB[t % NPB]
                self.act(p_.rearrange("p (c n) -> p c n", c=2), ps[:, sb[0]:sb[0] + 2, :], AF.Exp, [psB[sb[0]], psB[sb[1]], self.cB], [pB_], scale=0.125)
                off = near_off(t)
                if off is not None:
                    for comp in range(2):
                        self.tt("dve" if comp == 0 else "pool", p_[:, comp * 512:(comp + 1) * 512], p_[:, comp * 512:(comp + 1) * 512],
                                self.strip_h[:, off:off + 512], ALU.mult, [pB_, self.stripB], [pB_])
                for comp in range(2):
                    rl = sum_rhs(p_, comp)
                    for n_, r_ in enumerate(rl):
                        self.mm(ps[32 * comp:32 * comp + 32, 7, 0:ncol], ones32, r_, t == 0 and n_ == 0, t == nsteps - 1 and n_ == len(rl) - 1,
                                [pB_, self.cB], [q7B[comp]])
                av(t, p_, pB_)
                if t == 3:
                    flush()
            flush()

        def group_h(nsteps, qk, av, near, batches, HQ):
            qk(0)
            for t in range(nsteps):
                if t + 1 < nsteps:
                    qk(t + 1)
                sb = SB[t % 2]
                kb0, nseg = batches[t]
                wd = nseg * HQ
                p_, pB_ = pt[t % NPB], ptB[t % NPB]
                self.act(p_.rearrange("p (c n) -> p c n", c=2)[:, :, 0:wd], ps[:, sb[0]:sb[0] + 2, 0:wd], AF.Exp,
                         [psB[sb[0]], psB[sb[1]], self.cB], [pB_], scale=0.125)
                nr = near(t)
                if nr is not None:
                    c0, ns_, off = nr
                    fa = self.strip_h[:, off:off + ns_ * 128].rearrange("p (u c) -> p u c", c=128)[:, :, 0:HQ]
                    for comp in range(2):
                        pv = p_[:, comp * 512 + c0 * HQ: comp * 512 + (c0 + ns_) * HQ].rearrange("p (u c) -> p u c", c=HQ)
                        self.tt("dve" if comp == 0 else "pool", pv, pv, fa, ALU.mult, [pB_, self.stripB], [pB_])
                for comp in range(2):
                    self.mm(ps[32 * comp:32 * comp + 32, 7, 0:wd], ones32, p_[:, comp * 512:comp * 512 + wd], t == 0, t == nsteps - 1,
                            [pB_, self.cB], [q7B[comp]])
                av(t, p_, pB_)
                if t == 1:
                    flush()
            flush()

        load_kv(0)
        cnt_q = 0
        for h in range(4):
            pp = h % 2
            KT, VS = KTs[pp], VSs[pp]
            kB, vB = kvB[pp]
            self.strip_h = self.strip[:, h, :]
            if h + 1 < 4:
                load_kv(h + 1)
            for i in range(NSLOT):
                qq = cnt_q % 2
                cnt_q += 1
                q, qb_ = qt[qq], qB[qq]
                P.dma(q, self.d_QB.ap()[i, :, h, :], writes=[qb_])
                ybt, ybb = yb[qq], ybB[qq]
                nkb = 16 * i + 16
                near0 = 16 * i - 1

                def qk(t, KT=KT, q=q, kB=kB, qb_=qb_):
                    sb = SB[t % 2]
                    for comp in range(2):
                        self.mm(ps[:, sb[comp], :], KT[64 * comp:64 * comp + 64, t * 128:(t + 1) * 128], q[64 * comp:64 * comp + 64, 128:640],
                                True, True, [kB[t // 32], qb_], [psB[sb[comp]]])

                def av(t, p_, pB_, VS=VS, vB=vB, nkb=nkb):
                    for comp in range(2):
                        self.mm(ps[:, 4 + comp, :], VS[:, t, :], p_[:, comp * 512:(comp + 1) * 512], t == 0, t == nkb - 1,
                                [vB[t // 32], pB_], [psB[4 + comp]])

                group(nkb, qk, av, lambda t, near0=near0: (2048 - 128 * (t - near0)) if t >= near0 else None, 512, 512,
                      lambda p_, comp: [p_[:, comp * 512:(comp + 1) * 512]])
                finalize(512, lambda comp: ps[:, 4 + comp, :], [psB[4], psB[5]], 128, ybt, ybb, None)
                HQ = 32
                batches = [(16 * b, 16) for b in range(i)] + [(16 * i, 12)]
                nb = len(batches)

                def qkh(t, KT=KT, q=q, kB=kB, qb_=qb_, batches=batches):
                    sb = SB[t % 2]
                    kb0, nseg = batches[t]
                    for comp in range(2):
                        for u in range(nseg):
                            kb = kb0 + nseg - 1 - u
                            self.mm(ps[:, sb[comp], u * HQ:(u + 1) * HQ], KT[64 * comp:64 * comp + 64, kb * 128:(kb + 1) * 128],
                                    q[64 * comp:64 * comp + 64, 128 - HQ:128], True, True, [kB[kb // 32], qb_], [psB[sb[comp]]])

                def avh(t, p_, pB_, VS=VS, vB=vB, nb=nb, batches=batches):
                    kb0, nseg = batches[t]
                    for comp in range(2):
                        for u in range(nseg):
                            kb = kb0 + nseg - 1 - u
                            self.mm(ps[:, 6, comp * HQ:(comp + 1) * HQ], VS[:, kb, :], p_[:, comp * 512 + u * HQ: comp * 512 + (u + 1) * HQ],
                                    t == 0 and u == 0 and comp == 0, t == nb - 1 and u == nseg - 1, [vB[kb // 32], pB_], [psB[6]])

                def nearh(t, i=i, batches=batches):
                    kb0, nseg = batches[t]
                    if kb0 == 16 * i:
                        return (0, 12, 2048 - 128 * 13 + (128 - HQ))
                    if kb0 == 16 * i - 16:
                        return (0, 4, 2048 - 128 + (128 - HQ))
                    return None

                def odma(i=i, h=h, ybt=ybt, ybb=ybb):
                    P.dma(self.d_YB.ap()[i, :, h, :], ybt, reads=[ybb])

                group_h(nb, qkh, avh, nearh, batches, HQ)
                finalize(HQ, lambda comp: ps[:, 6, comp * HQ:(comp + 1) * HQ], [psB[6], psB[6]], 128 - HQ, ybt, ybb, odma,
                         nred=(16 if i > 0 else 12))
        flush()
        P.barrier()
        A.release(m)

    def phase4a(self):
        P, A = self.P, self.A
        ps, psB = self.ps, self.psB
        m = A.mark()
        win = self.d_win.ap()
        wgl = A.alloc(KC * 2048, BF16).rearrange("p (c n) -> p c n", c=KC)
        wbra = A.alloc(4 * D, BF16).rearrange("p (c n) -> p c n", c=4)
        wbrb = A.alloc(4 * D, BF16).rearrange("p (c n) -> p c n", c=4)
        wo = A.alloc(KC * D, BF16).rearrange("p (c n) -> p c n", c=KC)
        self.load_w(wgl, win[:, :, C_GL:C_GL + 2048], KC, 2048, gain=self.gmix)
        self.load_w(wbra, self.d_wbra.ap(), 4, D)
        self.load_w(wbrb, self.d_wbrb.ap(), 4, D)
        self.load_w(wo, self.d_wo.ap(), KC, D)
        wB = P.buf("w4a")
        xt = A.alloc(KC * HW, F32).rearrange("p (c n) -> p c n", c=KC)
        xB = P.buf("x")
        sq = [A.alloc(HW, BF16) for _ in range(2)]
        sqB = P.bufs(2, "sq")
        hT = A.alloc(KC * HW, BF16).rearrange("p (c n) -> p c n", c=KC)
        hB = P.bufs(KC, "h")
        rstd = A.alloc(HW, F32)
        rB = P.buf("r")
        gates = A.alloc(16 * HW, BF16).rearrange("p (c n) -> p c n", c=16)
        gB = P.bufs(16, "g")
        ya = A.alloc(4 * HW, BF16).rearrange("p (c n) -> p c n", c=4)
        yb = A.alloc(4 * HW, BF16).rearrange("p (c n) -> p c n", c=4)
        yB = P.bufs(2, "y")
        mixed = A.alloc(KC * HW, BF16).rearrange("p (c n) -> p c n", c=KC)
        mxB = P.bufs(KC, "mx")
        t1 = [A.alloc(512, BF16) for _ in range(2)]
        t2 = [A.alloc(512, BF16) for _ in range(2)]
        tB = [P.bufs(2, "t") for _ in range(2)]
        x2 = [A.alloc(512, F32) for _ in range(2)]
        x2B = P.bufs(2, "x2")
        xo = self.d_xo.ap()
        PIECES = ((0, 512), (512, 128))
        cnt = 0
        for i in range(NSLOT):
            P.dma(xt, xo[:, :, i * SLOTW + 128:(i + 1) * SLOTW], writes=[xB], key="x4")
            P.dma(ya, self.d_YA.ap()[i], writes=[yB[0]], key="ya")
            P.dma(yb, self.d_YB.ap()[i], writes=[yB[1]], key="yb")
            self.norm_tile(xt, xB, HW, sq, sqB, hT, hB, rstd, rB, [(0, 512, 0), (512, 128, 7)])
            for (o, w) in PIECES:
                for gc in range(16):
                    bk = 1 + gc % 2
                    for c in range(KC):
                        self.mm(ps[:, bk, 0:w], wgl[:, c, gc * 128:(gc + 1) * 128], hT[:, c, o:o + w], c == 0, c == KC - 1, [wB, hB[c]], [psB[bk]])
                    self.act(gates[:, gc, o:o + w], ps[:, bk, 0:w], AF.Sigmoid, [psB[bk], self.cB], [gB[gc]])
            for (o, w) in PIECES:
                for mc in range(KC):
                    k2 = cnt % 2
                    cnt += 1
                    ba, bb = 3 + 2 * k2, 4 + 2 * k2
                    for r in range(4):
                        self.mm(ps[:, ba, 0:w], wbra[:, r, mc * 128:(mc + 1) * 128], ya[:, r, o:o + w], r == 0, r == 3, [wB, yB[0]], [psB[ba]])
                    for r in range(4):
                        self.mm(ps[:, bb, 0:w], wbrb[:, r, mc * 128:(mc + 1) * 128], yb[:, r, o:o + w], r == 0, r == 3, [wB, yB[1]], [psB[bb]])
                    self.tt("dve", t1[k2][:, 0:w], ps[:, ba, 0:w], gates[:, mc, o:o + w], ALU.mult, [psB[ba], gB[mc]], [tB[k2][0]])
                    self.tt("dve", t2[k2][:, 0:w], ps[:, bb, 0:w], gates[:, 8 + mc, o:o + w], ALU.mult, [psB[bb], gB[8 + mc]], [tB[k2][1]])
                    self.tt("pool", mixed[:, mc, o:o + w], t1[k2][:, 0:w], t2[k2][:, 0:w], ALU.add, [tB[k2][0], tB[k2][1]], [mxB[mc]])
            for (o, w) in PIECES:
                for oc in range(KC):
                    k2 = cnt % 2
                    cnt += 1
                    bk = 1 + k2
                    for mc in range(KC):
                        self.mm(ps[:, bk, 0:w], wo[:, mc, oc * 128:(oc + 1) * 128], mixed[:, mc, o:o + w], mc == 0, mc == KC - 1, [wB, mxB[mc]], [psB[bk]])
                    self.tt("dve", x2[k2][:, 0:w], ps[:, bk, 0:w], xt[:, oc, o:o + w], ALU.add, [psB[bk], xB], [x2B[k2]])
                    P.dma(self.d_X2.ap()[i, :, oc, o:o + w], x2[k2][:, 0:w], reads=[x2B[k2]], key=f"x2o{k2}")
        P.barrier()
        A.release(m)

    def phase4b(self):
        P, A = self.P, self.A
        ps, psB = self.ps, self.psB
        A.release(self.pre_strip_mark)
        m = A.mark()
        wup = A.alloc(KC * 2 * DFF, BF16).rearrange("p (c n) -> p c n", c=KC)
        wdn = A.alloc(NFC * D, BF16).rearrange("p (c n) -> p c n", c=NFC)
        self.load_w(wup, self.d_wup.ap(), KC, 2 * DFF, gain=self.gffn)
        self.load_w(wdn, self.d_wdn.ap(), NFC, D)
        wB = P.buf("w4b")
        W2 = 514
        xa_off = A.alloc_raw(NFC * 512 * 2)
        xaB = P.buf("xa")
        sq = [A.alloc(W2, BF16) for _ in range(2)]
        sqB = P.bufs(2, "sq")
        hT = A.alloc(KC * W2, BF16).rearrange("p (c n) -> p c n", c=KC)
        hB = P.bufs(KC, "h")
        rstd = A.alloc(W2, F32)
        rB = P.buf("r")
        uh = A.alloc(44 * 2, F32).rearrange("p (c n) -> p c n", c=44)
        uhB = P.buf("uh")
        U = [A.alloc(W2, F32) for _ in range(4)]
        UB = P.bufs(4, "U")
        tg = [A.alloc(512, F32) for _ in range(2)]
        tv = [A.alloc(512, F32) for _ in range(2)]
        tgB = P.bufs(2, "tg")
        tvB = P.bufs(2, "tv")
        ot = [A.alloc(512, F32) for _ in range(2)]
        otB = P.bufs(2, "ot")
        xr = [A.alloc(512, F32) for _ in range(2)]
        xrB = P.bufs(2, "xr")
        cw = self.cw.rearrange("p (a b) -> p a b", a=3)
        outT = self.d_out.ap()
        xt = A.at(xa_off, KC * W2, F32).rearrange("p (c n) -> p c n", c=KC)
        aT = A.at(xa_off, NFC * 512, BF16).rearrange("p (c n) -> p c n", c=NFC)
        for i in range(NSLOT):
            P.dma(xt, self.d_X2.ap()[i, :, :, 126:640], writes=[xaB])
            self.norm_tile(xt, xaB, W2, sq, sqB, hT, hB, rstd, rB, [(0, 2, 7), (2, 512, 0)])
            for fc in range(44):
                for c in range(KC):
                    self.mm(ps[:, 7, fc * 2:fc * 2 + 2], wup[:, c, fc * 128:(fc + 1) * 128], hT[:, c, 0:2], c == 0, c == KC - 1, [wB, hB[c]], [psB[7]])
            self.cp("dve", uh, ps[:, 7, 0:88].rearrange("p (c n) -> p c n", c=44), [psB[7]], [uhB])
            for f in range(NFC):
                k2 = f % 2
                for half, fc in enumerate((f, NFC + f)):
                    bk = 1 + 2 * k2 + half
                    ui = 2 * k2 + half
                    for c in range(KC):
                        self.mm(ps[:, bk, :], wup[:, c, fc * 128:(fc + 1) * 128], hT[:, c, 2:W2], c == 0, c == KC - 1, [wB, hB[c]], [psB[bk]])
                    self.cp("pool", U[ui][:, 0:2], uh[:, fc, :], [uhB], [UB[ui]])
                    self.cp("act", U[ui][:, 2:W2], ps[:, bk, :], [psB[bk]], [UB[ui]])
                    t_, tb_ = (tg[k2], tgB[k2]) if half == 0 else (tv[k2], tvB[k2])
                    eng = "dve"
                    self.ts(eng, t_, U[ui][:, 0:512], cw[:, 0, fc:fc + 1], self.cb[:, fc:fc + 1], ALU.mult, ALU.add, [UB[ui], self.sB], [tb_])
                    self.stt(eng, t_, U[ui][:, 1:513], cw[:, 1, fc:fc + 1], t_, ALU.mult, ALU.add, [UB[ui], self.sB, tb_], [tb_])
                    self.stt(eng, t_, U[ui][:, 2:514], cw[:, 2, fc:fc + 1], t_, ALU.mult, ALU.add, [UB[ui], self.sB, tb_], [tb_])
                self.act(tg[k2], tg[k2], AF.Silu, [tgB[k2], self.cB], [tgB[k2]])
                self.tt("pool", aT[:, f, :], tg[k2], tv[k2], ALU.mult, [tgB[k2], tvB[k2]], [xaB])
            for oc in range(KC):
                k2 = oc % 2
                bk = 5 + k2
                P.dma(xr[k2], self.d_X2.ap()[i, :, oc, 128:640], writes=[xrB[k2]])
                for f in range(NFC):
                    self.mm(ps[:, bk, :], wdn[:, f, oc * 128:(oc + 1) * 128], aT[:, f, :], f == 0, f == NFC - 1, [wB, xaB], [psB[bk]])
                self.tt("dve", ot[k2], ps[:, bk, :], xr[k2], ALU.add, [psB[bk], xrB[k2]], [otB[k2]])
                P.dma(outT[:, oc, i * 512:(i + 1) * 512], ot[k2], reads=[otB[k2]])
        P.barrier()
        A.release(m)

    def build(self, nph=N_PHASES):
        self.declare()
        self.P.strict = True
        self.setup()
        self.P.strict = False
        self.P.barrier()
        phases = [self.phase1, self.phase2, self.phase3, self.phase4a, self.phase4b]
        for n_, ph in enumerate(phases[:nph]):
            if n_ == 0 and SKIP_P1:
                continue
            ph()
        self.P.barrier()
        self.P.emit()
        return self.nc


def _host_inputs(inputs):
    f = np.float32
    x = np.asarray(inputs["x"], dtype=f)

    def pc(w, kc):
        n = w.shape[1]
        return np.ascontiguousarray(w.reshape(kc, 128, n).transpose(1, 0, 2))

    w_br_a = np.asarray(inputs["w_br_a"][0], dtype=f)
    wa = w_br_a.reshape(2, 4, 64, D)
    wbra = np.ascontiguousarray(wa.transpose(0, 2, 1, 3).reshape(128, 4, D))
    common = {
        "w_in": pc(np.asarray(inputs["w_in"][0], dtype=f), KC),
        "w_br_a": wbra,
        "w_br_b": pc(np.asarray(inputs["w_br_b"][0], dtype=f), 4),
        "w_o": pc(np.asarray(inputs["w_o"][0], dtype=f), KC),
        "w_up": pc(np.asarray(inputs["w_up"][0], dtype=f), KC),
        "w_down": pc(np.asarray(inputs["w_down"][0], dtype=f), NFC),
        "g_mix": np.ascontiguousarray(np.asarray(inputs["g_mix"][0], dtype=f).reshape(KC, 128).T),
        "g_ffn": np.ascontiguousarray(np.asarray(inputs["g_ffn"][0], dtype=f).reshape(KC, 128).T),
        "conv_w": np.ascontiguousarray(np.asarray(inputs["conv_w"][0], dtype=f).reshape(3, 44, 128).transpose(2, 0, 1)),
        "conv_b": np.ascontiguousarray(np.asarray(inputs["conv_b"][0], dtype=f).reshape(44, 128).T),
        "rel_bias": np.ascontiguousarray(np.asarray(inputs["rel_bias"], dtype=f)),
        "qn_a": np.asarray(inputs["qn_a"], dtype=f).reshape(1, 64),
        "kn_a": np.asarray(inputs["kn_a"], dtype=f).reshape(1, 64),
        "qn_b": np.asarray(inputs["qn_b"], dtype=f).reshape(1, 64),
        "kn_b": np.asarray(inputs["kn_b"], dtype=f).reshape(1, 64),
        "sinks": np.asarray(inputs["sinks"], dtype=f).reshape(1, 8),
        "lam_q1": np.asarray(inputs["lam_q1"], dtype=f).reshape(1, 64),
        "lam_k1": np.asarray(inputs["lam_k1"], dtype=f).reshape(1, 64),
        "lam_q2": np.asarray(inputs["lam_q2"], dtype=f).reshape(1, 64),
        "lam_k2": np.asarray(inputs["lam_k2"], dtype=f).reshape(1, 64),
        "subln_b": np.asarray(inputs["subln_b"], dtype=f).reshape(128, 1),
        "Jmat": np.ascontiguousarray(np.eye(128, dtype=f)[::-1]),
        "bdones": np.kron(np.eye(2, dtype=f), np.ones((64, 64), dtype=f)),
    }
    oha = np.zeros((33, 384), dtype=f)
    for mm_ in range(384):
        d = mm_ - 127
        if 0 <= d < 128:
            oha[int(_t5_bucket_np(np.array(d))), mm_] = 1
        else:
            oha[32, mm_] = 1
    common["oh_a"] = oha
    in_maps = []
    for core in range(8):
        b, j = core // 4, core % 4
        xTb = np.ascontiguousarray(x[b].T.reshape(KC, 128, S).transpose(1, 0, 2))
        xo = np.zeros((128, KC, NSLOT, SLOTW), dtype=f)
        for i in range(NSLOT):
            G = 4 * i + j
            t0 = 512 * G - 256
            lo = max(t0, 0)
            xo[:, :, i, lo - t0:] = xTb[:, :, lo:t0 + SLOTW]
        ohd = np.zeros((33, VECD), dtype=f)
        d = np.arange(VECD) + 512 * j - 2047
        bk = _t5_bucket_np(d)
        for mm_ in range(VECD):
            if d[mm_] >= 0:
                ohd[bk[mm_], mm_] = 1
            else:
                ohd[32, mm_] = 1
        mp = dict(common)
        mp["xT"] = xTb
        mp["xo"] = xo.reshape(128, KC, NSLOT * SLOTW)
        mp["oh_d"] = ohd
        mp["m0"] = np.full((128, 1), 0.0 if j == 0 else 1.0, dtype=f)
        in_maps.append(mp)
    return in_maps


_NC_CACHE = {}


def kernel(**inputs):
    in_maps = _host_inputs(inputs)
    if "nc" not in _NC_CACHE:
        _NC_CACHE["nc"] = K().build()
    nc = _NC_CACHE["nc"]
    res = run_bass_kernel_spmd(nc, in_maps, core_ids=list(range(8)))
    out = np.zeros((2, S, D), dtype=np.float32)
    for core in range(8):
        b, j = core // 4, core % 4
        o = res.results[core]["outT"].reshape(128, KC, NSLOT, 512)
        for i in range(NSLOT):
            G = 4 * i + j
            out[b, 512 * G:512 * (G + 1), :] = o[:, :, i, :].transpose(2, 1, 0).reshape(512, D)
    return out
```

```python
import contextlib
import math
import numpy as np
import concourse.bass as bass
import concourse.mybir as mybir
from concourse.bass_utils import run_bass_kernel_spmd

F32 = mybir.dt.float32
BF16 = mybir.dt.bfloat16
AF = mybir.ActivationFunctionType
ALU = mybir.AluOpType

ENGS = ("pe", "act", "dve", "pool", "sp")
SEM_ROT = 3000


class Buf:
    __slots__ = ("name", "w", "rs")

    def __init__(self, name):
        self.name = name
        self.w = None
        self.rs = []


class Op:
    __slots__ = ("eng", "fn", "waits", "signal", "dma_key", "dma_sem", "dma_val", "sig_sem", "sig_val")

    def __init__(self, eng, fn, dma_key=None):
        self.eng = eng
        self.fn = fn
        self.waits = []
        self.signal = False
        self.dma_key = dma_key
        self.dma_sem = None
        self.dma_val = None
        self.sig_sem = None
        self.sig_val = None


class Prog:
    def __init__(self, nc):
        self.nc = nc
        self.ops = {e: [] for e in ENGS}
        self.dma_cnt = {}
        self.all_dma_last = {}
        self.stack = contextlib.ExitStack()
        self.nbufs = 0
        self.strict = False

    def buf(self, name=None):
        self.nbufs += 1
        return Buf(f"{name or 'b'}#{self.nbufs}")

    def bufs(self, n, name="b"):
        return [self.buf(f"{name}{i}") for i in range(n)]

    def _dep(self, op, y):
        if y is None or y is op:
            return
        if y.dma_key is None and y.eng == op.eng and op.dma_key is None and not self.strict:
            return
        if y.dma_key is None:
            y.signal = True
        if y not in op.waits:
            op.waits.append(y)

    def op(self, eng, fn, reads=(), writes=(), dma_key=None):
        o = Op(eng, fn, dma_key)
        for b in reads:
            self._dep(o, b.w)
        for b in writes:
            self._dep(o, b.w)
            for r in b.rs:
                self._dep(o, r)
        for b in reads:
            b.rs.append(o)
        for b in writes:
            b.w = o
            b.rs = []
        if dma_key is not None:
            st = self.dma_cnt.setdefault(dma_key, [0, 0])
            if st[1] + 16 > 4000:
                st[0] += 1
                st[1] = 0
            st[1] += 16
            o.dma_sem = (dma_key, st[0])
            o.dma_val = st[1]
            self.all_dma_last[dma_key] = o
        self.ops[eng].append(o)
        return o

    def dma(self, out, in_, reads=(), writes=(), key=None, eng="sp", **kw):
        prim = writes[0] if len(writes) else reads[0]
        key = prim.name
        return self.op(eng, lambda e: e.dma_start(out=out, in_=in_, **kw), reads, writes, dma_key=key)

    def barrier(self):
        lasts = [self.ops[e][-1] for e in ENGS if self.ops[e]]
        dmas = list(self.all_dma_last.values())
        news = []
        for e in ENGS:
            o = Op(e, None)
            for y in lasts:
                self._dep(o, y)
            for y in dmas:
                self._dep(o, y)
            news.append(o)
        for o in news:
            self.ops[o.eng].append(o)

    def emit(self):
        nc = self.nc
        semkeys = set()
        for e in ENGS:
            gen, cnt = 0, 0
            for o in self.ops[e]:
                if o.dma_key is not None:
                    semkeys.add(o.dma_sem)
                    continue
                if o.signal:
                    if cnt >= SEM_ROT:
                        gen += 1
                        cnt = 0
                    cnt += 1
                    o.sig_sem = ("eng", e, gen)
                    o.sig_val = cnt
                    semkeys.add(o.sig_sem)
        sems = {}
        for n, k in enumerate(sorted(semkeys, key=str)):
            sems[k] = self.stack.enter_context(nc.semaphore(f"sm{n}"))
        self.nsems = len(sems)
        block = self.stack.enter_context(nc.Block())
        engmap = {"pe": "tensor", "act": "scalar", "dve": "vector", "pool": "gpsimd", "sp": "sync"}

        def make(e):
            def body(eng):
                waited = {}
                for o in self.ops[e]:
                    for y in o.waits:
                        if y.dma_key is not None:
                            sk, v = y.dma_sem, y.dma_val
                        else:
                            sk, v = y.sig_sem, y.sig_val
                        if waited.get(sk, 0) >= v:
                            continue
                        waited[sk] = v
                        eng.wait_ge(sems[sk], v)
                    if o.fn is None:
                        if o.signal:
                            eng.nop().then_inc(sems[o.sig_sem], 1)
                        continue
                    ins = o.fn(eng)
                    if o.dma_key is not None:
                        ins.then_inc(sems[o.dma_sem], 16)
                    elif o.signal:
                        ins.then_inc(sems[o.sig_sem], 1)
            return body

        for e in ENGS:
            getattr(block, engmap[e])(make(e))
        self.stack.close()


class Arena:
    def __init__(self, prog, nbytes, name="arena"):
        nc = prog.nc
        self.t8 = prog.stack.enter_context(nc.sbuf_tensor(name, [128, nbytes], mybir.dt.uint8))
        self.views = {}
        self.nbytes = nbytes
        self.off = 0

    def view(self, dt):
        if dt not in self.views:
            self.views[dt] = self.t8.bitcast(dt)
        return self.views[dt]

    def alloc(self, nelem, dt):
        sz = mybir.dt.size(dt)
        self.off = (self.off + 63) // 64 * 64
        o = self.off
        self.off += nelem * sz
        assert self.off <= self.nbytes, f"arena overflow {self.off} > {self.nbytes}"
        return self.view(dt)[:, o // sz: o // sz + nelem]

    def alloc_raw(self, nbytes):
        self.off = (self.off + 63) // 64 * 64
        o = self.off
        self.off += nbytes
        assert self.off <= self.nbytes, f"arena overflow {self.off} > {self.nbytes}"
        return o

    def at(self, o, nelem, dt):
        sz = mybir.dt.size(dt)
        return self.view(dt)[:, o // sz: o // sz + nelem]

    def mark(self):
        return self.off

    def release(self, m):
        self.off = m


D = 1024
S = 16384
KC = 8
NSLOT = 8
SLOTW = 768
HW = 640
DFF = 2816
NFC = 22
EPS = 1e-6
LAM_INIT = 0.8 - 0.6 * math.exp(-0.3 * 0)
VECD = 2688
STRIPW = 2560
C_QA, C_KA, C_VA, C_QB, C_KB, C_VB, C_GL = 0, 512, 640, 768, 1280, 1792, 2304

DEBUG_SCRATCH = False
STORE_ENG = "pool"
P2_STAGE = 9
P2_SLOTS = 8
SKIP_P1 = False
N_PHASES = 5


def _t5_bucket_np(rel):
    n = np.maximum(rel, 0)
    nf = np.maximum(n, 1).astype(np.float32)
    large = 16 + (np.log(nf / np.float32(16)) / np.float32(math.log(8.0)) * np.float32(16)).astype(np.int32)
    large = np.minimum(large, 31)
    return np.where(n < 16, n, large)


class K:
    def __init__(self):
        nc = bass.Bass("TRN2", target_bir_lowering=False)
        self.nc = nc
        self.P = Prog(nc)
        self.A = Arena(self.P, 209920)
        self.ps = self.P.stack.enter_context(nc.psum_tensor("ps", [128, 8, 512], F32))
        self.psB = self.P.bufs(8, "psb")

    def mm(self, out, lhsT, rhs, start, stop, R, W):
        self.P.op("pe", lambda e: e.matmul(out, lhsT=lhsT, rhs=rhs, start=start, stop=stop), R, W)

    def act(self, out, in_, func, R, W, bias=None, scale=1.0):
        b = self.zero1 if bias is None else bias
        npart = out.shape[0]
        if npart != 128 and b.shape[0] == 128:
            b = b[0:npart, :]
        self.P.op("act", lambda e: e.activation(out=out, in_=in_, func=func, bias=b, scale=scale), R, W)

    def tt(self, eng, out, a, b, op, R, W):
        self.P.op(eng, lambda e: e.tensor_tensor(out=out, in0=a, in1=b, op=op), R, W)

    def ts(self, eng, out, a, s1, s2, op0, op1, R, W):
        if op1 is None:
            self.P.op(eng, lambda e: e.tensor_scalar(out=out, in0=a, scalar1=s1, scalar2=None, op0=op0), R, W)
        else:
            self.P.op(eng, lambda e: e.tensor_scalar(out=out, in0=a, scalar1=s1, scalar2=s2, op0=op0, op1=op1), R, W)

    def stt(self, eng, out, a, s, b, op0, op1, R, W):
        self.P.op(eng, lambda e: e.scalar_tensor_tensor(out=out, in0=a, scalar=s, in1=b, op0=op0, op1=op1), R, W)

    def cp(self, eng, out, in_, R, W):
        if eng == "act":
            self.act(out, in_, AF.Identity, R, W)
        else:
            self.P.op(eng, lambda e: e.tensor_copy(out=out, in_=in_), R, W)

    def rcp(self, out, in_, R, W):
        self.P.op("dve", lambda e: e.reciprocal(out=out, in_=in_), R, W)

    def rsqrt_act(self, out, in_, scale, R, W, power=-0.5, bias=None):
        self.act(out, in_, AF.Ln, R, W, bias=self.eps1 if bias is None else bias, scale=scale)
        old = self.P.strict
        if out.shape[-1] <= 256:
            self.P.strict = True
        self.act(out, out, AF.Exp, list(W) + [self.cB], W, scale=power)
        self.P.strict = old

    def memset(self, eng, ap, val, W):
        self.P.op(eng, lambda e: e.memset(ap, val), (), W)

    def declare(self):
        nc = self.nc

        def din(name, shape, dt=F32):
            return nc.dram_tensor(name, list(shape), dt, kind="ExternalInput")

        def dscr(name, shape, dt):
            return nc.dram_tensor(name, list(shape), dt, kind="ExternalOutput" if DEBUG_SCRATCH else "Internal")

        self.d_xT = din("xT", [128, KC, S])
        self.d_xo = din("xo", [128, KC, NSLOT * SLOTW])
        self.d_win = din("w_in", [128, KC, 4352])
        self.d_wbra = din("w_br_a", [128, 4, D])
        self.d_wbrb = din("w_br_b", [128, 4, D])
        self.d_wo = din("w_o", [128, KC, D])
        self.d_wup = din("w_up", [128, KC, 2 * DFF])
        self.d_wdn = din("w_down", [128, NFC, D])
        self.d_gmix = din("g_mix", [128, KC])
        self.d_gffn = din("g_ffn", [128, KC])
        self.d_cw = din("conv_w", [128, 3, 44])
        self.d_cb = din("conv_b", [128, 44])
        self.d_relb = din("rel_bias", [32, 12])
        self.d_qna = din("qn_a", [1, 64])
        self.d_kna = din("kn_a", [1, 64])
        self.d_qnb = din("qn_b", [1, 64])
        self.d_knb = din("kn_b", [1, 64])
        self.d_sinks = din("sinks", [1, 8])
        self.d_lq1 = din("lam_q1", [1, 64])
        self.d_lk1 = din("lam_k1", [1, 64])
        self.d_lq2 = din("lam_q2", [1, 64])
        self.d_lk2 = din("lam_k2", [1, 64])
        self.d_subln = din("subln_b", [128, 1])
        self.d_oha = din("oh_a", [33, 384])
        self.d_ohd = din("oh_d", [33, VECD])
        self.d_m0 = din("m0", [128, 1])
        self.d_J = din("Jmat", [128, 128])
        self.d_bd = din("bdones", [128, 128])
        self.d_out = nc.dram_tensor("outT", [128, KC, NSLOT * 512], F32, kind="ExternalOutput")
        self.d_KT = dscr("s_KT", [4, 128, S], BF16)
        self.d_VS = dscr("s_VS", [4, 128, 128, 128], BF16)
        self.d_QB = dscr("s_QB", [NSLOT, 128, 4, HW], BF16)
        self.d_YA = dscr("s_YA", [NSLOT, 128, 4, HW], BF16)
        self.d_YB = dscr("s_YB", [NSLOT, 128, 4, HW], BF16)
        self.d_X2 = dscr("s_X2", [NSLOT, 128, KC, HW], F32)
        if DEBUG_SCRATCH:
            self.d_dstrip = dscr("s_strip", [128, 4 * STRIPW], BF16)
            self.d_dsa = dscr("s_sa", [128, 2 * 8 * 128], BF16)
            self.d_dsmall = dscr("s_small", [128, 8], F32)
        self.d_bc = nc.dram_tensor("s_bc", [2, 3, 512], F32, kind="Internal")
        self.d_VA = dscr("s_VECA", [8, 384], F32)
        self.d_VD = dscr("s_VECD", [4, VECD], F32)

    def setup(self):
        P, A, nc = self.P, self.A, self.nc
        ps, psB = self.ps, self.psB
        cB = P.buf("consts")
        self.cB = cB
        self.zero1 = A.alloc(1, F32)
        self.eps1 = A.alloc(1, F32)
        self.tiny1 = A.alloc(1, F32)
        self.ones_bf = A.alloc(128, BF16)
        self.ones_f = A.alloc(128, F32)
        self.bd_bf = A.alloc(128, BF16)
        self.J = A.alloc(128, F32)
        bd_f = A.alloc(128, F32)
        self.memset("pool", self.zero1, 0.0, [cB])
        self.memset("pool", self.eps1, EPS, [cB])
        self.memset("pool", self.tiny1, 1e-30, [cB])
        self.memset("pool", self.ones_bf, 1.0, [cB])
        self.memset("pool", self.ones_f, 1.0, [cB])
        jB = P.buf("J")
        P.dma(self.J, self.d_J.ap(), writes=[jB], key="c0")
        P.dma(bd_f, self.d_bd.ap(), writes=[jB], key="c0")
        self.cp("dve", self.bd_bf, bd_f, [jB], [cB])
        sB = P.buf("small")
        self.gmix = A.alloc(KC, F32)
        self.gffn = A.alloc(KC, F32)
        self.cw = A.alloc(3 * 44, F32)
        self.cb = A.alloc(44, F32)
        self.m0 = A.alloc(1, F32)
        P.dma(self.gmix, self.d_gmix.ap(), writes=[sB], key="c1")
        P.dma(self.gffn, self.d_gffn.ap(), writes=[sB], key="c1")
        P.dma(self.cw, self.d_cw.ap().rearrange("p a b -> p (a b)"), writes=[sB], key="c1")
        P.dma(self.cb, self.d_cb.ap(), writes=[sB], key="c1")
        P.dma(self.m0, self.d_m0.ap(), writes=[sB], key="c1")
        self.gq_a = A.alloc(1, F32)
        self.gk_a = A.alloc(1, F32)
        self.gq_b = A.alloc(1, F32)
        self.gk_b = A.alloc(1, F32)
        for dst, src in ((self.gq_a, self.d_qna), (self.gk_a, self.d_kna), (self.gq_b, self.d_qnb), (self.gk_b, self.d_knb)):
            for hlf in range(2):
                P.dma(dst[64 * hlf:64 * hlf + 64, :], bass.AP(src, 0, [[1, 64], [1, 1]]), writes=[sB], key="c1")
        self.gsub = A.alloc(1, F32)
        P.dma(self.gsub, self.d_subln.ap(), writes=[sB], key="c1")
        self.ts("dve", self.gsub, self.gsub, 1.0 - LAM_INIT, None, ALU.mult, None, [sB], [sB])
        lam4 = A.alloc(4 * 64, F32)
        for n_, src in enumerate((self.d_lq1, self.d_lk1, self.d_lq2, self.d_lk2)):
            P.dma(lam4[:, n_ * 64:(n_ + 1) * 64], bass.AP(src, 0, [[0, 128], [1, 64]]), writes=[sB], key="c1")
        lp = A.alloc(2 * 64, F32)
        ls = A.alloc(2, F32)
        self.neglam = A.alloc(1, F32)
        self.tt("dve", lp[:, 0:64], lam4[:, 0:64], lam4[:, 64:128], ALU.mult, [sB], [sB])
        self.tt("dve", lp[:, 64:128], lam4[:, 128:192], lam4[:, 192:256], ALU.mult, [sB], [sB])
        P.op("dve", lambda e: e.reduce_sum(out=ls[:, 0:1], in_=lp[:, 0:64], axis=mybir.AxisListType.X), [sB], [sB])
        P.op("dve", lambda e: e.reduce_sum(out=ls[:, 1:2], in_=lp[:, 64:128], axis=mybir.AxisListType.X), [sB], [sB])
        self.act(ls, ls, AF.Exp, [sB, cB], [sB])
        self.tt("dve", self.neglam, ls[:, 1:2], ls[:, 0:1], ALU.subtract, [sB], [sB])
        self.ts("dve", self.neglam, self.neglam, -LAM_INIT, None, ALU.add, None, [sB], [sB])
        self.sB = sB
        self.esrow = A.alloc(2 * 512, F32)
        sk = A.alloc(8, F32)
        P.dma(sk[0:1, :], self.d_sinks.ap(), writes=[sB], key="c1")
        self.act(sk[0:1, :], sk[0:1, :], AF.Exp, [sB, cB], [sB])
        for hq in range(8):
            g_, r_ = hq // 4, hq % 4
            col = g_ * 512 + ((r_ % 2) * 2 + r_ // 2) * 128
            self.ts("dve", self.esrow[0:1, col:col + 128], self.ones_f[0:1, 0:128], sk[0:1, hq:hq + 1], None, ALU.mult, None, [sB, cB], [sB])

        tB = P.buf("tab")
        tabp = A.alloc(12, F32)
        tab31 = A.alloc(4, F32)
        self.memset("pool", tabp[32:33, :], -30000.0, [tB])
        P.dma(tabp[0:32, :], self.d_relb.ap(), writes=[tB], key="c2")
        P.dma(tab31[0:32, :], bass.AP(self.d_relb, 31 * 12 + 8, [[0, 32], [1, 4]]), writes=[tB], key="c2")
        self.tt("dve", tabp[0:32, 8:12], tabp[0:32, 8:12], tab31[0:32, :], ALU.subtract, [tB], [tB])
        m = A.mark()
        oha = A.alloc(384, F32)
        ohd = A.alloc(VECD, F32)
        veca = A.alloc(384, F32)
        vecd = A.alloc(VECD, F32)
        ohB = P.buf("oh")
        P.dma(oha[0:33, :], self.d_oha.ap(), writes=[ohB], key="c2")
        P.dma(ohd[0:33, :], self.d_ohd.ap(), writes=[ohB], key="c2")
        vB = P.buf("vec")
        self.mm(ps[0:8, 0, 0:384], tabp[0:33, 0:8], oha[0:33, :], True, True, [tB, ohB], [psB[0]])
        self.act(veca[0:8, :], ps[0:8, 0, 0:384], AF.Exp, [psB[0], cB], [vB])
        for pc in range(6):
            w = 512 if pc < 5 else VECD - 2560
            bk = 1 + pc % 2
            self.mm(ps[0:4, bk, 0:w], tabp[0:33, 8:12], ohd[0:33, pc * 512: pc * 512 + w], True, True, [tB, ohB], [psB[bk]])
            self.act(vecd[0:4, pc * 512: pc * 512 + w], ps[0:4, bk, 0:w], AF.Exp, [psB[bk], cB], [vB])
        dvB = P.buf("dvec")
        P.dma(self.d_VA.ap(), veca[0:8, :], reads=[vB], writes=[dvB], key="c3")
        P.dma(self.d_VD.ap(), vecd[0:4, :], reads=[vB], writes=[dvB], key="c3")
        A.release(m)
        self.pre_strip_mark = A.mark()
        self.strip = A.alloc(4 * STRIPW, BF16).rearrange("p (h u) -> p h u", h=4)
        self.sa = A.alloc(2 * 8 * 128, BF16).rearrange("p (t h q) -> p t h q", t=2, h=8)
        self.stripB = P.buf("strip")
        self.persist_mark = A.mark()
        m = A.mark()
        rev = A.alloc(STRIPW, F32)
        revB = P.bufs(2, "rev")
        P.dma(rev[:, 0:2048].rearrange("p (h u) -> p h u", h=8), bass.AP(self.d_VA, 0, [[1, 128], [384, 8], [1, 256]]),
              reads=[dvB], writes=[revB[0]], key="c4")
        for pi in range(4):
            bk = pi % 2
            self.mm(ps[:, bk, :], self.J, rev[:, pi * 512:(pi + 1) * 512], True, True, [jB, revB[0]], [psB[bk]])
            src = ps[:, bk, :].rearrange("p (h t q) -> p h t q", h=2, t=2)
            for ty in range(2):
                self.cp("dve" if ty == 0 else "act", self.sa[:, ty, 2 * pi:2 * pi + 2, :], src[:, :, ty, :], [psB[bk]], [self.stripB])
        for h in range(4):
            rb = revB[(h + 1) % 2]
            P.dma(rev, bass.AP(self.d_VD, h * VECD, [[1, 128], [1, STRIPW]]), reads=[dvB], writes=[revB[0], revB[1]], key="c4")
            for pc in range(5):
                bk = pc % 2
                self.mm(ps[:, bk, :], self.J, rev[:, pc * 512:(pc + 1) * 512], True, True, [jB, revB[0], revB[1]], [psB[bk]])
                self.cp("dve" if pc % 2 == 0 else "act", self.strip[:, h, pc * 512:(pc + 1) * 512], ps[:, bk, :], [psB[bk]], [self.stripB])
        A.release(m)
        if DEBUG_SCRATCH:
            P.dma(self.d_dstrip.ap(), self.strip.rearrange("p h u -> p (h u)"), reads=[self.stripB])
            P.dma(self.d_dsa.ap(), self.sa.rearrange("p t h q -> p (t h q)"), reads=[self.stripB])
            for n_, t_ in enumerate((self.neglam, self.gsub, self.gq_b, self.gk_b)):
                P.dma(self.d_dsmall.ap()[:, n_:n_ + 1], t_, reads=[self.sB], allow_slow_non_contiguous=True)

    def load_w(self, dst, src, kc, ncols, gain=None, dup=None, key="w"):
        P, A = self.P, self.A
        m = A.mark()
        pw = 512 if kc <= 8 else 128
        stg = [A.alloc(kc * pw, F32).rearrange("p (c n) -> p c n", c=kc) for _ in range(2)]
        sb = P.bufs(2, "stg")
        wB = P.buf("wdst")
        engs = ("dve", "pool", "act")
        n = 0
        for i, c0 in enumerate(range(0, ncols, pw)):
            w = min(pw, ncols - c0)
            s, b = stg[i % 2], sb[i % 2]
            P.dma(s[:, :, 0:w], src[:, :, c0:c0 + w], writes=[b], key=f"{key}{i % 2}")
            for c in range(kc):
                eng = engs[n % 3]
                n += 1
                if gain is None:
                    self.cp(eng, dst[:, c, c0:c0 + w], s[:, c, 0:w], [b], [wB])
                elif eng == "act":
                    self.act(dst[:, c, c0:c0 + w], s[:, c, 0:w], AF.Identity, [b, self.sB, self.cB], [wB], scale=gain[:, c:c + 1])
                else:
                    self.ts(eng, dst[:, c, c0:c0 + w], s[:, c, 0:w], gain[:, c:c + 1], None, ALU.mult, None, [b, self.sB], [wB])
        self.P.barrier()
        A.release(m)
        return wB

    def norm_tile(self, xt, xB, n, sq, sqB, hT, hB, rstd, rB, pieces):
        ps, psB = self.ps, self.psB
        for c in range(KC):
            self.act(sq[c % 2][:, 0:n], xt[:, c, :], AF.Square, [xB, self.cB], [sqB[c % 2]])
            for (o, w, bank) in pieces:
                self.mm(ps[:, bank, 0:w], self.ones_bf, sq[c % 2][:, o:o + w], c == 0, c == KC - 1, [sqB[c % 2], self.cB], [psB[bank]])
        for (o, w, bank) in pieces:
            small = w < 128
            self.P.strict = small
            self.rsqrt_act(rstd[:, o:o + w], ps[:, bank, 0:w], 1.0 / D, [psB[bank], self.cB], [rB])
            self.P.strict = False
        for c in range(KC):
            self.tt("dve" if c % 2 == 0 else "pool", hT[:, c, 0:n], xt[:, c, :], rstd[:, 0:n], ALU.mult, [xB, rB], [hB[c]])

    def ph_tasks(self, tasks, hT, hB, wB, tmp, tmpB, pbanks, nbanks, mid=None):
        ps, psB = self.ps, self.psB
        n = len(tasks)

        def proj(k):
            wfn, o, w, out, outB, gain = tasks[k]
            bk = pbanks[k % len(pbanks)]
            ksq, _ = tmp[k % len(tmp)]
            for c in range(KC):
                self.mm(ps[:, bk, 0:w], wfn(c), hT[:, c, o:o + w], c == 0, c == KC - 1, [wB, hB[c]], [psB[bk]])
            self.act(ksq[:, 0:w], ps[:, bk, 0:w], AF.Square, [psB[bk], self.cB], [tmpB[k % len(tmp)][0]])

        def norm(k):
            wfn, o, w, out, outB, gain = tasks[k]
            bk = pbanks[k % len(pbanks)]
            bn = nbanks[k % len(nbanks)]
            ksq, rk = tmp[k % len(tmp)]
            tb = tmpB[k % len(tmp)]
            self.mm(ps[:, bn, 0:w], self.bd_bf, ksq[:, 0:w], True, True, [tb[0], self.cB], [psB[bn]])
            self.rsqrt_act(rk[:, 0:w], ps[:, bn, 0:w], 1.0 / 64, [psB[bn], self.cB], [tb[1]])
            self.stt("dve", out, ps[:, bk, 0:w], gain, rk[:, 0:w], ALU.mult, ALU.mult, [psB[bk], tb[1], self.sB], [outB])

        for k in range(n + 1):
            if k < n:
                proj(k)
            if k == n and mid is not None:
                mid()
            if k >= 1:
                norm(k - 1)

    def phase1(self):
        P, A = self.P, self.A
        ps, psB = self.ps, self.psB
        m = A.mark()
        wkb = A.alloc(KC * 512, BF16).rearrange("p (c n) -> p c n", c=KC)
        wvb = A.alloc(KC * 512, BF16).rearrange("p (c n) -> p c n", c=KC)
        win = self.d_win.ap()
        wB1 = self.load_w(wkb, win[:, :, C_KB:C_KB + 512], KC, 512, gain=self.gmix)
        wB2 = self.load_w(wvb, win[:, :, C_VB:C_VB + 512], KC, 512, gain=self.gmix)
        xt = [A.alloc(KC * 512, F32).rearrange("p (c n) -> p c n", c=KC) for _ in range(2)]
        xB = P.bufs(2, "x")
        sq = [A.alloc(512, BF16) for _ in range(2)]
        sqB = P.bufs(2, "sq")
        hT = [A.alloc(KC * 512, BF16).rearrange("p (c n) -> p c n", c=KC) for _ in range(2)]
        hB = [P.bufs(KC, "h") for _ in range(2)]
        rstd = [A.alloc(512, F32) for _ in range(2)]
        rB = P.bufs(2, "r")
        tmp = [(A.alloc(512, BF16), A.alloc(512, F32)) for _ in range(3)]
        tmpB = [P.bufs(2, "tmp") for _ in range(3)]
        kout = [A.alloc(4 * 512, BF16).rearrange("p (h n) -> p h n", h=4) for _ in range(2)]
        koB = [P.bufs(4, "ko") for _ in range(2)]
        vout = [A.alloc(4 * 512, BF16).rearrange("p (b n) -> p b n", b=4) for _ in range(2)]
        voB = [P.bufs(4, "vo") for _ in range(2)]
        xT = self.d_xT.ap()
        NT = S // 512

        def stats(T):
            pp = T % 2
            P.dma(xt[pp], xT[:, :, T * 512:(T + 1) * 512], writes=[xB[pp]], key=f"x{pp}")
            self.norm_tile(xt[pp], xB[pp], 512, sq, sqB, hT[pp], hB[pp], rstd[pp], rB[pp], [(0, 512, 0)])

        stats(0)
        for T in range(NT):
            pp = T % 2
            if T + 1 < NT:
                stats(T + 1)
            tasks = [((lambda c, hh=hh: wkb[:, c, hh * 128:(hh + 1) * 128]), 0, 512, kout[pp][:, hh, :], koB[pp][hh], self.gk_b) for hh in range(4)]
            self.ph_tasks(tasks, hT[pp], hB[pp], wB1, tmp, tmpB, (1, 2, 3), (6, 7))
            for hh in range(4):
                P.dma(self.d_KT.ap()[hh, :, T * 512:(T + 1) * 512], kout[pp][:, hh, :], reads=[koB[pp][hh]], key=f"ko{pp}", eng=STORE_ENG)
            for blk in range(4):
                bk = 4 + blk % 2
                for c in range(KC):
                    self.mm(ps[:, bk, :], hT[pp][:, c, blk * 128:(blk + 1) * 128], wvb[:, c, :], c == 0, c == KC - 1,
                            [hB[pp][c], wB2], [psB[bk]])
                self.cp("act" if blk % 2 == 0 else "dve", vout[pp][:, blk, :], ps[:, bk, :], [psB[bk]], [voB[pp][blk]])
            for hh in range(4):
                P.dma(self.d_VS.ap()[hh, :, 4 * T:4 * T + 4, :], vout[pp][:, :, hh * 128:(hh + 1) * 128],
                      reads=voB[pp], key=f"vo{pp}", eng=STORE_ENG)
        P.barrier()
        A.release(m)

    def phase2(self):
        P, A = self.P, self.A
        ps, psB = self.ps, self.psB
        m = A.mark()
        win = self.d_win.ap()

        def walloc(n):
            return A.alloc(KC * n, BF16).rearrange("p (c n) -> p c n", c=KC)

        wqa, wka2, wva, wqb = walloc(512), walloc(256), walloc(128), walloc(512)
        wBq = self.load_w(wqa, win[:, :, C_QA:C_QA + 512], KC, 512, gain=self.gmix)
        for g in range(2):
            for hlf in range(2):
                self.load_w(wka2[:, :, g * 128 + hlf * 64: g * 128 + hlf * 64 + 64], win[:, :, C_KA + 64 * g:C_KA + 64 * g + 64], KC, 64,
                            gain=self.gmix)
        self.load_w(wva, win[:, :, C_VA:C_VA + 128], KC, 128, gain=self.gmix)
        self.load_w(wqb, win[:, :, C_QB:C_QB + 512], KC, 512, gain=self.gmix)
        wB = P.buf("w2")
        xts = [A.alloc(KC * SLOTW, F32).rearrange("p (c n) -> p c n", c=KC) for _ in range(2)]
        xBs = P.bufs(2, "x")
        sq = [A.alloc(SLOTW, BF16) for _ in range(2)]
        sqB = P.bufs(2, "sq")
        hTs = [A.alloc(KC * SLOTW, BF16).rearrange("p (c n) -> p c n", c=KC) for _ in range(2)]
        hBs = [P.bufs(KC, "h") for _ in range(2)]
        rstds = [A.alloc(SLOTW, F32) for _ in range(2)]
        rBs = P.bufs(2, "r")
        tmp = [(A.alloc(512, BF16), A.alloc(512, F32)) for _ in range(3)]
        tmpB = [P.bufs(2, "tmp") for _ in range(3)]
        qaT = A.alloc(4 * HW, BF16).rearrange("p (c n) -> p c n", c=4)
        qaB = P.bufs(4, "qa")
        kaT = A.alloc(2 * SLOTW, BF16).rearrange("p (g n) -> p g n", g=2)
        kaB = P.bufs(2, "ka")
        va = A.alloc(6 * 128, BF16).rearrange("p (b n) -> p b n", b=6)
        vaB = P.buf("va")
        qbT = A.alloc(4 * HW, BF16).rearrange("p (c n) -> p c n", c=4)
        qbB = P.buf("qb")
        yaT = A.alloc(4 * HW, BF16).rearrange("p (c n) -> p c n", c=4)
        yaB = P.buf("ya")
        pt = [A.alloc(512, BF16) for _ in range(4)]
        ptB = P.bufs(4, "pt")
        den = A.alloc(512, F32)
        denB = P.buf("den")
        xo = self.d_xo.ap()
        nsl = min(NSLOT, P2_SLOTS)

        def stats(i):
            pp = i % 2
            P.dma(xts[pp], xo[:, :, i * SLOTW:(i + 1) * SLOTW], writes=[xBs[pp]], key="x2")
            self.norm_tile(xts[pp], xBs[pp], SLOTW, sq, sqB, hTs[pp], hBs[pp], rstds[pp], rBs[pp], [(0, 512, 0), (512, 256, 7)])

        stats(0)
        for i in range(nsl):
            hT, hB = hTs[i % 2], hBs[i % 2]
            tasks = []
            for (o, w) in ((128, 512), (640, 128)):
                for cm in range(4):
                    tasks.append(((lambda c, cm=cm: wqa[:, c, cm * 128:(cm + 1) * 128]), o, w, qaT[:, cm, o - 128:o - 128 + w], qaB[cm], self.gq_a))
                for cm in range(4):
                    tasks.append(((lambda c, cm=cm: wqb[:, c, cm * 128:(cm + 1) * 128]), o, w, qbT[:, cm, o - 128:o - 128 + w], qbB, self.gq_b))
            for (o, w) in ((0, 512), (512, 256)):
                for g in range(2):
                    tasks.append(((lambda c, g=g: wka2[:, c, g * 128:(g + 1) * 128]), o, w, kaT[:, g, o:o + w], kaB[g], self.gk_a))
            self.ph_tasks(tasks, hT, hB, wB, tmp, tmpB, (1, 2, 3), (5, 6))
            P.dma(self.d_QB.ap()[i], qbT, reads=[qbB], key="qbo", eng=STORE_ENG)
            for half in range(2):
                bk = 4 + half
                for bl in range(3):
                    blk = half * 3 + bl
                    for c in range(KC):
                        self.mm(ps[:, bk, bl * 128:(bl + 1) * 128], hT[:, c, blk * 128:(blk + 1) * 128], wva[:, c, :], c == 0, c == KC - 1,
                                [hB[c], wB], [psB[bk]])
                self.cp("act", va[:, half * 3:half * 3 + 3, :], ps[:, bk, 0:384].rearrange("p (b n) -> p b n", b=3), [psB[bk]], [vaB])
            if i + 1 < nsl:
                stats(i + 1)
            for n in range(1, 6 if P2_STAGE >= 1 else 0):
                qo = (n - 1) * 128
                for g in range(2):
                    for kk, kblk in enumerate((n - 1, n)):
                        idx = g * 2 + kk
                        b0 = (2, 6)[idx % 2]
                        for r in range(4):
                            par, rr = r % 2, r // 2
                            pb = 64 * par
                            self.mm(ps[:, b0 + par, rr * 128:(rr + 1) * 128], kaT[pb:pb + 64, g, kblk * 128:(kblk + 1) * 128],
                                    qaT[pb:pb + 64, 2 * g + rr, qo:qo + 128], True, True,
                                    [kaB[g], qaB[2 * g + rr]], [psB[b0 + par]])
                        self.act(pt[idx].rearrange("p (a n) -> p a n", a=2), ps[:, b0:b0 + 2, 0:256], AF.Exp,
                                 [psB[b0], psB[b0 + 1], self.cB], [ptB[idx]], scale=0.125)
                        ty = 1 if kk == 0 else 0
                        fa = self.sa[:, ty, 4 * g:4 * g + 4, :].rearrange("p (rr par) q -> p par rr q", par=2)
                        p4 = pt[idx].rearrange("p (par rr q) -> p par rr q", par=2, rr=2)
                        self.tt("pool", p4, p4, fa, ALU.mult, [ptB[idx], self.stripB], [ptB[idx]])
                        if i == 0 and n == 2 and kk == 0:
                            self.ts("pool", pt[idx], pt[idx], self.m0[:, 0:1], None, ALU.mult, None, [ptB[idx], self.sB], [ptB[idx]])
                if P2_STAGE < 2:
                    continue
                for g in range(2):
                    for kk, kblk in enumerate((n - 1, n)):
                        idx = g * 2 + kk
                        self.mm(ps[64 * g:64 * g + 64, 4, :], va[:, kblk, 64 * g:64 * g + 64], pt[idx], kk == 0, kk == 1,
                                [vaB, ptB[idx]], [psB[4]])
                    for kk in range(2):
                        idx = g * 2 + kk
                        self.mm(ps[64 * g:64 * g + 64, 5, :], self.ones_bf[:, 0:64], pt[idx], kk == 0, (kk == 1 and P2_STAGE < 3),
                                [ptB[idx], self.cB], [psB[5]])
                    if P2_STAGE >= 3:
                        self.mm(ps[64 * g:64 * g + 64, 5, :], self.ones_f[0:1, 0:64], self.esrow[0:1, g * 512:(g + 1) * 512], False, True,
                                [self.sB, self.cB], [psB[5]])
                self.rsqrt_act(den, ps[:, 5, :], 1.0, [psB[5], self.cB], [denB], power=-1.0, bias=self.zero1)
                self.tt("dve", yaT[:, :, qo:qo + 128].rearrange("p (rr par) q -> p par rr q", par=2),
                        ps[:, 4, :].rearrange("p (par rr q) -> p par rr q", par=2, rr=2),
                        den.rearrange("p (par rr q) -> p par rr q", par=2, rr=2), ALU.mult, [psB[4], denB], [yaB])
            P.dma(self.d_YA.ap()[i], yaT, reads=[yaB], key="yao", eng=STORE_ENG)
        P.barrier()
        A.release(m)

    def phase3(self):
        P, A = self.P, self.A
        ps, psB = self.ps, self.psB
        m = A.mark()
        KTs = [A.alloc(S, BF16) for _ in range(2)]
        VSs = [A.alloc(128 * 128, BF16).rearrange("p (b e) -> p b e", b=128) for _ in range(2)]
        kvB = [(P.bufs(4, "kt"), P.bufs(4, "vs")) for _ in range(2)]
        qt = [A.alloc(HW, BF16) for _ in range(2)]
        qB = P.bufs(2, "q")
        NPB = 3
        pt = [A.alloc(1024, BF16) for _ in range(NPB)]
        ptB = P.bufs(NPB, "pt")
        ssum = [A.alloc(512, F32) for _ in range(2)]
        ssumB = P.bufs(2, "ssum")
        rbc = [A.alloc(1024, F32) for _ in range(2)]
        rbcB = P.bufs(2, "rbc")
        dd = [A.alloc(1024, F32) for _ in range(2)]
        ddB = P.bufs(2, "dd")
        dsq = [A.alloc(512, BF16) for _ in range(2)]
        dsqB = P.bufs(2, "dsq")
        rrow = [A.alloc(512, F32) for _ in range(2)]
        rrowB = P.bufs(2, "rrow")
        rrbc = [A.alloc(512, F32) for _ in range(2)]
        rrbcB = P.bufs(2, "rrbc")
        yb = [A.alloc(HW, BF16) for _ in range(2)]
        ybB = P.bufs(2, "yb")
        dbcB = [P.bufs(2, "dbc") for _ in range(2)]
        q7B = P.bufs(4, "ps7q")
        ones32 = self.ones_bf[:, 0:32]
        SB = [(0, 1), (2, 3)]
        dbc = self.d_bc

        def load_kv(h):
            pp = h % 2
            for q4 in range(4):
                P.dma(KTs[pp][:, q4 * 4096:(q4 + 1) * 4096], self.d_KT.ap()[h, :, q4 * 4096:(q4 + 1) * 4096], writes=[kvB[pp][0][q4]])
            for q4 in range(4):
                P.dma(VSs[pp][:, q4 * 32:(q4 + 1) * 32, :], self.d_VS.ap()[h, :, q4 * 32:(q4 + 1) * 32, :], writes=[kvB[pp][1][q4]])

        pending = []
        gcount = [0]

        def flush():
            while pending:
                pending.pop(0)()

        def finalize(ncol, o_ap, obufs, ycols, ybt, ybb, out_dma, nred=1):
            gp = gcount[0] % 2
            gcount[0] += 1
            small = ncol < 128
            P.strict = small
            ss_, rb_, dd_, dq_, rw_, rrb_ = ssum[gp], rbc[gp], dd[gp], dsq[gp], rrow[gp], rrbc[gp]
            if nred == 1:
                self.cp("dve", ss_[0:64, 0:ncol], ps[0:64, 7, 0:ncol], [q7B[0], q7B[1]], [ssumB[gp]])
            else:
                wtot = nred * ncol
                self.cp("dve", ss_[0:64, 0:wtot], ps[0:64, 7, 0:wtot], [q7B[0], q7B[1]], [ssumB[gp]])
                P.strict = True
                if nred == 12:
                    steps = [(4 * ncol, 8 * ncol, 4 * ncol), (4 * ncol, 4 * ncol, 4 * ncol), (2 * ncol, 2 * ncol, 2 * ncol), (ncol, ncol, ncol)]
                else:
                    steps = [(8 * ncol, 8 * ncol, 8 * ncol), (4 * ncol, 4 * ncol, 4 * ncol), (2 * ncol, 2 * ncol, 2 * ncol), (ncol, ncol, ncol)]
                for (wd_, src_, _) in steps:
                    self.tt("dve", ss_[0:64, 0:wd_], ss_[0:64, 0:wd_], ss_[0:64, src_:src_ + wd_], ALU.add, [ssumB[gp]], [ssumB[gp]])
                P.strict = small
            for comp in range(2):
                P.dma(dbc.ap()[gp, comp:comp + 1, 0:ncol], ss_[32 * comp:32 * comp + 1, 0:ncol], reads=[ssumB[gp]], writes=[dbcB[gp][0]])
            P.dma(rb_.rearrange("p (c n) -> p c n", c=2)[:, :, 0:ncol], bass.AP(dbc, gp * 3 * 512, [[0, 128], [512, 2], [1, ncol]]),
                  reads=[dbcB[gp][0]], writes=[rbcB[gp]])
            for comp in range(2):
                r_ = rb_[:, comp * 512:comp * 512 + ncol]
                self.ts("dve", r_, r_, self.tiny1[:, 0:1], None, ALU.add, None, [rbcB[gp], self.cB], [rbcB[gp]])
                self.rcp(r_, r_, [rbcB[gp]], [rbcB[gp]])
                self.tt("dve", dd_[:, comp * 512: comp * 512 + ncol], o_ap(comp), r_, ALU.mult, [obufs[comp], rbcB[gp]], [ddB[gp]])
            self.stt("dve", dd_[:, 0:ncol], dd_[:, 512:512 + ncol], self.neglam[:, 0:1], dd_[:, 0:ncol], ALU.mult, ALU.add, [ddB[gp], self.sB], [ddB[gp]])
            self.tt("pool", dq_[:, 0:ncol], dd_[:, 0:ncol], dd_[:, 0:ncol], ALU.mult, [ddB[gp]], [dsqB[gp]])
            P.strict = False

            def stage1():
                P.strict = small
                self.mm(ps[64:96, 7, 0:ncol], ones32, dq_[:, 0:ncol], True, True, [dsqB[gp], self.cB], [q7B[2]])
                self.act(rw_[64:96, 0:ncol], ps[64:96, 7, 0:ncol], AF.Ln, [q7B[2], self.cB], [rrowB[gp]], bias=self.eps1[64:96, :], scale=1.0 / 128)
                P.strict = True
                self.act(rw_[64:96, 0:ncol], rw_[64:96, 0:ncol], AF.Exp, [rrowB[gp], self.cB], [rrowB[gp]], bias=self.zero1[64:96, :], scale=-0.5)
                P.strict = small
                P.dma(dbc.ap()[gp, 2:3, 0:ncol], rw_[64:65, 0:ncol], reads=[rrowB[gp]], writes=[dbcB[gp][1]])
                P.dma(rrb_[:, 0:ncol], bass.AP(dbc, (gp * 3 + 2) * 512, [[0, 128], [1, ncol]]), reads=[dbcB[gp][1]], writes=[rrbcB[gp]])
                self.stt("dve", ybt[:, ycols:ycols + ncol], dd_[:, 0:ncol], self.gsub[:, 0:1], rrb_[:, 0:ncol], ALU.mult, ALU.mult,
                         [ddB[gp], rrbcB[gp], self.sB], [ybb])
                P.strict = False
                if out_dma is not None:
                    out_dma()
            pending.append(stage1)

        def group(nsteps, qk, av, near_off, ncol, ncols_sum, sum_rhs):
            qk(0)
            for t in range(nsteps):
                if t + 1 < nsteps:
                    qk(t + 1)
                sb = SB[t % 2]
                p_, pB_ = pt[t % NPB], ptB[t % NPB]
                self.act(p_.rearrange("p (c n) -> p c n", c=2), ps[:, sb[0]:sb[0] + 2, :], AF.Exp, [psB[sb[0]], psB[sb[1]], self.cB], [pB_], scale=0.125)
                off = near_off(t)
                if off is not None:
                    for comp in range(2):
                        self.tt("dve", p_[:, comp * 512:(comp + 1) * 512], p_[:, comp * 512:(comp + 1) * 512],
                                self.strip_h[:, off:off + 512], ALU.mult, [pB_, self.stripB], [pB_])
                for comp in range(2):
                    rl = sum_rhs(p_, comp)
                    for n_, r_ in enumerate(rl):
                        self.mm(ps[32 * comp:32 * comp + 32, 7, 0:ncol], ones32, r_, t == 0 and n_ == 0, t == nsteps - 1 and n_ == len(rl) - 1,
                                [pB_, self.cB], [q7B[comp]])
                av(t, p_, pB_)
                if t == min(10, nsteps - 1):
                    flush()

        def group_h(nsteps, qk, av, near, batches, HQ):
            qk(0)
            for t in range(nsteps):
                if t + 1 < nsteps:
                    qk(t + 1)
                sb = SB[t % 2]
                kb0, nseg = batches[t]
                wd = nseg * HQ
                p_, pB_ = pt[t % NPB], ptB[t % NPB]
                self.act(p_.rearrange("p (c n) -> p c n", c=2)[:, :, 0:wd], ps[:, sb[0]:sb[0] + 2, 0:wd], AF.Exp,
                         [psB[sb[0]], psB[sb[1]], self.cB], [pB_], scale=0.125)
                nr = near(t)
                if nr is not None:
                    c0, ns_, off = nr
                    fa = self.strip_h[:, off:off + ns_ * 128].rearrange("p (u c) -> p u c", c=128)[:, :, 0:HQ]
                    for comp in range(2):
                        pv = p_[:, comp * 512 + c0 * HQ: comp * 512 + (c0 + ns_) * HQ].rearrange("p (u c) -> p u c", c=HQ)
                        self.tt("dve", pv, pv, fa, ALU.mult, [pB_, self.stripB], [pB_])
                for comp in range(2):
                    self.mm(ps[32 * comp:32 * comp + 32, 7, 0:wd], ones32, p_[:, comp * 512:comp * 512 + wd], t == 0, t == nsteps - 1,
                            [pB_, self.cB], [q7B[comp]])
                av(t, p_, pB_)

        load_kv(0)
        cnt_q = 0
        for h in range(4):
            pp = h % 2
            KT, VS = KTs[pp], VSs[pp]
            kB, vB = kvB[pp]
            self.strip_h = self.strip[:, h, :]
            if h + 1 < 4:
                load_kv(h + 1)
            for i in range(NSLOT):
                qq = cnt_q % 2
                cnt_q += 1
                q, qb_ = qt[qq], qB[qq]
                if h == 0 and i == 0:
                    P.dma(q, self.d_QB.ap()[0, :, 0, :], writes=[qb_])
                nh, ni = (h, i + 1) if i + 1 < NSLOT else (h + 1, 0)
                if nh < 4:
                    P.dma(qt[cnt_q % 2], self.d_QB.ap()[ni, :, nh, :], writes=[qB[cnt_q % 2]])
                ybt, ybb = yb[qq], ybB[qq]
                nkb = 16 * i + 16
                near0 = 16 * i - 1

                def qk(t, KT=KT, q=q, kB=kB, qb_=qb_):
                    sb = SB[t % 2]
                    for comp in range(2):
                        self.mm(ps[:, sb[comp], :], KT[64 * comp:64 * comp + 64, t * 128:(t + 1) * 128], q[64 * comp:64 * comp + 64, 128:640],
                                True, True, [kB[t // 32], qb_], [psB[sb[comp]]])

                def av(t, p_, pB_, VS=VS, vB=vB, nkb=nkb):
                    for comp in range(2):
                        self.mm(ps[:, 4 + comp, :], VS[:, t, :], p_[:, comp * 512:(comp + 1) * 512], t == 0, t == nkb - 1,
                                [vB[t // 32], pB_], [psB[4 + comp]])

                group(nkb, qk, av, lambda t, near0=near0: (2048 - 128 * (t - near0)) if t >= near0 else None, 512, 512,
                      lambda p_, comp: [p_[:, comp * 512:(comp + 1) * 512]])
                finalize(512, lambda comp: ps[:, 4 + comp, :], [psB[4], psB[5]], 128, ybt, ybb, None)
                HQ = 32
                batches = [(16 * b, 16) for b in range(i)] + [(16 * i, 12)]
                nb = len(batches)

                def qkh(t, KT=KT, q=q, kB=kB, qb_=qb_, batches=batches):
                    sb = SB[t % 2]
                    kb0, nseg = batches[t]
                    for comp in range(2):
                        for u in range(nseg):
                            kb = kb0 + nseg - 1 - u
                            self.mm(ps[:, sb[comp], u * HQ:(u + 1) * HQ], KT[64 * comp:64 * comp + 64, kb * 128:(kb + 1) * 128],
                                    q[64 * comp:64 * comp + 64, 128 - HQ:128], True, True, [kB[kb // 32], qb_], [psB[sb[comp]]])

                def avh(t, p_, pB_, VS=VS, vB=vB, nb=nb, batches=batches):
                    kb0, nseg = batches[t]
                    for comp in range(2):
                        for u in range(nseg):
                            kb = kb0 + nseg - 1 - u
                            self.mm(ps[:, 6, comp * HQ:(comp + 1) * HQ], VS[:, kb, :], p_[:, comp * 512 + u * HQ: comp * 512 + (u + 1) * HQ],
                                    t == 0 and u == 0 and comp == 0, t == nb - 1 and u == nseg - 1, [vB[kb // 32], pB_], [psB[6]])

                def nearh(t, i=i, batches=batches):
                    kb0, nseg = batches[t]
                    if kb0 == 16 * i:
                        return (0, 12, 2048 - 128 * 13 + (128 - HQ))
                    if kb0 == 16 * i - 16:
                        return (0, 4, 2048 - 128 + (128 - HQ))
                    return None

                def odma(i=i, h=h, ybt=ybt, ybb=ybb):
                    P.dma(self.d_YB.ap()[i, :, h, :], ybt, reads=[ybb])

                group_h(nb, qkh, avh, nearh, batches, HQ)
                finalize(HQ, lambda comp: ps[:, 6, comp * HQ:(comp + 1) * HQ], [psB[6], psB[6]], 128 - HQ, ybt, ybb, odma,
                         nred=(16 if i > 0 else 12))
        flush()
        P.barrier()
        A.release(m)

    def phase4a(self):
        P, A = self.P, self.A
        ps, psB = self.ps, self.psB
        m = A.mark()
        win = self.d_win.ap()
        wgl = A.alloc(KC * 2048, BF16).rearrange("p (c n) -> p c n", c=KC)
        wbra = A.alloc(4 * D, BF16).rearrange("p (c n) -> p c n", c=4)
        wbrb = A.alloc(4 * D, BF16).rearrange("p (c n) -> p c n", c=4)
        wo = A.alloc(KC * D, BF16).rearrange("p (c n) -> p c n", c=KC)
        self.load_w(wgl, win[:, :, C_GL:C_GL + 2048], KC, 2048, gain=self.gmix)
        self.load_w(wbra, self.d_wbra.ap(), 4, D)
        self.load_w(wbrb, self.d_wbrb.ap(), 4, D)
        self.load_w(wo, self.d_wo.ap(), KC, D)
        wB = P.buf("w4a")
        xt = A.alloc(KC * HW, F32).rearrange("p (c n) -> p c n", c=KC)
        xB = P.buf("x")
        sq = [A.alloc(HW, BF16) for _ in range(2)]
        sqB = P.bufs(2, "sq")
        hT = A.alloc(KC * HW, BF16).rearrange("p (c n) -> p c n", c=KC)
        hB = P.bufs(KC, "h")
        rstd = A.alloc(HW, F32)
        rB = P.buf("r")
        gates = A.alloc(16 * HW, BF16).rearrange("p (c n) -> p c n", c=16)
        gB = P.bufs(16, "g")
        ya = A.alloc(4 * HW, BF16).rearrange("p (c n) -> p c n", c=4)
        yb = A.alloc(4 * HW, BF16).rearrange("p (c n) -> p c n", c=4)
        yB = P.bufs(2, "y")
        mixed = A.alloc(KC * HW, BF16).rearrange("p (c n) -> p c n", c=KC)
        mxB = P.bufs(KC, "mx")
        t1 = [A.alloc(512, BF16) for _ in range(2)]
        t2 = [A.alloc(512, BF16) for _ in range(2)]
        tB = [P.bufs(2, "t") for _ in range(2)]
        x2 = [A.alloc(512, F32) for _ in range(2)]
        x2B = P.bufs(2, "x2")
        xo = self.d_xo.ap()
        PIECES = ((0, 512), (512, 128))
        cnt = 0
        for i in range(NSLOT):
            P.dma(xt, xo[:, :, i * SLOTW + 128:(i + 1) * SLOTW], writes=[xB], key="x4")
            P.dma(ya, self.d_YA.ap()[i], writes=[yB[0]], key="ya")
            P.dma(yb, self.d_YB.ap()[i], writes=[yB[1]], key="yb")
            self.norm_tile(xt, xB, HW, sq, sqB, hT, hB, rstd, rB, [(0, 512, 0), (512, 128, 7)])
            for (o, w) in PIECES:
                for gc in range(16):
                    bk = 1 + gc % 2
                    for c in range(KC):
                        self.mm(ps[:, bk, 0:w], wgl[:, c, gc * 128:(gc + 1) * 128], hT[:, c, o:o + w], c == 0, c == KC - 1, [wB, hB[c]], [psB[bk]])
                    self.act(gates[:, gc, o:o + w], ps[:, bk, 0:w], AF.Sigmoid, [psB[bk], self.cB], [gB[gc]])
            for (o, w) in PIECES:
                for mc in range(KC):
                    k2 = cnt % 2
                    cnt += 1
                    ba, bb = 3 + 2 * k2, 4 + 2 * k2
                    for r in range(4):
                        self.mm(ps[:, ba, 0:w], wbra[:, r, mc * 128:(mc + 1) * 128], ya[:, r, o:o + w], r == 0, r == 3, [wB, yB[0]], [psB[ba]])
                    for r in range(4):
                        self.mm(ps[:, bb, 0:w], wbrb[:, r, mc * 128:(mc + 1) * 128], yb[:, r, o:o + w], r == 0, r == 3, [wB, yB[1]], [psB[bb]])
                    self.tt("dve", t1[k2][:, 0:w], ps[:, ba, 0:w], gates[:, mc, o:o + w], ALU.mult, [psB[ba], gB[mc]], [tB[k2][0]])
                    self.tt("dve", t2[k2][:, 0:w], ps[:, bb, 0:w], gates[:, 8 + mc, o:o + w], ALU.mult, [psB[bb], gB[8 + mc]], [tB[k2][1]])
                    self.tt("pool", mixed[:, mc, o:o + w], t1[k2][:, 0:w], t2[k2][:, 0:w], ALU.add, [tB[k2][0], tB[k2][1]], [mxB[mc]])
            for (o, w) in PIECES:
                for oc in range(KC):
                    k2 = cnt % 2
                    cnt += 1
                    bk = 1 + k2
                    for mc in range(KC):
                        self.mm(ps[:, bk, 0:w], wo[:, mc, oc * 128:(oc + 1) * 128], mixed[:, mc, o:o + w], mc == 0, mc == KC - 1, [wB, mxB[mc]], [psB[bk]])
                    self.tt("dve", x2[k2][:, 0:w], ps[:, bk, 0:w], xt[:, oc, o:o + w], ALU.add, [psB[bk], xB], [x2B[k2]])
                    P.dma(self.d_X2.ap()[i, :, oc, o:o + w], x2[k2][:, 0:w], reads=[x2B[k2]], key=f"x2o{k2}", eng=STORE_ENG)
        P.barrier()
        A.release(m)

    def phase4b(self):
        P, A = self.P, self.A
        ps, psB = self.ps, self.psB
        A.release(self.pre_strip_mark)
        m = A.mark()
        wup = A.alloc(KC * 2 * DFF, BF16).rearrange("p (c n) -> p c n", c=KC)
        wdn = A.alloc(NFC * D, BF16).rearrange("p (c n) -> p c n", c=NFC)
        self.load_w(wup, self.d_wup.ap(), KC, 2 * DFF, gain=self.gffn)
        self.load_w(wdn, self.d_wdn.ap(), NFC, D)
        wB = P.buf("w4b")
        W2 = 514
        xa_off = A.alloc_raw(NFC * 512 * 2)
        xaB = P.buf("xa")
        sq = [A.alloc(W2, BF16) for _ in range(2)]
        sqB = P.bufs(2, "sq")
        hT = A.alloc(KC * W2, BF16).rearrange("p (c n) -> p c n", c=KC)
        hB = P.bufs(KC, "h")
        rstd = A.alloc(W2, F32)
        rB = P.buf("r")
        uh = A.alloc(44 * 2, F32).rearrange("p (c n) -> p c n", c=44)
        uhB = P.buf("uh")
        U = [A.alloc(W2, F32) for _ in range(4)]
        UB = P.bufs(4, "U")
        tg = [A.alloc(512, F32) for _ in range(2)]
        tv = [A.alloc(512, F32) for _ in range(2)]
        tgB = P.bufs(2, "tg")
        tvB = P.bufs(2, "tv")
        ot = [A.alloc(512, F32) for _ in range(2)]
        otB = P.bufs(2, "ot")
        xr = [A.alloc(512, F32) for _ in range(2)]
        xrB = P.bufs(2, "xr")
        cw = self.cw.rearrange("p (a b) -> p a b", a=3)
        outT = self.d_out.ap()
        xt = A.at(xa_off, KC * W2, F32).rearrange("p (c n) -> p c n", c=KC)
        aT = A.at(xa_off, NFC * 512, BF16).rearrange("p (c n) -> p c n", c=NFC)
        for i in range(NSLOT):
            P.dma(xt, self.d_X2.ap()[i, :, :, 126:640], writes=[xaB])
            self.norm_tile(xt, xaB, W2, sq, sqB, hT, hB, rstd, rB, [(0, 2, 7), (2, 512, 0)])
            for fc in range(44):
                for c in range(KC):
                    self.mm(ps[:, 7, fc * 2:fc * 2 + 2], wup[:, c, fc * 128:(fc + 1) * 128], hT[:, c, 0:2], c == 0, c == KC - 1, [wB, hB[c]], [psB[7]])
            self.cp("dve", uh, ps[:, 7, 0:88].rearrange("p (c n) -> p c n", c=44), [psB[7]], [uhB])
            for f in range(NFC):
                k2 = f % 2
                for half, fc in enumerate((f, NFC + f)):
                    bk = 1 + 2 * k2 + half
                    ui = 2 * k2 + half
                    for c in range(KC):
                        self.mm(ps[:, bk, :], wup[:, c, fc * 128:(fc + 1) * 128], hT[:, c, 2:W2], c == 0, c == KC - 1, [wB, hB[c]], [psB[bk]])
                    self.cp("pool", U[ui][:, 0:2], uh[:, fc, :], [uhB], [UB[ui]])
                    self.cp("act", U[ui][:, 2:W2], ps[:, bk, :], [psB[bk]], [UB[ui]])
                    t_, tb_ = (tg[k2], tgB[k2]) if half == 0 else (tv[k2], tvB[k2])
                    eng = "dve"
                    self.ts(eng, t_, U[ui][:, 0:512], cw[:, 0, fc:fc + 1], self.cb[:, fc:fc + 1], ALU.mult, ALU.add, [UB[ui], self.sB], [tb_])
                    self.stt(eng, t_, U[ui][:, 1:513], cw[:, 1, fc:fc + 1], t_, ALU.mult, ALU.add, [UB[ui], self.sB, tb_], [tb_])
                    self.stt(eng, t_, U[ui][:, 2:514], cw[:, 2, fc:fc + 1], t_, ALU.mult, ALU.add, [UB[ui], self.sB, tb_], [tb_])
                self.act(tg[k2], tg[k2], AF.Silu, [tgB[k2], self.cB], [tgB[k2]])
                self.tt("pool", aT[:, f, :], tg[k2], tv[k2], ALU.mult, [tgB[k2], tvB[k2]], [xaB])
            for oc in range(KC):
                k2 = oc % 2
                bk = 5 + k2
                P.dma(xr[k2], self.d_X2.ap()[i, :, oc, 128:640], writes=[xrB[k2]])
                for f in range(NFC):
                    self.mm(ps[:, bk, :], wdn[:, f, oc * 128:(oc + 1) * 128], aT[:, f, :], f == 0, f == NFC - 1, [wB, xaB], [psB[bk]])
                self.tt("dve", ot[k2], ps[:, bk, :], xr[k2], ALU.add, [psB[bk], xrB[k2]], [otB[k2]])
                P.dma(outT[:, oc, i * 512:(i + 1) * 512], ot[k2], reads=[otB[k2]], eng=STORE_ENG)
        P.barrier()
        A.release(m)

    def build(self, nph=N_PHASES):
        self.declare()
        self.P.strict = True
        self.setup()
        self.P.strict = False
        self.P.barrier()
        phases = [self.phase1, self.phase2, self.phase3, self.phase4a, self.phase4b]
        for n_, ph in enumerate(phases[:nph]):
            if n_ == 0 and SKIP_P1:
                continue
            ph()
        self.P.barrier()
        self.P.emit()
        return self.nc


def _host_inputs(inputs):
    f = np.float32
    x = np.asarray(inputs["x"], dtype=f)

    def pc(w, kc):
        n = w.shape[1]
        return np.ascontiguousarray(w.reshape(kc, 128, n).transpose(1, 0, 2))

    w_br_a = np.asarray(inputs["w_br_a"][0], dtype=f)
    wa = w_br_a.reshape(2, 4, 64, D)
    wbra = np.ascontiguousarray(wa.transpose(0, 2, 1, 3).reshape(128, 4, D))
    common = {
        "w_in": pc(np.asarray(inputs["w_in"][0], dtype=f), KC),
        "w_br_a": wbra,
        "w_br_b": pc(np.asarray(inputs["w_br_b"][0], dtype=f), 4),
        "w_o": pc(np.asarray(inputs["w_o"][0], dtype=f), KC),
        "w_up": pc(np.asarray(inputs["w_up"][0], dtype=f), KC),
        "w_down": pc(np.asarray(inputs["w_down"][0], dtype=f), NFC),
        "g_mix": np.ascontiguousarray(np.asarray(inputs["g_mix"][0], dtype=f).reshape(KC, 128).T),
        "g_ffn": np.ascontiguousarray(np.asarray(inputs["g_ffn"][0], dtype=f).reshape(KC, 128).T),
        "conv_w": np.ascontiguousarray(np.asarray(inputs["conv_w"][0], dtype=f).reshape(3, 44, 128).transpose(2, 0, 1)),
        "conv_b": np.ascontiguousarray(np.asarray(inputs["conv_b"][0], dtype=f).reshape(44, 128).T),
        "rel_bias": np.ascontiguousarray(np.asarray(inputs["rel_bias"], dtype=f)),
        "qn_a": np.asarray(inputs["qn_a"], dtype=f).reshape(1, 64),
        "kn_a": np.asarray(inputs["kn_a"], dtype=f).reshape(1, 64),
        "qn_b": np.asarray(inputs["qn_b"], dtype=f).reshape(1, 64),
        "kn_b": np.asarray(inputs["kn_b"], dtype=f).reshape(1, 64),
        "sinks": np.asarray(inputs["sinks"], dtype=f).reshape(1, 8),
        "lam_q1": np.asarray(inputs["lam_q1"], dtype=f).reshape(1, 64),
        "lam_k1": np.asarray(inputs["lam_k1"], dtype=f).reshape(1, 64),
        "lam_q2": np.asarray(inputs["lam_q2"], dtype=f).reshape(1, 64),
        "lam_k2": np.asarray(inputs["lam_k2"], dtype=f).reshape(1, 64),
        "subln_b": np.asarray(inputs["subln_b"], dtype=f).reshape(128, 1),
        "Jmat": np.ascontiguousarray(np.eye(128, dtype=f)[::-1]),
        "bdones": np.kron(np.eye(2, dtype=f), np.ones((64, 64), dtype=f)),
    }
    oha = np.zeros((33, 384), dtype=f)
    for mm_ in range(384):
        d = mm_ - 127
        if 0 <= d < 128:
            oha[int(_t5_bucket_np(np.array(d))), mm_] = 1
        else:
            oha[32, mm_] = 1
    common["oh_a"] = oha
    in_maps = []
    for core in range(8):
        b, j = core // 4, core % 4
        xTb = np.ascontiguousarray(x[b].T.reshape(KC, 128, S).transpose(1, 0, 2))
        xo = np.zeros((128, KC, NSLOT, SLOTW), dtype=f)
        for i in range(NSLOT):
            G = 4 * i + j
            t0 = 512 * G - 256
            lo = max(t0, 0)
            xo[:, :, i, lo - t0:] = xTb[:, :, lo:t0 + SLOTW]
        ohd = np.zeros((33, VECD), dtype=f)
        d = np.arange(VECD) + 512 * j - 2047
        bk = _t5_bucket_np(d)
        for mm_ in range(VECD):
            if d[mm_] >= 0:
                ohd[bk[mm_], mm_] = 1
            else:
                ohd[32, mm_] = 1
        mp = dict(common)
        mp["xT"] = xTb
        mp["xo"] = xo.reshape(128, KC, NSLOT * SLOTW)
        mp["oh_d"] = ohd
        mp["m0"] = np.full((128, 1), 0.0 if j == 0 else 1.0, dtype=f)
        in_maps.append(mp)
    return in_maps


_NC_CACHE = {}


def kernel(**inputs):
    in_maps = _host_inputs(inputs)
    if "nc" not in _NC_CACHE:
        _NC_CACHE["nc"] = K().build()
    nc = _NC_CACHE["nc"]
    res = run_bass_kernel_spmd(nc, in_maps, core_ids=list(range(8)))
    out = np.zeros((2, S, D), dtype=np.float32)
    for core in range(8):
        b, j = core // 4, core % 4
        o = res.results[core]["outT"].reshape(128, KC, NSLOT, 512)
        for i in range(NSLOT):
            G = 4 * i + j
            out[b, 512 * G:512 * (G + 1), :] = o[:, :, i, :].transpose(2, 1, 0).reshape(512, D)
    return out
```

```python
import contextlib
import math
import numpy as np
import concourse.bass as bass
import concourse.mybir as mybir
from concourse.bass_utils import run_bass_kernel_spmd

F32 = mybir.dt.float32
BF16 = mybir.dt.bfloat16
AF = mybir.ActivationFunctionType
ALU = mybir.AluOpType

ENGS = ("pe", "act", "dve", "pool", "sp")
SEM_ROT = 3000


class Buf:
    __slots__ = ("name", "w", "rs")

    def __init__(self, name):
        self.name = name
        self.w = None
        self.rs = []


class Op:
    __slots__ = ("eng", "fn", "waits", "signal", "dma_key", "dma_sem", "dma_val", "sig_sem", "sig_val")

    def __init__(self, eng, fn, dma_key=None):
        self.eng = eng
        self.fn = fn
        self.waits = []
        self.signal = False
        self.dma_key = dma_key
        self.dma_sem = None
        self.dma_val = None
        self.sig_sem = None
        self.sig_val = None


class Prog:
    def __init__(self, nc):
        self.nc = nc
        self.ops = {e: [] for e in ENGS}
        self.dma_cnt = {}
        self.all_dma_last = {}
        self.stack = contextlib.ExitStack()
        self.nbufs = 0
        self.strict = False

    def buf(self, name=None):
        self.nbufs += 1
        return Buf(f"{name or 'b'}#{self.nbufs}")

    def bufs(self, n, name="b"):
        return [self.buf(f"{name}{i}") for i in range(n)]

    def _dep(self, op, y):
        if y is None or y is op:
            return
        if y.dma_key is None and y.eng == op.eng and op.dma_key is None and not self.strict:
            return
        if y.dma_key is None:
            y.signal = True
        if y not in op.waits:
            op.waits.append(y)

    def op(self, eng, fn, reads=(), writes=(), dma_key=None):
        o = Op(eng, fn, dma_key)
        for b in reads:
            self._dep(o, b.w)
        for b in writes:
            self._dep(o, b.w)
            for r in b.rs:
                self._dep(o, r)
        for b in reads:
            b.rs.append(o)
        for b in writes:
            b.w = o
            b.rs = []
        if dma_key is not None:
            st = self.dma_cnt.setdefault(dma_key, [0, 0])
            if st[1] + 16 > 4000:
                st[0] += 1
                st[1] = 0
            st[1] += 16
            o.dma_sem = (dma_key, st[0])
            o.dma_val = st[1]
            self.all_dma_last[dma_key] = o
        self.ops[eng].append(o)
        return o

    def dma(self, out, in_, reads=(), writes=(), key=None, eng="sp", **kw):
        prim = writes[0] if len(writes) else reads[0]
        key = prim.name
        return self.op(eng, lambda e: e.dma_start(out=out, in_=in_, **kw), reads, writes, dma_key=key)

    def barrier(self):
        lasts = [self.ops[e][-1] for e in ENGS if self.ops[e]]
        dmas = list(self.all_dma_last.values())
        news = []
        for e in ENGS:
            o = Op(e, None)
            for y in lasts:
                self._dep(o, y)
            for y in dmas:
                self._dep(o, y)
            news.append(o)
        for o in news:
            self.ops[o.eng].append(o)

    def emit(self):
        nc = self.nc
        semkeys = set()
        for e in ENGS:
            gen, cnt = 0, 0
            for o in self.ops[e]:
                if o.dma_key is not None:
                    semkeys.add(o.dma_sem)
                    continue
                if o.signal:
                    if cnt >= SEM_ROT:
                        gen += 1
                        cnt = 0
                    cnt += 1
                    o.sig_sem = ("eng", e, gen)
                    o.sig_val = cnt
                    semkeys.add(o.sig_sem)
        sems = {}
        for n, k in enumerate(sorted(semkeys, key=str)):
            sems[k] = self.stack.enter_context(nc.semaphore(f"sm{n}"))
        self.nsems = len(sems)
        block = self.stack.enter_context(nc.Block())
        engmap = {"pe": "tensor", "act": "scalar", "dve": "vector", "pool": "gpsimd", "sp": "sync"}

        def make(e):
            def body(eng):
                waited = {}
                for o in self.ops[e]:
                    for y in o.waits:
                        if y.dma_key is not None:
                            sk, v = y.dma_sem, y.dma_val
                        else:
                            sk, v = y.sig_sem, y.sig_val
                        if waited.get(sk, 0) >= v:
                            continue
                        waited[sk] = v
                        eng.wait_ge(sems[sk], v)
                    if o.fn is None:
                        if o.signal:
                            eng.nop().then_inc(sems[o.sig_sem], 1)
                        continue
                    ins = o.fn(eng)
                    if o.dma_key is not None:
                        ins.then_inc(sems[o.dma_sem], 16)
                    elif o.signal:
                        ins.then_inc(sems[o.sig_sem], 1)
            return body

        for e in ENGS:
            getattr(block, engmap[e])(make(e))
        self.stack.close()


class Arena:
    def __init__(self, prog, nbytes, name="arena"):
        nc = prog.nc
        self.t8 = prog.stack.enter_context(nc.sbuf_tensor(name, [128, nbytes], mybir.dt.uint8))
        self.views = {}
        self.nbytes = nbytes
        self.off = 0

    def view(self, dt):
        if dt not in self.views:
            self.views[dt] = self.t8.bitcast(dt)
        return self.views[dt]

    def alloc(self, nelem, dt):
        sz = mybir.dt.size(dt)
        self.off = (self.off + 63) // 64 * 64
        o = self.off
        self.off += nelem * sz
        assert self.off <= self.nbytes, f"arena overflow {self.off} > {self.nbytes}"
        return self.view(dt)[:, o // sz: o // sz + nelem]

    def alloc_raw(self, nbytes):
        self.off = (self.off + 63) // 64 * 64
        o = self.off
        self.off += nbytes
        assert self.off <= self.nbytes, f"arena overflow {self.off} > {self.nbytes}"
        return o

    def at(self, o, nelem, dt):
        sz = mybir.dt.size(dt)
        return self.view(dt)[:, o // sz: o // sz + nelem]

    def mark(self):
        return self.off

    def release(self, m):
        self.off = m


D = 1024
S = 16384
KC = 8
NSLOT = 8
SLOTW = 768
HW = 640
DFF = 2816
NFC = 22
EPS = 1e-6
LAM_INIT = 0.8 - 0.6 * math.exp(-0.3 * 0)
VECD = 2688
STRIPW = 2560
C_QA, C_KA, C_VA, C_QB, C_KB, C_VB, C_GL = 0, 512, 640, 768, 1280, 1792, 2304

DEBUG_SCRATCH = False
STORE_ENG = "pool"
P2_STAGE = 9
P2_SLOTS = 8
SKIP_P1 = False
N_PHASES = 5


def _t5_bucket_np(rel):
    n = np.maximum(rel, 0)
    nf = np.maximum(n, 1).astype(np.float32)
    large = 16 + (np.log(nf / np.float32(16)) / np.float32(math.log(8.0)) * np.float32(16)).astype(np.int32)
    large = np.minimum(large, 31)
    return np.where(n < 16, n, large)


class K:
    def __init__(self):
        nc = bass.Bass("TRN2", target_bir_lowering=False)
        self.nc = nc
        self.P = Prog(nc)
        self.A = Arena(self.P, 209920)
        self.ps = self.P.stack.enter_context(nc.psum_tensor("ps", [128, 8, 512], F32))
        self.psB = self.P.bufs(8, "psb")

    def mm(self, out, lhsT, rhs, start, stop, R, W):
        self.P.op("pe", lambda e: e.matmul(out, lhsT=lhsT, rhs=rhs, start=start, stop=stop), R, W)

    def act(self, out, in_, func, R, W, bias=None, scale=1.0):
        b = self.zero1 if bias is None else bias
        npart = out.shape[0]
        if npart != 128 and b.shape[0] == 128:
            b = b[0:npart, :]
        self.P.op("act", lambda e: e.activation(out=out, in_=in_, func=func, bias=b, scale=scale), R, W)

    def tt(self, eng, out, a, b, op, R, W):
        self.P.op(eng, lambda e: e.tensor_tensor(out=out, in0=a, in1=b, op=op), R, W)

    def ts(self, eng, out, a, s1, s2, op0, op1, R, W):
        if op1 is None:
            self.P.op(eng, lambda e: e.tensor_scalar(out=out, in0=a, scalar1=s1, scalar2=None, op0=op0), R, W)
        else:
            self.P.op(eng, lambda e: e.tensor_scalar(out=out, in0=a, scalar1=s1, scalar2=s2, op0=op0, op1=op1), R, W)

    def stt(self, eng, out, a, s, b, op0, op1, R, W):
        self.P.op(eng, lambda e: e.scalar_tensor_tensor(out=out, in0=a, scalar=s, in1=b, op0=op0, op1=op1), R, W)

    def cp(self, eng, out, in_, R, W):
        if eng == "act":
            self.act(out, in_, AF.Identity, R, W)
        else:
            self.P.op(eng, lambda e: e.tensor_copy(out=out, in_=in_), R, W)

    def rcp(self, out, in_, R, W):
        self.P.op("dve", lambda e: e.reciprocal(out=out, in_=in_), R, W)

    def rsqrt_act(self, out, in_, scale, R, W, power=-0.5, bias=None):
        self.act(out, in_, AF.Ln, R, W, bias=self.eps1 if bias is None else bias, scale=scale)
        old = self.P.strict
        if out.shape[-1] <= 256:
            self.P.strict = True
        self.act(out, out, AF.Exp, list(W) + [self.cB], W, scale=power)
        self.P.strict = old

    def memset(self, eng, ap, val, W):
        self.P.op(eng, lambda e: e.memset(ap, val), (), W)

    def declare(self):
        nc = self.nc

        def din(name, shape, dt=F32):
            return nc.dram_tensor(name, list(shape), dt, kind="ExternalInput")

        def dscr(name, shape, dt):
            return nc.dram_tensor(name, list(shape), dt, kind="ExternalOutput" if DEBUG_SCRATCH else "Internal")

        self.d_xT = din("xT", [128, KC, S])
        self.d_xo = din("xo", [128, KC, NSLOT * SLOTW])
        self.d_win = din("w_in", [128, KC, 4352])
        self.d_wbra = din("w_br_a", [128, 4, D])
        self.d_wbrb = din("w_br_b", [128, 4, D])
        self.d_wo = din("w_o", [128, KC, D])
        self.d_wup = din("w_up", [128, KC, 2 * DFF])
        self.d_wdn = din("w_down", [128, NFC, D])
        self.d_gmix = din("g_mix", [128, KC])
        self.d_gffn = din("g_ffn", [128, KC])
        self.d_cw = din("conv_w", [128, 3, 44])
        self.d_cb = din("conv_b", [128, 44])
        self.d_relb = din("rel_bias", [32, 12])
        self.d_qna = din("qn_a", [1, 64])
        self.d_kna = din("kn_a", [1, 64])
        self.d_qnb = din("qn_b", [1, 64])
        self.d_knb = din("kn_b", [1, 64])
        self.d_sinks = din("sinks", [1, 8])
        self.d_lq1 = din("lam_q1", [1, 64])
        self.d_lk1 = din("lam_k1", [1, 64])
        self.d_lq2 = din("lam_q2", [1, 64])
        self.d_lk2 = din("lam_k2", [1, 64])
        self.d_subln = din("subln_b", [128, 1])
        self.d_oha = din("oh_a", [33, 384])
        self.d_ohd = din("oh_d", [33, VECD])
        self.d_m0 = din("m0", [128, 1])
        self.d_J = din("Jmat", [128, 128])
        self.d_bd = din("bdones", [128, 128])
        self.d_out = nc.dram_tensor("outT", [128, KC, NSLOT * 512], F32, kind="ExternalOutput")
        self.d_KT = dscr("s_KT", [4, 128, S], BF16)
        self.d_VS = dscr("s_VS", [4, 128, 128, 128], BF16)
        self.d_QB = dscr("s_QB", [NSLOT, 128, 4, HW], BF16)
        self.d_YA = dscr("s_YA", [NSLOT, 128, 4, HW], BF16)
        self.d_YB = dscr("s_YB", [NSLOT, 128, 4, HW], BF16)
        self.d_X2 = dscr("s_X2", [NSLOT, 128, KC, HW], F32)
        if DEBUG_SCRATCH:
            self.d_dstrip = dscr("s_strip", [128, 4 * STRIPW], BF16)
            self.d_dsa = dscr("s_sa", [128, 2 * 8 * 128], BF16)
            self.d_dsmall = dscr("s_small", [128, 8], F32)
        self.d_bc = nc.dram_tensor("s_bc", [2, 3, 512], F32, kind="Internal")
        self.d_VA = dscr("s_VECA", [8, 384], F32)
        self.d_VD = dscr("s_VECD", [4, VECD], F32)

    def setup(self):
        P, A, nc = self.P, self.A, self.nc
        ps, psB = self.ps, self.psB
        cB = P.buf("consts")
        self.cB = cB
        self.zero1 = A.alloc(1, F32)
        self.eps1 = A.alloc(1, F32)
        self.tiny1 = A.alloc(1, F32)
        self.ones_bf = A.alloc(128, BF16)
        self.ones_f = A.alloc(128, F32)
        self.bd_bf = A.alloc(128, BF16)
        self.J = A.alloc(128, F32)
        bd_f = A.alloc(128, F32)
        self.memset("pool", self.zero1, 0.0, [cB])
        self.memset("pool", self.eps1, EPS, [cB])
        self.memset("pool", self.tiny1, 1e-30, [cB])
        self.memset("pool", self.ones_bf, 1.0, [cB])
        self.memset("pool", self.ones_f, 1.0, [cB])
        jB = P.buf("J")
        P.dma(self.J, self.d_J.ap(), writes=[jB], key="c0")
        P.dma(bd_f, self.d_bd.ap(), writes=[jB], key="c0")
        self.cp("dve", self.bd_bf, bd_f, [jB], [cB])
        sB = P.buf("small")
        self.gmix = A.alloc(KC, F32)
        self.gffn = A.alloc(KC, F32)
        self.cw = A.alloc(3 * 44, F32)
        self.cb = A.alloc(44, F32)
        self.m0 = A.alloc(1, F32)
        P.dma(self.gmix, self.d_gmix.ap(), writes=[sB], key="c1")
        P.dma(self.gffn, self.d_gffn.ap(), writes=[sB], key="c1")
        P.dma(self.cw, self.d_cw.ap().rearrange("p a b -> p (a b)"), writes=[sB], key="c1")
        P.dma(self.cb, self.d_cb.ap(), writes=[sB], key="c1")
        P.dma(self.m0, self.d_m0.ap(), writes=[sB], key="c1")
        self.gq_a = A.alloc(1, F32)
        self.gk_a = A.alloc(1, F32)
        self.gq_b = A.alloc(1, F32)
        self.gk_b = A.alloc(1, F32)
        for dst, src in ((self.gq_a, self.d_qna), (self.gk_a, self.d_kna), (self.gq_b, self.d_qnb), (self.gk_b, self.d_knb)):
            for hlf in range(2):
                P.dma(dst[64 * hlf:64 * hlf + 64, :], bass.AP(src, 0, [[1, 64], [1, 1]]), writes=[sB], key="c1")
        self.gsub = A.alloc(1, F32)
        P.dma(self.gsub, self.d_subln.ap(), writes=[sB], key="c1")
        self.ts("dve", self.gsub, self.gsub, 1.0 - LAM_INIT, None, ALU.mult, None, [sB], [sB])
        lam4 = A.alloc(4 * 64, F32)
        for n_, src in enumerate((self.d_lq1, self.d_lk1, self.d_lq2, self.d_lk2)):
            P.dma(lam4[:, n_ * 64:(n_ + 1) * 64], bass.AP(src, 0, [[0, 128], [1, 64]]), writes=[sB], key="c1")
        lp = A.alloc(2 * 64, F32)
        ls = A.alloc(2, F32)
        self.neglam = A.alloc(1, F32)
        self.tt("dve", lp[:, 0:64], lam4[:, 0:64], lam4[:, 64:128], ALU.mult, [sB], [sB])
        self.tt("dve", lp[:, 64:128], lam4[:, 128:192], lam4[:, 192:256], ALU.mult, [sB], [sB])
        P.op("dve", lambda e: e.reduce_sum(out=ls[:, 0:1], in_=lp[:, 0:64], axis=mybir.AxisListType.X), [sB], [sB])
        P.op("dve", lambda e: e.reduce_sum(out=ls[:, 1:2], in_=lp[:, 64:128], axis=mybir.AxisListType.X), [sB], [sB])
        self.act(ls, ls, AF.Exp, [sB, cB], [sB])
        self.tt("dve", self.neglam, ls[:, 1:2], ls[:, 0:1], ALU.subtract, [sB], [sB])
        self.ts("dve", self.neglam, self.neglam, -LAM_INIT, None, ALU.add, None, [sB], [sB])
        self.sB = sB
        self.esrow = A.alloc(2 * 512, F32)
        sk = A.alloc(8, F32)
        P.dma(sk[0:1, :], self.d_sinks.ap(), writes=[sB], key="c1")
        self.act(sk[0:1, :], sk[0:1, :], AF.Exp, [sB, cB], [sB])
        for hq in range(8):
            g_, r_ = hq // 4, hq % 4
            col = g_ * 512 + ((r_ % 2) * 2 + r_ // 2) * 128
            self.ts("dve", self.esrow[0:1, col:col + 128], self.ones_f[0:1, 0:128], sk[0:1, hq:hq + 1], None, ALU.mult, None, [sB, cB], [sB])

        tB = P.buf("tab")
        tabp = A.alloc(12, F32)
        tab31 = A.alloc(4, F32)
        self.memset("pool", tabp[32:33, :], -30000.0, [tB])
        P.dma(tabp[0:32, :], self.d_relb.ap(), writes=[tB], key="c2")
        P.dma(tab31[0:32, :], bass.AP(self.d_relb, 31 * 12 + 8, [[0, 32], [1, 4]]), writes=[tB], key="c2")
        self.tt("dve", tabp[0:32, 8:12], tabp[0:32, 8:12], tab31[0:32, :], ALU.subtract, [tB], [tB])
        m = A.mark()
        oha = A.alloc(384, F32)
        ohd = A.alloc(VECD, F32)
        veca = A.alloc(384, F32)
        vecd = A.alloc(VECD, F32)
        ohB = P.buf("oh")
        P.dma(oha[0:33, :], self.d_oha.ap(), writes=[ohB], key="c2")
        P.dma(ohd[0:33, :], self.d_ohd.ap(), writes=[ohB], key="c2")
        vB = P.buf("vec")
        self.mm(ps[0:8, 0, 0:384], tabp[0:33, 0:8], oha[0:33, :], True, True, [tB, ohB], [psB[0]])
        self.act(veca[0:8, :], ps[0:8, 0, 0:384], AF.Exp, [psB[0], cB], [vB])
        for pc in range(6):
            w = 512 if pc < 5 else VECD - 2560
            bk = 1 + pc % 2
            self.mm(ps[0:4, bk, 0:w], tabp[0:33, 8:12], ohd[0:33, pc * 512: pc * 512 + w], True, True, [tB, ohB], [psB[bk]])
            self.act(vecd[0:4, pc * 512: pc * 512 + w], ps[0:4, bk, 0:w], AF.Exp, [psB[bk], cB], [vB])
        dvB = P.buf("dvec")
        P.dma(self.d_VA.ap(), veca[0:8, :], reads=[vB], writes=[dvB], key="c3")
        P.dma(self.d_VD.ap(), vecd[0:4, :], reads=[vB], writes=[dvB], key="c3")
        A.release(m)
        self.pre_strip_mark = A.mark()
        self.strip = A.alloc(4 * STRIPW, BF16).rearrange("p (h u) -> p h u", h=4)
        self.sa = A.alloc(2 * 8 * 128, BF16).rearrange("p (t h q) -> p t h q", t=2, h=8)
        self.stripB = P.buf("strip")
        self.persist_mark = A.mark()
        m = A.mark()
        rev = A.alloc(STRIPW, F32)
        revB = P.bufs(2, "rev")
        P.dma(rev[:, 0:2048].rearrange("p (h u) -> p h u", h=8), bass.AP(self.d_VA, 0, [[1, 128], [384, 8], [1, 256]]),
              reads=[dvB], writes=[revB[0]], key="c4")
        for pi in range(4):
            bk = pi % 2
            self.mm(ps[:, bk, :], self.J, rev[:, pi * 512:(pi + 1) * 512], True, True, [jB, revB[0]], [psB[bk]])
            src = ps[:, bk, :].rearrange("p (h t q) -> p h t q", h=2, t=2)
            for ty in range(2):
                self.cp("dve" if ty == 0 else "act", self.sa[:, ty, 2 * pi:2 * pi + 2, :], src[:, :, ty, :], [psB[bk]], [self.stripB])
        for h in range(4):
            rb = revB[(h + 1) % 2]
            P.dma(rev, bass.AP(self.d_VD, h * VECD, [[1, 128], [1, STRIPW]]), reads=[dvB], writes=[revB[0], revB[1]], key="c4")
            for pc in range(5):
                bk = pc % 2
                self.mm(ps[:, bk, :], self.J, rev[:, pc * 512:(pc + 1) * 512], True, True, [jB, revB[0], revB[1]], [psB[bk]])
                self.cp("dve" if pc % 2 == 0 else "act", self.strip[:, h, pc * 512:(pc + 1) * 512], ps[:, bk, :], [psB[bk]], [self.stripB])
        A.release(m)
        if DEBUG_SCRATCH:
            P.dma(self.d_dstrip.ap(), self.strip.rearrange("p h u -> p (h u)"), reads=[self.stripB])
            P.dma(self.d_dsa.ap(), self.sa.rearrange("p t h q -> p (t h q)"), reads=[self.stripB])
            for n_, t_ in enumerate((self.neglam, self.gsub, self.gq_b, self.gk_b)):
                P.dma(self.d_dsmall.ap()[:, n_:n_ + 1], t_, reads=[self.sB], allow_slow_non_contiguous=True)

    def load_w(self, dst, src, kc, ncols, gain=None, dup=None, key="w"):
        P, A = self.P, self.A
        m = A.mark()
        pw = 512 if kc <= 8 else 192
        stg = [A.alloc(kc * pw, F32).rearrange("p (c n) -> p c n", c=kc) for _ in range(2)]
        sb = P.bufs(2, "stg")
        wB = P.buf("wdst")
        engs = ("dve", "act")
        n = 0
        for i, c0 in enumerate(range(0, ncols, pw)):
            w = min(pw, ncols - c0)
            s, b = stg[i % 2], sb[i % 2]
            P.dma(s[:, :, 0:w], src[:, :, c0:c0 + w], writes=[b], key=f"{key}{i % 2}")
            for c in range(kc):
                eng = engs[n % 2]
                n += 1
                if gain is None:
                    self.cp(eng, dst[:, c, c0:c0 + w], s[:, c, 0:w], [b], [wB])
                elif eng == "act":
                    self.act(dst[:, c, c0:c0 + w], s[:, c, 0:w], AF.Identity, [b, self.sB, self.cB], [wB], scale=gain[:, c:c + 1])
                else:
                    self.ts(eng, dst[:, c, c0:c0 + w], s[:, c, 0:w], gain[:, c:c + 1], None, ALU.mult, None, [b, self.sB], [wB])
        self.P.barrier()
        A.release(m)
        return wB

    def norm_tile(self, xt, xB, n, sq, sqB, hT, hB, rstd, rB, pieces):
        ps, psB = self.ps, self.psB
        for c in range(KC):
            self.act(sq[c % 2][:, 0:n], xt[:, c, :], AF.Square, [xB, self.cB], [sqB[c % 2]])
            for (o, w, bank) in pieces:
                self.mm(ps[:, bank, 0:w], self.ones_bf, sq[c % 2][:, o:o + w], c == 0, c == KC - 1, [sqB[c % 2], self.cB], [psB[bank]])
        for (o, w, bank) in pieces:
            small = w < 128
            self.P.strict = small
            self.rsqrt_act(rstd[:, o:o + w], ps[:, bank, 0:w], 1.0 / D, [psB[bank], self.cB], [rB])
            self.P.strict = False
        for c in range(KC):
            self.tt("dve" if c % 2 == 0 else "pool", hT[:, c, 0:n], xt[:, c, :], rstd[:, 0:n], ALU.mult, [xB, rB], [hB[c]])

    def ph_tasks(self, tasks, hT, hB, wB, tmp, tmpB, pbanks, nbanks, mid=None):
        ps, psB = self.ps, self.psB
        n = len(tasks)

        def proj(k):
            wfn, o, w, out, outB, gain = tasks[k]
            bk = pbanks[k % len(pbanks)]
            ksq, _ = tmp[k % len(tmp)]
            for c in range(KC):
                self.mm(ps[:, bk, 0:w], wfn(c), hT[:, c, o:o + w], c == 0, c == KC - 1, [wB, hB[c]], [psB[bk]])
            self.act(ksq[:, 0:w], ps[:, bk, 0:w], AF.Square, [psB[bk], self.cB], [tmpB[k % len(tmp)][0]])

        def norm(k):
            wfn, o, w, out, outB, gain = tasks[k]
            bk = pbanks[k % len(pbanks)]
            bn = nbanks[k % len(nbanks)]
            ksq, rk = tmp[k % len(tmp)]
            tb = tmpB[k % len(tmp)]
            self.mm(ps[:, bn, 0:w], self.bd_bf, ksq[:, 0:w], True, True, [tb[0], self.cB], [psB[bn]])
            self.rsqrt_act(rk[:, 0:w], ps[:, bn, 0:w], 1.0 / 64, [psB[bn], self.cB], [tb[1]])
            self.stt("dve", out, ps[:, bk, 0:w], gain, rk[:, 0:w], ALU.mult, ALU.mult, [psB[bk], tb[1], self.sB], [outB])

        for k in range(n + 1):
            if k < n:
                proj(k)
            if k == n and mid is not None:
                mid()
            if k >= 1:
                norm(k - 1)

    def phase1(self):
        P, A = self.P, self.A
        ps, psB = self.ps, self.psB
        m = A.mark()
        wkb = A.alloc(KC * 512, BF16).rearrange("p (c n) -> p c n", c=KC)
        wvb = A.alloc(KC * 512, BF16).rearrange("p (c n) -> p c n", c=KC)
        win = self.d_win.ap()
        wB1 = self.load_w(wkb, win[:, :, C_KB:C_KB + 512], KC, 512, gain=self.gmix)
        wB2 = self.load_w(wvb, win[:, :, C_VB:C_VB + 512], KC, 512, gain=self.gmix)
        xt = [A.alloc(KC * 512, F32).rearrange("p (c n) -> p c n", c=KC) for _ in range(2)]
        xB = P.bufs(2, "x")
        sq = [A.alloc(512, BF16) for _ in range(2)]
        sqB = P.bufs(2, "sq")
        hT = [A.alloc(KC * 512, BF16).rearrange("p (c n) -> p c n", c=KC) for _ in range(2)]
        hB = [P.bufs(KC, "h") for _ in range(2)]
        rstd = [A.alloc(512, F32) for _ in range(2)]
        rB = P.bufs(2, "r")
        tmp = [(A.alloc(512, BF16), A.alloc(512, F32)) for _ in range(3)]
        tmpB = [P.bufs(2, "tmp") for _ in range(3)]
        kout = [A.alloc(4 * 512, BF16).rearrange("p (h n) -> p h n", h=4) for _ in range(2)]
        koB = [P.bufs(4, "ko") for _ in range(2)]
        vout = [A.alloc(4 * 512, BF16).rearrange("p (b n) -> p b n", b=4) for _ in range(2)]
        voB = [P.bufs(4, "vo") for _ in range(2)]
        xT = self.d_xT.ap()
        NT = S // 512

        def stats(T):
            pp = T % 2
            P.dma(xt[pp], xT[:, :, T * 512:(T + 1) * 512], writes=[xB[pp]], key=f"x{pp}")
            self.norm_tile(xt[pp], xB[pp], 512, sq, sqB, hT[pp], hB[pp], rstd[pp], rB[pp], [(0, 512, 0)])

        stats(0)
        for T in range(NT):
            pp = T % 2
            if T + 1 < NT:
                stats(T + 1)
            tasks = [((lambda c, hh=hh: wkb[:, c, hh * 128:(hh + 1) * 128]), 0, 512, kout[pp][:, hh, :], koB[pp][hh], self.gk_b) for hh in range(4)]
            self.ph_tasks(tasks, hT[pp], hB[pp], wB1, tmp, tmpB, (1, 2, 3), (6, 7))
            for hh in range(4):
                P.dma(self.d_KT.ap()[hh, :, T * 512:(T + 1) * 512], kout[pp][:, hh, :], reads=[koB[pp][hh]], key=f"ko{pp}", eng=STORE_ENG)
            for blk in range(4):
                bk = 4 + blk % 2
                for c in range(KC):
                    self.mm(ps[:, bk, :], hT[pp][:, c, blk * 128:(blk + 1) * 128], wvb[:, c, :], c == 0, c == KC - 1,
                            [hB[pp][c], wB2], [psB[bk]])
                self.cp("act" if blk % 2 == 0 else "dve", vout[pp][:, blk, :], ps[:, bk, :], [psB[bk]], [voB[pp][blk]])
            for hh in range(4):
                P.dma(self.d_VS.ap()[hh, :, 4 * T:4 * T + 4, :], vout[pp][:, :, hh * 128:(hh + 1) * 128],
                      reads=voB[pp], key=f"vo{pp}", eng=STORE_ENG)
        P.barrier()
        A.release(m)

    def phase2(self):
        P, A = self.P, self.A
        ps, psB = self.ps, self.psB
        m = A.mark()
        win = self.d_win.ap()

        def walloc(n):
            return A.alloc(KC * n, BF16).rearrange("p (c n) -> p c n", c=KC)

        wqa, wka2, wva, wqb = walloc(512), walloc(256), walloc(128), walloc(512)
        wBq = self.load_w(wqa, win[:, :, C_QA:C_QA + 512], KC, 512, gain=self.gmix)
        for g in range(2):
            for hlf in range(2):
                self.load_w(wka2[:, :, g * 128 + hlf * 64: g * 128 + hlf * 64 + 64], win[:, :, C_KA + 64 * g:C_KA + 64 * g + 64], KC, 64,
                            gain=self.gmix)
        self.load_w(wva, win[:, :, C_VA:C_VA + 128], KC, 128, gain=self.gmix)
        self.load_w(wqb, win[:, :, C_QB:C_QB + 512], KC, 512, gain=self.gmix)
        wB = P.buf("w2")
        xts = [A.alloc(KC * SLOTW, F32).rearrange("p (c n) -> p c n", c=KC) for _ in range(2)]
        xBs = P.bufs(2, "x")
        sq = [A.alloc(SLOTW, BF16) for _ in range(2)]
        sqB = P.bufs(2, "sq")
        hTs = [A.alloc(KC * SLOTW, BF16).rearrange("p (c n) -> p c n", c=KC) for _ in range(2)]
        hBs = [P.bufs(KC, "h") for _ in range(2)]
        rstds = [A.alloc(SLOTW, F32) for _ in range(2)]
        rBs = P.bufs(2, "r")
        tmp = [(A.alloc(512, BF16), A.alloc(512, F32)) for _ in range(3)]
        tmpB = [P.bufs(2, "tmp") for _ in range(3)]
        qaT = A.alloc(4 * HW, BF16).rearrange("p (c n) -> p c n", c=4)
        qaB = P.bufs(4, "qa")
        kaT = A.alloc(2 * SLOTW, BF16).rearrange("p (g n) -> p g n", g=2)
        kaB = P.bufs(2, "ka")
        va = A.alloc(6 * 128, BF16).rearrange("p (b n) -> p b n", b=6)
        vaB = P.buf("va")
        qbT = A.alloc(4 * HW, BF16).rearrange("p (c n) -> p c n", c=4)
        qbB = P.buf("qb")
        yaT = A.alloc(4 * HW, BF16).rearrange("p (c n) -> p c n", c=4)
        yaB = P.buf("ya")
        pt = [A.alloc(512, BF16) for _ in range(4)]
        ptB = P.bufs(4, "pt")
        den = A.alloc(512, F32)
        denB = P.buf("den")
        xo = self.d_xo.ap()
        nsl = min(NSLOT, P2_SLOTS)

        def stats(i):
            pp = i % 2
            P.dma(xts[pp], xo[:, :, i * SLOTW:(i + 1) * SLOTW], writes=[xBs[pp]], key="x2")
            self.norm_tile(xts[pp], xBs[pp], SLOTW, sq, sqB, hTs[pp], hBs[pp], rstds[pp], rBs[pp], [(0, 512, 0), (512, 256, 7)])

        stats(0)
        for i in range(nsl):
            hT, hB = hTs[i % 2], hBs[i % 2]
            tasks = []
            for (o, w) in ((128, 512), (640, 128)):
                for cm in range(4):
                    tasks.append(((lambda c, cm=cm: wqa[:, c, cm * 128:(cm + 1) * 128]), o, w, qaT[:, cm, o - 128:o - 128 + w], qaB[cm], self.gq_a))
                for cm in range(4):
                    tasks.append(((lambda c, cm=cm: wqb[:, c, cm * 128:(cm + 1) * 128]), o, w, qbT[:, cm, o - 128:o - 128 + w], qbB, self.gq_b))
            for (o, w) in ((0, 512), (512, 256)):
                for g in range(2):
                    tasks.append(((lambda c, g=g: wka2[:, c, g * 128:(g + 1) * 128]), o, w, kaT[:, g, o:o + w], kaB[g], self.gk_a))
            self.ph_tasks(tasks, hT, hB, wB, tmp, tmpB, (1, 2, 3), (5, 6))
            P.dma(self.d_QB.ap()[i], qbT, reads=[qbB], key="qbo", eng=STORE_ENG)
            for half in range(2):
                bk = 4 + half
                for bl in range(3):
                    blk = half * 3 + bl
                    for c in range(KC):
                        self.mm(ps[:, bk, bl * 128:(bl + 1) * 128], hT[:, c, blk * 128:(blk + 1) * 128], wva[:, c, :], c == 0, c == KC - 1,
                                [hB[c], wB], [psB[bk]])
                self.cp("act", va[:, half * 3:half * 3 + 3, :], ps[:, bk, 0:384].rearrange("p (b n) -> p b n", b=3), [psB[bk]], [vaB])
            if i + 1 < nsl:
                stats(i + 1)
            for n in range(1, 6 if P2_STAGE >= 1 else 0):
                qo = (n - 1) * 128
                for g in range(2):
                    for kk, kblk in enumerate((n - 1, n)):
                        idx = g * 2 + kk
                        b0 = (2, 6)[idx % 2]
                        for r in range(4):
                            par, rr = r % 2, r // 2
                            pb = 64 * par
                            self.mm(ps[:, b0 + par, rr * 128:(rr + 1) * 128], kaT[pb:pb + 64, g, kblk * 128:(kblk + 1) * 128],
                                    qaT[pb:pb + 64, 2 * g + rr, qo:qo + 128], True, True,
                                    [kaB[g], qaB[2 * g + rr]], [psB[b0 + par]])
                        self.act(pt[idx].rearrange("p (a n) -> p a n", a=2), ps[:, b0:b0 + 2, 0:256], AF.Exp,
                                 [psB[b0], psB[b0 + 1], self.cB], [ptB[idx]], scale=0.125)
                        ty = 1 if kk == 0 else 0
                        fa = self.sa[:, ty, 4 * g:4 * g + 4, :].rearrange("p (rr par) q -> p par rr q", par=2)
                        p4 = pt[idx].rearrange("p (par rr q) -> p par rr q", par=2, rr=2)
                        self.tt("pool", p4, p4, fa, ALU.mult, [ptB[idx], self.stripB], [ptB[idx]])
                        if i == 0 and n == 2 and kk == 0:
                            self.ts("pool", pt[idx], pt[idx], self.m0[:, 0:1], None, ALU.mult, None, [ptB[idx], self.sB], [ptB[idx]])
                if P2_STAGE < 2:
                    continue
                for g in range(2):
                    for kk, kblk in enumerate((n - 1, n)):
                        idx = g * 2 + kk
                        self.mm(ps[64 * g:64 * g + 64, 4, :], va[:, kblk, 64 * g:64 * g + 64], pt[idx], kk == 0, kk == 1,
                                [vaB, ptB[idx]], [psB[4]])
                    for kk in range(2):
                        idx = g * 2 + kk
                        self.mm(ps[64 * g:64 * g + 64, 5, :], self.ones_bf[:, 0:64], pt[idx], kk == 0, (kk == 1 and P2_STAGE < 3),
                                [ptB[idx], self.cB], [psB[5]])
                    if P2_STAGE >= 3:
                        self.mm(ps[64 * g:64 * g + 64, 5, :], self.ones_f[0:1, 0:64], self.esrow[0:1, g * 512:(g + 1) * 512], False, True,
                                [self.sB, self.cB], [psB[5]])
                self.rsqrt_act(den, ps[:, 5, :], 1.0, [psB[5], self.cB], [denB], power=-1.0, bias=self.zero1)
                self.tt("dve", yaT[:, :, qo:qo + 128].rearrange("p (rr par) q -> p par rr q", par=2),
                        ps[:, 4, :].rearrange("p (par rr q) -> p par rr q", par=2, rr=2),
                        den.rearrange("p (par rr q) -> p par rr q", par=2, rr=2), ALU.mult, [psB[4], denB], [yaB])
            P.dma(self.d_YA.ap()[i], yaT, reads=[yaB], key="yao", eng=STORE_ENG)
        P.barrier()
        A.release(m)

    def phase3(self):
        P, A = self.P, self.A
        ps, psB = self.ps, self.psB
        m = A.mark()
        KTs = [A.alloc(S, BF16) for _ in range(2)]
        VSs = [A.alloc(128 * 128, BF16).rearrange("p (b e) -> p b e", b=128) for _ in range(2)]
        kvB = [(P.bufs(4, "kt"), P.bufs(4, "vs")) for _ in range(2)]
        qt = [A.alloc(HW, BF16) for _ in range(2)]
        qB = P.bufs(2, "q")
        NPB = 4
        pt = [A.alloc(1024, BF16) for _ in range(NPB)]
        ptB = P.bufs(NPB, "pt")
        ssum = [A.alloc(512, F32) for _ in range(2)]
        ssumB = P.bufs(2, "ssum")
        rbc = [A.alloc(1024, F32) for _ in range(2)]
        rbcB = P.bufs(2, "rbc")
        dd = [A.alloc(1024, F32) for _ in range(2)]
        ddB = P.bufs(2, "dd")
        dsq = [A.alloc(512, BF16) for _ in range(2)]
        dsqB = P.bufs(2, "dsq")
        rrow = [A.alloc(512, F32) for _ in range(2)]
        rrowB = P.bufs(2, "rrow")
        rrbc = [A.alloc(512, F32) for _ in range(2)]
        rrbcB = P.bufs(2, "rrbc")
        yb = [A.alloc(HW, BF16) for _ in range(2)]
        ybB = P.bufs(2, "yb")
        dbcB = [P.bufs(2, "dbc") for _ in range(2)]
        q7B = P.bufs(4, "ps7q")
        ones32 = self.ones_bf[:, 0:32]
        SB = [(0, 1), (2, 3)]
        dbc = self.d_bc

        def load_kv(h):
            pp = h % 2
            for q4 in range(4):
                P.dma(KTs[pp][:, q4 * 4096:(q4 + 1) * 4096], self.d_KT.ap()[h, :, q4 * 4096:(q4 + 1) * 4096], writes=[kvB[pp][0][q4]])
            for q4 in range(4):
                P.dma(VSs[pp][:, q4 * 32:(q4 + 1) * 32, :], self.d_VS.ap()[h, :, q4 * 32:(q4 + 1) * 32, :], writes=[kvB[pp][1][q4]])

        pending = []
        gcount = [0]

        def flush():
            while pending:
                pending.pop(0)()

        def finalize(ncol, o_ap, obufs, ycols, ybt, ybb, out_dma, nred=1):
            gp = gcount[0] % 2
            gcount[0] += 1
            small = ncol < 128
            P.strict = small
            ss_, rb_, dd_, dq_, rw_, rrb_ = ssum[gp], rbc[gp], dd[gp], dsq[gp], rrow[gp], rrbc[gp]
            if nred == 1:
                self.cp("dve", ss_[0:64, 0:ncol], ps[0:64, 7, 0:ncol], [q7B[0], q7B[1]], [ssumB[gp]])
            else:
                wtot = nred * ncol
                self.cp("dve", ss_[0:64, 0:wtot], ps[0:64, 7, 0:wtot], [q7B[0], q7B[1]], [ssumB[gp]])
                P.strict = True
                if nred == 12:
                    steps = [(4 * ncol, 8 * ncol, 4 * ncol), (4 * ncol, 4 * ncol, 4 * ncol), (2 * ncol, 2 * ncol, 2 * ncol), (ncol, ncol, ncol)]
                else:
                    steps = [(8 * ncol, 8 * ncol, 8 * ncol), (4 * ncol, 4 * ncol, 4 * ncol), (2 * ncol, 2 * ncol, 2 * ncol), (ncol, ncol, ncol)]
                for (wd_, src_, _) in steps:
                    self.tt("dve", ss_[0:64, 0:wd_], ss_[0:64, 0:wd_], ss_[0:64, src_:src_ + wd_], ALU.add, [ssumB[gp]], [ssumB[gp]])
                P.strict = small
            for comp in range(2):
                P.dma(dbc.ap()[gp, comp:comp + 1, 0:ncol], ss_[32 * comp:32 * comp + 1, 0:ncol], reads=[ssumB[gp]], writes=[dbcB[gp][0]])
            P.dma(rb_.rearrange("p (c n) -> p c n", c=2)[:, :, 0:ncol], bass.AP(dbc, gp * 3 * 512, [[0, 128], [512, 2], [1, ncol]]),
                  reads=[dbcB[gp][0]], writes=[rbcB[gp]])
            for comp in range(2):
                r_ = rb_[:, comp * 512:comp * 512 + ncol]
                self.ts("dve", r_, r_, self.tiny1[:, 0:1], None, ALU.add, None, [rbcB[gp], self.cB], [rbcB[gp]])
                self.rcp(r_, r_, [rbcB[gp]], [rbcB[gp]])
                self.tt("dve", dd_[:, comp * 512: comp * 512 + ncol], o_ap(comp), r_, ALU.mult, [obufs[comp], rbcB[gp]], [ddB[gp]])
            self.stt("dve", dd_[:, 0:ncol], dd_[:, 512:512 + ncol], self.neglam[:, 0:1], dd_[:, 0:ncol], ALU.mult, ALU.add, [ddB[gp], self.sB], [ddB[gp]])
            self.tt("pool", dq_[:, 0:ncol], dd_[:, 0:ncol], dd_[:, 0:ncol], ALU.mult, [ddB[gp]], [dsqB[gp]])
            P.strict = False

            def stage1():
                P.strict = small
                self.mm(ps[64:96, 7, 0:ncol], ones32, dq_[:, 0:ncol], True, True, [dsqB[gp], self.cB], [q7B[2]])
                self.act(rw_[64:96, 0:ncol], ps[64:96, 7, 0:ncol], AF.Ln, [q7B[2], self.cB], [rrowB[gp]], bias=self.eps1[64:96, :], scale=1.0 / 128)
                P.strict = True
                self.act(rw_[64:96, 0:ncol], rw_[64:96, 0:ncol], AF.Exp, [rrowB[gp], self.cB], [rrowB[gp]], bias=self.zero1[64:96, :], scale=-0.5)
                P.strict = small
                P.dma(dbc.ap()[gp, 2:3, 0:ncol], rw_[64:65, 0:ncol], reads=[rrowB[gp]], writes=[dbcB[gp][1]])
                P.dma(rrb_[:, 0:ncol], bass.AP(dbc, (gp * 3 + 2) * 512, [[0, 128], [1, ncol]]), reads=[dbcB[gp][1]], writes=[rrbcB[gp]])
                self.stt("dve", ybt[:, ycols:ycols + ncol], dd_[:, 0:ncol], self.gsub[:, 0:1], rrb_[:, 0:ncol], ALU.mult, ALU.mult,
                         [ddB[gp], rrbcB[gp], self.sB], [ybb])
                P.strict = False
                if out_dma is not None:
                    out_dma()
            pending.append(stage1)

        def group(nsteps, qk, av, near_off, ncol, ncols_sum, sum_rhs):
            def pe_tail(t):
                p_, pB_ = pt[t % NPB], ptB[t % NPB]
                for comp in range(2):
                    rl = sum_rhs(p_, comp)
                    for n_, r_ in enumerate(rl):
                        self.mm(ps[32 * comp:32 * comp + 32, 7, 0:ncol], ones32, r_, t == 0 and n_ == 0, t == nsteps - 1 and n_ == len(rl) - 1,
                                [pB_, self.cB], [q7B[comp]])
                av(t, p_, pB_)

            qk(0)
            for t in range(nsteps):
                if t + 1 < nsteps:
                    qk(t + 1)
                sb = SB[t % 2]
                p_, pB_ = pt[t % NPB], ptB[t % NPB]
                self.act(p_.rearrange("p (c n) -> p c n", c=2), ps[:, sb[0]:sb[0] + 2, :], AF.Exp, [psB[sb[0]], psB[sb[1]], self.cB], [pB_], scale=0.125)
                off = near_off(t)
                if off is not None:
                    for comp in range(2):
                        self.tt("dve", p_[:, comp * 512:(comp + 1) * 512], p_[:, comp * 512:(comp + 1) * 512],
                                self.strip_h[:, off:off + 512], ALU.mult, [pB_, self.stripB], [pB_])
                if t >= 1:
                    pe_tail(t - 1)
                if t == min(10, nsteps - 1):
                    flush()
            pe_tail(nsteps - 1)

        def group_h(nsteps, qk, av, near, batches, HQ):
            def pe_tail(t):
                kb0, nseg = batches[t]
                wd = nseg * HQ
                p_, pB_ = pt[t % NPB], ptB[t % NPB]
                for comp in range(2):
                    self.mm(ps[32 * comp:32 * comp + 32, 7, 0:wd], ones32, p_[:, comp * 512:comp * 512 + wd], t == 0, t == nsteps - 1,
                            [pB_, self.cB], [q7B[comp]])
                av(t, p_, pB_)

            qk(0)
            for t in range(nsteps):
                if t + 1 < nsteps:
                    qk(t + 1)
                sb = SB[t % 2]
                kb0, nseg = batches[t]
                wd = nseg * HQ
                p_, pB_ = pt[t % NPB], ptB[t % NPB]
                self.act(p_.rearrange("p (c n) -> p c n", c=2)[:, :, 0:wd], ps[:, sb[0]:sb[0] + 2, 0:wd], AF.Exp,
                         [psB[sb[0]], psB[sb[1]], self.cB], [pB_], scale=0.125)
                nr = near(t)
                if nr is not None:
                    c0, ns_, off = nr
                    fa = self.strip_h[:, off:off + ns_ * 128].rearrange("p (u c) -> p u c", c=128)[:, :, 0:HQ]
                    for comp in range(2):
                        pv = p_[:, comp * 512 + c0 * HQ: comp * 512 + (c0 + ns_) * HQ].rearrange("p (u c) -> p u c", c=HQ)
                        self.tt("dve", pv, pv, fa, ALU.mult, [pB_, self.stripB], [pB_])
                if t >= 1:
                    pe_tail(t - 1)
            pe_tail(nsteps - 1)

        load_kv(0)
        cnt_q = 0
        for h in range(4):
            pp = h % 2
            KT, VS = KTs[pp], VSs[pp]
            kB, vB = kvB[pp]
            self.strip_h = self.strip[:, h, :]
            if h + 1 < 4:
                load_kv(h + 1)
            for i in range(NSLOT):
                qq = cnt_q % 2
                cnt_q += 1
                q, qb_ = qt[qq], qB[qq]
                if h == 0 and i == 0:
                    P.dma(q, self.d_QB.ap()[0, :, 0, :], writes=[qb_])
                nh, ni = (h, i + 1) if i + 1 < NSLOT else (h + 1, 0)
                if nh < 4:
                    P.dma(qt[cnt_q % 2], self.d_QB.ap()[ni, :, nh, :], writes=[qB[cnt_q % 2]])
                ybt, ybb = yb[qq], ybB[qq]
                nkb = 16 * i + 16
                near0 = 16 * i - 1

                def qk(t, KT=KT, q=q, kB=kB, qb_=qb_):
                    sb = SB[t % 2]
                    for comp in range(2):
                        self.mm(ps[:, sb[comp], :], KT[64 * comp:64 * comp + 64, t * 128:(t + 1) * 128], q[64 * comp:64 * comp + 64, 128:640],
                                True, True, [kB[t // 32], qb_], [psB[sb[comp]]])

                def av(t, p_, pB_, VS=VS, vB=vB, nkb=nkb):
                    for comp in range(2):
                        self.mm(ps[:, 4 + comp, :], VS[:, t, :], p_[:, comp * 512:(comp + 1) * 512], t == 0, t == nkb - 1,
                                [vB[t // 32], pB_], [psB[4 + comp]])

                group(nkb, qk, av, lambda t, near0=near0: (2048 - 128 * (t - near0)) if t >= near0 else None, 512, 512,
                      lambda p_, comp: [p_[:, comp * 512:(comp + 1) * 512]])
                finalize(512, lambda comp: ps[:, 4 + comp, :], [psB[4], psB[5]], 128, ybt, ybb, None)
                HQ = 32
                batches = [(16 * b, 16) for b in range(i)] + [(16 * i, 12)]
                nb = len(batches)

                def qkh(t, KT=KT, q=q, kB=kB, qb_=qb_, batches=batches):
                    sb = SB[t % 2]
                    kb0, nseg = batches[t]
                    for comp in range(2):
                        for u in range(nseg):
                            kb = kb0 + nseg - 1 - u
                            self.mm(ps[:, sb[comp], u * HQ:(u + 1) * HQ], KT[64 * comp:64 * comp + 64, kb * 128:(kb + 1) * 128],
                                    q[64 * comp:64 * comp + 64, 128 - HQ:128], True, True, [kB[kb // 32], qb_], [psB[sb[comp]]])

                def avh(t, p_, pB_, VS=VS, vB=vB, nb=nb, batches=batches):
                    kb0, nseg = batches[t]
                    for comp in range(2):
                        for u in range(nseg):
                            kb = kb0 + nseg - 1 - u
                            self.mm(ps[:, 6, comp * HQ:(comp + 1) * HQ], VS[:, kb, :], p_[:, comp * 512 + u * HQ: comp * 512 + (u + 1) * HQ],
                                    t == 0 and u == 0 and comp == 0, t == nb - 1 and u == nseg - 1, [vB[kb // 32], pB_], [psB[6]])

                def nearh(t, i=i, batches=batches):
                    kb0, nseg = batches[t]
                    if kb0 == 16 * i:
                        return (0, 12, 2048 - 128 * 13 + (128 - HQ))
                    if kb0 == 16 * i - 16:
                        return (0, 4, 2048 - 128 + (128 - HQ))
                    return None

                def odma(i=i, h=h, ybt=ybt, ybb=ybb):
                    P.dma(self.d_YB.ap()[i, :, h, :], ybt, reads=[ybb])

                group_h(nb, qkh, avh, nearh, batches, HQ)
                finalize(HQ, lambda comp: ps[:, 6, comp * HQ:(comp + 1) * HQ], [psB[6], psB[6]], 128 - HQ, ybt, ybb, odma,
                         nred=(16 if i > 0 else 12))
        flush()
        P.barrier()
        A.release(m)

    def phase4a(self):
        P, A = self.P, self.A
        ps, psB = self.ps, self.psB
        m = A.mark()
        win = self.d_win.ap()
        wgl = A.alloc(KC * 2048, BF16).rearrange("p (c n) -> p c n", c=KC)
        wbra = A.alloc(4 * D, BF16).rearrange("p (c n) -> p c n", c=4)
        wbrb = A.alloc(4 * D, BF16).rearrange("p (c n) -> p c n", c=4)
        wo = A.alloc(KC * D, BF16).rearrange("p (c n) -> p c n", c=KC)
        self.load_w(wgl, win[:, :, C_GL:C_GL + 2048], KC, 2048, gain=self.gmix)
        self.load_w(wbra, self.d_wbra.ap(), 4, D)
        self.load_w(wbrb, self.d_wbrb.ap(), 4, D)
        self.load_w(wo, self.d_wo.ap(), KC, D)
        wB = P.buf("w4a")
        xt = A.alloc(KC * HW, F32).rearrange("p (c n) -> p c n", c=KC)
        xB = P.buf("x")
        sq = [A.alloc(HW, BF16) for _ in range(2)]
        sqB = P.bufs(2, "sq")
        hT = A.alloc(KC * HW, BF16).rearrange("p (c n) -> p c n", c=KC)
        hB = P.bufs(KC, "h")
        rstd = A.alloc(HW, F32)
        rB = P.buf("r")
        gates = A.alloc(16 * HW, BF16).rearrange("p (c n) -> p c n", c=16)
        gB = P.bufs(16, "g")
        ya = A.alloc(4 * HW, BF16).rearrange("p (c n) -> p c n", c=4)
        yb = A.alloc(4 * HW, BF16).rearrange("p (c n) -> p c n", c=4)
        yB = P.bufs(2, "y")
        mixed = A.alloc(KC * HW, BF16).rearrange("p (c n) -> p c n", c=KC)
        mxB = P.bufs(KC, "mx")
        t1 = [A.alloc(512, BF16) for _ in range(2)]
        t2 = [A.alloc(512, BF16) for _ in range(2)]
        tB = [P.bufs(2, "t") for _ in range(2)]
        x2 = [A.alloc(512, F32) for _ in range(2)]
        x2B = P.bufs(2, "x2")
        xo = self.d_xo.ap()
        PIECES = ((96, 512), (608, 32))
        cnt = 0
        for i in range(NSLOT):
            P.dma(xt, xo[:, :, i * SLOTW + 128:(i + 1) * SLOTW], writes=[xB], key="x4")
            P.dma(ya, self.d_YA.ap()[i], writes=[yB[0]], key="ya")
            P.dma(yb, self.d_YB.ap()[i], writes=[yB[1]], key="yb")
            self.norm_tile(xt, xB, HW, sq, sqB, hT, hB, rstd, rB, [(96, 512, 0), (608, 32, 7)])
            for (o, w) in PIECES:
                for gc in range(16):
                    bk = 1 + gc % 2
                    for c in range(KC):
                        self.mm(ps[:, bk, 0:w], wgl[:, c, gc * 128:(gc + 1) * 128], hT[:, c, o:o + w], c == 0, c == KC - 1, [wB, hB[c]], [psB[bk]])
                    self.act(gates[:, gc, o:o + w], ps[:, bk, 0:w], AF.Sigmoid, [psB[bk], self.cB], [gB[gc]])
            for (o, w) in PIECES:
                for mc in range(KC):
                    k2 = cnt % 2
                    cnt += 1
                    ba, bb = 3 + 2 * k2, 4 + 2 * k2
                    for r in range(4):
                        self.mm(ps[:, ba, 0:w], wbra[:, r, mc * 128:(mc + 1) * 128], ya[:, r, o:o + w], r == 0, r == 3, [wB, yB[0]], [psB[ba]])
                    for r in range(4):
                        self.mm(ps[:, bb, 0:w], wbrb[:, r, mc * 128:(mc + 1) * 128], yb[:, r, o:o + w], r == 0, r == 3, [wB, yB[1]], [psB[bb]])
                    self.tt("dve", t1[k2][:, 0:w], ps[:, ba, 0:w], gates[:, mc, o:o + w], ALU.mult, [psB[ba], gB[mc]], [tB[k2][0]])
                    self.tt("dve", t2[k2][:, 0:w], ps[:, bb, 0:w], gates[:, 8 + mc, o:o + w], ALU.mult, [psB[bb], gB[8 + mc]], [tB[k2][1]])
                    self.tt("pool", mixed[:, mc, o:o + w], t1[k2][:, 0:w], t2[k2][:, 0:w], ALU.add, [tB[k2][0], tB[k2][1]], [mxB[mc]])
            for (o, w) in PIECES:
                for oc in range(KC):
                    k2 = cnt % 2
                    cnt += 1
                    bk = 1 + k2
                    for mc in range(KC):
                        self.mm(ps[:, bk, 0:w], wo[:, mc, oc * 128:(oc + 1) * 128], mixed[:, mc, o:o + w], mc == 0, mc == KC - 1, [wB, mxB[mc]], [psB[bk]])
                    self.tt("dve", x2[k2][:, 0:w], ps[:, bk, 0:w], xt[:, oc, o:o + w], ALU.add, [psB[bk], xB], [x2B[k2]])
                    P.dma(self.d_X2.ap()[i, :, oc, o:o + w], x2[k2][:, 0:w], reads=[x2B[k2]], key=f"x2o{k2}", eng=STORE_ENG)
        P.barrier()
        A.release(m)

    def phase4b(self):
        P, A = self.P, self.A
        ps, psB = self.ps, self.psB
        A.release(self.pre_strip_mark)
        m = A.mark()
        wup = A.alloc(KC * 2 * DFF, BF16).rearrange("p (c n) -> p c n", c=KC)
        wdn = A.alloc(NFC * D, BF16).rearrange("p (c n) -> p c n", c=NFC)
        self.load_w(wup, self.d_wup.ap(), KC, 2 * DFF, gain=self.gffn)
        self.load_w(wdn, self.d_wdn.ap(), NFC, D)
        wB = P.buf("w4b")
        W2 = 514
        xa_off = A.alloc_raw(NFC * 512 * 2)
        xaB = P.buf("xa")
        sq = [A.alloc(W2, BF16) for _ in range(2)]
        sqB = P.bufs(2, "sq")
        hT = A.alloc(KC * W2, BF16).rearrange("p (c n) -> p c n", c=KC)
        hB = P.bufs(KC, "h")
        rstd = A.alloc(W2, F32)
        rB = P.buf("r")
        uh = A.alloc(44 * 2, F32).rearrange("p (c n) -> p c n", c=44)
        uhB = P.buf("uh")
        U = [A.alloc(W2, F32) for _ in range(4)]
        UB = P.bufs(4, "U")
        tg = [A.alloc(512, F32) for _ in range(2)]
        tv = [A.alloc(512, F32) for _ in range(2)]
        tgB = P.bufs(2, "tg")
        tvB = P.bufs(2, "tv")
        ot = [A.alloc(512, F32) for _ in range(2)]
        otB = P.bufs(2, "ot")
        xr = [A.alloc(512, F32) for _ in range(2)]
        xrB = P.bufs(2, "xr")
        cw = self.cw.rearrange("p (a b) -> p a b", a=3)
        outT = self.d_out.ap()
        xt = A.at(xa_off, KC * W2, F32).rearrange("p (c n) -> p c n", c=KC)
        aT = A.at(xa_off, NFC * 512, BF16).rearrange("p (c n) -> p c n", c=NFC)
        for i in range(NSLOT):
            P.dma(xt, self.d_X2.ap()[i, :, :, 126:640], writes=[xaB])
            self.norm_tile(xt, xaB, W2, sq, sqB, hT, hB, rstd, rB, [(0, 2, 7), (2, 512, 0)])
            for fc in range(44):
                for c in range(KC):
                    self.mm(ps[:, 7, fc * 2:fc * 2 + 2], wup[:, c, fc * 128:(fc + 1) * 128], hT[:, c, 0:2], c == 0, c == KC - 1, [wB, hB[c]], [psB[7]])
            self.cp("dve", uh, ps[:, 7, 0:88].rearrange("p (c n) -> p c n", c=44), [psB[7]], [uhB])
            for f in range(NFC):
                k2 = f % 2
                for half, fc in enumerate((f, NFC + f)):
                    bk = 1 + 2 * k2 + half
                    ui = 2 * k2 + half
                    for c in range(KC):
                        self.mm(ps[:, bk, :], wup[:, c, fc * 128:(fc + 1) * 128], hT[:, c, 2:W2], c == 0, c == KC - 1, [wB, hB[c]], [psB[bk]])
                    self.cp("pool", U[ui][:, 0:2], uh[:, fc, :], [uhB], [UB[ui]])
                    self.cp("act", U[ui][:, 2:W2], ps[:, bk, :], [psB[bk]], [UB[ui]])
                    t_, tb_ = (tg[k2], tgB[k2]) if half == 0 else (tv[k2], tvB[k2])
                    eng = "dve"
                    self.ts(eng, t_, U[ui][:, 0:512], cw[:, 0, fc:fc + 1], self.cb[:, fc:fc + 1], ALU.mult, ALU.add, [UB[ui], self.sB], [tb_])
                    self.stt(eng, t_, U[ui][:, 1:513], cw[:, 1, fc:fc + 1], t_, ALU.mult, ALU.add, [UB[ui], self.sB, tb_], [tb_])
                    self.stt(eng, t_, U[ui][:, 2:514], cw[:, 2, fc:fc + 1], t_, ALU.mult, ALU.add, [UB[ui], self.sB, tb_], [tb_])
                self.act(tg[k2], tg[k2], AF.Silu, [tgB[k2], self.cB], [tgB[k2]])
                self.tt("pool", aT[:, f, :], tg[k2], tv[k2], ALU.mult, [tgB[k2], tvB[k2]], [xaB])
            for oc in range(KC):
                k2 = oc % 2
                bk = 5 + k2
                P.dma(xr[k2], self.d_X2.ap()[i, :, oc, 128:640], writes=[xrB[k2]])
                for f in range(NFC):
                    self.mm(ps[:, bk, :], wdn[:, f, oc * 128:(oc + 1) * 128], aT[:, f, :], f == 0, f == NFC - 1, [wB, xaB], [psB[bk]])
                self.tt("dve", ot[k2], ps[:, bk, :], xr[k2], ALU.add, [psB[bk], xrB[k2]], [otB[k2]])
                P.dma(outT[:, oc, i * 512:(i + 1) * 512], ot[k2], reads=[otB[k2]], eng=STORE_ENG)
        P.barrier()
        A.release(m)

    def build(self, nph=N_PHASES):
        self.declare()
        self.P.strict = True
        self.setup()
        self.P.strict = False
        self.P.barrier()
        phases = [self.phase1, self.phase2, self.phase3, self.phase4a, self.phase4b]
        for n_, ph in enumerate(phases[:nph]):
            if n_ == 0 and SKIP_P1:
                continue
            ph()
        self.P.barrier()
        self.P.emit()
        return self.nc


def _host_inputs(inputs):
    f = np.float32
    x = np.asarray(inputs["x"], dtype=f)

    def pc(w, kc):
        n = w.shape[1]
        return np.ascontiguousarray(w.reshape(kc, 128, n).transpose(1, 0, 2))

    w_br_a = np.asarray(inputs["w_br_a"][0], dtype=f)
    wa = w_br_a.reshape(2, 4, 64, D)
    wbra = np.ascontiguousarray(wa.transpose(0, 2, 1, 3).reshape(128, 4, D))
    common = {
        "w_in": pc(np.asarray(inputs["w_in"][0], dtype=f), KC),
        "w_br_a": wbra,
        "w_br_b": pc(np.asarray(inputs["w_br_b"][0], dtype=f), 4),
        "w_o": pc(np.asarray(inputs["w_o"][0], dtype=f), KC),
        "w_up": pc(np.asarray(inputs["w_up"][0], dtype=f), KC),
        "w_down": pc(np.asarray(inputs["w_down"][0], dtype=f), NFC),
        "g_mix": np.ascontiguousarray(np.asarray(inputs["g_mix"][0], dtype=f).reshape(KC, 128).T),
        "g_ffn": np.ascontiguousarray(np.asarray(inputs["g_ffn"][0], dtype=f).reshape(KC, 128).T),
        "conv_w": np.ascontiguousarray(np.asarray(inputs["conv_w"][0], dtype=f).reshape(3, 44, 128).transpose(2, 0, 1)),
        "conv_b": np.ascontiguousarray(np.asarray(inputs["conv_b"][0], dtype=f).reshape(44, 128).T),
        "rel_bias": np.ascontiguousarray(np.asarray(inputs["rel_bias"], dtype=f)),
        "qn_a": np.asarray(inputs["qn_a"], dtype=f).reshape(1, 64),
        "kn_a": np.asarray(inputs["kn_a"], dtype=f).reshape(1, 64),
        "qn_b": np.asarray(inputs["qn_b"], dtype=f).reshape(1, 64),
        "kn_b": np.asarray(inputs["kn_b"], dtype=f).reshape(1, 64),
        "sinks": np.asarray(inputs["sinks"], dtype=f).reshape(1, 8),
        "lam_q1": np.asarray(inputs["lam_q1"], dtype=f).reshape(1, 64),
        "lam_k1": np.asarray(inputs["lam_k1"], dtype=f).reshape(1, 64),
        "lam_q2": np.asarray(inputs["lam_q2"], dtype=f).reshape(1, 64),
        "lam_k2": np.asarray(inputs["lam_k2"], dtype=f).reshape(1, 64),
        "subln_b": np.asarray(inputs["subln_b"], dtype=f).reshape(128, 1),
        "Jmat": np.ascontiguousarray(np.eye(128, dtype=f)[::-1]),
        "bdones": np.kron(np.eye(2, dtype=f), np.ones((64, 64), dtype=f)),
    }
    oha = np.zeros((33, 384), dtype=f)
    for mm_ in range(384):
        d = mm_ - 127
        if 0 <= d < 128:
            oha[int(_t5_bucket_np(np.array(d))), mm_] = 1
        else:
            oha[32, mm_] = 1
    common["oh_a"] = oha
    in_maps = []
    for core in range(8):
        b, j = core // 4, core % 4
        xTb = np.ascontiguousarray(x[b].T.reshape(KC, 128, S).transpose(1, 0, 2))
        xo = np.zeros((128, KC, NSLOT, SLOTW), dtype=f)
        for i in range(NSLOT):
            G = 4 * i + j
            t0 = 512 * G - 256
            lo = max(t0, 0)
            xo[:, :, i, lo - t0:] = xTb[:, :, lo:t0 + SLOTW]
        ohd = np.zeros((33, VECD), dtype=f)
        d = np.arange(VECD) + 512 * j - 2047
        bk = _t5_bucket_np(d)
        for mm_ in range(VECD):
            if d[mm_] >= 0:
                ohd[bk[mm_], mm_] = 1
            else:
                ohd[32, mm_] = 1
        mp = dict(common)
        mp["xT"] = xTb
        mp["xo"] = xo.reshape(128, KC, NSLOT * SLOTW)
        mp["oh_d"] = ohd
        mp["m0"] = np.full((128, 1), 0.0 if j == 0 else 1.0, dtype=f)
        in_maps.append(mp)
    return in_maps


_NC_CACHE = {}


def kernel(**inputs):
    in_maps = _host_inputs(inputs)
    if "nc" not in _NC_CACHE:
        _NC_CACHE["nc"] = K().build()
    nc = _NC_CACHE["nc"]
    res = run_bass_kernel_spmd(nc, in_maps, core_ids=list(range(8)))
    out = np.zeros((2, S, D), dtype=np.float32)
    for core in range(8):
        b, j = core // 4, core % 4
        o = res.results[core]["outT"].reshape(128, KC, NSLOT, 512)
        for i in range(NSLOT):
            G = 4 * i + j
            out[b, 512 * G:512 * (G + 1), :] = o[:, :, i, :].transpose(2, 1, 0).reshape(512, D)
    return out
```

```python
import contextlib
import math
import numpy as np
import concourse.bass as bass
import concourse.mybir as mybir
from concourse.bass_utils import run_bass_kernel_spmd

F32 = mybir.dt.float32
BF16 = mybir.dt.bfloat16
AF = mybir.ActivationFunctionType
ALU = mybir.AluOpType

ENGS = ("pe", "act", "dve", "pool", "sp")
SEM_ROT = 3000


class Buf:
    __slots__ = ("name", "w", "rs")

    def __init__(self, name):
        self.name = name
        self.w = None
        self.rs = []


class Op:
    __slots__ = ("eng", "fn", "waits", "signal", "dma_key", "dma_sem", "dma_val", "sig_sem", "sig_val")

    def __init__(self, eng, fn, dma_key=None):
        self.eng = eng
        self.fn = fn
        self.waits = []
        self.signal = False
        self.dma_key = dma_key
        self.dma_sem = None
        self.dma_val = None
        self.sig_sem = None
        self.sig_val = None


class Prog:
    def __init__(self, nc):
        self.nc = nc
        self.ops = {e: [] for e in ENGS}
        self.dma_cnt = {}
        self.all_dma_last = {}
        self.stack = contextlib.ExitStack()
        self.nbufs = 0
        self.strict = False

    def buf(self, name=None):
        self.nbufs += 1
        return Buf(f"{name or 'b'}#{self.nbufs}")

    def bufs(self, n, name="b"):
        return [self.buf(f"{name}{i}") for i in range(n)]

    def _dep(self, op, y):
        if y is None or y is op:
            return
        if y.dma_key is None and y.eng == op.eng and op.dma_key is None and not self.strict:
            return
        if y.dma_key is None:
            y.signal = True
        if y not in op.waits:
            op.waits.append(y)

    def op(self, eng, fn, reads=(), writes=(), dma_key=None):
        o = Op(eng, fn, dma_key)
        for b in reads:
            self._dep(o, b.w)
        for b in writes:
            self._dep(o, b.w)
            for r in b.rs:
                self._dep(o, r)
        for b in reads:
            b.rs.append(o)
        for b in writes:
            b.w = o
            b.rs = []
        if dma_key is not None:
            st = self.dma_cnt.setdefault(dma_key, [0, 0])
            if st[1] + 16 > 4000:
                st[0] += 1
                st[1] = 0
            st[1] += 16
            o.dma_sem = (dma_key, st[0])
            o.dma_val = st[1]
            self.all_dma_last[dma_key] = o
        self.ops[eng].append(o)
        return o

    def dma(self, out, in_, reads=(), writes=(), key=None, eng="sp", **kw):
        prim = writes[0] if len(writes) else reads[0]
        key = prim.name
        return self.op(eng, lambda e: e.dma_start(out=out, in_=in_, **kw), reads, writes, dma_key=key)

    def barrier(self):
        lasts = [self.ops[e][-1] for e in ENGS if self.ops[e]]
        dmas = list(self.all_dma_last.values())
        news = []
        for e in ENGS:
            o = Op(e, None)
            for y in lasts:
                self._dep(o, y)
            for y in dmas:
                self._dep(o, y)
            news.append(o)
        for o in news:
            self.ops[o.eng].append(o)

    def emit(self):
        nc = self.nc
        semkeys = set()
        for e in ENGS:
            gen, cnt = 0, 0
            for o in self.ops[e]:
                if o.dma_key is not None:
                    semkeys.add(o.dma_sem)
                    continue
                if o.signal:
                    if cnt >= SEM_ROT:
                        gen += 1
                        cnt = 0
                    cnt += 1
                    o.sig_sem = ("eng", e, gen)
                    o.sig_val = cnt
                    semkeys.add(o.sig_sem)
        sems = {}
        for n, k in enumerate(sorted(semkeys, key=str)):
            sems[k] = self.stack.enter_context(nc.semaphore(f"sm{n}"))
        self.nsems = len(sems)
        block = self.stack.enter_context(nc.Block())
        engmap = {"pe": "tensor", "act": "scalar", "dve": "vector", "pool": "gpsimd", "sp": "sync"}

        def make(e):
            def body(eng):
                waited = {}
                for o in self.ops[e]:
                    for y in o.waits:
                        if y.dma_key is not None:
                            sk, v = y.dma_sem, y.dma_val
                        else:
                            sk, v = y.sig_sem, y.sig_val
                        if waited.get(sk, 0) >= v:
                            continue
                        waited[sk] = v
                        eng.wait_ge(sems[sk], v)
                    if o.fn is None:
                        if o.signal:
                            eng.nop().then_inc(sems[o.sig_sem], 1)
                        continue
                    ins = o.fn(eng)
                    if o.dma_key is not None:
                        ins.then_inc(sems[o.dma_sem], 16)
                    elif o.signal:
                        ins.then_inc(sems[o.sig_sem], 1)
            return body

        for e in ENGS:
            getattr(block, engmap[e])(make(e))
        self.stack.close()


class Arena:
    def __init__(self, prog, nbytes, name="arena"):
        nc = prog.nc
        self.t8 = prog.stack.enter_context(nc.sbuf_tensor(name, [128, nbytes], mybir.dt.uint8))
        self.views = {}
        self.nbytes = nbytes
        self.off = 0

    def view(self, dt):
        if dt not in self.views:
            self.views[dt] = self.t8.bitcast(dt)
        return self.views[dt]

    def alloc(self, nelem, dt):
        sz = mybir.dt.size(dt)
        self.off = (self.off + 63) // 64 * 64
        o = self.off
        self.off += nelem * sz
        assert self.off <= self.nbytes, f"arena overflow {self.off} > {self.nbytes}"
        return self.view(dt)[:, o // sz: o // sz + nelem]

    def alloc_raw(self, nbytes):
        self.off = (self.off + 63) // 64 * 64
        o = self.off
        self.off += nbytes
        assert self.off <= self.nbytes, f"arena overflow {self.off} > {self.nbytes}"
        return o

    def at(self, o, nelem, dt):
        sz = mybir.dt.size(dt)
        return self.view(dt)[:, o // sz: o // sz + nelem]

    def mark(self):
        return self.off

    def release(self, m):
        self.off = m


D = 1024
S = 16384
KC = 8
NSLOT = 8
SLOTW = 768
HW = 640
DFF = 2816
NFC = 22
EPS = 1e-6
LAM_INIT = 0.8 - 0.6 * math.exp(-0.3 * 0)
VECD = 2688
STRIPW = 2560
C_QA, C_KA, C_VA, C_QB, C_KB, C_VB, C_GL = 0, 512, 640, 768, 1280, 1792, 2304

DEBUG_SCRATCH = False
STORE_ENG = "pool"
P2_STAGE = 9
P2_SLOTS = 8
SKIP_P1 = False
N_PHASES = 5


def _t5_bucket_np(rel):
    n = np.maximum(rel, 0)
    nf = np.maximum(n, 1).astype(np.float32)
    large = 16 + (np.log(nf / np.float32(16)) / np.float32(math.log(8.0)) * np.float32(16)).astype(np.int32)
    large = np.minimum(large, 31)
    return np.where(n < 16, n, large)


class K:
    def __init__(self):
        nc = bass.Bass("TRN2", target_bir_lowering=False)
        self.nc = nc
        self.P = Prog(nc)
        self.A = Arena(self.P, 209920)
        self.ps = self.P.stack.enter_context(nc.psum_tensor("ps", [128, 8, 512], F32))
        self.psB = self.P.bufs(8, "psb")

    def mm(self, out, lhsT, rhs, start, stop, R, W):
        self.P.op("pe", lambda e: e.matmul(out, lhsT=lhsT, rhs=rhs, start=start, stop=stop), R, W)

    def act(self, out, in_, func, R, W, bias=None, scale=1.0):
        b = self.zero1 if bias is None else bias
        npart = out.shape[0]
        if npart != 128 and b.shape[0] == 128:
            b = b[0:npart, :]
        self.P.op("act", lambda e: e.activation(out=out, in_=in_, func=func, bias=b, scale=scale), R, W)

    def tt(self, eng, out, a, b, op, R, W):
        self.P.op(eng, lambda e: e.tensor_tensor(out=out, in0=a, in1=b, op=op), R, W)

    def ts(self, eng, out, a, s1, s2, op0, op1, R, W):
        if op1 is None:
            self.P.op(eng, lambda e: e.tensor_scalar(out=out, in0=a, scalar1=s1, scalar2=None, op0=op0), R, W)
        else:
            self.P.op(eng, lambda e: e.tensor_scalar(out=out, in0=a, scalar1=s1, scalar2=s2, op0=op0, op1=op1), R, W)

    def stt(self, eng, out, a, s, b, op0, op1, R, W):
        self.P.op(eng, lambda e: e.scalar_tensor_tensor(out=out, in0=a, scalar=s, in1=b, op0=op0, op1=op1), R, W)

    def cp(self, eng, out, in_, R, W):
        if eng == "act":
            self.act(out, in_, AF.Identity, R, W)
        else:
            self.P.op(eng, lambda e: e.tensor_copy(out=out, in_=in_), R, W)

    def rcp(self, out, in_, R, W):
        self.P.op("dve", lambda e: e.reciprocal(out=out, in_=in_), R, W)

    def rsqrt_act(self, out, in_, scale, R, W, power=-0.5, bias=None):
        self.act(out, in_, AF.Ln, R, W, bias=self.eps1 if bias is None else bias, scale=scale)
        old = self.P.strict
        if out.shape[-1] <= 256:
            self.P.strict = True
        self.act(out, out, AF.Exp, list(W) + [self.cB], W, scale=power)
        self.P.strict = old

    def memset(self, eng, ap, val, W):
        self.P.op(eng, lambda e: e.memset(ap, val), (), W)

    def declare(self):
        nc = self.nc

        def din(name, shape, dt=F32):
            return nc.dram_tensor(name, list(shape), dt, kind="ExternalInput")

        def dscr(name, shape, dt):
            return nc.dram_tensor(name, list(shape), dt, kind="ExternalOutput" if DEBUG_SCRATCH else "Internal")

        self.d_xT = din("xT", [128, KC, S])
        self.d_xo = din("xo", [128, KC, NSLOT * SLOTW])
        self.d_win = din("w_in", [128, KC, 4352])
        self.d_wbra = din("w_br_a", [128, 4, D])
        self.d_wbrb = din("w_br_b", [128, 4, D])
        self.d_wo = din("w_o", [128, KC, D])
        self.d_wup = din("w_up", [128, KC, 2 * DFF])
        self.d_wdn = din("w_down", [128, NFC, D])
        self.d_gmix = din("g_mix", [128, KC])
        self.d_gffn = din("g_ffn", [128, KC])
        self.d_cw = din("conv_w", [128, 3, 44])
        self.d_cb = din("conv_b", [128, 44])
        self.d_relb = din("rel_bias", [32, 12])
        self.d_qna = din("qn_a", [1, 64])
        self.d_kna = din("kn_a", [1, 64])
        self.d_qnb = din("qn_b", [1, 64])
        self.d_knb = din("kn_b", [1, 64])
        self.d_sinks = din("sinks", [1, 8])
        self.d_lq1 = din("lam_q1", [1, 64])
        self.d_lk1 = din("lam_k1", [1, 64])
        self.d_lq2 = din("lam_q2", [1, 64])
        self.d_lk2 = din("lam_k2", [1, 64])
        self.d_subln = din("subln_b", [128, 1])
        self.d_oha = din("oh_a", [33, 384])
        self.d_ohd = din("oh_d", [33, VECD])
        self.d_m0 = din("m0", [128, 1])
        self.d_J = din("Jmat", [128, 128])
        self.d_bd = din("bdones", [128, 128])
        self.d_out = nc.dram_tensor("outT", [128, KC, NSLOT * 512], F32, kind="ExternalOutput")
        self.d_KT = dscr("s_KT", [4, 128, S], BF16)
        self.d_VS = dscr("s_VS", [4, 128, 128, 128], BF16)
        self.d_QB = dscr("s_QB", [NSLOT, 128, 4, HW], BF16)
        self.d_YA = dscr("s_YA", [NSLOT, 128, 4, HW], BF16)
        self.d_YB = dscr("s_YB", [NSLOT, 128, 4, HW], BF16)
        self.d_X2 = dscr("s_X2", [NSLOT, 128, KC, HW], F32)
        if DEBUG_SCRATCH:
            self.d_dstrip = dscr("s_strip", [128, 4 * STRIPW], BF16)
            self.d_dsa = dscr("s_sa", [128, 2 * 8 * 128], BF16)
            self.d_dsmall = dscr("s_small", [128, 8], F32)
        self.d_bc = nc.dram_tensor("s_bc", [2, 3, 512], F32, kind="Internal")
        self.d_VA = dscr("s_VECA", [8, 384], F32)
        self.d_VD = dscr("s_VECD", [4, VECD], F32)

    def setup(self):
        P, A, nc = self.P, self.A, self.nc
        ps, psB = self.ps, self.psB
        cB = P.buf("consts")
        self.cB = cB
        self.zero1 = A.alloc(1, F32)
        self.eps1 = A.alloc(1, F32)
        self.tiny1 = A.alloc(1, F32)
        self.ones_bf = A.alloc(128, BF16)
        self.ones_f = A.alloc(128, F32)
        self.bd_bf = A.alloc(128, BF16)
        self.J = A.alloc(128, F32)
        bd_f = A.alloc(128, F32)
        self.memset("pool", self.zero1, 0.0, [cB])
        self.memset("pool", self.eps1, EPS, [cB])
        self.memset("pool", self.tiny1, 1e-30, [cB])
        self.memset("pool", self.ones_bf, 1.0, [cB])
        self.memset("pool", self.ones_f, 1.0, [cB])
        jB = P.buf("J")
        P.dma(self.J, self.d_J.ap(), writes=[jB], key="c0")
        P.dma(bd_f, self.d_bd.ap(), writes=[jB], key="c0")
        self.cp("dve", self.bd_bf, bd_f, [jB], [cB])
        sB = P.buf("small")
        self.gmix = A.alloc(KC, F32)
        self.gffn = A.alloc(KC, F32)
        self.cw = A.alloc(3 * 44, F32)
        self.cb = A.alloc(44, F32)
        self.m0 = A.alloc(1, F32)
        P.dma(self.gmix, self.d_gmix.ap(), writes=[sB], key="c1")
        P.dma(self.gffn, self.d_gffn.ap(), writes=[sB], key="c1")
        P.dma(self.cw, self.d_cw.ap().rearrange("p a b -> p (a b)"), writes=[sB], key="c1")
        P.dma(self.cb, self.d_cb.ap(), writes=[sB], key="c1")
        P.dma(self.m0, self.d_m0.ap(), writes=[sB], key="c1")
        self.gq_a = A.alloc(1, F32)
        self.gk_a = A.alloc(1, F32)
        self.gq_b = A.alloc(1, F32)
        self.gk_b = A.alloc(1, F32)
        for dst, src in ((self.gq_a, self.d_qna), (self.gk_a, self.d_kna), (self.gq_b, self.d_qnb), (self.gk_b, self.d_knb)):
            for hlf in range(2):
                P.dma(dst[64 * hlf:64 * hlf + 64, :], bass.AP(src, 0, [[1, 64], [1, 1]]), writes=[sB], key="c1")
        self.gsub = A.alloc(1, F32)
        P.dma(self.gsub, self.d_subln.ap(), writes=[sB], key="c1")
        self.ts("dve", self.gsub, self.gsub, 1.0 - LAM_INIT, None, ALU.mult, None, [sB], [sB])
        lam4 = A.alloc(4 * 64, F32)
        for n_, src in enumerate((self.d_lq1, self.d_lk1, self.d_lq2, self.d_lk2)):
            P.dma(lam4[:, n_ * 64:(n_ + 1) * 64], bass.AP(src, 0, [[0, 128], [1, 64]]), writes=[sB], key="c1")
        lp = A.alloc(2 * 64, F32)
        ls = A.alloc(2, F32)
        self.neglam = A.alloc(1, F32)
        self.tt("dve", lp[:, 0:64], lam4[:, 0:64], lam4[:, 64:128], ALU.mult, [sB], [sB])
        self.tt("dve", lp[:, 64:128], lam4[:, 128:192], lam4[:, 192:256], ALU.mult, [sB], [sB])
        P.op("dve", lambda e: e.reduce_sum(out=ls[:, 0:1], in_=lp[:, 0:64], axis=mybir.AxisListType.X), [sB], [sB])
        P.op("dve", lambda e: e.reduce_sum(out=ls[:, 1:2], in_=lp[:, 64:128], axis=mybir.AxisListType.X), [sB], [sB])
        self.act(ls, ls, AF.Exp, [sB, cB], [sB])
        self.tt("dve", self.neglam, ls[:, 1:2], ls[:, 0:1], ALU.subtract, [sB], [sB])
        self.ts("dve", self.neglam, self.neglam, -LAM_INIT, None, ALU.add, None, [sB], [sB])
        self.sB = sB
        self.esrow = A.alloc(2 * 512, F32)
        sk = A.alloc(8, F32)
        P.dma(sk[0:1, :], self.d_sinks.ap(), writes=[sB], key="c1")
        self.act(sk[0:1, :], sk[0:1, :], AF.Exp, [sB, cB], [sB])
        for hq in range(8):
            g_, r_ = hq // 4, hq % 4
            col = g_ * 512 + ((r_ % 2) * 2 + r_ // 2) * 128
            self.ts("dve", self.esrow[0:1, col:col + 128], self.ones_f[0:1, 0:128], sk[0:1, hq:hq + 1], None, ALU.mult, None, [sB, cB], [sB])

        tB = P.buf("tab")
        tabp = A.alloc(12, F32)
        tab31 = A.alloc(4, F32)
        self.memset("pool", tabp[32:33, :], -30000.0, [tB])
        P.dma(tabp[0:32, :], self.d_relb.ap(), writes=[tB], key="c2")
        P.dma(tab31[0:32, :], bass.AP(self.d_relb, 31 * 12 + 8, [[0, 32], [1, 4]]), writes=[tB], key="c2")
        self.tt("dve", tabp[0:32, 8:12], tabp[0:32, 8:12], tab31[0:32, :], ALU.subtract, [tB], [tB])
        m = A.mark()
        oha = A.alloc(384, F32)
        ohd = A.alloc(VECD, F32)
        veca = A.alloc(384, F32)
        vecd = A.alloc(VECD, F32)
        ohB = P.buf("oh")
        P.dma(oha[0:33, :], self.d_oha.ap(), writes=[ohB], key="c2")
        P.dma(ohd[0:33, :], self.d_ohd.ap(), writes=[ohB], key="c2")
        vB = P.buf("vec")
        self.mm(ps[0:8, 0, 0:384], tabp[0:33, 0:8], oha[0:33, :], True, True, [tB, ohB], [psB[0]])
        self.act(veca[0:8, :], ps[0:8, 0, 0:384], AF.Exp, [psB[0], cB], [vB])
        for pc in range(6):
            w = 512 if pc < 5 else VECD - 2560
            bk = 1 + pc % 2
            self.mm(ps[0:4, bk, 0:w], tabp[0:33, 8:12], ohd[0:33, pc * 512: pc * 512 + w], True, True, [tB, ohB], [psB[bk]])
            self.act(vecd[0:4, pc * 512: pc * 512 + w], ps[0:4, bk, 0:w], AF.Exp, [psB[bk], cB], [vB])
        dvB = P.buf("dvec")
        P.dma(self.d_VA.ap(), veca[0:8, :], reads=[vB], writes=[dvB], key="c3")
        P.dma(self.d_VD.ap(), vecd[0:4, :], reads=[vB], writes=[dvB], key="c3")
        A.release(m)
        self.pre_strip_mark = A.mark()
        self.strip = A.alloc(4 * STRIPW, BF16).rearrange("p (h u) -> p h u", h=4)
        self.sa = A.alloc(2 * 8 * 128, BF16).rearrange("p (t h q) -> p t h q", t=2, h=8)
        self.stripB = P.buf("strip")
        self.persist_mark = A.mark()
        m = A.mark()
        rev = A.alloc(STRIPW, F32)
        revB = P.bufs(2, "rev")
        P.dma(rev[:, 0:2048].rearrange("p (h u) -> p h u", h=8), bass.AP(self.d_VA, 0, [[1, 128], [384, 8], [1, 256]]),
              reads=[dvB], writes=[revB[0]], key="c4")
        for pi in range(4):
            bk = pi % 2
            self.mm(ps[:, bk, :], self.J, rev[:, pi * 512:(pi + 1) * 512], True, True, [jB, revB[0]], [psB[bk]])
            src = ps[:, bk, :].rearrange("p (h t q) -> p h t q", h=2, t=2)
            for ty in range(2):
                self.cp("dve" if ty == 0 else "act", self.sa[:, ty, 2 * pi:2 * pi + 2, :], src[:, :, ty, :], [psB[bk]], [self.stripB])
        for h in range(4):
            rb = revB[(h + 1) % 2]
            P.dma(rev, bass.AP(self.d_VD, h * VECD, [[1, 128], [1, STRIPW]]), reads=[dvB], writes=[revB[0], revB[1]], key="c4")
            for pc in range(5):
                bk = pc % 2
                self.mm(ps[:, bk, :], self.J, rev[:, pc * 512:(pc + 1) * 512], True, True, [jB, revB[0], revB[1]], [psB[bk]])
                self.cp("dve" if pc % 2 == 0 else "act", self.strip[:, h, pc * 512:(pc + 1) * 512], ps[:, bk, :], [psB[bk]], [self.stripB])
        A.release(m)
        if DEBUG_SCRATCH:
            P.dma(self.d_dstrip.ap(), self.strip.rearrange("p h u -> p (h u)"), reads=[self.stripB])
            P.dma(self.d_dsa.ap(), self.sa.rearrange("p t h q -> p (t h q)"), reads=[self.stripB])
            for n_, t_ in enumerate((self.neglam, self.gsub, self.gq_b, self.gk_b)):
                P.dma(self.d_dsmall.ap()[:, n_:n_ + 1], t_, reads=[self.sB], allow_slow_non_contiguous=True)

    def load_w(self, dst, src, kc, ncols, gain=None, dup=None, key="w"):
        P, A = self.P, self.A
        m = A.mark()
        pw = 512 if kc <= 8 else 192
        stg = [A.alloc(kc * pw, F32).rearrange("p (c n) -> p c n", c=kc) for _ in range(2)]
        sb = P.bufs(2, "stg")
        wB = P.buf("wdst")
        engs = ("dve", "act")
        n = 0
        for i, c0 in enumerate(range(0, ncols, pw)):
            w = min(pw, ncols - c0)
            s, b = stg[i % 2], sb[i % 2]
            P.dma(s[:, :, 0:w], src[:, :, c0:c0 + w], writes=[b], key=f"{key}{i % 2}")
            for c in range(kc):
                eng = engs[n % 2]
                n += 1
                if gain is None:
                    self.cp(eng, dst[:, c, c0:c0 + w], s[:, c, 0:w], [b], [wB])
                elif eng == "act":
                    self.act(dst[:, c, c0:c0 + w], s[:, c, 0:w], AF.Identity, [b, self.sB, self.cB], [wB], scale=gain[:, c:c + 1])
                else:
                    self.ts(eng, dst[:, c, c0:c0 + w], s[:, c, 0:w], gain[:, c:c + 1], None, ALU.mult, None, [b, self.sB], [wB])
        self.P.barrier()
        A.release(m)
        return wB

    def norm_tile(self, xt, xB, n, sq, sqB, hT, hB, rstd, rB, pieces):
        ps, psB = self.ps, self.psB
        for c in range(KC):
            self.act(sq[c % 2][:, 0:n], xt[:, c, :], AF.Square, [xB, self.cB], [sqB[c % 2]])
            for (o, w, bank) in pieces:
                self.mm(ps[:, bank, 0:w], self.ones_bf, sq[c % 2][:, o:o + w], c == 0, c == KC - 1, [sqB[c % 2], self.cB], [psB[bank]])
        for (o, w, bank) in pieces:
            small = w < 128
            self.P.strict = small
            self.rsqrt_act(rstd[:, o:o + w], ps[:, bank, 0:w], 1.0 / D, [psB[bank], self.cB], [rB])
            self.P.strict = False
        for c in range(KC):
            self.tt("dve" if c % 2 == 0 else "pool", hT[:, c, 0:n], xt[:, c, :], rstd[:, 0:n], ALU.mult, [xB, rB], [hB[c]])

    def ph_tasks(self, tasks, hT, hB, wB, tmp, tmpB, pbanks, nbanks, mid=None):
        ps, psB = self.ps, self.psB
        n = len(tasks)

        def proj(k):
            wfn, o, w, out, outB, gain = tasks[k]
            bk = pbanks[k % len(pbanks)]
            ksq, _ = tmp[k % len(tmp)]
            for c in range(KC):
                self.mm(ps[:, bk, 0:w], wfn(c), hT[:, c, o:o + w], c == 0, c == KC - 1, [wB, hB[c]], [psB[bk]])
            self.act(ksq[:, 0:w], ps[:, bk, 0:w], AF.Square, [psB[bk], self.cB], [tmpB[k % len(tmp)][0]])

        def norm(k):
            wfn, o, w, out, outB, gain = tasks[k]
            bk = pbanks[k % len(pbanks)]
            bn = nbanks[k % len(nbanks)]
            ksq, rk = tmp[k % len(tmp)]
            tb = tmpB[k % len(tmp)]
            self.mm(ps[:, bn, 0:w], self.bd_bf, ksq[:, 0:w], True, True, [tb[0], self.cB], [psB[bn]])
            self.rsqrt_act(rk[:, 0:w], ps[:, bn, 0:w], 1.0 / 64, [psB[bn], self.cB], [tb[1]])
            self.stt("dve", out, ps[:, bk, 0:w], gain, rk[:, 0:w], ALU.mult, ALU.mult, [psB[bk], tb[1], self.sB], [outB])

        for k in range(n + 1):
            if k < n:
                proj(k)
            if k == n and mid is not None:
                mid()
            if k >= 1:
                norm(k - 1)

    def phase1(self):
        P, A = self.P, self.A
        ps, psB = self.ps, self.psB
        m = A.mark()
        wkb = A.alloc(KC * 512, BF16).rearrange("p (c n) -> p c n", c=KC)
        wvb = A.alloc(KC * 512, BF16).rearrange("p (c n) -> p c n", c=KC)
        win = self.d_win.ap()
        wB1 = self.load_w(wkb, win[:, :, C_KB:C_KB + 512], KC, 512, gain=self.gmix)
        wB2 = self.load_w(wvb, win[:, :, C_VB:C_VB + 512], KC, 512, gain=self.gmix)
        xt = [A.alloc(KC * 512, F32).rearrange("p (c n) -> p c n", c=KC) for _ in range(2)]
        xB = P.bufs(2, "x")
        sq = [A.alloc(512, BF16) for _ in range(2)]
        sqB = P.bufs(2, "sq")
        hT = [A.alloc(KC * 512, BF16).rearrange("p (c n) -> p c n", c=KC) for _ in range(2)]
        hB = [P.bufs(KC, "h") for _ in range(2)]
        rstd = [A.alloc(512, F32) for _ in range(2)]
        rB = P.bufs(2, "r")
        tmp = [(A.alloc(512, BF16), A.alloc(512, F32)) for _ in range(3)]
        tmpB = [P.bufs(2, "tmp") for _ in range(3)]
        kout = [A.alloc(4 * 512, BF16).rearrange("p (h n) -> p h n", h=4) for _ in range(2)]
        koB = [P.bufs(4, "ko") for _ in range(2)]
        vout = [A.alloc(4 * 512, BF16).rearrange("p (b n) -> p b n", b=4) for _ in range(2)]
        voB = [P.bufs(4, "vo") for _ in range(2)]
        xT = self.d_xT.ap()
        NT = S // 512

        def stats(T):
            pp = T % 2
            P.dma(xt[pp], xT[:, :, T * 512:(T + 1) * 512], writes=[xB[pp]], key=f"x{pp}")
            self.norm_tile(xt[pp], xB[pp], 512, sq, sqB, hT[pp], hB[pp], rstd[pp], rB[pp], [(0, 512, 0)])

        stats(0)
        for T in range(NT):
            pp = T % 2
            if T + 1 < NT:
                stats(T + 1)
            tasks = [((lambda c, hh=hh: wkb[:, c, hh * 128:(hh + 1) * 128]), 0, 512, kout[pp][:, hh, :], koB[pp][hh], self.gk_b) for hh in range(4)]
            self.ph_tasks(tasks, hT[pp], hB[pp], wB1, tmp, tmpB, (1, 2, 3), (6, 7))
            for hh in range(4):
                P.dma(self.d_KT.ap()[hh, :, T * 512:(T + 1) * 512], kout[pp][:, hh, :], reads=[koB[pp][hh]], key=f"ko{pp}", eng=STORE_ENG)
            for blk in range(4):
                bk = 4 + blk % 2
                for c in range(KC):
                    self.mm(ps[:, bk, :], hT[pp][:, c, blk * 128:(blk + 1) * 128], wvb[:, c, :], c == 0, c == KC - 1,
                            [hB[pp][c], wB2], [psB[bk]])
                self.cp("act" if blk % 2 == 0 else "dve", vout[pp][:, blk, :], ps[:, bk, :], [psB[bk]], [voB[pp][blk]])
            for hh in range(4):
                P.dma(self.d_VS.ap()[hh, :, 4 * T:4 * T + 4, :], vout[pp][:, :, hh * 128:(hh + 1) * 128],
                      reads=voB[pp], key=f"vo{pp}", eng=STORE_ENG)
        P.barrier()
        A.release(m)

    def phase2(self):
        P, A = self.P, self.A
        ps, psB = self.ps, self.psB
        m = A.mark()
        win = self.d_win.ap()

        def walloc(n):
            return A.alloc(KC * n, BF16).rearrange("p (c n) -> p c n", c=KC)

        wqa, wka2, wva, wqb = walloc(512), walloc(256), walloc(128), walloc(512)
        wBq = self.load_w(wqa, win[:, :, C_QA:C_QA + 512], KC, 512, gain=self.gmix)
        for g in range(2):
            for hlf in range(2):
                self.load_w(wka2[:, :, g * 128 + hlf * 64: g * 128 + hlf * 64 + 64], win[:, :, C_KA + 64 * g:C_KA + 64 * g + 64], KC, 64,
                            gain=self.gmix)
        self.load_w(wva, win[:, :, C_VA:C_VA + 128], KC, 128, gain=self.gmix)
        self.load_w(wqb, win[:, :, C_QB:C_QB + 512], KC, 512, gain=self.gmix)
        wB = P.buf("w2")
        xts = [A.alloc(KC * SLOTW, F32).rearrange("p (c n) -> p c n", c=KC) for _ in range(2)]
        xBs = P.bufs(2, "x")
        sq = [A.alloc(SLOTW, BF16) for _ in range(2)]
        sqB = P.bufs(2, "sq")
        hTs = [A.alloc(KC * SLOTW, BF16).rearrange("p (c n) -> p c n", c=KC) for _ in range(2)]
        hBs = [P.bufs(KC, "h") for _ in range(2)]
        rstds = [A.alloc(SLOTW, F32) for _ in range(2)]
        rBs = P.bufs(2, "r")
        tmp = [(A.alloc(512, BF16), A.alloc(512, F32)) for _ in range(3)]
        tmpB = [P.bufs(2, "tmp") for _ in range(3)]
        qaT = A.alloc(4 * HW, BF16).rearrange("p (c n) -> p c n", c=4)
        qaB = P.bufs(4, "qa")
        kaT = A.alloc(2 * SLOTW, BF16).rearrange("p (g n) -> p g n", g=2)
        kaB = P.bufs(2, "ka")
        va = A.alloc(6 * 128, BF16).rearrange("p (b n) -> p b n", b=6)
        vaB = P.buf("va")
        qbT = A.alloc(4 * HW, BF16).rearrange("p (c n) -> p c n", c=4)
        qbB = P.buf("qb")
        yaT = A.alloc(4 * HW, BF16).rearrange("p (c n) -> p c n", c=4)
        yaB = P.buf("ya")
        pt = [A.alloc(512, BF16) for _ in range(4)]
        ptB = P.bufs(4, "pt")
        den = A.alloc(512, F32)
        denB = P.buf("den")
        xo = self.d_xo.ap()
        nsl = min(NSLOT, P2_SLOTS)

        def stats(i):
            pp = i % 2
            P.dma(xts[pp], xo[:, :, i * SLOTW:(i + 1) * SLOTW], writes=[xBs[pp]], key="x2")
            self.norm_tile(xts[pp], xBs[pp], SLOTW, sq, sqB, hTs[pp], hBs[pp], rstds[pp], rBs[pp], [(0, 512, 0), (512, 256, 7)])

        stats(0)
        for i in range(nsl):
            hT, hB = hTs[i % 2], hBs[i % 2]
            tasks = []
            for (o, w) in ((128, 512), (640, 128)):
                for cm in range(4):
                    tasks.append(((lambda c, cm=cm: wqa[:, c, cm * 128:(cm + 1) * 128]), o, w, qaT[:, cm, o - 128:o - 128 + w], qaB[cm], self.gq_a))
                for cm in range(4):
                    tasks.append(((lambda c, cm=cm: wqb[:, c, cm * 128:(cm + 1) * 128]), o, w, qbT[:, cm, o - 128:o - 128 + w], qbB, self.gq_b))
            for (o, w) in ((0, 512), (512, 256)):
                for g in range(2):
                    tasks.append(((lambda c, g=g: wka2[:, c, g * 128:(g + 1) * 128]), o, w, kaT[:, g, o:o + w], kaB[g], self.gk_a))
            self.ph_tasks(tasks, hT, hB, wB, tmp, tmpB, (1, 2, 3), (5, 6))
            P.dma(self.d_QB.ap()[i], qbT, reads=[qbB], key="qbo", eng=STORE_ENG)
            for half in range(2):
                bk = 4 + half
                for bl in range(3):
                    blk = half * 3 + bl
                    for c in range(KC):
                        self.mm(ps[:, bk, bl * 128:(bl + 1) * 128], hT[:, c, blk * 128:(blk + 1) * 128], wva[:, c, :], c == 0, c == KC - 1,
                                [hB[c], wB], [psB[bk]])
                self.cp("act", va[:, half * 3:half * 3 + 3, :], ps[:, bk, 0:384].rearrange("p (b n) -> p b n", b=3), [psB[bk]], [vaB])
            if i + 1 < nsl:
                stats(i + 1)
            for n in range(1, 6 if P2_STAGE >= 1 else 0):
                qo = (n - 1) * 128
                for g in range(2):
                    for kk, kblk in enumerate((n - 1, n)):
                        idx = g * 2 + kk
                        b0 = (2, 6)[idx % 2]
                        for r in range(4):
                            par, rr = r % 2, r // 2
                            pb = 64 * par
                            self.mm(ps[:, b0 + par, rr * 128:(rr + 1) * 128], kaT[pb:pb + 64, g, kblk * 128:(kblk + 1) * 128],
                                    qaT[pb:pb + 64, 2 * g + rr, qo:qo + 128], True, True,
                                    [kaB[g], qaB[2 * g + rr]], [psB[b0 + par]])
                        self.act(pt[idx].rearrange("p (a n) -> p a n", a=2), ps[:, b0:b0 + 2, 0:256], AF.Exp,
                                 [psB[b0], psB[b0 + 1], self.cB], [ptB[idx]], scale=0.125)
                        ty = 1 if kk == 0 else 0
                        fa = self.sa[:, ty, 4 * g:4 * g + 4, :].rearrange("p (rr par) q -> p par rr q", par=2)
                        p4 = pt[idx].rearrange("p (par rr q) -> p par rr q", par=2, rr=2)
                        self.tt("pool", p4, p4, fa, ALU.mult, [ptB[idx], self.stripB], [ptB[idx]])
                        if i == 0 and n == 2 and kk == 0:
                            self.ts("pool", pt[idx], pt[idx], self.m0[:, 0:1], None, ALU.mult, None, [ptB[idx], self.sB], [ptB[idx]])
                if P2_STAGE < 2:
                    continue
                for g in range(2):
                    for kk, kblk in enumerate((n - 1, n)):
                        idx = g * 2 + kk
                        self.mm(ps[64 * g:64 * g + 64, 4, :], va[:, kblk, 64 * g:64 * g + 64], pt[idx], kk == 0, kk == 1,
                                [vaB, ptB[idx]], [psB[4]])
                    for kk in range(2):
                        idx = g * 2 + kk
                        self.mm(ps[64 * g:64 * g + 64, 5, :], self.ones_bf[:, 0:64], pt[idx], kk == 0, (kk == 1 and P2_STAGE < 3),
                                [ptB[idx], self.cB], [psB[5]])
                    if P2_STAGE >= 3:
                        self.mm(ps[64 * g:64 * g + 64, 5, :], self.ones_f[0:1, 0:64], self.esrow[0:1, g * 512:(g + 1) * 512], False, True,
                                [self.sB, self.cB], [psB[5]])
                self.rsqrt_act(den, ps[:, 5, :], 1.0, [psB[5], self.cB], [denB], power=-1.0, bias=self.zero1)
                self.tt("dve", yaT[:, :, qo:qo + 128].rearrange("p (rr par) q -> p par rr q", par=2),
                        ps[:, 4, :].rearrange("p (par rr q) -> p par rr q", par=2, rr=2),
                        den.rearrange("p (par rr q) -> p par rr q", par=2, rr=2), ALU.mult, [psB[4], denB], [yaB])
            P.dma(self.d_YA.ap()[i], yaT, reads=[yaB], key="yao", eng=STORE_ENG)
        P.barrier()
        A.release(m)

    def phase3(self):
        P, A = self.P, self.A
        ps, psB = self.ps, self.psB
        m = A.mark()
        KTs = [A.alloc(S, BF16) for _ in range(2)]
        VSs = [A.alloc(128 * 128, BF16).rearrange("p (b e) -> p b e", b=128) for _ in range(2)]
        kvB = [(P.bufs(4, "kt"), P.bufs(4, "vs")) for _ in range(2)]
        qt = [A.alloc(HW, BF16) for _ in range(2)]
        qB = P.bufs(2, "q")
        NPB = 4
        pt = [A.alloc(1024, BF16) for _ in range(NPB)]
        ptB = P.bufs(NPB, "pt")
        ssum = [A.alloc(512, F32) for _ in range(2)]
        ssumB = P.bufs(2, "ssum")
        rbc = [A.alloc(1024, F32) for _ in range(2)]
        rbcB = P.bufs(2, "rbc")
        dd = [A.alloc(1024, F32) for _ in range(2)]
        ddB = P.bufs(2, "dd")
        dsq = [A.alloc(512, BF16) for _ in range(2)]
        dsqB = P.bufs(2, "dsq")
        rrow = [A.alloc(512, F32) for _ in range(2)]
        rrowB = P.bufs(2, "rrow")
        rrbc = [A.alloc(512, F32) for _ in range(2)]
        rrbcB = P.bufs(2, "rrbc")
        yb = [A.alloc(HW, BF16) for _ in range(2)]
        ybB = P.bufs(2, "yb")
        dbcB = [P.bufs(2, "dbc") for _ in range(2)]
        q7B = P.bufs(4, "ps7q")
        ones32 = self.ones_bf[:, 0:32]
        SB = [(0, 1), (2, 3)]
        dbc = self.d_bc

        def load_kv(h):
            pp = h % 2
            for q4 in range(4):
                P.dma(KTs[pp][:, q4 * 4096:(q4 + 1) * 4096], self.d_KT.ap()[h, :, q4 * 4096:(q4 + 1) * 4096], writes=[kvB[pp][0][q4]])
            for q4 in range(4):
                P.dma(VSs[pp][:, q4 * 32:(q4 + 1) * 32, :], self.d_VS.ap()[h, :, q4 * 32:(q4 + 1) * 32, :], writes=[kvB[pp][1][q4]])

        pending = []
        gcount = [0]
        sbase = [0]

        def flush():
            while pending:
                pending.pop(0)()

        def finalize(ncol, o_ap, obufs, ycols, ybt, ybb, out_dma, nred=1):
            gp = gcount[0] % 2
            gcount[0] += 1
            small = ncol < 128
            P.strict = small
            ss_, rb_, dd_, dq_, rw_, rrb_ = ssum[gp], rbc[gp], dd[gp], dsq[gp], rrow[gp], rrbc[gp]
            if nred == 1:
                self.cp("dve", ss_[0:64, 0:ncol], ps[0:64, 7, 0:ncol], [q7B[0], q7B[1]], [ssumB[gp]])
            else:
                wtot = nred * ncol
                self.cp("dve", ss_[0:64, 0:wtot], ps[0:64, 7, 0:wtot], [q7B[0], q7B[1]], [ssumB[gp]])
                P.strict = True
                if nred == 12:
                    steps = [(4 * ncol, 8 * ncol, 4 * ncol), (4 * ncol, 4 * ncol, 4 * ncol), (2 * ncol, 2 * ncol, 2 * ncol), (ncol, ncol, ncol)]
                else:
                    steps = [(8 * ncol, 8 * ncol, 8 * ncol), (4 * ncol, 4 * ncol, 4 * ncol), (2 * ncol, 2 * ncol, 2 * ncol), (ncol, ncol, ncol)]
                for (wd_, src_, _) in steps:
                    self.tt("dve", ss_[0:64, 0:wd_], ss_[0:64, 0:wd_], ss_[0:64, src_:src_ + wd_], ALU.add, [ssumB[gp]], [ssumB[gp]])
                P.strict = small
            for comp in range(2):
                P.dma(dbc.ap()[gp, comp:comp + 1, 0:ncol], ss_[32 * comp:32 * comp + 1, 0:ncol], reads=[ssumB[gp]], writes=[dbcB[gp][0]])
            P.dma(rb_.rearrange("p (c n) -> p c n", c=2)[:, :, 0:ncol], bass.AP(dbc, gp * 3 * 512, [[0, 128], [512, 2], [1, ncol]]),
                  reads=[dbcB[gp][0]], writes=[rbcB[gp]])
            for comp in range(2):
                r_ = rb_[:, comp * 512:comp * 512 + ncol]
                self.ts("dve", r_, r_, self.tiny1[:, 0:1], None, ALU.add, None, [rbcB[gp], self.cB], [rbcB[gp]])
                self.rcp(r_, r_, [rbcB[gp]], [rbcB[gp]])
                self.tt("dve", dd_[:, comp * 512: comp * 512 + ncol], o_ap(comp), r_, ALU.mult, [obufs[comp], rbcB[gp]], [ddB[gp]])
            self.stt("dve", dd_[:, 0:ncol], dd_[:, 512:512 + ncol], self.neglam[:, 0:1], dd_[:, 0:ncol], ALU.mult, ALU.add, [ddB[gp], self.sB], [ddB[gp]])
            self.tt("pool", dq_[:, 0:ncol], dd_[:, 0:ncol], dd_[:, 0:ncol], ALU.mult, [ddB[gp]], [dsqB[gp]])
            P.strict = False

            def stage1():
                P.strict = small
                self.mm(ps[64:96, 7, 0:ncol], ones32, dq_[:, 0:ncol], True, True, [dsqB[gp], self.cB], [q7B[2]])
                self.act(rw_[64:96, 0:ncol], ps[64:96, 7, 0:ncol], AF.Ln, [q7B[2], self.cB], [rrowB[gp]], bias=self.eps1[64:96, :], scale=1.0 / 128)
                P.strict = True
                self.act(rw_[64:96, 0:ncol], rw_[64:96, 0:ncol], AF.Exp, [rrowB[gp], self.cB], [rrowB[gp]], bias=self.zero1[64:96, :], scale=-0.5)
                P.strict = small
                P.dma(dbc.ap()[gp, 2:3, 0:ncol], rw_[64:65, 0:ncol], reads=[rrowB[gp]], writes=[dbcB[gp][1]])
                P.dma(rrb_[:, 0:ncol], bass.AP(dbc, (gp * 3 + 2) * 512, [[0, 128], [1, ncol]]), reads=[dbcB[gp][1]], writes=[rrbcB[gp]])
                self.stt("dve", ybt[:, ycols:ycols + ncol], dd_[:, 0:ncol], self.gsub[:, 0:1], rrb_[:, 0:ncol], ALU.mult, ALU.mult,
                         [ddB[gp], rrbcB[gp], self.sB], [ybb])
                P.strict = False
                if out_dma is not None:
                    out_dma()
            pending.append(stage1)

        def group(nsteps, qk, av, near_off, ncol, ncols_sum, sum_rhs, pre_done=False, next_qk0=None):
            def pe_tail(t):
                p_, pB_ = pt[t % NPB], ptB[t % NPB]
                for comp in range(2):
                    rl = sum_rhs(p_, comp)
                    for n_, r_ in enumerate(rl):
                        self.mm(ps[32 * comp:32 * comp + 32, 7, 0:ncol], ones32, r_, t == 0 and n_ == 0, t == nsteps - 1 and n_ == len(rl) - 1,
                                [pB_, self.cB], [q7B[comp]])
                av(t, p_, pB_)

            b0 = sbase[0]
            sbase[0] += nsteps
            if not pre_done:
                qk(0, b0 % 2)
            for t in range(nsteps):
                if t + 1 < nsteps:
                    qk(t + 1, (b0 + t + 1) % 2)
                elif next_qk0 is not None:
                    next_qk0((b0 + nsteps) % 2)
                sb = SB[(b0 + t) % 2]
                p_, pB_ = pt[t % NPB], ptB[t % NPB]
                self.act(p_.rearrange("p (c n) -> p c n", c=2), ps[:, sb[0]:sb[0] + 2, :], AF.Exp, [psB[sb[0]], psB[sb[1]], self.cB], [pB_], scale=0.125)
                off = near_off(t)
                if off is not None:
                    for comp in range(2):
                        self.tt("dve", p_[:, comp * 512:(comp + 1) * 512], p_[:, comp * 512:(comp + 1) * 512],
                                self.strip_h[:, off:off + 512], ALU.mult, [pB_, self.stripB], [pB_])
                if t >= 1:
                    pe_tail(t - 1)
                if t == min(10, nsteps - 1):
                    flush()
            pe_tail(nsteps - 1)

        def group_h(nsteps, qk, av, near, batches, HQ, pre_done=False, next_qk0=None):
            def pe_tail(t):
                kb0, nseg = batches[t]
                wd = nseg * HQ
                p_, pB_ = pt[t % NPB], ptB[t % NPB]
                for comp in range(2):
                    self.mm(ps[32 * comp:32 * comp + 32, 7, 0:wd], ones32, p_[:, comp * 512:comp * 512 + wd], t == 0, t == nsteps - 1,
                            [pB_, self.cB], [q7B[comp]])
                av(t, p_, pB_)

            b0 = sbase[0]
            sbase[0] += nsteps
            if not pre_done:
                qk(0, b0 % 2)
            for t in range(nsteps):
                if t + 1 < nsteps:
                    qk(t + 1, (b0 + t + 1) % 2)
                elif next_qk0 is not None:
                    next_qk0((b0 + nsteps) % 2)
                sb = SB[(b0 + t) % 2]
                kb0, nseg = batches[t]
                wd = nseg * HQ
                p_, pB_ = pt[t % NPB], ptB[t % NPB]
                self.act(p_.rearrange("p (c n) -> p c n", c=2)[:, :, 0:wd], ps[:, sb[0]:sb[0] + 2, 0:wd], AF.Exp,
                         [psB[sb[0]], psB[sb[1]], self.cB], [pB_], scale=0.125)
                nr = near(t)
                if nr is not None:
                    c0, ns_, off = nr
                    fa = self.strip_h[:, off:off + ns_ * 128].rearrange("p (u c) -> p u c", c=128)[:, :, 0:HQ]
                    for comp in range(2):
                        pv = p_[:, comp * 512 + c0 * HQ: comp * 512 + (c0 + ns_) * HQ].rearrange("p (u c) -> p u c", c=HQ)
                        self.tt("dve", pv, pv, fa, ALU.mult, [pB_, self.stripB], [pB_])
                if t >= 1:
                    pe_tail(t - 1)
            pe_tail(nsteps - 1)

        def make_slot(h, i, qq):
            pp = h % 2
            KT, VS = KTs[pp], VSs[pp]
            kB, vB = kvB[pp]
            q, qb_ = qt[qq], qB[qq]
            ybt, ybb = yb[qq], ybB[qq]
            strip_h = self.strip[:, h, :]
            nkb = 16 * i + 16
            near0 = 16 * i - 1
            HQ = 32
            batches = [(16 * b, 16) for b in range(i)] + [(16 * i, 12)]
            nb = len(batches)

            def qk(t, par):
                sb = SB[par]
                for comp in range(2):
                    self.mm(ps[:, sb[comp], :], KT[64 * comp:64 * comp + 64, t * 128:(t + 1) * 128], q[64 * comp:64 * comp + 64, 128:640],
                            True, True, [kB[t // 32], qb_], [psB[sb[comp]]])

            def av(t, p_, pB_):
                for comp in range(2):
                    self.mm(ps[:, 4 + comp, :], VS[:, t, :], p_[:, comp * 512:(comp + 1) * 512], t == 0, t == nkb - 1,
                            [vB[t // 32], pB_], [psB[4 + comp]])

            def qkh(t, par):
                sb = SB[par]
                kb0, nseg = batches[t]
                for comp in range(2):
                    for u in range(nseg):
                        kb = kb0 + nseg - 1 - u
                        self.mm(ps[:, sb[comp], u * HQ:(u + 1) * HQ], KT[64 * comp:64 * comp + 64, kb * 128:(kb + 1) * 128],
                                q[64 * comp:64 * comp + 64, 128 - HQ:128], True, True, [kB[kb // 32], qb_], [psB[sb[comp]]])

            def avh(t, p_, pB_):
                kb0, nseg = batches[t]
                for comp in range(2):
                    for u in range(nseg):
                        kb = kb0 + nseg - 1 - u
                        self.mm(ps[:, 6, comp * HQ:(comp + 1) * HQ], VS[:, kb, :], p_[:, comp * 512 + u * HQ: comp * 512 + (u + 1) * HQ],
                                t == 0 and u == 0 and comp == 0, t == nb - 1 and u == nseg - 1, [vB[kb // 32], pB_], [psB[6]])

            def nearh(t):
                kb0, nseg = batches[t]
                if kb0 == 16 * i:
                    return (0, 12, 2048 - 128 * 13 + (128 - HQ))
                if kb0 == 16 * i - 16:
                    return (0, 4, 2048 - 128 + (128 - HQ))
                return None

            def odma():
                P.dma(self.d_YB.ap()[i, :, h, :], ybt, reads=[ybb])

            def run_main(pre_done, next_qk0):
                self.strip_h = strip_h
                group(nkb, qk, av, lambda t: (2048 - 128 * (t - near0)) if t >= near0 else None, 512, 512,
                      lambda p_, comp: [p_[:, comp * 512:(comp + 1) * 512]], pre_done, next_qk0)
                finalize(512, lambda comp: ps[:, 4 + comp, :], [psB[4], psB[5]], 128, ybt, ybb, None)

            def run_halo(pre_done, next_qk0):
                self.strip_h = strip_h
                group_h(nb, qkh, avh, nearh, batches, HQ, pre_done, next_qk0)
                finalize(HQ, lambda comp: ps[:, 6, comp * HQ:(comp + 1) * HQ], [psB[6], psB[6]], 128 - HQ, ybt, ybb, odma,
                         nred=(16 if i > 0 else 12))

            return dict(run_main=run_main, run_halo=run_halo, qk0=lambda par: qk(0, par), qkh0=lambda par: qkh(0, par))

        slots = [(h, i) for h in range(4) for i in range(NSLOT)]
        descs = {}

        def get(k):
            if k not in descs:
                descs[k] = make_slot(slots[k][0], slots[k][1], k % 2)
            return descs[k]

        load_kv(0)
        P.dma(qt[0], self.d_QB.ap()[0, :, 0, :], writes=[qB[0]])
        pre = False
        for k, (h, i) in enumerate(slots):
            if i == 0 and h + 1 < 4:
                load_kv(h + 1)
            if k + 1 < len(slots):
                nh, ni = slots[k + 1]
                P.dma(qt[(k + 1) % 2], self.d_QB.ap()[ni, :, nh, :], writes=[qB[(k + 1) % 2]])
            d = get(k)
            d["run_main"](pre, d["qkh0"])
            nxt = get(k + 1)["qk0"] if k + 1 < len(slots) else None
            d["run_halo"](True, nxt)
            pre = nxt is not None
            descs.pop(k, None)
        flush()
        P.barrier()
        A.release(m)

    def phase4a(self):
        P, A = self.P, self.A
        ps, psB = self.ps, self.psB
        m = A.mark()
        win = self.d_win.ap()
        wgl = A.alloc(KC * 2048, BF16).rearrange("p (c n) -> p c n", c=KC)
        wbra = A.alloc(4 * D, BF16).rearrange("p (c n) -> p c n", c=4)
        wbrb = A.alloc(4 * D, BF16).rearrange("p (c n) -> p c n", c=4)
        wo = A.alloc(KC * D, BF16).rearrange("p (c n) -> p c n", c=KC)
        self.load_w(wgl, win[:, :, C_GL:C_GL + 2048], KC, 2048, gain=self.gmix)
        self.load_w(wbra, self.d_wbra.ap(), 4, D)
        self.load_w(wbrb, self.d_wbrb.ap(), 4, D)
        self.load_w(wo, self.d_wo.ap(), KC, D)
        wB = P.buf("w4a")
        xt = A.alloc(KC * HW, F32).rearrange("p (c n) -> p c n", c=KC)
        xB = P.buf("x")
        sq = [A.alloc(HW, BF16) for _ in range(2)]
        sqB = P.bufs(2, "sq")
        hT = A.alloc(KC * HW, BF16).rearrange("p (c n) -> p c n", c=KC)
        hB = P.bufs(KC, "h")
        rstd = A.alloc(HW, F32)
        rB = P.buf("r")
        gates = A.alloc(16 * HW, BF16).rearrange("p (c n) -> p c n", c=16)
        gB = P.bufs(16, "g")
        ya = A.alloc(4 * HW, BF16).rearrange("p (c n) -> p c n", c=4)
        yb = A.alloc(4 * HW, BF16).rearrange("p (c n) -> p c n", c=4)
        yB = P.bufs(2, "y")
        mixed = A.alloc(KC * HW, BF16).rearrange("p (c n) -> p c n", c=KC)
        mxB = P.bufs(KC, "mx")
        t1 = [A.alloc(512, BF16) for _ in range(2)]
        t2 = [A.alloc(512, BF16) for _ in range(2)]
        tB = [P.bufs(2, "t") for _ in range(2)]
        x2 = [A.alloc(512, F32) for _ in range(2)]
        x2B = P.bufs(2, "x2")
        xo = self.d_xo.ap()
        PIECES = ((96, 512), (608, 32))
        cnt = 0
        for i in range(NSLOT):
            P.dma(xt, xo[:, :, i * SLOTW + 128:(i + 1) * SLOTW], writes=[xB], key="x4")
            P.dma(ya, self.d_YA.ap()[i], writes=[yB[0]], key="ya")
            P.dma(yb, self.d_YB.ap()[i], writes=[yB[1]], key="yb")
            self.norm_tile(xt, xB, HW, sq, sqB, hT, hB, rstd, rB, [(96, 512, 0), (608, 32, 7)])
            for (o, w) in PIECES:
                for gc in range(16):
                    bk = 1 + gc % 2
                    for c in range(KC):
                        self.mm(ps[:, bk, 0:w], wgl[:, c, gc * 128:(gc + 1) * 128], hT[:, c, o:o + w], c == 0, c == KC - 1, [wB, hB[c]], [psB[bk]])
                    self.act(gates[:, gc, o:o + w], ps[:, bk, 0:w], AF.Sigmoid, [psB[bk], self.cB], [gB[gc]])
            for (o, w) in PIECES:
                for mc in range(KC):
                    k2 = cnt % 2
                    cnt += 1
                    ba, bb = 3 + 2 * k2, 4 + 2 * k2
                    for r in range(4):
                        self.mm(ps[:, ba, 0:w], wbra[:, r, mc * 128:(mc + 1) * 128], ya[:, r, o:o + w], r == 0, r == 3, [wB, yB[0]], [psB[ba]])
                    for r in range(4):
                        self.mm(ps[:, bb, 0:w], wbrb[:, r, mc * 128:(mc + 1) * 128], yb[:, r, o:o + w], r == 0, r == 3, [wB, yB[1]], [psB[bb]])
                    self.tt("dve", t1[k2][:, 0:w], ps[:, ba, 0:w], gates[:, mc, o:o + w], ALU.mult, [psB[ba], gB[mc]], [tB[k2][0]])
                    self.tt("dve", t2[k2][:, 0:w], ps[:, bb, 0:w], gates[:, 8 + mc, o:o + w], ALU.mult, [psB[bb], gB[8 + mc]], [tB[k2][1]])
                    self.tt("pool", mixed[:, mc, o:o + w], t1[k2][:, 0:w], t2[k2][:, 0:w], ALU.add, [tB[k2][0], tB[k2][1]], [mxB[mc]])
            for (o, w) in PIECES:
                for oc in range(KC):
                    k2 = cnt % 2
                    cnt += 1
                    bk = 1 + k2
                    for mc in range(KC):
                        self.mm(ps[:, bk, 0:w], wo[:, mc, oc * 128:(oc + 1) * 128], mixed[:, mc, o:o + w], mc == 0, mc == KC - 1, [wB, mxB[mc]], [psB[bk]])
                    self.tt("dve", x2[k2][:, 0:w], ps[:, bk, 0:w], xt[:, oc, o:o + w], ALU.add, [psB[bk], xB], [x2B[k2]])
                    P.dma(self.d_X2.ap()[i, :, oc, o:o + w], x2[k2][:, 0:w], reads=[x2B[k2]], key=f"x2o{k2}", eng=STORE_ENG)
        P.barrier()
        A.release(m)

    def phase4b(self):
        P, A = self.P, self.A
        ps, psB = self.ps, self.psB
        A.release(self.pre_strip_mark)
        m = A.mark()
        wup = A.alloc(KC * 2 * DFF, BF16).rearrange("p (c n) -> p c n", c=KC)
        wdn = A.alloc(NFC * D, BF16).rearrange("p (c n) -> p c n", c=NFC)
        self.load_w(wup, self.d_wup.ap(), KC, 2 * DFF, gain=self.gffn)
        self.load_w(wdn, self.d_wdn.ap(), NFC, D)
        wB = P.buf("w4b")
        W2 = 514
        xa_off = A.alloc_raw(NFC * 512 * 2)
        xaB = P.buf("xa")
        sq = [A.alloc(W2, BF16) for _ in range(2)]
        sqB = P.bufs(2, "sq")
        hT = A.alloc(KC * W2, BF16).rearrange("p (c n) -> p c n", c=KC)
        hB = P.bufs(KC, "h")
        rstd = A.alloc(W2, F32)
        rB = P.buf("r")
        uh = A.alloc(44 * 2, F32).rearrange("p (c n) -> p c n", c=44)
        uhB = P.buf("uh")
        U = [A.alloc(W2, F32) for _ in range(4)]
        UB = P.bufs(4, "U")
        tg = [A.alloc(512, F32) for _ in range(3)]
        tv = [A.alloc(512, F32) for _ in range(3)]
        tgB = P.bufs(3, "tg")
        tvB = P.bufs(3, "tv")
        ot = [A.alloc(512, F32) for _ in range(2)]
        otB = P.bufs(2, "ot")
        xr = [A.alloc(512, F32) for _ in range(2)]
        xrB = P.bufs(2, "xr")
        cw = self.cw.rearrange("p (a b) -> p a b", a=3)
        outT = self.d_out.ap()
        xt = A.at(xa_off, KC * W2, F32).rearrange("p (c n) -> p c n", c=KC)
        aT = A.at(xa_off, NFC * 512, BF16).rearrange("p (c n) -> p c n", c=NFC)
        for i in range(NSLOT):
            P.dma(xt, self.d_X2.ap()[i, :, :, 126:640], writes=[xaB])
            self.norm_tile(xt, xaB, W2, sq, sqB, hT, hB, rstd, rB, [(0, 2, 7), (2, 512, 0)])
            for fc in range(44):
                for c in range(KC):
                    self.mm(ps[:, 7, fc * 2:fc * 2 + 2], wup[:, c, fc * 128:(fc + 1) * 128], hT[:, c, 0:2], c == 0, c == KC - 1, [wB, hB[c]], [psB[7]])
            self.cp("dve", uh, ps[:, 7, 0:88].rearrange("p (c n) -> p c n", c=44), [psB[7]], [uhB])
            def gate_tail(f):
                k2 = f % 3
                self.act(tg[k2], tg[k2], AF.Silu, [tgB[k2], self.cB], [tgB[k2]])
                self.tt("pool", aT[:, f, :], tg[k2], tv[k2], ALU.mult, [tgB[k2], tvB[k2]], [xaB])

            for f in range(NFC):
                k2 = f % 2
                k3 = f % 3
                for half, fc in enumerate((f, NFC + f)):
                    bk = 1 + 2 * k2 + half
                    ui = 2 * k2 + half
                    for c in range(KC):
                        self.mm(ps[:, bk, :], wup[:, c, fc * 128:(fc + 1) * 128], hT[:, c, 2:W2], c == 0, c == KC - 1, [wB, hB[c]], [psB[bk]])
                    self.cp("pool", U[ui][:, 0:2], uh[:, fc, :], [uhB], [UB[ui]])
                    self.cp("act", U[ui][:, 2:W2], ps[:, bk, :], [psB[bk]], [UB[ui]])
                    t_, tb_ = (tg[k3], tgB[k3]) if half == 0 else (tv[k3], tvB[k3])
                    eng = "dve"
                    self.ts(eng, t_, U[ui][:, 0:512], cw[:, 0, fc:fc + 1], self.cb[:, fc:fc + 1], ALU.mult, ALU.add, [UB[ui], self.sB], [tb_])
                    self.stt(eng, t_, U[ui][:, 1:513], cw[:, 1, fc:fc + 1], t_, ALU.mult, ALU.add, [UB[ui], self.sB, tb_], [tb_])
                    self.stt(eng, t_, U[ui][:, 2:514], cw[:, 2, fc:fc + 1], t_, ALU.mult, ALU.add, [UB[ui], self.sB, tb_], [tb_])
                if f >= 1:
                    gate_tail(f - 1)
            gate_tail(NFC - 1)
            for oc in range(KC):
                k2 = oc % 2
                bk = 5 + k2
                P.dma(xr[k2], self.d_X2.ap()[i, :, oc, 128:640], writes=[xrB[k2]])
                for f in range(NFC):
                    self.mm(ps[:, bk, :], wdn[:, f, oc * 128:(oc + 1) * 128], aT[:, f, :], f == 0, f == NFC - 1, [wB, xaB], [psB[bk]])
                self.tt("dve", ot[k2], ps[:, bk, :], xr[k2], ALU.add, [psB[bk], xrB[k2]], [otB[k2]])
                P.dma(outT[:, oc, i * 512:(i + 1) * 512], ot[k2], reads=[otB[k2]], eng=STORE_ENG)
        P.barrier()
        A.release(m)

    def build(self, nph=N_PHASES):
        self.declare()
        self.P.strict = True
        self.setup()
        self.P.strict = False
        self.P.barrier()
        phases = [self.phase1, self.phase2, self.phase3, self.phase4a, self.phase4b]
        for n_, ph in enumerate(phases[:nph]):
            if n_ == 0 and SKIP_P1:
                continue
            ph()
        self.P.barrier()
        self.P.emit()
        return self.nc


def _host_inputs(inputs):
    f = np.float32
    x = np.asarray(inputs["x"], dtype=f)

    def pc(w, kc):
        n = w.shape[1]
        return np.ascontiguousarray(w.reshape(kc, 128, n).transpose(1, 0, 2))

    w_br_a = np.asarray(inputs["w_br_a"][0], dtype=f)
    wa = w_br_a.reshape(2, 4, 64, D)
    wbra = np.ascontiguousarray(wa.transpose(0, 2, 1, 3).reshape(128, 4, D))
    common = {
        "w_in": pc(np.asarray(inputs["w_in"][0], dtype=f), KC),
        "w_br_a": wbra,
        "w_br_b": pc(np.asarray(inputs["w_br_b"][0], dtype=f), 4),
        "w_o": pc(np.asarray(inputs["w_o"][0], dtype=f), KC),
        "w_up": pc(np.asarray(inputs["w_up"][0], dtype=f), KC),
        "w_down": pc(np.asarray(inputs["w_down"][0], dtype=f), NFC),
        "g_mix": np.ascontiguousarray(np.asarray(inputs["g_mix"][0], dtype=f).reshape(KC, 128).T),
        "g_ffn": np.ascontiguousarray(np.asarray(inputs["g_ffn"][0], dtype=f).reshape(KC, 128).T),
        "conv_w": np.ascontiguousarray(np.asarray(inputs["conv_w"][0], dtype=f).reshape(3, 44, 128).transpose(2, 0, 1)),
        "conv_b": np.ascontiguousarray(np.asarray(inputs["conv_b"][0], dtype=f).reshape(44, 128).T),
        "rel_bias": np.ascontiguousarray(np.asarray(inputs["rel_bias"], dtype=f)),
        "qn_a": np.asarray(inputs["qn_a"], dtype=f).reshape(1, 64),
        "kn_a": np.asarray(inputs["kn_a"], dtype=f).reshape(1, 64),
        "qn_b": np.asarray(inputs["qn_b"], dtype=f).reshape(1, 64),
        "kn_b": np.asarray(inputs["kn_b"], dtype=f).reshape(1, 64),
        "sinks": np.asarray(inputs["sinks"], dtype=f).reshape(1, 8),
        "lam_q1": np.asarray(inputs["lam_q1"], dtype=f).reshape(1, 64),
        "lam_k1": np.asarray(inputs["lam_k1"], dtype=f).reshape(1, 64),
        "lam_q2": np.asarray(inputs["lam_q2"], dtype=f).reshape(1, 64),
        "lam_k2": np.asarray(inputs["lam_k2"], dtype=f).reshape(1, 64),
        "subln_b": np.asarray(inputs["subln_b"], dtype=f).reshape(128, 1),
        "Jmat": np.ascontiguousarray(np.eye(128, dtype=f)[::-1]),
        "bdones": np.kron(np.eye(2, dtype=f), np.ones((64, 64), dtype=f)),
    }
    oha = np.zeros((33, 384), dtype=f)
    for mm_ in range(384):
        d = mm_ - 127
        if 0 <= d < 128:
            oha[int(_t5_bucket_np(np.array(d))), mm_] = 1
        else:
            oha[32, mm_] = 1
    common["oh_a"] = oha
    in_maps = []
    for core in range(8):
        b, j = core // 4, core % 4
        xTb = np.ascontiguousarray(x[b].T.reshape(KC, 128, S).transpose(1, 0, 2))
        xo = np.zeros((128, KC, NSLOT, SLOTW), dtype=f)
        for i in range(NSLOT):
            G = 4 * i + j
            t0 = 512 * G - 256
            lo = max(t0, 0)
            xo[:, :, i, lo - t0:] = xTb[:, :, lo:t0 + SLOTW]
        ohd = np.zeros((33, VECD), dtype=f)
        d = np.arange(VECD) + 512 * j - 2047
        bk = _t5_bucket_np(d)
        for mm_ in range(VECD):
            if d[mm_] >= 0:
                ohd[bk[mm_], mm_] = 1
            else:
                ohd[32, mm_] = 1
        mp = dict(common)
        mp["xT"] = xTb
        mp["xo"] = xo.reshape(128, KC, NSLOT * SLOTW)
        mp["oh_d"] = ohd
        mp["m0"] = np.full((128, 1), 0.0 if j == 0 else 1.0, dtype=f)
        in_maps.append(mp)
    return in_maps


_NC_CACHE = {}


def kernel(**inputs):
    in_maps = _host_inputs(inputs)
    if "nc" not in _NC_CACHE:
        _NC_CACHE["nc"] = K().build()
    nc = _NC_CACHE["nc"]
    res = run_bass_kernel_spmd(nc, in_maps, core_ids=list(range(8)))
    out = np.zeros((2, S, D), dtype=np.float32)
    for core in range(8):
        b, j = core // 4, core % 4
        o = res.results[core]["outT"].reshape(128, KC, NSLOT, 512)
        for i in range(NSLOT):
            G = 4 * i + j
            out[b, 512 * G:512 * (G + 1), :] = o[:, :, i, :].transpose(2, 1, 0).reshape(512, D)
    return out
```

```python
import contextlib
import math
import numpy as np
import concourse.bass as bass
import concourse.mybir as mybir
from concourse.bass_utils import run_bass_kernel_spmd

F32 = mybir.dt.float32
BF16 = mybir.dt.bfloat16
AF = mybir.ActivationFunctionType
ALU = mybir.AluOpType

ENGS = ("pe", "act", "dve", "pool", "sp")
SEM_ROT = 3000


class Buf:
    __slots__ = ("name", "w", "rs")

    def __init__(self, name):
        self.name = name
        self.w = None
        self.rs = []


class Op:
    __slots__ = ("eng", "fn", "waits", "signal", "dma_key", "dma_sem", "dma_val", "sig_sem", "sig_val")

    def __init__(self, eng, fn, dma_key=None):
        self.eng = eng
        self.fn = fn
        self.waits = []
        self.signal = False
        self.dma_key = dma_key
        self.dma_sem = None
        self.dma_val = None
        self.sig_sem = None
        self.sig_val = None


class Prog:
    def __init__(self, nc):
        self.nc = nc
        self.ops = {e: [] for e in ENGS}
        self.dma_cnt = {}
        self.all_dma_last = {}
        self.stack = contextlib.ExitStack()
        self.nbufs = 0
        self.strict = False

    def buf(self, name=None):
        self.nbufs += 1
        return Buf(f"{name or 'b'}#{self.nbufs}")

    def bufs(self, n, name="b"):
        return [self.buf(f"{name}{i}") for i in range(n)]

    def _dep(self, op, y):
        if y is None or y is op:
            return
        if y.dma_key is None and y.eng == op.eng and op.dma_key is None and not self.strict:
            return
        if y.dma_key is None:
            y.signal = True
        if y not in op.waits:
            op.waits.append(y)

    def op(self, eng, fn, reads=(), writes=(), dma_key=None):
        o = Op(eng, fn, dma_key)
        for b in reads:
            self._dep(o, b.w)
        for b in writes:
            self._dep(o, b.w)
            for r in b.rs:
                self._dep(o, r)
        for b in reads:
            b.rs.append(o)
        for b in writes:
            b.w = o
            b.rs = []
        if dma_key is not None:
            st = self.dma_cnt.setdefault(dma_key, [0, 0])
            if st[1] + 16 > 4000:
                st[0] += 1
                st[1] = 0
            st[1] += 16
            o.dma_sem = (dma_key, st[0])
            o.dma_val = st[1]
            self.all_dma_last[dma_key] = o
        self.ops[eng].append(o)
        return o

    def dma(self, out, in_, reads=(), writes=(), key=None, eng="sp", **kw):
        prim = writes[0] if len(writes) else reads[0]
        key = key if (key is not None and key.startswith("SH_")) else prim.name
        return self.op(eng, lambda e: e.dma_start(out=out, in_=in_, **kw), reads, writes, dma_key=key)

    def barrier(self):
        lasts = [self.ops[e][-1] for e in ENGS if self.ops[e]]
        dmas = list(self.all_dma_last.values())
        news = []
        for e in ENGS:
            o = Op(e, None)
            for y in lasts:
                self._dep(o, y)
            for y in dmas:
                self._dep(o, y)
            news.append(o)
        for o in news:
            self.ops[o.eng].append(o)

    def emit(self):
        nc = self.nc
        semkeys = set()
        for e in ENGS:
            gen, cnt = 0, 0
            for o in self.ops[e]:
                if o.dma_key is not None:
                    semkeys.add(o.dma_sem)
                    continue
                if o.signal:
                    if cnt >= SEM_ROT:
                        gen += 1
                        cnt = 0
                    cnt += 1
                    o.sig_sem = ("eng", e, gen)
                    o.sig_val = cnt
                    semkeys.add(o.sig_sem)
        sems = {}
        for n, k in enumerate(sorted(semkeys, key=str)):
            sems[k] = self.stack.enter_context(nc.semaphore(f"sm{n}"))
        self.nsems = len(sems)
        block = self.stack.enter_context(nc.Block())
        engmap = {"pe": "tensor", "act": "scalar", "dve": "vector", "pool": "gpsimd", "sp": "sync"}

        def make(e):
            def body(eng):
                waited = {}
                for o in self.ops[e]:
                    for y in o.waits:
                        if y.dma_key is not None:
                            sk, v = y.dma_sem, y.dma_val
                        else:
                            sk, v = y.sig_sem, y.sig_val
                        if waited.get(sk, 0) >= v:
                            continue
                        waited[sk] = v
                        eng.wait_ge(sems[sk], v)
                    if o.fn is None:
                        if o.signal:
                            eng.nop().then_inc(sems[o.sig_sem], 1)
                        continue
                    ins = o.fn(eng)
                    if o.dma_key is not None:
                        ins.then_inc(sems[o.dma_sem], 16)
                    elif o.signal:
                        ins.then_inc(sems[o.sig_sem], 1)
            return body

        for e in ENGS:
            getattr(block, engmap[e])(make(e))
        self.stack.close()


class Arena:
    def __init__(self, prog, nbytes, name="arena"):
        nc = prog.nc
        self.t8 = prog.stack.enter_context(nc.sbuf_tensor(name, [128, nbytes], mybir.dt.uint8))
        self.views = {}
        self.nbytes = nbytes
        self.off = 0

    def view(self, dt):
        if dt not in self.views:
            self.views[dt] = self.t8.bitcast(dt)
        return self.views[dt]

    def alloc(self, nelem, dt):
        sz = mybir.dt.size(dt)
        self.off = (self.off + 63) // 64 * 64
        o = self.off
        self.off += nelem * sz
        assert self.off <= self.nbytes, f"arena overflow {self.off} > {self.nbytes}"
        return self.view(dt)[:, o // sz: o // sz + nelem]

    def alloc_raw(self, nbytes):
        self.off = (self.off + 63) // 64 * 64
        o = self.off
        self.off += nbytes
        assert self.off <= self.nbytes, f"arena overflow {self.off} > {self.nbytes}"
        return o

    def at(self, o, nelem, dt):
        sz = mybir.dt.size(dt)
        return self.view(dt)[:, o // sz: o // sz + nelem]

    def mark(self):
        return self.off

    def release(self, m):
        self.off = m


D = 1024
S = 16384
KC = 8
NSLOT = 8
SLOTW = 768
HW = 640
DFF = 2816
NFC = 22
EPS = 1e-6
LAM_INIT = 0.8 - 0.6 * math.exp(-0.3 * 0)
VECD = 2688
STRIPW = 2560
C_QA, C_KA, C_VA, C_QB, C_KB, C_VB, C_GL = 0, 512, 640, 768, 1280, 1792, 2304

DEBUG_SCRATCH = False
STORE_ENG = "pool"
P2_STAGE = 9
P2_SLOTS = 8
SKIP_P1 = False
N_PHASES = 5


def _t5_bucket_np(rel):
    n = np.maximum(rel, 0)
    nf = np.maximum(n, 1).astype(np.float32)
    large = 16 + (np.log(nf / np.float32(16)) / np.float32(math.log(8.0)) * np.float32(16)).astype(np.int32)
    large = np.minimum(large, 31)
    return np.where(n < 16, n, large)


class K:
    def __init__(self):
        nc = bass.Bass("TRN2", target_bir_lowering=False)
        self.nc = nc
        self.P = Prog(nc)
        self.A = Arena(self.P, 209920)
        self.ps = self.P.stack.enter_context(nc.psum_tensor("ps", [128, 8, 512], F32))
        self.psB = self.P.bufs(8, "psb")

    def mm(self, out, lhsT, rhs, start, stop, R, W):
        self.P.op("pe", lambda e: e.matmul(out, lhsT=lhsT, rhs=rhs, start=start, stop=stop), R, W)

    def act(self, out, in_, func, R, W, bias=None, scale=1.0):
        b = self.zero1 if bias is None else bias
        npart = out.shape[0]
        if npart != 128 and b.shape[0] == 128:
            b = b[0:npart, :]
        self.P.op("act", lambda e: e.activation(out=out, in_=in_, func=func, bias=b, scale=scale), R, W)

    def tt(self, eng, out, a, b, op, R, W):
        self.P.op(eng, lambda e: e.tensor_tensor(out=out, in0=a, in1=b, op=op), R, W)

    def ts(self, eng, out, a, s1, s2, op0, op1, R, W):
        if op1 is None:
            self.P.op(eng, lambda e: e.tensor_scalar(out=out, in0=a, scalar1=s1, scalar2=None, op0=op0), R, W)
        else:
            self.P.op(eng, lambda e: e.tensor_scalar(out=out, in0=a, scalar1=s1, scalar2=s2, op0=op0, op1=op1), R, W)

    def stt(self, eng, out, a, s, b, op0, op1, R, W):
        self.P.op(eng, lambda e: e.scalar_tensor_tensor(out=out, in0=a, scalar=s, in1=b, op0=op0, op1=op1), R, W)

    def cp(self, eng, out, in_, R, W):
        if eng == "act":
            self.act(out, in_, AF.Identity, R, W)
        else:
            self.P.op(eng, lambda e: e.tensor_copy(out=out, in_=in_), R, W)

    def rcp(self, out, in_, R, W):
        self.P.op("dve", lambda e: e.reciprocal(out=out, in_=in_), R, W)

    def rsqrt_act(self, out, in_, scale, R, W, power=-0.5, bias=None):
        self.act(out, in_, AF.Ln, R, W, bias=self.eps1 if bias is None else bias, scale=scale)
        old = self.P.strict
        if out.shape[-1] <= 256:
            self.P.strict = True
        self.act(out, out, AF.Exp, list(W) + [self.cB], W, scale=power)
        self.P.strict = old

    def memset(self, eng, ap, val, W):
        self.P.op(eng, lambda e: e.memset(ap, val), (), W)

    def declare(self):
        nc = self.nc

        def din(name, shape, dt=F32):
            return nc.dram_tensor(name, list(shape), dt, kind="ExternalInput")

        def dscr(name, shape, dt):
            return nc.dram_tensor(name, list(shape), dt, kind="ExternalOutput" if DEBUG_SCRATCH else "Internal")

        self.d_xT = din("xT", [128, KC, S])
        self.d_xo = din("xo", [128, KC, NSLOT * SLOTW])
        self.d_win = din("w_in", [128, KC, 4352])
        self.d_wbra = din("w_br_a", [128, 4, D])
        self.d_wbrb = din("w_br_b", [128, 4, D])
        self.d_wo = din("w_o", [128, KC, D])
        self.d_wup = din("w_up", [128, KC, 2 * DFF])
        self.d_wdn = din("w_down", [128, NFC, D])
        self.d_gmix = din("g_mix", [128, KC])
        self.d_gffn = din("g_ffn", [128, KC])
        self.d_cw = din("conv_w", [128, 3, 44])
        self.d_cb = din("conv_b", [128, 44])
        self.d_relb = din("rel_bias", [32, 12])
        self.d_qna = din("qn_a", [1, 64])
        self.d_kna = din("kn_a", [1, 64])
        self.d_qnb = din("qn_b", [1, 64])
        self.d_knb = din("kn_b", [1, 64])
        self.d_sinks = din("sinks", [1, 8])
        self.d_lq1 = din("lam_q1", [1, 64])
        self.d_lk1 = din("lam_k1", [1, 64])
        self.d_lq2 = din("lam_q2", [1, 64])
        self.d_lk2 = din("lam_k2", [1, 64])
        self.d_subln = din("subln_b", [128, 1])
        self.d_oha = din("oh_a", [33, 384])
        self.d_ohd = din("oh_d", [33, VECD])
        self.d_m0 = din("m0", [128, 1])
        self.d_J = din("Jmat", [128, 128])
        self.d_bd = din("bdones", [128, 128])
        self.d_out = nc.dram_tensor("outT", [128, KC, NSLOT * 512], F32, kind="ExternalOutput")
        self.d_KT = dscr("s_KT", [4, 128, S], BF16)
        self.d_VS = dscr("s_VS", [4, 128, 128, 128], BF16)
        self.d_QB = dscr("s_QB", [NSLOT, 128, 4, HW], BF16)
        self.d_YA = dscr("s_YA", [NSLOT, 128, 4, HW], BF16)
        self.d_YB = dscr("s_YB", [NSLOT, 128, 4, HW], BF16)
        self.d_X2 = dscr("s_X2", [NSLOT, 128, KC, HW], F32)
        self.d_H = dscr("s_H", [NSLOT, 128, KC, HW], BF16)
        if DEBUG_SCRATCH:
            self.d_dstrip = dscr("s_strip", [128, 4 * STRIPW], BF16)
            self.d_dsa = dscr("s_sa", [128, 2 * 8 * 128], BF16)
            self.d_dsmall = dscr("s_small", [128, 8], F32)
        self.d_bc = nc.dram_tensor("s_bc", [2, 3, 512], F32, kind="Internal")
        self.d_VA = dscr("s_VECA", [8, 384], F32)
        self.d_VD = dscr("s_VECD", [4, VECD], F32)

    def setup(self):
        P, A, nc = self.P, self.A, self.nc
        ps, psB = self.ps, self.psB
        cB = P.buf("consts")
        self.cB = cB
        self.zero1 = A.alloc(1, F32)
        self.eps1 = A.alloc(1, F32)
        self.tiny1 = A.alloc(1, F32)
        self.ones_bf = A.alloc(128, BF16)
        self.ones_f = A.alloc(128, F32)
        self.bd_bf = A.alloc(128, BF16)
        self.J = A.alloc(128, F32)
        bd_f = A.alloc(128, F32)
        self.memset("pool", self.zero1, 0.0, [cB])
        self.memset("pool", self.eps1, EPS, [cB])
        self.memset("pool", self.tiny1, 1e-30, [cB])
        self.memset("pool", self.ones_bf, 1.0, [cB])
        self.memset("pool", self.ones_f, 1.0, [cB])
        jB = P.buf("J")
        P.dma(self.J, self.d_J.ap(), writes=[jB], key="c0")
        P.dma(bd_f, self.d_bd.ap(), writes=[jB], key="c0")
        self.cp("dve", self.bd_bf, bd_f, [jB], [cB])
        sB = P.buf("small")
        self.gmix = A.alloc(KC, F32)
        self.gffn = A.alloc(KC, F32)
        self.cw = A.alloc(3 * 44, F32)
        self.cb = A.alloc(44, F32)
        self.m0 = A.alloc(1, F32)
        P.dma(self.gmix, self.d_gmix.ap(), writes=[sB], key="c1")
        P.dma(self.gffn, self.d_gffn.ap(), writes=[sB], key="c1")
        P.dma(self.cw, self.d_cw.ap().rearrange("p a b -> p (a b)"), writes=[sB], key="c1")
        P.dma(self.cb, self.d_cb.ap(), writes=[sB], key="c1")
        P.dma(self.m0, self.d_m0.ap(), writes=[sB], key="c1")
        self.gq_a = A.alloc(1, F32)
        self.gk_a = A.alloc(1, F32)
        self.gq_b = A.alloc(1, F32)
        self.gk_b = A.alloc(1, F32)
        for dst, src in ((self.gq_a, self.d_qna), (self.gk_a, self.d_kna), (self.gq_b, self.d_qnb), (self.gk_b, self.d_knb)):
            for hlf in range(2):
                P.dma(dst[64 * hlf:64 * hlf + 64, :], bass.AP(src, 0, [[1, 64], [1, 1]]), writes=[sB], key="c1")
        self.gsub = A.alloc(1, F32)
        P.dma(self.gsub, self.d_subln.ap(), writes=[sB], key="c1")
        self.ts("dve", self.gsub, self.gsub, 1.0 - LAM_INIT, None, ALU.mult, None, [sB], [sB])
        lam4 = A.alloc(4 * 64, F32)
        for n_, src in enumerate((self.d_lq1, self.d_lk1, self.d_lq2, self.d_lk2)):
            P.dma(lam4[:, n_ * 64:(n_ + 1) * 64], bass.AP(src, 0, [[0, 128], [1, 64]]), writes=[sB], key="c1")
        lp = A.alloc(2 * 64, F32)
        ls = A.alloc(2, F32)
        self.neglam = A.alloc(1, F32)
        self.tt("dve", lp[:, 0:64], lam4[:, 0:64], lam4[:, 64:128], ALU.mult, [sB], [sB])
        self.tt("dve", lp[:, 64:128], lam4[:, 128:192], lam4[:, 192:256], ALU.mult, [sB], [sB])
        P.op("dve", lambda e: e.reduce_sum(out=ls[:, 0:1], in_=lp[:, 0:64], axis=mybir.AxisListType.X), [sB], [sB])
        P.op("dve", lambda e: e.reduce_sum(out=ls[:, 1:2], in_=lp[:, 64:128], axis=mybir.AxisListType.X), [sB], [sB])
        self.act(ls, ls, AF.Exp, [sB, cB], [sB])
        self.tt("dve", self.neglam, ls[:, 1:2], ls[:, 0:1], ALU.subtract, [sB], [sB])
        self.ts("dve", self.neglam, self.neglam, -LAM_INIT, None, ALU.add, None, [sB], [sB])
        self.sB = sB
        self.esrow = A.alloc(2 * 512, F32)
        sk = A.alloc(8, F32)
        P.dma(sk[0:1, :], self.d_sinks.ap(), writes=[sB], key="c1")
        self.act(sk[0:1, :], sk[0:1, :], AF.Exp, [sB, cB], [sB])
        for hq in range(8):
            g_, r_ = hq // 4, hq % 4
            col = g_ * 512 + ((r_ % 2) * 2 + r_ // 2) * 128
            self.ts("dve", self.esrow[0:1, col:col + 128], self.ones_f[0:1, 0:128], sk[0:1, hq:hq + 1], None, ALU.mult, None, [sB, cB], [sB])

        tB = P.buf("tab")
        tabp = A.alloc(12, F32)
        tab31 = A.alloc(4, F32)
        self.memset("pool", tabp[32:33, :], -30000.0, [tB])
        P.dma(tabp[0:32, :], self.d_relb.ap(), writes=[tB], key="c2")
        P.dma(tab31[0:32, :], bass.AP(self.d_relb, 31 * 12 + 8, [[0, 32], [1, 4]]), writes=[tB], key="c2")
        self.tt("dve", tabp[0:32, 8:12], tabp[0:32, 8:12], tab31[0:32, :], ALU.subtract, [tB], [tB])
        m = A.mark()
        oha = A.alloc(384, F32)
        ohd = A.alloc(VECD, F32)
        veca = A.alloc(384, F32)
        vecd = A.alloc(VECD, F32)
        ohB = P.buf("oh")
        P.dma(oha[0:33, :], self.d_oha.ap(), writes=[ohB], key="c2")
        P.dma(ohd[0:33, :], self.d_ohd.ap(), writes=[ohB], key="c2")
        vB = P.buf("vec")
        self.mm(ps[0:8, 0, 0:384], tabp[0:33, 0:8], oha[0:33, :], True, True, [tB, ohB], [psB[0]])
        self.act(veca[0:8, :], ps[0:8, 0, 0:384], AF.Exp, [psB[0], cB], [vB])
        for pc in range(6):
            w = 512 if pc < 5 else VECD - 2560
            bk = 1 + pc % 2
            self.mm(ps[0:4, bk, 0:w], tabp[0:33, 8:12], ohd[0:33, pc * 512: pc * 512 + w], True, True, [tB, ohB], [psB[bk]])
            self.act(vecd[0:4, pc * 512: pc * 512 + w], ps[0:4, bk, 0:w], AF.Exp, [psB[bk], cB], [vB])
        dvB = P.buf("dvec")
        P.dma(self.d_VA.ap(), veca[0:8, :], reads=[vB], writes=[dvB], key="c3")
        P.dma(self.d_VD.ap(), vecd[0:4, :], reads=[vB], writes=[dvB], key="c3")
        A.release(m)
        self.pre_strip_mark = A.mark()
        self.strip = A.alloc(4 * STRIPW, BF16).rearrange("p (h u) -> p h u", h=4)
        self.sa = A.alloc(2 * 8 * 128, BF16).rearrange("p (t h q) -> p t h q", t=2, h=8)
        self.stripB = P.buf("strip")
        self.persist_mark = A.mark()
        m = A.mark()
        rev = A.alloc(STRIPW, F32)
        revB = P.bufs(2, "rev")
        P.dma(rev[:, 0:2048].rearrange("p (h u) -> p h u", h=8), bass.AP(self.d_VA, 0, [[1, 128], [384, 8], [1, 256]]),
              reads=[dvB], writes=[revB[0]], key="c4")
        for pi in range(4):
            bk = pi % 2
            self.mm(ps[:, bk, :], self.J, rev[:, pi * 512:(pi + 1) * 512], True, True, [jB, revB[0]], [psB[bk]])
            src = ps[:, bk, :].rearrange("p (h t q) -> p h t q", h=2, t=2)
            for ty in range(2):
                self.cp("dve" if ty == 0 else "act", self.sa[:, ty, 2 * pi:2 * pi + 2, :], src[:, :, ty, :], [psB[bk]], [self.stripB])
        for h in range(4):
            rb = revB[(h + 1) % 2]
            P.dma(rev, bass.AP(self.d_VD, h * VECD, [[1, 128], [1, STRIPW]]), reads=[dvB], writes=[revB[0], revB[1]], key="c4")
            for pc in range(5):
                bk = pc % 2
                self.mm(ps[:, bk, :], self.J, rev[:, pc * 512:(pc + 1) * 512], True, True, [jB, revB[0], revB[1]], [psB[bk]])
                self.cp("dve" if pc % 2 == 0 else "act", self.strip[:, h, pc * 512:(pc + 1) * 512], ps[:, bk, :], [psB[bk]], [self.stripB])
        A.release(m)
        if DEBUG_SCRATCH:
            P.dma(self.d_dstrip.ap(), self.strip.rearrange("p h u -> p (h u)"), reads=[self.stripB])
            P.dma(self.d_dsa.ap(), self.sa.rearrange("p t h q -> p (t h q)"), reads=[self.stripB])
            for n_, t_ in enumerate((self.neglam, self.gsub, self.gq_b, self.gk_b)):
                P.dma(self.d_dsmall.ap()[:, n_:n_ + 1], t_, reads=[self.sB], allow_slow_non_contiguous=True)

    def load_w(self, dst, src, kc, ncols, gain=None, dup=None, key="w"):
        P, A = self.P, self.A
        m = A.mark()
        pw = 512 if kc <= 8 else 192
        stg = [A.alloc(kc * pw, F32).rearrange("p (c n) -> p c n", c=kc) for _ in range(2)]
        sb = P.bufs(2, "stg")
        wB = P.buf("wdst")
        engs = ("dve", "act")
        n = 0
        for i, c0 in enumerate(range(0, ncols, pw)):
            w = min(pw, ncols - c0)
            s, b = stg[i % 2], sb[i % 2]
            P.dma(s[:, :, 0:w], src[:, :, c0:c0 + w], writes=[b], key=f"SH_stg{i % 2}")
            for c in range(kc):
                eng = engs[n % 2]
                n += 1
                if gain is None:
                    self.cp(eng, dst[:, c, c0:c0 + w], s[:, c, 0:w], [b], [wB])
                elif eng == "act":
                    self.act(dst[:, c, c0:c0 + w], s[:, c, 0:w], AF.Identity, [b, self.sB, self.cB], [wB], scale=gain[:, c:c + 1])
                else:
                    self.ts(eng, dst[:, c, c0:c0 + w], s[:, c, 0:w], gain[:, c:c + 1], None, ALU.mult, None, [b, self.sB], [wB])
        self.P.barrier()
        A.release(m)
        return wB

    def norm_tile(self, xt, xB, n, sq, sqB, hT, hB, rstd, rB, pieces):
        ps, psB = self.ps, self.psB
        for c in range(KC):
            self.act(sq[c % 2][:, 0:n], xt[:, c, :], AF.Square, [xB, self.cB], [sqB[c % 2]])
            for (o, w, bank) in pieces:
                self.mm(ps[:, bank, 0:w], self.ones_bf, sq[c % 2][:, o:o + w], c == 0, c == KC - 1, [sqB[c % 2], self.cB], [psB[bank]])
        for (o, w, bank) in pieces:
            small = w < 128
            self.P.strict = small
            self.rsqrt_act(rstd[:, o:o + w], ps[:, bank, 0:w], 1.0 / D, [psB[bank], self.cB], [rB])
            self.P.strict = False
        for c in range(KC):
            self.tt("dve" if c % 2 == 0 else "pool", hT[:, c, 0:n], xt[:, c, :], rstd[:, 0:n], ALU.mult, [xB, rB], [hB[c]])

    def ph_tasks(self, tasks, hT, hB, wB, tmp, tmpB, pbanks, nbanks, mid=None):
        ps, psB = self.ps, self.psB
        n = len(tasks)

        def proj(k):
            wfn, o, w, out, outB, gain = tasks[k]
            bk = pbanks[k % len(pbanks)]
            ksq, _ = tmp[k % len(tmp)]
            for c in range(KC):
                self.mm(ps[:, bk, 0:w], wfn(c), hT[:, c, o:o + w], c == 0, c == KC - 1, [wB, hB[c]], [psB[bk]])
            self.act(ksq[:, 0:w], ps[:, bk, 0:w], AF.Square, [psB[bk], self.cB], [tmpB[k % len(tmp)][0]])

        def norm(k):
            wfn, o, w, out, outB, gain = tasks[k]
            bk = pbanks[k % len(pbanks)]
            bn = nbanks[k % len(nbanks)]
            ksq, rk = tmp[k % len(tmp)]
            tb = tmpB[k % len(tmp)]
            self.mm(ps[:, bn, 0:w], self.bd_bf, ksq[:, 0:w], True, True, [tb[0], self.cB], [psB[bn]])
            self.rsqrt_act(rk[:, 0:w], ps[:, bn, 0:w], 1.0 / 64, [psB[bn], self.cB], [tb[1]])
            self.stt("dve", out, ps[:, bk, 0:w], gain, rk[:, 0:w], ALU.mult, ALU.mult, [psB[bk], tb[1], self.sB], [outB])

        for k in range(n + 1):
            if k < n:
                proj(k)
            if k == n and mid is not None:
                mid()
            if k >= 1:
                norm(k - 1)

    def phase1(self):
        P, A = self.P, self.A
        ps, psB = self.ps, self.psB
        m = A.mark()
        wkb = A.alloc(KC * 512, BF16).rearrange("p (c n) -> p c n", c=KC)
        wvb = A.alloc(KC * 512, BF16).rearrange("p (c n) -> p c n", c=KC)
        win = self.d_win.ap()
        wB1 = self.load_w(wkb, win[:, :, C_KB:C_KB + 512], KC, 512, gain=self.gmix)
        wB2 = self.load_w(wvb, win[:, :, C_VB:C_VB + 512], KC, 512, gain=self.gmix)
        xt = [A.alloc(KC * 512, F32).rearrange("p (c n) -> p c n", c=KC) for _ in range(2)]
        xB = P.bufs(2, "x")
        sq = [A.alloc(512, BF16) for _ in range(2)]
        sqB = P.bufs(2, "sq")
        hT = [A.alloc(KC * 512, BF16).rearrange("p (c n) -> p c n", c=KC) for _ in range(2)]
        hB = [P.bufs(KC, "h") for _ in range(2)]
        rstd = [A.alloc(512, F32) for _ in range(2)]
        rB = P.bufs(2, "r")
        tmp = [(A.alloc(512, BF16), A.alloc(512, F32)) for _ in range(3)]
        tmpB = [P.bufs(2, "tmp") for _ in range(3)]
        kout = [A.alloc(4 * 512, BF16).rearrange("p (h n) -> p h n", h=4) for _ in range(2)]
        koB = [P.bufs(4, "ko") for _ in range(2)]
        vout = [A.alloc(4 * 512, BF16).rearrange("p (b n) -> p b n", b=4) for _ in range(2)]
        voB = [P.bufs(4, "vo") for _ in range(2)]
        xT = self.d_xT.ap()
        NT = S // 512

        def stats(T):
            pp = T % 2
            P.dma(xt[pp], xT[:, :, T * 512:(T + 1) * 512], writes=[xB[pp]], key=f"x{pp}")
            self.norm_tile(xt[pp], xB[pp], 512, sq, sqB, hT[pp], hB[pp], rstd[pp], rB[pp], [(0, 512, 0)])

        stats(0)
        for T in range(NT):
            pp = T % 2
            if T + 1 < NT:
                stats(T + 1)
            tasks = [((lambda c, hh=hh: wkb[:, c, hh * 128:(hh + 1) * 128]), 0, 512, kout[pp][:, hh, :], koB[pp][hh], self.gk_b) for hh in range(4)]
            self.ph_tasks(tasks, hT[pp], hB[pp], wB1, tmp, tmpB, (1, 2, 3), (6, 7))
            for hh in range(4):
                P.dma(self.d_KT.ap()[hh, :, T * 512:(T + 1) * 512], kout[pp][:, hh, :], reads=[koB[pp][hh]], key=f"ko{pp}", eng=STORE_ENG)
            for blk in range(4):
                bk = 4 + blk % 2
                for c in range(KC):
                    self.mm(ps[:, bk, :], hT[pp][:, c, blk * 128:(blk + 1) * 128], wvb[:, c, :], c == 0, c == KC - 1,
                            [hB[pp][c], wB2], [psB[bk]])
                self.cp("act" if blk % 2 == 0 else "dve", vout[pp][:, blk, :], ps[:, bk, :], [psB[bk]], [voB[pp][blk]])
            for hh in range(4):
                P.dma(self.d_VS.ap()[hh, :, 4 * T:4 * T + 4, :], vout[pp][:, :, hh * 128:(hh + 1) * 128],
                      reads=voB[pp], key=f"vo{pp}", eng=STORE_ENG)
        P.barrier()
        A.release(m)

    def phase2(self):
        P, A = self.P, self.A
        ps, psB = self.ps, self.psB
        m = A.mark()
        win = self.d_win.ap()

        def walloc(n):
            return A.alloc(KC * n, BF16).rearrange("p (c n) -> p c n", c=KC)

        wqa, wka2, wva, wqb = walloc(512), walloc(256), walloc(128), walloc(512)
        wBq = self.load_w(wqa, win[:, :, C_QA:C_QA + 512], KC, 512, gain=self.gmix)
        for g in range(2):
            for hlf in range(2):
                self.load_w(wka2[:, :, g * 128 + hlf * 64: g * 128 + hlf * 64 + 64], win[:, :, C_KA + 64 * g:C_KA + 64 * g + 64], KC, 64,
                            gain=self.gmix)
        self.load_w(wva, win[:, :, C_VA:C_VA + 128], KC, 128, gain=self.gmix)
        self.load_w(wqb, win[:, :, C_QB:C_QB + 512], KC, 512, gain=self.gmix)
        wB = P.buf("w2")
        xts = [A.alloc(KC * SLOTW, F32).rearrange("p (c n) -> p c n", c=KC) for _ in range(2)]
        xBs = P.bufs(2, "x")
        sq = [A.alloc(SLOTW, BF16) for _ in range(2)]
        sqB = P.bufs(2, "sq")
        hTs = [A.alloc(KC * SLOTW, BF16).rearrange("p (c n) -> p c n", c=KC) for _ in range(2)]
        hBs = [P.bufs(KC, "h") for _ in range(2)]
        rstds = [A.alloc(SLOTW, F32) for _ in range(2)]
        rBs = P.bufs(2, "r")
        tmp = [(A.alloc(512, BF16), A.alloc(512, F32)) for _ in range(3)]
        tmpB = [P.bufs(2, "tmp") for _ in range(3)]
        qaT = A.alloc(4 * HW, BF16).rearrange("p (c n) -> p c n", c=4)
        qaB = P.bufs(4, "qa")
        kaT = A.alloc(2 * SLOTW, BF16).rearrange("p (g n) -> p g n", g=2)
        kaB = P.bufs(2, "ka")
        va = A.alloc(6 * 128, BF16).rearrange("p (b n) -> p b n", b=6)
        vaB = P.buf("va")
        qbT = A.alloc(4 * HW, BF16).rearrange("p (c n) -> p c n", c=4)
        qbB = P.buf("qb")
        yaT = A.alloc(4 * HW, BF16).rearrange("p (c n) -> p c n", c=4)
        yaB = P.buf("ya")
        pt = [A.alloc(512, BF16) for _ in range(4)]
        ptB = P.bufs(4, "pt")
        den = A.alloc(512, F32)
        denB = P.buf("den")
        xo = self.d_xo.ap()
        nsl = min(NSLOT, P2_SLOTS)

        def stats(i):
            pp = i % 2
            P.dma(xts[pp], xo[:, :, i * SLOTW:(i + 1) * SLOTW], writes=[xBs[pp]], key="x2")
            self.norm_tile(xts[pp], xBs[pp], SLOTW, sq, sqB, hTs[pp], hBs[pp], rstds[pp], rBs[pp], [(0, 512, 0), (512, 256, 7)])

        stats(0)
        for i in range(nsl):
            hT, hB = hTs[i % 2], hBs[i % 2]
            tasks = []
            for (o, w) in ((128, 512), (640, 128)):
                for cm in range(4):
                    tasks.append(((lambda c, cm=cm: wqa[:, c, cm * 128:(cm + 1) * 128]), o, w, qaT[:, cm, o - 128:o - 128 + w], qaB[cm], self.gq_a))
                for cm in range(4):
                    tasks.append(((lambda c, cm=cm: wqb[:, c, cm * 128:(cm + 1) * 128]), o, w, qbT[:, cm, o - 128:o - 128 + w], qbB, self.gq_b))
            for (o, w) in ((0, 512), (512, 256)):
                for g in range(2):
                    tasks.append(((lambda c, g=g: wka2[:, c, g * 128:(g + 1) * 128]), o, w, kaT[:, g, o:o + w], kaB[g], self.gk_a))
            P.dma(self.d_H.ap()[i], hT[:, :, 128:SLOTW], reads=hB, eng=STORE_ENG)
            self.ph_tasks(tasks, hT, hB, wB, tmp, tmpB, (1, 2, 3), (5, 6))
            P.dma(self.d_QB.ap()[i], qbT, reads=[qbB], key="qbo", eng=STORE_ENG)
            for half in range(2):
                bk = 4 + half
                for bl in range(3):
                    blk = half * 3 + bl
                    for c in range(KC):
                        self.mm(ps[:, bk, bl * 128:(bl + 1) * 128], hT[:, c, blk * 128:(blk + 1) * 128], wva[:, c, :], c == 0, c == KC - 1,
                                [hB[c], wB], [psB[bk]])
                self.cp("act", va[:, half * 3:half * 3 + 3, :], ps[:, bk, 0:384].rearrange("p (b n) -> p b n", b=3), [psB[bk]], [vaB])
            if i + 1 < nsl:
                stats(i + 1)
            for n in range(1, 6 if P2_STAGE >= 1 else 0):
                qo = (n - 1) * 128
                for g in range(2):
                    for kk, kblk in enumerate((n - 1, n)):
                        idx = g * 2 + kk
                        b0 = (2, 6)[idx % 2]
                        for r in range(4):
                            par, rr = r % 2, r // 2
                            pb = 64 * par
                            self.mm(ps[:, b0 + par, rr * 128:(rr + 1) * 128], kaT[pb:pb + 64, g, kblk * 128:(kblk + 1) * 128],
                                    qaT[pb:pb + 64, 2 * g + rr, qo:qo + 128], True, True,
                                    [kaB[g], qaB[2 * g + rr]], [psB[b0 + par]])
                        self.act(pt[idx].rearrange("p (a n) -> p a n", a=2), ps[:, b0:b0 + 2, 0:256], AF.Exp,
                                 [psB[b0], psB[b0 + 1], self.cB], [ptB[idx]], scale=0.125)
                        ty = 1 if kk == 0 else 0
                        fa = self.sa[:, ty, 4 * g:4 * g + 4, :].rearrange("p (rr par) q -> p par rr q", par=2)
                        p4 = pt[idx].rearrange("p (par rr q) -> p par rr q", par=2, rr=2)
                        self.tt("pool", p4, p4, fa, ALU.mult, [ptB[idx], self.stripB], [ptB[idx]])
                        if i == 0 and n == 2 and kk == 0:
                            self.ts("pool", pt[idx], pt[idx], self.m0[:, 0:1], None, ALU.mult, None, [ptB[idx], self.sB], [ptB[idx]])
                if P2_STAGE < 2:
                    continue
                for g in range(2):
                    for kk, kblk in enumerate((n - 1, n)):
                        idx = g * 2 + kk
                        self.mm(ps[64 * g:64 * g + 64, 4, :], va[:, kblk, 64 * g:64 * g + 64], pt[idx], kk == 0, kk == 1,
                                [vaB, ptB[idx]], [psB[4]])
                    for kk in range(2):
                        idx = g * 2 + kk
                        self.mm(ps[64 * g:64 * g + 64, 5, :], self.ones_bf[:, 0:64], pt[idx], kk == 0, (kk == 1 and P2_STAGE < 3),
                                [ptB[idx], self.cB], [psB[5]])
                    if P2_STAGE >= 3:
                        self.mm(ps[64 * g:64 * g + 64, 5, :], self.ones_f[0:1, 0:64], self.esrow[0:1, g * 512:(g + 1) * 512], False, True,
                                [self.sB, self.cB], [psB[5]])
                self.rsqrt_act(den, ps[:, 5, :], 1.0, [psB[5], self.cB], [denB], power=-1.0, bias=self.zero1)
                self.tt("dve", yaT[:, :, qo:qo + 128].rearrange("p (rr par) q -> p par rr q", par=2),
                        ps[:, 4, :].rearrange("p (par rr q) -> p par rr q", par=2, rr=2),
                        den.rearrange("p (par rr q) -> p par rr q", par=2, rr=2), ALU.mult, [psB[4], denB], [yaB])
            P.dma(self.d_YA.ap()[i], yaT, reads=[yaB], key="yao", eng=STORE_ENG)
        P.barrier()
        A.release(m)

    def phase3(self):
        P, A = self.P, self.A
        ps, psB = self.ps, self.psB
        m = A.mark()
        KTs = [A.alloc(S, BF16) for _ in range(2)]
        VSs = [A.alloc(128 * 128, BF16).rearrange("p (b e) -> p b e", b=128) for _ in range(2)]
        kvB = [(P.bufs(4, "kt"), P.bufs(4, "vs")) for _ in range(2)]
        qt = [A.alloc(HW, BF16) for _ in range(2)]
        qB = P.bufs(2, "q")
        NPB = 4
        pt = [A.alloc(1024, BF16) for _ in range(NPB)]
        ptB = P.bufs(NPB, "pt")
        ssum = [A.alloc(512, F32) for _ in range(2)]
        ssumB = P.bufs(2, "ssum")
        rbc = [A.alloc(1024, F32) for _ in range(2)]
        rbcB = P.bufs(2, "rbc")
        dd = [A.alloc(1024, F32) for _ in range(2)]
        ddB = P.bufs(2, "dd")
        dsq = [A.alloc(512, BF16) for _ in range(2)]
        dsqB = P.bufs(2, "dsq")
        rrow = [A.alloc(512, F32) for _ in range(2)]
        rrowB = P.bufs(2, "rrow")
        rrbc = [A.alloc(512, F32) for _ in range(2)]
        rrbcB = P.bufs(2, "rrbc")
        yb = [A.alloc(HW, BF16) for _ in range(2)]
        ybB = P.bufs(2, "yb")
        dbcB = [P.bufs(2, "dbc") for _ in range(2)]
        q7B = P.bufs(4, "ps7q")
        ones32 = self.ones_bf[:, 0:32]
        SB = [(0, 1), (2, 3)]
        dbc = self.d_bc

        def load_kv(h):
            pp = h % 2
            for q4 in range(4):
                P.dma(KTs[pp][:, q4 * 4096:(q4 + 1) * 4096], self.d_KT.ap()[h, :, q4 * 4096:(q4 + 1) * 4096], writes=[kvB[pp][0][q4]])
            for q4 in range(4):
                P.dma(VSs[pp][:, q4 * 32:(q4 + 1) * 32, :], self.d_VS.ap()[h, :, q4 * 32:(q4 + 1) * 32, :], writes=[kvB[pp][1][q4]])

        pending = []
        gcount = [0]
        sbase = [0]

        def flush():
            while pending:
                pending.pop(0)()

        def finalize(ncol, o_ap, obufs, ycols, ybt, ybb, out_dma, nred=1):
            gp = gcount[0] % 2
            gcount[0] += 1
            small = ncol < 128
            P.strict = small
            ss_, rb_, dd_, dq_, rw_, rrb_ = ssum[gp], rbc[gp], dd[gp], dsq[gp], rrow[gp], rrbc[gp]
            if nred == 1:
                self.cp("dve", ss_[0:64, 0:ncol], ps[0:64, 7, 0:ncol], [q7B[0], q7B[1]], [ssumB[gp]])
            else:
                wtot = nred * ncol
                self.cp("dve", ss_[0:64, 0:wtot], ps[0:64, 7, 0:wtot], [q7B[0], q7B[1]], [ssumB[gp]])
                P.strict = True
                if nred == 12:
                    steps = [(4 * ncol, 8 * ncol, 4 * ncol), (4 * ncol, 4 * ncol, 4 * ncol), (2 * ncol, 2 * ncol, 2 * ncol), (ncol, ncol, ncol)]
                else:
                    steps = [(8 * ncol, 8 * ncol, 8 * ncol), (4 * ncol, 4 * ncol, 4 * ncol), (2 * ncol, 2 * ncol, 2 * ncol), (ncol, ncol, ncol)]
                for (wd_, src_, _) in steps:
                    self.tt("dve", ss_[0:64, 0:wd_], ss_[0:64, 0:wd_], ss_[0:64, src_:src_ + wd_], ALU.add, [ssumB[gp]], [ssumB[gp]])
                P.strict = small
            for comp in range(2):
                P.dma(dbc.ap()[gp, comp:comp + 1, 0:ncol], ss_[32 * comp:32 * comp + 1, 0:ncol], reads=[ssumB[gp]], writes=[dbcB[gp][0]])
            P.dma(rb_.rearrange("p (c n) -> p c n", c=2)[:, :, 0:ncol], bass.AP(dbc, gp * 3 * 512, [[0, 128], [512, 2], [1, ncol]]),
                  reads=[dbcB[gp][0]], writes=[rbcB[gp]])
            for comp in range(2):
                r_ = rb_[:, comp * 512:comp * 512 + ncol]
                self.ts("dve", r_, r_, self.tiny1[:, 0:1], None, ALU.add, None, [rbcB[gp], self.cB], [rbcB[gp]])
                self.rcp(r_, r_, [rbcB[gp]], [rbcB[gp]])
                self.tt("dve", dd_[:, comp * 512: comp * 512 + ncol], o_ap(comp), r_, ALU.mult, [obufs[comp], rbcB[gp]], [ddB[gp]])
            self.stt("dve", dd_[:, 0:ncol], dd_[:, 512:512 + ncol], self.neglam[:, 0:1], dd_[:, 0:ncol], ALU.mult, ALU.add, [ddB[gp], self.sB], [ddB[gp]])
            self.tt("pool", dq_[:, 0:ncol], dd_[:, 0:ncol], dd_[:, 0:ncol], ALU.mult, [ddB[gp]], [dsqB[gp]])
            P.strict = False

            def stage1():
                P.strict = small
                self.mm(ps[64:96, 7, 0:ncol], ones32, dq_[:, 0:ncol], True, True, [dsqB[gp], self.cB], [q7B[2]])
                self.act(rw_[64:96, 0:ncol], ps[64:96, 7, 0:ncol], AF.Ln, [q7B[2], self.cB], [rrowB[gp]], bias=self.eps1[64:96, :], scale=1.0 / 128)
                P.strict = True
                self.act(rw_[64:96, 0:ncol], rw_[64:96, 0:ncol], AF.Exp, [rrowB[gp], self.cB], [rrowB[gp]], bias=self.zero1[64:96, :], scale=-0.5)
                P.strict = small
                P.dma(dbc.ap()[gp, 2:3, 0:ncol], rw_[64:65, 0:ncol], reads=[rrowB[gp]], writes=[dbcB[gp][1]])
                P.dma(rrb_[:, 0:ncol], bass.AP(dbc, (gp * 3 + 2) * 512, [[0, 128], [1, ncol]]), reads=[dbcB[gp][1]], writes=[rrbcB[gp]])
                self.stt("dve", ybt[:, ycols:ycols + ncol], dd_[:, 0:ncol], self.gsub[:, 0:1], rrb_[:, 0:ncol], ALU.mult, ALU.mult,
                         [ddB[gp], rrbcB[gp], self.sB], [ybb])
                P.strict = False
                if out_dma is not None:
                    out_dma()
            pending.append(stage1)

        def group(nsteps, qk, av, near_off, ncol, ncols_sum, sum_rhs, pre_done=False, next_qk0=None):
            def pe_tail(t):
                p_, pB_ = pt[t % NPB], ptB[t % NPB]
                for comp in range(2):
                    rl = sum_rhs(p_, comp)
                    for n_, r_ in enumerate(rl):
                        self.mm(ps[32 * comp:32 * comp + 32, 7, 0:ncol], ones32, r_, t == 0 and n_ == 0, t == nsteps - 1 and n_ == len(rl) - 1,
                                [pB_, self.cB], [q7B[comp]])
                av(t, p_, pB_)

            b0 = sbase[0]
            sbase[0] += nsteps
            if not pre_done:
                qk(0, b0 % 2)
            for t in range(nsteps):
                if t + 1 < nsteps:
                    qk(t + 1, (b0 + t + 1) % 2)
                elif next_qk0 is not None:
                    next_qk0((b0 + nsteps) % 2)
                sb = SB[(b0 + t) % 2]
                p_, pB_ = pt[t % NPB], ptB[t % NPB]
                self.act(p_.rearrange("p (c n) -> p c n", c=2), ps[:, sb[0]:sb[0] + 2, :], AF.Exp, [psB[sb[0]], psB[sb[1]], self.cB], [pB_], scale=0.125)
                off = near_off(t)
                if off is not None:
                    for comp in range(2):
                        self.tt("dve", p_[:, comp * 512:(comp + 1) * 512], p_[:, comp * 512:(comp + 1) * 512],
                                self.strip_h[:, off:off + 512], ALU.mult, [pB_, self.stripB], [pB_])
                if t >= 1:
                    pe_tail(t - 1)
                if t == min(10, nsteps - 1):
                    flush()
            pe_tail(nsteps - 1)

        def group_h(nsteps, qk, av, near, batches, HQ, pre_done=False, next_qk0=None):
            def pe_tail(t):
                kb0, nseg = batches[t]
                wd = nseg * HQ
                p_, pB_ = pt[t % NPB], ptB[t % NPB]
                for comp in range(2):
                    self.mm(ps[32 * comp:32 * comp + 32, 7, 0:wd], ones32, p_[:, comp * 512:comp * 512 + wd], t == 0, t == nsteps - 1,
                            [pB_, self.cB], [q7B[comp]])
                av(t, p_, pB_)

            b0 = sbase[0]
            sbase[0] += nsteps
            if not pre_done:
                qk(0, b0 % 2)
            for t in range(nsteps):
                if t + 1 < nsteps:
                    qk(t + 1, (b0 + t + 1) % 2)
                elif next_qk0 is not None:
                    next_qk0((b0 + nsteps) % 2)
                sb = SB[(b0 + t) % 2]
                kb0, nseg = batches[t]
                wd = nseg * HQ
                p_, pB_ = pt[t % NPB], ptB[t % NPB]
                self.act(p_.rearrange("p (c n) -> p c n", c=2)[:, :, 0:wd], ps[:, sb[0]:sb[0] + 2, 0:wd], AF.Exp,
                         [psB[sb[0]], psB[sb[1]], self.cB], [pB_], scale=0.125)
                nr = near(t)
                if nr is not None:
                    c0, ns_, off = nr
                    fa = self.strip_h[:, off:off + ns_ * 128].rearrange("p (u c) -> p u c", c=128)[:, :, 0:HQ]
                    for comp in range(2):
                        pv = p_[:, comp * 512 + c0 * HQ: comp * 512 + (c0 + ns_) * HQ].rearrange("p (u c) -> p u c", c=HQ)
                        self.tt("dve", pv, pv, fa, ALU.mult, [pB_, self.stripB], [pB_])
                if t >= 1:
                    pe_tail(t - 1)
            pe_tail(nsteps - 1)

        def make_slot(h, i, qq):
            pp = h % 2
            KT, VS = KTs[pp], VSs[pp]
            kB, vB = kvB[pp]
            q, qb_ = qt[qq], qB[qq]
            ybt, ybb = yb[qq], ybB[qq]
            strip_h = self.strip[:, h, :]
            nkb = 16 * i + 16
            near0 = 16 * i - 1
            HQ = 32
            batches = [(16 * b, 16) for b in range(i)] + [(16 * i, 12)]
            nb = len(batches)

            def qk(t, par):
                sb = SB[par]
                for comp in range(2):
                    self.mm(ps[:, sb[comp], :], KT[64 * comp:64 * comp + 64, t * 128:(t + 1) * 128], q[64 * comp:64 * comp + 64, 128:640],
                            True, True, [kB[t // 32], qb_], [psB[sb[comp]]])

            def av(t, p_, pB_):
                for comp in range(2):
                    self.mm(ps[:, 4 + comp, :], VS[:, t, :], p_[:, comp * 512:(comp + 1) * 512], t == 0, t == nkb - 1,
                            [vB[t // 32], pB_], [psB[4 + comp]])

            def qkh(t, par):
                sb = SB[par]
                kb0, nseg = batches[t]
                for comp in range(2):
                    for u in range(nseg):
                        kb = kb0 + nseg - 1 - u
                        self.mm(ps[:, sb[comp], u * HQ:(u + 1) * HQ], KT[64 * comp:64 * comp + 64, kb * 128:(kb + 1) * 128],
                                q[64 * comp:64 * comp + 64, 128 - HQ:128], True, True, [kB[kb // 32], qb_], [psB[sb[comp]]])

            def avh(t, p_, pB_):
                kb0, nseg = batches[t]
                for comp in range(2):
                    for u in range(nseg):
                        kb = kb0 + nseg - 1 - u
                        self.mm(ps[:, 6, comp * HQ:(comp + 1) * HQ], VS[:, kb, :], p_[:, comp * 512 + u * HQ: comp * 512 + (u + 1) * HQ],
                                t == 0 and u == 0 and comp == 0, t == nb - 1 and u == nseg - 1, [vB[kb // 32], pB_], [psB[6]])

            def nearh(t):
                kb0, nseg = batches[t]
                if kb0 == 16 * i:
                    return (0, 12, 2048 - 128 * 13 + (128 - HQ))
                if kb0 == 16 * i - 16:
                    return (0, 4, 2048 - 128 + (128 - HQ))
                return None

            def odma():
                P.dma(self.d_YB.ap()[i, :, h, :], ybt, reads=[ybb])

            def run_main(pre_done, next_qk0):
                self.strip_h = strip_h
                group(nkb, qk, av, lambda t: (2048 - 128 * (t - near0)) if t >= near0 else None, 512, 512,
                      lambda p_, comp: [p_[:, comp * 512:(comp + 1) * 512]], pre_done, next_qk0)
                finalize(512, lambda comp: ps[:, 4 + comp, :], [psB[4], psB[5]], 128, ybt, ybb, None)

            def run_halo(pre_done, next_qk0):
                self.strip_h = strip_h
                group_h(nb, qkh, avh, nearh, batches, HQ, pre_done, next_qk0)
                finalize(HQ, lambda comp: ps[:, 6, comp * HQ:(comp + 1) * HQ], [psB[6], psB[6]], 128 - HQ, ybt, ybb, odma,
                         nred=(16 if i > 0 else 12))

            return dict(run_main=run_main, run_halo=run_halo, qk0=lambda par: qk(0, par), qkh0=lambda par: qkh(0, par))

        slots = [(h, i) for h in range(4) for i in range(NSLOT)]
        descs = {}

        def get(k):
            if k not in descs:
                descs[k] = make_slot(slots[k][0], slots[k][1], k % 2)
            return descs[k]

        load_kv(0)
        P.dma(qt[0], self.d_QB.ap()[0, :, 0, :], writes=[qB[0]])
        pre = False
        for k, (h, i) in enumerate(slots):
            if i == 0 and h + 1 < 4:
                load_kv(h + 1)
            if k + 1 < len(slots):
                nh, ni = slots[k + 1]
                P.dma(qt[(k + 1) % 2], self.d_QB.ap()[ni, :, nh, :], writes=[qB[(k + 1) % 2]])
            d = get(k)
            d["run_main"](pre, d["qkh0"])
            nxt = get(k + 1)["qk0"] if k + 1 < len(slots) else None
            d["run_halo"](True, nxt)
            pre = nxt is not None
            descs.pop(k, None)
        flush()
        P.barrier()
        A.release(m)

    def phase4a(self):
        P, A = self.P, self.A
        ps, psB = self.ps, self.psB
        A.release(self.pre_strip_mark)
        m = A.mark()
        win = self.d_win.ap()
        wgl = A.alloc(KC * 2048, BF16).rearrange("p (c n) -> p c n", c=KC)
        wbra = A.alloc(4 * D, BF16).rearrange("p (c n) -> p c n", c=4)
        wbrb = A.alloc(4 * D, BF16).rearrange("p (c n) -> p c n", c=4)
        wo = A.alloc(KC * D, BF16).rearrange("p (c n) -> p c n", c=KC)
        self.load_w(wgl, win[:, :, C_GL:C_GL + 2048], KC, 2048, gain=self.gmix)
        self.load_w(wbra, self.d_wbra.ap(), 4, D)
        self.load_w(wbrb, self.d_wbrb.ap(), 4, D)
        self.load_w(wo, self.d_wo.ap(), KC, D)
        wB = P.buf("w4a")
        xts = [A.alloc(KC * HW, F32).rearrange("p (c n) -> p c n", c=KC) for _ in range(2)]
        xBs = P.bufs(2, "x")
        hTs = [A.alloc(KC * HW, BF16).rearrange("p (c n) -> p c n", c=KC) for _ in range(2)]
        hBs = P.bufs(2, "h")
        yas = [A.alloc(4 * HW, BF16).rearrange("p (c n) -> p c n", c=4) for _ in range(2)]
        ybs = [A.alloc(4 * HW, BF16).rearrange("p (c n) -> p c n", c=4) for _ in range(2)]
        yBs = [P.bufs(2, "y") for _ in range(2)]
        gates = A.alloc(16 * HW, BF16).rearrange("p (c n) -> p c n", c=16)
        gB = P.bufs(16, "g")
        mixed = A.alloc(KC * HW, BF16).rearrange("p (c n) -> p c n", c=KC)
        mxB = P.bufs(KC, "mx")
        t1 = [A.alloc(512, BF16) for _ in range(2)]
        t2 = [A.alloc(512, BF16) for _ in range(2)]
        tB = [P.bufs(2, "t") for _ in range(2)]
        x2 = [A.alloc(512, F32) for _ in range(2)]
        x2B = P.bufs(2, "x2")
        xo = self.d_xo.ap()
        PIECES = ((96, 512), (608, 32))
        cnt = 0
        def loads(i):
            pp = i % 2
            P.dma(hTs[pp], self.d_H.ap()[i], writes=[hBs[pp]])
            P.dma(yas[pp], self.d_YA.ap()[i], writes=[yBs[pp][0]])
            P.dma(ybs[pp], self.d_YB.ap()[i], writes=[yBs[pp][1]])
            P.dma(xts[pp], xo[:, :, i * SLOTW + 128:(i + 1) * SLOTW], writes=[xBs[pp]])

        loads(0)
        for i in range(NSLOT):
            pp = i % 2
            if i + 1 < NSLOT:
                loads(i + 1)
            xt, xB, hT, ya, yb, yB = xts[pp], xBs[pp], hTs[pp], yas[pp], ybs[pp], yBs[pp]
            hB = [hBs[pp]] * KC
            for (o, w) in PIECES:
                for gc in range(16):
                    bk = 1 + gc % 2
                    for c in range(KC):
                        self.mm(ps[:, bk, 0:w], wgl[:, c, gc * 128:(gc + 1) * 128], hT[:, c, o:o + w], c == 0, c == KC - 1, [wB, hB[c]], [psB[bk]])
                    self.act(gates[:, gc, o:o + w], ps[:, bk, 0:w], AF.Sigmoid, [psB[bk], self.cB], [gB[gc]])
            for (o, w) in PIECES:
                for mc in range(KC):
                    k2 = cnt % 2
                    cnt += 1
                    ba, bb = 3 + 2 * k2, 4 + 2 * k2
                    for r in range(4):
                        self.mm(ps[:, ba, 0:w], wbra[:, r, mc * 128:(mc + 1) * 128], ya[:, r, o:o + w], r == 0, r == 3, [wB, yB[0]], [psB[ba]])
                    for r in range(4):
                        self.mm(ps[:, bb, 0:w], wbrb[:, r, mc * 128:(mc + 1) * 128], yb[:, r, o:o + w], r == 0, r == 3, [wB, yB[1]], [psB[bb]])
                    self.tt("dve", t1[k2][:, 0:w], ps[:, ba, 0:w], gates[:, mc, o:o + w], ALU.mult, [psB[ba], gB[mc]], [tB[k2][0]])
                    self.tt("dve", t2[k2][:, 0:w], ps[:, bb, 0:w], gates[:, 8 + mc, o:o + w], ALU.mult, [psB[bb], gB[8 + mc]], [tB[k2][1]])
                    self.tt("pool", mixed[:, mc, o:o + w], t1[k2][:, 0:w], t2[k2][:, 0:w], ALU.add, [tB[k2][0], tB[k2][1]], [mxB[mc]])
            for (o, w) in PIECES:
                for oc in range(KC):
                    k2 = cnt % 2
                    cnt += 1
                    bk = 1 + k2
                    for mc in range(KC):
                        self.mm(ps[:, bk, 0:w], wo[:, mc, oc * 128:(oc + 1) * 128], mixed[:, mc, o:o + w], mc == 0, mc == KC - 1, [wB, mxB[mc]], [psB[bk]])
                    self.tt("dve", x2[k2][:, 0:w], ps[:, bk, 0:w], xt[:, oc, o:o + w], ALU.add, [psB[bk], xB], [x2B[k2]])
                    P.dma(self.d_X2.ap()[i, :, oc, o:o + w], x2[k2][:, 0:w], reads=[x2B[k2]], key=f"x2o{k2}", eng=STORE_ENG)
        P.barrier()
        A.release(m)

    def phase4b(self):
        P, A = self.P, self.A
        ps, psB = self.ps, self.psB
        A.release(self.pre_strip_mark)
        m = A.mark()
        wup = A.alloc(KC * 2 * DFF, BF16).rearrange("p (c n) -> p c n", c=KC)
        wdn = A.alloc(NFC * D, BF16).rearrange("p (c n) -> p c n", c=NFC)
        self.load_w(wup, self.d_wup.ap(), KC, 2 * DFF, gain=self.gffn)
        self.load_w(wdn, self.d_wdn.ap(), NFC, D)
        wB = P.buf("w4b")
        W2 = 514
        xa_off = A.alloc_raw(NFC * 512 * 2)
        xaB = P.buf("xa")
        sq = [A.alloc(W2, BF16) for _ in range(2)]
        sqB = P.bufs(2, "sq")
        hT = A.alloc(KC * W2, BF16).rearrange("p (c n) -> p c n", c=KC)
        hB = P.bufs(KC, "h")
        rstd = A.alloc(W2, F32)
        rB = P.buf("r")
        uh = A.alloc(44 * 2, F32).rearrange("p (c n) -> p c n", c=44)
        uhB = P.buf("uh")
        U = [A.alloc(W2, F32) for _ in range(4)]
        UB = P.bufs(4, "U")
        tg = [A.alloc(512, F32) for _ in range(3)]
        tv = [A.alloc(512, F32) for _ in range(3)]
        tgB = P.bufs(3, "tg")
        tvB = P.bufs(3, "tv")
        ot = [A.alloc(512, F32) for _ in range(2)]
        otB = P.bufs(2, "ot")
        xr = [A.alloc(512, F32) for _ in range(2)]
        xrB = P.bufs(2, "xr")
        cw = self.cw.rearrange("p (a b) -> p a b", a=3)
        outT = self.d_out.ap()
        xt = A.at(xa_off, KC * W2, F32).rearrange("p (c n) -> p c n", c=KC)
        aT = A.at(xa_off, NFC * 512, BF16).rearrange("p (c n) -> p c n", c=NFC)
        for i in range(NSLOT):
            P.dma(xt, self.d_X2.ap()[i, :, :, 126:640], writes=[xaB])
            self.norm_tile(xt, xaB, W2, sq, sqB, hT, hB, rstd, rB, [(0, 2, 7), (2, 512, 0)])
            for fc in range(44):
                for c in range(KC):
                    self.mm(ps[:, 7, fc * 2:fc * 2 + 2], wup[:, c, fc * 128:(fc + 1) * 128], hT[:, c, 0:2], c == 0, c == KC - 1, [wB, hB[c]], [psB[7]])
            self.cp("dve", uh, ps[:, 7, 0:88].rearrange("p (c n) -> p c n", c=44), [psB[7]], [uhB])
            def gate_tail(f):
                k2 = f % 3
                self.act(tg[k2], tg[k2], AF.Silu, [tgB[k2], self.cB], [tgB[k2]])
                self.tt("pool", aT[:, f, :], tg[k2], tv[k2], ALU.mult, [tgB[k2], tvB[k2]], [xaB])

            for f in range(NFC):
                k2 = f % 2
                k3 = f % 3
                for half, fc in enumerate((f, NFC + f)):
                    bk = 1 + 2 * k2 + half
                    ui = 2 * k2 + half
                    for c in range(KC):
                        self.mm(ps[:, bk, :], wup[:, c, fc * 128:(fc + 1) * 128], hT[:, c, 2:W2], c == 0, c == KC - 1, [wB, hB[c]], [psB[bk]])
                    self.cp("pool", U[ui][:, 0:2], uh[:, fc, :], [uhB], [UB[ui]])
                    self.cp("act", U[ui][:, 2:W2], ps[:, bk, :], [psB[bk]], [UB[ui]])
                    t_, tb_ = (tg[k3], tgB[k3]) if half == 0 else (tv[k3], tvB[k3])
                    eng = "dve"
                    self.ts(eng, t_, U[ui][:, 0:512], cw[:, 0, fc:fc + 1], self.cb[:, fc:fc + 1], ALU.mult, ALU.add, [UB[ui], self.sB], [tb_])
                    self.stt(eng, t_, U[ui][:, 1:513], cw[:, 1, fc:fc + 1], t_, ALU.mult, ALU.add, [UB[ui], self.sB, tb_], [tb_])
                    self.stt(eng, t_, U[ui][:, 2:514], cw[:, 2, fc:fc + 1], t_, ALU.mult, ALU.add, [UB[ui], self.sB, tb_], [tb_])
                if f >= 1:
                    gate_tail(f - 1)
            gate_tail(NFC - 1)
            for oc in range(KC):
                k2 = oc % 2
                bk = 5 + k2
                P.dma(xr[k2], self.d_X2.ap()[i, :, oc, 128:640], writes=[xrB[k2]])
                for f in range(NFC):
                    self.mm(ps[:, bk, :], wdn[:, f, oc * 128:(oc + 1) * 128], aT[:, f, :], f == 0, f == NFC - 1, [wB, xaB], [psB[bk]])
                self.tt("dve", ot[k2], ps[:, bk, :], xr[k2], ALU.add, [psB[bk], xrB[k2]], [otB[k2]])
                P.dma(outT[:, oc, i * 512:(i + 1) * 512], ot[k2], reads=[otB[k2]], eng=STORE_ENG)
        P.barrier()
        A.release(m)

    def build(self, nph=N_PHASES):
        self.declare()
        self.P.strict = True
        self.setup()
        self.P.strict = False
        self.P.barrier()
        phases = [self.phase1, self.phase2, self.phase3, self.phase4a, self.phase4b]
        for n_, ph in enumerate(phases[:nph]):
            if n_ == 0 and SKIP_P1:
                continue
            ph()
        self.P.barrier()
        self.P.emit()
        return self.nc


def _host_inputs(inputs):
    f = np.float32
    x = np.asarray(inputs["x"], dtype=f)

    def pc(w, kc):
        n = w.shape[1]
        return np.ascontiguousarray(w.reshape(kc, 128, n).transpose(1, 0, 2))

    w_br_a = np.asarray(inputs["w_br_a"][0], dtype=f)
    wa = w_br_a.reshape(2, 4, 64, D)
    wbra = np.ascontiguousarray(wa.transpose(0, 2, 1, 3).reshape(128, 4, D))
    common = {
        "w_in": pc(np.asarray(inputs["w_in"][0], dtype=f), KC),
        "w_br_a": wbra,
        "w_br_b": pc(np.asarray(inputs["w_br_b"][0], dtype=f), 4),
        "w_o": pc(np.asarray(inputs["w_o"][0], dtype=f), KC),
        "w_up": pc(np.asarray(inputs["w_up"][0], dtype=f), KC),
        "w_down": pc(np.asarray(inputs["w_down"][0], dtype=f), NFC),
        "g_mix": np.ascontiguousarray(np.asarray(inputs["g_mix"][0], dtype=f).reshape(KC, 128).T),
        "g_ffn": np.ascontiguousarray(np.asarray(inputs["g_ffn"][0], dtype=f).reshape(KC, 128).T),
        "conv_w": np.ascontiguousarray(np.asarray(inputs["conv_w"][0], dtype=f).reshape(3, 44, 128).transpose(2, 0, 1)),
        "conv_b": np.ascontiguousarray(np.asarray(inputs["conv_b"][0], dtype=f).reshape(44, 128).T),
        "rel_bias": np.ascontiguousarray(np.asarray(inputs["rel_bias"], dtype=f)),
        "qn_a": np.asarray(inputs["qn_a"], dtype=f).reshape(1, 64),
        "kn_a": np.asarray(inputs["kn_a"], dtype=f).reshape(1, 64),
        "qn_b": np.asarray(inputs["qn_b"], dtype=f).reshape(1, 64),
        "kn_b": np.asarray(inputs["kn_b"], dtype=f).reshape(1, 64),
        "sinks": np.asarray(inputs["sinks"], dtype=f).reshape(1, 8),
        "lam_q1": np.asarray(inputs["lam_q1"], dtype=f).reshape(1, 64),
        "lam_k1": np.asarray(inputs["lam_k1"], dtype=f).reshape(1, 64),
        "lam_q2": np.asarray(inputs["lam_q2"], dtype=f).reshape(1, 64),
        "lam_k2": np.asarray(inputs["lam_k2"], dtype=f).reshape(1, 64),
        "subln_b": np.asarray(inputs["subln_b"], dtype=f).reshape(128, 1),
        "Jmat": np.ascontiguousarray(np.eye(128, dtype=f)[::-1]),
        "bdones": np.kron(np.eye(2, dtype=f), np.ones((64, 64), dtype=f)),
    }
    oha = np.zeros((33, 384), dtype=f)
    for mm_ in range(384):
        d = mm_ - 127
        if 0 <= d < 128:
            oha[int(_t5_bucket_np(np.array(d))), mm_] = 1
        else:
            oha[32, mm_] = 1
    common["oh_a"] = oha
    in_maps = []
    for core in range(8):
        b, j = core // 4, core % 4
        xTb = np.ascontiguousarray(x[b].T.reshape(KC, 128, S).transpose(1, 0, 2))
        xo = np.zeros((128, KC, NSLOT, SLOTW), dtype=f)
        for i in range(NSLOT):
            G = 4 * i + j
            t0 = 512 * G - 256
            lo = max(t0, 0)
            xo[:, :, i, lo - t0:] = xTb[:, :, lo:t0 + SLOTW]
        ohd = np.zeros((33, VECD), dtype=f)
        d = np.arange(VECD) + 512 * j - 2047
        bk = _t5_bucket_np(d)
        for mm_ in range(VECD):
            if d[mm_] >= 0:
                ohd[bk[mm_], mm_] = 1
            else:
                ohd[32, mm_] = 1
        mp = dict(common)
        mp["xT"] = xTb
        mp["xo"] = xo.reshape(128, KC, NSLOT * SLOTW)
        mp["oh_d"] = ohd
        mp["m0"] = np.full((128, 1), 0.0 if j == 0 else 1.0, dtype=f)
        in_maps.append(mp)
    return in_maps


_NC_CACHE = {}


def kernel(**inputs):
    in_maps = _host_inputs(inputs)
    if "nc" not in _NC_CACHE:
        _NC_CACHE["nc"] = K().build()
    nc = _NC_CACHE["nc"]
    res = run_bass_kernel_spmd(nc, in_maps, core_ids=list(range(8)))
    out = np.zeros((2, S, D), dtype=np.float32)
    for core in range(8):
        b, j = core // 4, core % 4
        o = res.results[core]["outT"].reshape(128, KC, NSLOT, 512)
        for i in range(NSLOT):
            G = 4 * i + j
            out[b, 512 * G:512 * (G + 1), :] = o[:, :, i, :].transpose(2, 1, 0).reshape(512, D)
    return out
```

```python
import contextlib
import math
import numpy as np
import concourse.bass as bass
import concourse.mybir as mybir
from concourse.bass_utils import run_bass_kernel_spmd

F32 = mybir.dt.float32
BF16 = mybir.dt.bfloat16
AF = mybir.ActivationFunctionType
ALU = mybir.AluOpType

ENGS = ("pe", "act", "dve", "pool", "sp")
SEM_ROT = 3000


class Buf:
    __slots__ = ("name", "w", "rs")

    def __init__(self, name):
        self.name = name
        self.w = None
        self.rs = []


class Op:
    __slots__ = ("eng", "fn", "waits", "signal", "dma_key", "dma_sem", "dma_val", "sig_sem", "sig_val")

    def __init__(self, eng, fn, dma_key=None):
        self.eng = eng
        self.fn = fn
        self.waits = []
        self.signal = False
        self.dma_key = dma_key
        self.dma_sem = None
        self.dma_val = None
        self.sig_sem = None
        self.sig_val = None


class Prog:
    def __init__(self, nc):
        self.nc = nc
        self.ops = {e: [] for e in ENGS}
        self.dma_cnt = {}
        self.all_dma_last = {}
        self.stack = contextlib.ExitStack()
        self.nbufs = 0
        self.strict = False

    def buf(self, name=None):
        self.nbufs += 1
        return Buf(f"{name or 'b'}#{self.nbufs}")

    def bufs(self, n, name="b"):
        return [self.buf(f"{name}{i}") for i in range(n)]

    def _dep(self, op, y):
        if y is None or y is op:
            return
        if y.dma_key is None and y.eng == op.eng and op.dma_key is None and not self.strict:
            return
        if y.dma_key is None:
            y.signal = True
        if y not in op.waits:
            op.waits.append(y)

    def op(self, eng, fn, reads=(), writes=(), dma_key=None):
        o = Op(eng, fn, dma_key)
        for b in reads:
            self._dep(o, b.w)
        for b in writes:
            self._dep(o, b.w)
            for r in b.rs:
                self._dep(o, r)
        for b in reads:
            b.rs.append(o)
        for b in writes:
            b.w = o
            b.rs = []
        if dma_key is not None:
            st = self.dma_cnt.setdefault(dma_key, [0, 0])
            if st[1] + 16 > 4000:
                st[0] += 1
                st[1] = 0
            st[1] += 16
            o.dma_sem = (dma_key, st[0])
            o.dma_val = st[1]
            self.all_dma_last[dma_key] = o
        self.ops[eng].append(o)
        return o

    def dma(self, out, in_, reads=(), writes=(), key=None, eng="sp", **kw):
        prim = writes[0] if len(writes) else reads[0]
        key = key if (key is not None and key.startswith("SH_")) else prim.name
        return self.op(eng, lambda e: e.dma_start(out=out, in_=in_, **kw), reads, writes, dma_key=key)

    def barrier(self):
        lasts = [self.ops[e][-1] for e in ENGS if self.ops[e]]
        dmas = list(self.all_dma_last.values())
        news = []
        for e in ENGS:
            o = Op(e, None)
            for y in lasts:
                self._dep(o, y)
            for y in dmas:
                self._dep(o, y)
            news.append(o)
        for o in news:
            self.ops[o.eng].append(o)

    def emit(self):
        nc = self.nc
        semkeys = set()
        for e in ENGS:
            gen, cnt = 0, 0
            for o in self.ops[e]:
                if o.dma_key is not None:
                    semkeys.add(o.dma_sem)
                    continue
                if o.signal:
                    if cnt >= SEM_ROT:
                        gen += 1
                        cnt = 0
                    cnt += 1
                    o.sig_sem = ("eng", e, gen)
                    o.sig_val = cnt
                    semkeys.add(o.sig_sem)
        sems = {}
        for n, k in enumerate(sorted(semkeys, key=str)):
            sems[k] = self.stack.enter_context(nc.semaphore(f"sm{n}"))
        self.nsems = len(sems)
        block = self.stack.enter_context(nc.Block())
        engmap = {"pe": "tensor", "act": "scalar", "dve": "vector", "pool": "gpsimd", "sp": "sync"}

        def make(e):
            def body(eng):
                waited = {}
                for o in self.ops[e]:
                    for y in o.waits:
                        if y.dma_key is not None:
                            sk, v = y.dma_sem, y.dma_val
                        else:
                            sk, v = y.sig_sem, y.sig_val
                        if waited.get(sk, 0) >= v:
                            continue
                        waited[sk] = v
                        eng.wait_ge(sems[sk], v)
                    if o.fn is None:
                        if o.signal:
                            eng.nop().then_inc(sems[o.sig_sem], 1)
                        continue
                    ins = o.fn(eng)
                    if o.dma_key is not None:
                        ins.then_inc(sems[o.dma_sem], 16)
                    elif o.signal:
                        ins.then_inc(sems[o.sig_sem], 1)
            return body

        for e in ENGS:
            getattr(block, engmap[e])(make(e))
        self.stack.close()


class Arena:
    def __init__(self, prog, nbytes, name="arena"):
        nc = prog.nc
        self.t8 = prog.stack.enter_context(nc.sbuf_tensor(name, [128, nbytes], mybir.dt.uint8))
        self.views = {}
        self.nbytes = nbytes
        self.off = 0

    def view(self, dt):
        if dt not in self.views:
            self.views[dt] = self.t8.bitcast(dt)
        return self.views[dt]

    def alloc(self, nelem, dt):
        sz = mybir.dt.size(dt)
        self.off = (self.off + 63) // 64 * 64
        o = self.off
        self.off += nelem * sz
        assert self.off <= self.nbytes, f"arena overflow {self.off} > {self.nbytes}"
        return self.view(dt)[:, o // sz: o // sz + nelem]

    def alloc_raw(self, nbytes):
        self.off = (self.off + 63) // 64 * 64
        o = self.off
        self.off += nbytes
        assert self.off <= self.nbytes, f"arena overflow {self.off} > {self.nbytes}"
        return o

    def at(self, o, nelem, dt):
        sz = mybir.dt.size(dt)
        return self.view(dt)[:, o // sz: o // sz + nelem]

    def mark(self):
        return self.off

    def release(self, m):
        self.off = m


D = 1024
S = 16384
KC = 8
NSLOT = 8
SLOTW = 768
HW = 640
DFF = 2816
NFC = 22
EPS = 1e-6
LAM_INIT = 0.8 - 0.6 * math.exp(-0.3 * 0)
VECD = 2688
STRIPW = 2560
C_QA, C_KA, C_VA, C_QB, C_KB, C_VB, C_GL = 0, 512, 640, 768, 1280, 1792, 2304

DEBUG_SCRATCH = False
STORE_ENG = "pool"
P2_STAGE = 9
P2_SLOTS = 8
SKIP_P1 = False
N_PHASES = 5


def _t5_bucket_np(rel):
    n = np.maximum(rel, 0)
    nf = np.maximum(n, 1).astype(np.float32)
    large = 16 + (np.log(nf / np.float32(16)) / np.float32(math.log(8.0)) * np.float32(16)).astype(np.int32)
    large = np.minimum(large, 31)
    return np.where(n < 16, n, large)


class K:
    def __init__(self):
        nc = bass.Bass("TRN2", target_bir_lowering=False)
        self.nc = nc
        self.P = Prog(nc)
        self.A = Arena(self.P, 209920)
        self.ps = self.P.stack.enter_context(nc.psum_tensor("ps", [128, 8, 512], F32))
        self.psB = self.P.bufs(8, "psb")

    def mm(self, out, lhsT, rhs, start, stop, R, W):
        self.P.op("pe", lambda e: e.matmul(out, lhsT=lhsT, rhs=rhs, start=start, stop=stop), R, W)

    def act(self, out, in_, func, R, W, bias=None, scale=1.0):
        b = self.zero1 if bias is None else bias
        npart = out.shape[0]
        if npart != 128 and b.shape[0] == 128:
            b = b[0:npart, :]
        self.P.op("act", lambda e: e.activation(out=out, in_=in_, func=func, bias=b, scale=scale), R, W)

    def tt(self, eng, out, a, b, op, R, W):
        self.P.op(eng, lambda e: e.tensor_tensor(out=out, in0=a, in1=b, op=op), R, W)

    def ts(self, eng, out, a, s1, s2, op0, op1, R, W):
        if op1 is None:
            self.P.op(eng, lambda e: e.tensor_scalar(out=out, in0=a, scalar1=s1, scalar2=None, op0=op0), R, W)
        else:
            self.P.op(eng, lambda e: e.tensor_scalar(out=out, in0=a, scalar1=s1, scalar2=s2, op0=op0, op1=op1), R, W)

    def stt(self, eng, out, a, s, b, op0, op1, R, W):
        self.P.op(eng, lambda e: e.scalar_tensor_tensor(out=out, in0=a, scalar=s, in1=b, op0=op0, op1=op1), R, W)

    def cp(self, eng, out, in_, R, W):
        if eng == "act":
            self.act(out, in_, AF.Identity, R, W)
        else:
            self.P.op(eng, lambda e: e.tensor_copy(out=out, in_=in_), R, W)

    def rcp(self, out, in_, R, W):
        self.P.op("dve", lambda e: e.reciprocal(out=out, in_=in_), R, W)

    def rsqrt_act(self, out, in_, scale, R, W, power=-0.5, bias=None):
        self.act(out, in_, AF.Ln, R, W, bias=self.eps1 if bias is None else bias, scale=scale)
        old = self.P.strict
        if out.shape[-1] <= 256:
            self.P.strict = True
        self.act(out, out, AF.Exp, list(W) + [self.cB], W, scale=power)
        self.P.strict = old

    def memset(self, eng, ap, val, W):
        self.P.op(eng, lambda e: e.memset(ap, val), (), W)

    def declare(self):
        nc = self.nc

        def din(name, shape, dt=F32):
            return nc.dram_tensor(name, list(shape), dt, kind="ExternalInput")

        def dscr(name, shape, dt):
            return nc.dram_tensor(name, list(shape), dt, kind="ExternalOutput" if DEBUG_SCRATCH else "Internal")

        self.d_xT = din("xT", [128, KC, S])
        self.d_xo = din("xo", [128, KC, NSLOT * SLOTW])
        self.d_win = din("w_in", [128, KC, 4352])
        self.d_wbra = din("w_br_a", [128, 4, D])
        self.d_wbrb = din("w_br_b", [128, 4, D])
        self.d_wo = din("w_o", [128, KC, D])
        self.d_wup = din("w_up", [128, KC, 2 * DFF])
        self.d_wdn = din("w_down", [128, NFC, D])
        self.d_gmix = din("g_mix", [128, KC])
        self.d_gffn = din("g_ffn", [128, KC])
        self.d_cw = din("conv_w", [128, 3, 44])
        self.d_cb = din("conv_b", [128, 44])
        self.d_relb = din("rel_bias", [32, 12])
        self.d_qna = din("qn_a", [1, 64])
        self.d_kna = din("kn_a", [1, 64])
        self.d_qnb = din("qn_b", [1, 64])
        self.d_knb = din("kn_b", [1, 64])
        self.d_sinks = din("sinks", [1, 8])
        self.d_lq1 = din("lam_q1", [1, 64])
        self.d_lk1 = din("lam_k1", [1, 64])
        self.d_lq2 = din("lam_q2", [1, 64])
        self.d_lk2 = din("lam_k2", [1, 64])
        self.d_subln = din("subln_b", [128, 1])
        self.d_oha = din("oh_a", [33, 384])
        self.d_ohd = din("oh_d", [33, VECD])
        self.d_m0 = din("m0", [128, 1])
        self.d_J = din("Jmat", [128, 128])
        self.d_bd = din("bdones", [128, 128])
        self.d_out = nc.dram_tensor("outT", [128, KC, NSLOT * 512], F32, kind="ExternalOutput")
        self.d_KT = dscr("s_KT", [4, 128, S], BF16)
        self.d_VS = dscr("s_VS", [4, 128, 128, 128], BF16)
        self.d_QB = dscr("s_QB", [NSLOT, 128, 4, HW], BF16)
        self.d_YA = dscr("s_YA", [NSLOT, 128, 4, HW], BF16)
        self.d_YB = dscr("s_YB", [NSLOT, 128, 4, HW], BF16)
        self.d_X2 = dscr("s_X2", [NSLOT, 128, KC, HW], F32)
        self.d_H = dscr("s_H", [NSLOT, 128, KC, HW], BF16)
        if DEBUG_SCRATCH:
            self.d_dstrip = dscr("s_strip", [128, 4 * STRIPW], BF16)
            self.d_dsa = dscr("s_sa", [128, 2 * 8 * 128], BF16)
            self.d_dsmall = dscr("s_small", [128, 8], F32)
        self.d_bc = nc.dram_tensor("s_bc", [2, 3, 512], F32, kind="Internal")
        self.d_VA = dscr("s_VECA", [8, 384], F32)
        self.d_VD = dscr("s_VECD", [4, VECD], F32)

    def setup(self):
        P, A, nc = self.P, self.A, self.nc
        ps, psB = self.ps, self.psB
        cB = P.buf("consts")
        self.cB = cB
        self.zero1 = A.alloc(1, F32)
        self.eps1 = A.alloc(1, F32)
        self.tiny1 = A.alloc(1, F32)
        self.ones_bf = A.alloc(128, BF16)
        self.ones_f = A.alloc(128, F32)
        self.bd_bf = A.alloc(128, BF16)
        self.J = A.alloc(128, F32)
        bd_f = A.alloc(128, F32)
        self.memset("pool", self.zero1, 0.0, [cB])
        self.memset("pool", self.eps1, EPS, [cB])
        self.memset("pool", self.tiny1, 1e-30, [cB])
        self.memset("pool", self.ones_bf, 1.0, [cB])
        self.memset("pool", self.ones_f, 1.0, [cB])
        jB = P.buf("J")
        P.dma(self.J, self.d_J.ap(), writes=[jB], key="c0")
        P.dma(bd_f, self.d_bd.ap(), writes=[jB], key="c0")
        self.cp("dve", self.bd_bf, bd_f, [jB], [cB])
        sB = P.buf("small")
        self.gmix = A.alloc(KC, F32)
        self.gffn = A.alloc(KC, F32)
        self.cw = A.alloc(3 * 44, F32)
        self.cb = A.alloc(44, F32)
        self.m0 = A.alloc(1, F32)
        P.dma(self.gmix, self.d_gmix.ap(), writes=[sB], key="c1")
        P.dma(self.gffn, self.d_gffn.ap(), writes=[sB], key="c1")
        P.dma(self.cw, self.d_cw.ap().rearrange("p a b -> p (a b)"), writes=[sB], key="c1")
        P.dma(self.cb, self.d_cb.ap(), writes=[sB], key="c1")
        P.dma(self.m0, self.d_m0.ap(), writes=[sB], key="c1")
        self.gq_a = A.alloc(1, F32)
        self.gk_a = A.alloc(1, F32)
        self.gq_b = A.alloc(1, F32)
        self.gk_b = A.alloc(1, F32)
        for dst, src in ((self.gq_a, self.d_qna), (self.gk_a, self.d_kna), (self.gq_b, self.d_qnb), (self.gk_b, self.d_knb)):
            for hlf in range(2):
                P.dma(dst[64 * hlf:64 * hlf + 64, :], bass.AP(src, 0, [[1, 64], [1, 1]]), writes=[sB], key="c1")
        self.gsub = A.alloc(1, F32)
        P.dma(self.gsub, self.d_subln.ap(), writes=[sB], key="c1")
        self.ts("dve", self.gsub, self.gsub, 1.0 - LAM_INIT, None, ALU.mult, None, [sB], [sB])
        lam4 = A.alloc(4 * 64, F32)
        for n_, src in enumerate((self.d_lq1, self.d_lk1, self.d_lq2, self.d_lk2)):
            P.dma(lam4[:, n_ * 64:(n_ + 1) * 64], bass.AP(src, 0, [[0, 128], [1, 64]]), writes=[sB], key="c1")
        lp = A.alloc(2 * 64, F32)
        ls = A.alloc(2, F32)
        self.neglam = A.alloc(1, F32)
        self.tt("dve", lp[:, 0:64], lam4[:, 0:64], lam4[:, 64:128], ALU.mult, [sB], [sB])
        self.tt("dve", lp[:, 64:128], lam4[:, 128:192], lam4[:, 192:256], ALU.mult, [sB], [sB])
        P.op("dve", lambda e: e.reduce_sum(out=ls[:, 0:1], in_=lp[:, 0:64], axis=mybir.AxisListType.X), [sB], [sB])
        P.op("dve", lambda e: e.reduce_sum(out=ls[:, 1:2], in_=lp[:, 64:128], axis=mybir.AxisListType.X), [sB], [sB])
        self.act(ls, ls, AF.Exp, [sB, cB], [sB])
        self.tt("dve", self.neglam, ls[:, 1:2], ls[:, 0:1], ALU.subtract, [sB], [sB])
        self.ts("dve", self.neglam, self.neglam, -LAM_INIT, None, ALU.add, None, [sB], [sB])
        self.sB = sB
        self.esrow = A.alloc(2 * 512, F32)
        sk = A.alloc(8, F32)
        P.dma(sk[0:1, :], self.d_sinks.ap(), writes=[sB], key="c1")
        self.act(sk[0:1, :], sk[0:1, :], AF.Exp, [sB, cB], [sB])
        for hq in range(8):
            g_, r_ = hq // 4, hq % 4
            col = g_ * 512 + ((r_ % 2) * 2 + r_ // 2) * 128
            self.ts("dve", self.esrow[0:1, col:col + 128], self.ones_f[0:1, 0:128], sk[0:1, hq:hq + 1], None, ALU.mult, None, [sB, cB], [sB])

        tB = P.buf("tab")
        tabp = A.alloc(12, F32)
        tab31 = A.alloc(4, F32)
        self.memset("pool", tabp[32:33, :], -30000.0, [tB])
        P.dma(tabp[0:32, :], self.d_relb.ap(), writes=[tB], key="c2")
        P.dma(tab31[0:32, :], bass.AP(self.d_relb, 31 * 12 + 8, [[0, 32], [1, 4]]), writes=[tB], key="c2")
        self.tt("dve", tabp[0:32, 8:12], tabp[0:32, 8:12], tab31[0:32, :], ALU.subtract, [tB], [tB])
        m = A.mark()
        oha = A.alloc(384, F32)
        ohd = A.alloc(VECD, F32)
        veca = A.alloc(384, F32)
        vecd = A.alloc(VECD, F32)
        ohB = P.buf("oh")
        P.dma(oha[0:33, :], self.d_oha.ap(), writes=[ohB], key="c2")
        P.dma(ohd[0:33, :], self.d_ohd.ap(), writes=[ohB], key="c2")
        vB = P.buf("vec")
        self.mm(ps[0:8, 0, 0:384], tabp[0:33, 0:8], oha[0:33, :], True, True, [tB, ohB], [psB[0]])
        self.act(veca[0:8, :], ps[0:8, 0, 0:384], AF.Exp, [psB[0], cB], [vB])
        for pc in range(6):
            w = 512 if pc < 5 else VECD - 2560
            bk = 1 + pc % 2
            self.mm(ps[0:4, bk, 0:w], tabp[0:33, 8:12], ohd[0:33, pc * 512: pc * 512 + w], True, True, [tB, ohB], [psB[bk]])
            self.act(vecd[0:4, pc * 512: pc * 512 + w], ps[0:4, bk, 0:w], AF.Exp, [psB[bk], cB], [vB])
        dvB = P.buf("dvec")
        P.dma(self.d_VA.ap(), veca[0:8, :], reads=[vB], writes=[dvB], key="c3")
        P.dma(self.d_VD.ap(), vecd[0:4, :], reads=[vB], writes=[dvB], key="c3")
        A.release(m)
        self.pre_strip_mark = A.mark()
        self.strip = A.alloc(4 * STRIPW, BF16).rearrange("p (h u) -> p h u", h=4)
        self.sa = A.alloc(2 * 8 * 128, BF16).rearrange("p (t h q) -> p t h q", t=2, h=8)
        self.stripB = P.buf("strip")
        self.persist_mark = A.mark()
        m = A.mark()
        rev = A.alloc(STRIPW, F32)
        revB = P.bufs(2, "rev")
        P.dma(rev[:, 0:2048].rearrange("p (h u) -> p h u", h=8), bass.AP(self.d_VA, 0, [[1, 128], [384, 8], [1, 256]]),
              reads=[dvB], writes=[revB[0]], key="c4")
        for pi in range(4):
            bk = pi % 2
            self.mm(ps[:, bk, :], self.J, rev[:, pi * 512:(pi + 1) * 512], True, True, [jB, revB[0]], [psB[bk]])
            src = ps[:, bk, :].rearrange("p (h t q) -> p h t q", h=2, t=2)
            for ty in range(2):
                self.cp("dve" if ty == 0 else "act", self.sa[:, ty, 2 * pi:2 * pi + 2, :], src[:, :, ty, :], [psB[bk]], [self.stripB])
        for h in range(4):
            rb = revB[(h + 1) % 2]
            P.dma(rev, bass.AP(self.d_VD, h * VECD, [[1, 128], [1, STRIPW]]), reads=[dvB], writes=[revB[0], revB[1]], key="c4")
            for pc in range(5):
                bk = pc % 2
                self.mm(ps[:, bk, :], self.J, rev[:, pc * 512:(pc + 1) * 512], True, True, [jB, revB[0], revB[1]], [psB[bk]])
                self.cp("dve" if pc % 2 == 0 else "act", self.strip[:, h, pc * 512:(pc + 1) * 512], ps[:, bk, :], [psB[bk]], [self.stripB])
        A.release(m)
        if DEBUG_SCRATCH:
            P.dma(self.d_dstrip.ap(), self.strip.rearrange("p h u -> p (h u)"), reads=[self.stripB])
            P.dma(self.d_dsa.ap(), self.sa.rearrange("p t h q -> p (t h q)"), reads=[self.stripB])
            for n_, t_ in enumerate((self.neglam, self.gsub, self.gq_b, self.gk_b)):
                P.dma(self.d_dsmall.ap()[:, n_:n_ + 1], t_, reads=[self.sB], allow_slow_non_contiguous=True)

    def load_w(self, dst, src, kc, ncols, gain=None, dup=None, key="w"):
        P, A = self.P, self.A
        m = A.mark()
        pw = 512 if kc <= 8 else 192
        stg = [A.alloc(kc * pw, F32).rearrange("p (c n) -> p c n", c=kc) for _ in range(2)]
        sb = P.bufs(2, "stg")
        wB = P.buf("wdst")
        engs = ("dve", "act")
        n = 0
        for i, c0 in enumerate(range(0, ncols, pw)):
            w = min(pw, ncols - c0)
            s, b = stg[i % 2], sb[i % 2]
            P.dma(s[:, :, 0:w], src[:, :, c0:c0 + w], writes=[b], key=f"SH_stg{i % 2}")
            for c in range(kc):
                eng = engs[n % 2]
                n += 1
                if gain is None:
                    self.cp(eng, dst[:, c, c0:c0 + w], s[:, c, 0:w], [b], [wB])
                elif eng == "act":
                    self.act(dst[:, c, c0:c0 + w], s[:, c, 0:w], AF.Identity, [b, self.sB, self.cB], [wB], scale=gain[:, c:c + 1])
                else:
                    self.ts(eng, dst[:, c, c0:c0 + w], s[:, c, 0:w], gain[:, c:c + 1], None, ALU.mult, None, [b, self.sB], [wB])
        self.P.barrier()
        A.release(m)
        return wB

    def norm_tile(self, xt, xB, n, sq, sqB, hT, hB, rstd, rB, pieces):
        ps, psB = self.ps, self.psB
        for c in range(KC):
            self.act(sq[c % 2][:, 0:n], xt[:, c, :], AF.Square, [xB, self.cB], [sqB[c % 2]])
            for (o, w, bank) in pieces:
                self.mm(ps[:, bank, 0:w], self.ones_bf, sq[c % 2][:, o:o + w], c == 0, c == KC - 1, [sqB[c % 2], self.cB], [psB[bank]])
        for (o, w, bank) in pieces:
            small = w < 128
            self.P.strict = small
            self.rsqrt_act(rstd[:, o:o + w], ps[:, bank, 0:w], 1.0 / D, [psB[bank], self.cB], [rB])
            self.P.strict = False
        for c in range(KC):
            self.tt("dve" if c % 2 == 0 else "pool", hT[:, c, 0:n], xt[:, c, :], rstd[:, 0:n], ALU.mult, [xB, rB], [hB[c]])

    def ph_tasks(self, tasks, hT, hB, wB, tmp, tmpB, pbanks, nbanks, mid=None):
        ps, psB = self.ps, self.psB
        n = len(tasks)

        def proj(k):
            wfn, o, w, out, outB, gain = tasks[k]
            bk = pbanks[k % len(pbanks)]
            ksq, _ = tmp[k % len(tmp)]
            for c in range(KC):
                self.mm(ps[:, bk, 0:w], wfn(c), hT[:, c, o:o + w], c == 0, c == KC - 1, [wB, hB[c]], [psB[bk]])
            self.act(ksq[:, 0:w], ps[:, bk, 0:w], AF.Square, [psB[bk], self.cB], [tmpB[k % len(tmp)][0]])

        def norm(k):
            wfn, o, w, out, outB, gain = tasks[k]
            bk = pbanks[k % len(pbanks)]
            bn = nbanks[k % len(nbanks)]
            ksq, rk = tmp[k % len(tmp)]
            tb = tmpB[k % len(tmp)]
            self.mm(ps[:, bn, 0:w], self.bd_bf, ksq[:, 0:w], True, True, [tb[0], self.cB], [psB[bn]])
            self.rsqrt_act(rk[:, 0:w], ps[:, bn, 0:w], 1.0 / 64, [psB[bn], self.cB], [tb[1]])
            self.stt("dve", out, ps[:, bk, 0:w], gain, rk[:, 0:w], ALU.mult, ALU.mult, [psB[bk], tb[1], self.sB], [outB])

        for k in range(n + 1):
            if k < n:
                proj(k)
            if k == n and mid is not None:
                mid()
            if k >= 1:
                norm(k - 1)

    def phase1(self):
        P, A = self.P, self.A
        ps, psB = self.ps, self.psB
        m = A.mark()
        wkb = A.alloc(KC * 512, BF16).rearrange("p (c n) -> p c n", c=KC)
        wvb = A.alloc(KC * 512, BF16).rearrange("p (c n) -> p c n", c=KC)
        win = self.d_win.ap()
        wB1 = self.load_w(wkb, win[:, :, C_KB:C_KB + 512], KC, 512, gain=self.gmix)
        wB2 = self.load_w(wvb, win[:, :, C_VB:C_VB + 512], KC, 512, gain=self.gmix)
        xt = [A.alloc(KC * 512, F32).rearrange("p (c n) -> p c n", c=KC) for _ in range(2)]
        xB = P.bufs(2, "x")
        sq = [A.alloc(512, BF16) for _ in range(2)]
        sqB = P.bufs(2, "sq")
        hT = [A.alloc(KC * 512, BF16).rearrange("p (c n) -> p c n", c=KC) for _ in range(2)]
        hB = [P.bufs(KC, "h") for _ in range(2)]
        rstd = [A.alloc(512, F32) for _ in range(2)]
        rB = P.bufs(2, "r")
        tmp = [(A.alloc(512, BF16), A.alloc(512, F32)) for _ in range(3)]
        tmpB = [P.bufs(2, "tmp") for _ in range(3)]
        kout = [A.alloc(4 * 512, BF16).rearrange("p (h n) -> p h n", h=4) for _ in range(2)]
        koB = [P.bufs(4, "ko") for _ in range(2)]
        vout = [A.alloc(4 * 512, BF16).rearrange("p (b n) -> p b n", b=4) for _ in range(2)]
        voB = [P.bufs(4, "vo") for _ in range(2)]
        xT = self.d_xT.ap()
        NT = S // 512

        def stats(T):
            pp = T % 2
            P.dma(xt[pp], xT[:, :, T * 512:(T + 1) * 512], writes=[xB[pp]], key=f"x{pp}")
            self.norm_tile(xt[pp], xB[pp], 512, sq, sqB, hT[pp], hB[pp], rstd[pp], rB[pp], [(0, 512, 0)])

        stats(0)
        for T in range(NT):
            pp = T % 2
            if T + 1 < NT:
                stats(T + 1)
            tasks = [((lambda c, hh=hh: wkb[:, c, hh * 128:(hh + 1) * 128]), 0, 512, kout[pp][:, hh, :], koB[pp][hh], self.gk_b) for hh in range(4)]
            self.ph_tasks(tasks, hT[pp], hB[pp], wB1, tmp, tmpB, (1, 2, 3), (6, 7))
            for hh in range(4):
                P.dma(self.d_KT.ap()[hh, :, T * 512:(T + 1) * 512], kout[pp][:, hh, :], reads=[koB[pp][hh]], key=f"ko{pp}", eng=STORE_ENG)
            for blk in range(4):
                bk = 4 + blk % 2
                for c in range(KC):
                    self.mm(ps[:, bk, :], hT[pp][:, c, blk * 128:(blk + 1) * 128], wvb[:, c, :], c == 0, c == KC - 1,
                            [hB[pp][c], wB2], [psB[bk]])
                self.cp("act" if blk % 2 == 0 else "dve", vout[pp][:, blk, :], ps[:, bk, :], [psB[bk]], [voB[pp][blk]])
            for hh in range(4):
                P.dma(self.d_VS.ap()[hh, :, 4 * T:4 * T + 4, :], vout[pp][:, :, hh * 128:(hh + 1) * 128],
                      reads=voB[pp], key=f"vo{pp}", eng=STORE_ENG)
        P.barrier()
        A.release(m)

    def phase2(self):
        P, A = self.P, self.A
        ps, psB = self.ps, self.psB
        m = A.mark()
        win = self.d_win.ap()

        def walloc(n):
            return A.alloc(KC * n, BF16).rearrange("p (c n) -> p c n", c=KC)

        wqa, wka2, wva, wqb = walloc(512), walloc(256), walloc(128), walloc(512)
        wBq = self.load_w(wqa, win[:, :, C_QA:C_QA + 512], KC, 512, gain=self.gmix)
        for g in range(2):
            for hlf in range(2):
                self.load_w(wka2[:, :, g * 128 + hlf * 64: g * 128 + hlf * 64 + 64], win[:, :, C_KA + 64 * g:C_KA + 64 * g + 64], KC, 64,
                            gain=self.gmix)
        self.load_w(wva, win[:, :, C_VA:C_VA + 128], KC, 128, gain=self.gmix)
        self.load_w(wqb, win[:, :, C_QB:C_QB + 512], KC, 512, gain=self.gmix)
        wB = P.buf("w2")
        xts = [A.alloc(KC * SLOTW, F32).rearrange("p (c n) -> p c n", c=KC) for _ in range(2)]
        xBs = P.bufs(2, "x")
        sq = [A.alloc(SLOTW, BF16) for _ in range(2)]
        sqB = P.bufs(2, "sq")
        hTs = [A.alloc(KC * SLOTW, BF16).rearrange("p (c n) -> p c n", c=KC) for _ in range(2)]
        hBs = [P.bufs(KC, "h") for _ in range(2)]
        rstds = [A.alloc(SLOTW, F32) for _ in range(2)]
        rBs = P.bufs(2, "r")
        tmp = [(A.alloc(512, BF16), A.alloc(512, F32)) for _ in range(3)]
        tmpB = [P.bufs(2, "tmp") for _ in range(3)]
        qaT = A.alloc(4 * HW, BF16).rearrange("p (c n) -> p c n", c=4)
        qaB = P.bufs(4, "qa")
        kaT = A.alloc(2 * SLOTW, BF16).rearrange("p (g n) -> p g n", g=2)
        kaB = P.bufs(2, "ka")
        va = A.alloc(6 * 128, BF16).rearrange("p (b n) -> p b n", b=6)
        vaB = P.buf("va")
        qbT = A.alloc(4 * HW, BF16).rearrange("p (c n) -> p c n", c=4)
        qbB = P.buf("qb")
        yaT = A.alloc(4 * HW, BF16).rearrange("p (c n) -> p c n", c=4)
        yaB = P.buf("ya")
        pt = [A.alloc(512, BF16) for _ in range(8)]
        ptB = P.bufs(8, "pt")
        den = [A.alloc(512, F32) for _ in range(2)]
        denB = P.bufs(2, "den")
        xo = self.d_xo.ap()
        nsl = min(NSLOT, P2_SLOTS)

        def stats(i):
            pp = i % 2
            P.dma(xts[pp], xo[:, :, i * SLOTW:(i + 1) * SLOTW], writes=[xBs[pp]], key="x2")
            self.norm_tile(xts[pp], xBs[pp], SLOTW, sq, sqB, hTs[pp], hBs[pp], rstds[pp], rBs[pp], [(0, 512, 0), (512, 256, 7)])

        stats(0)
        for i in range(nsl):
            hT, hB = hTs[i % 2], hBs[i % 2]
            tasks = []
            for (o, w) in ((128, 512), (640, 128)):
                for cm in range(4):
                    tasks.append(((lambda c, cm=cm: wqa[:, c, cm * 128:(cm + 1) * 128]), o, w, qaT[:, cm, o - 128:o - 128 + w], qaB[cm], self.gq_a))
                for cm in range(4):
                    tasks.append(((lambda c, cm=cm: wqb[:, c, cm * 128:(cm + 1) * 128]), o, w, qbT[:, cm, o - 128:o - 128 + w], qbB, self.gq_b))
            for (o, w) in ((0, 512), (512, 256)):
                for g in range(2):
                    tasks.append(((lambda c, g=g: wka2[:, c, g * 128:(g + 1) * 128]), o, w, kaT[:, g, o:o + w], kaB[g], self.gk_a))
            P.dma(self.d_H.ap()[i], hT[:, :, 128:SLOTW], reads=hB, eng=STORE_ENG)
            self.ph_tasks(tasks, hT, hB, wB, tmp, tmpB, (1, 2, 3), (5, 6))
            P.dma(self.d_QB.ap()[i], qbT, reads=[qbB], key="qbo", eng=STORE_ENG)
            for half in range(2):
                bk = 4 + half
                for bl in range(3):
                    blk = half * 3 + bl
                    for c in range(KC):
                        self.mm(ps[:, bk, bl * 128:(bl + 1) * 128], hT[:, c, blk * 128:(blk + 1) * 128], wva[:, c, :], c == 0, c == KC - 1,
                                [hB[c], wB], [psB[bk]])
                self.cp("act", va[:, half * 3:half * 3 + 3, :], ps[:, bk, 0:384].rearrange("p (b n) -> p b n", b=3), [psB[bk]], [vaB])
            if i + 1 < nsl:
                stats(i + 1)
            def swa_front(n, i=i):
                qo = (n - 1) * 128
                for g in range(2):
                    for kk, kblk in enumerate((n - 1, n)):
                        idx = g * 2 + kk
                        pi = (n % 2) * 4 + idx
                        b0 = (2, 6)[idx % 2]
                        for r in range(4):
                            par, rr = r % 2, r // 2
                            pb = 64 * par
                            self.mm(ps[:, b0 + par, rr * 128:(rr + 1) * 128], kaT[pb:pb + 64, g, kblk * 128:(kblk + 1) * 128],
                                    qaT[pb:pb + 64, 2 * g + rr, qo:qo + 128], True, True,
                                    [kaB[g], qaB[2 * g + rr]], [psB[b0 + par]])
                        self.act(pt[pi].rearrange("p (a n) -> p a n", a=2), ps[:, b0:b0 + 2, 0:256], AF.Exp,
                                 [psB[b0], psB[b0 + 1], self.cB], [ptB[pi]], scale=0.125)
                        ty = 1 if kk == 0 else 0
                        fa = self.sa[:, ty, 4 * g:4 * g + 4, :].rearrange("p (rr par) q -> p par rr q", par=2)
                        p4 = pt[pi].rearrange("p (par rr q) -> p par rr q", par=2, rr=2)
                        self.tt("pool", p4, p4, fa, ALU.mult, [ptB[pi], self.stripB], [ptB[pi]])
                        if i == 0 and n == 2 and kk == 0:
                            self.ts("pool", pt[pi], pt[pi], self.m0[:, 0:1], None, ALU.mult, None, [ptB[pi], self.sB], [ptB[pi]])

            def swa_back(n):
                qo = (n - 1) * 128
                for g in range(2):
                    for kk, kblk in enumerate((n - 1, n)):
                        pi = (n % 2) * 4 + g * 2 + kk
                        self.mm(ps[64 * g:64 * g + 64, 4, :], va[:, kblk, 64 * g:64 * g + 64], pt[pi], kk == 0, kk == 1,
                                [vaB, ptB[pi]], [psB[4]])
                    for kk in range(2):
                        pi = (n % 2) * 4 + g * 2 + kk
                        self.mm(ps[64 * g:64 * g + 64, 5, :], self.ones_bf[:, 0:64], pt[pi], kk == 0, False,
                                [ptB[pi], self.cB], [psB[5]])
                    self.mm(ps[64 * g:64 * g + 64, 5, :], self.ones_f[0:1, 0:64], self.esrow[0:1, g * 512:(g + 1) * 512], False, True,
                            [self.sB, self.cB], [psB[5]])
                dn = den[n % 2]
                self.rsqrt_act(dn, ps[:, 5, :], 1.0, [psB[5], self.cB], [denB[n % 2]], power=-1.0, bias=self.zero1)
                self.tt("dve", yaT[:, :, qo:qo + 128].rearrange("p (rr par) q -> p par rr q", par=2),
                        ps[:, 4, :].rearrange("p (par rr q) -> p par rr q", par=2, rr=2),
                        dn.rearrange("p (par rr q) -> p par rr q", par=2, rr=2), ALU.mult, [psB[4], denB[n % 2]], [yaB])

            swa_front(1)
            for n in range(1, 6):
                if n < 5:
                    swa_front(n + 1)
                swa_back(n)
            P.dma(self.d_YA.ap()[i], yaT, reads=[yaB], key="yao", eng=STORE_ENG)
        P.barrier()
        A.release(m)

    def phase3(self):
        P, A = self.P, self.A
        ps, psB = self.ps, self.psB
        m = A.mark()
        KTs = [A.alloc(S, BF16) for _ in range(2)]
        VSs = [A.alloc(128 * 128, BF16).rearrange("p (b e) -> p b e", b=128) for _ in range(2)]
        kvB = [(P.bufs(4, "kt"), P.bufs(4, "vs")) for _ in range(2)]
        qt = [A.alloc(HW, BF16) for _ in range(2)]
        qB = P.bufs(2, "q")
        NPB = 4
        pt = [A.alloc(1024, BF16) for _ in range(NPB)]
        ptB = P.bufs(NPB, "pt")
        ssum = [A.alloc(512, F32) for _ in range(2)]
        ssumB = P.bufs(2, "ssum")
        rbc = [A.alloc(1024, F32) for _ in range(2)]
        rbcB = P.bufs(2, "rbc")
        dd = [A.alloc(1024, F32) for _ in range(2)]
        ddB = P.bufs(2, "dd")
        dsq = [A.alloc(512, BF16) for _ in range(2)]
        dsqB = P.bufs(2, "dsq")
        rrow = [A.alloc(512, F32) for _ in range(2)]
        rrowB = P.bufs(2, "rrow")
        rrbc = [A.alloc(512, F32) for _ in range(2)]
        rrbcB = P.bufs(2, "rrbc")
        yb = [A.alloc(HW, BF16) for _ in range(2)]
        ybB = P.bufs(2, "yb")
        dbcB = [P.bufs(2, "dbc") for _ in range(2)]
        q7B = P.bufs(4, "ps7q")
        ones32 = self.ones_bf[:, 0:32]
        SB = [(0, 1), (2, 3)]
        dbc = self.d_bc

        def load_kv(h):
            pp = h % 2
            for q4 in range(4):
                P.dma(KTs[pp][:, q4 * 4096:(q4 + 1) * 4096], self.d_KT.ap()[h, :, q4 * 4096:(q4 + 1) * 4096], writes=[kvB[pp][0][q4]])
            for q4 in range(4):
                P.dma(VSs[pp][:, q4 * 32:(q4 + 1) * 32, :], self.d_VS.ap()[h, :, q4 * 32:(q4 + 1) * 32, :], writes=[kvB[pp][1][q4]])

        pending = []
        gcount = [0]
        sbase = [0]

        def flush():
            while pending:
                pending.pop(0)()

        def finalize(ncol, o_ap, obufs, ycols, ybt, ybb, out_dma, nred=1):
            gp = gcount[0] % 2
            gcount[0] += 1
            small = ncol < 128
            P.strict = small
            ss_, rb_, dd_, dq_, rw_, rrb_ = ssum[gp], rbc[gp], dd[gp], dsq[gp], rrow[gp], rrbc[gp]
            if nred == 1:
                self.cp("dve", ss_[0:64, 0:ncol], ps[0:64, 7, 0:ncol], [q7B[0], q7B[1]], [ssumB[gp]])
            else:
                wtot = nred * ncol
                self.cp("dve", ss_[0:64, 0:wtot], ps[0:64, 7, 0:wtot], [q7B[0], q7B[1]], [ssumB[gp]])
                P.strict = True
                if nred == 12:
                    steps = [(4 * ncol, 8 * ncol, 4 * ncol), (4 * ncol, 4 * ncol, 4 * ncol), (2 * ncol, 2 * ncol, 2 * ncol), (ncol, ncol, ncol)]
                else:
                    steps = [(8 * ncol, 8 * ncol, 8 * ncol), (4 * ncol, 4 * ncol, 4 * ncol), (2 * ncol, 2 * ncol, 2 * ncol), (ncol, ncol, ncol)]
                for (wd_, src_, _) in steps:
                    self.tt("dve", ss_[0:64, 0:wd_], ss_[0:64, 0:wd_], ss_[0:64, src_:src_ + wd_], ALU.add, [ssumB[gp]], [ssumB[gp]])
                P.strict = small
            for comp in range(2):
                P.dma(dbc.ap()[gp, comp:comp + 1, 0:ncol], ss_[32 * comp:32 * comp + 1, 0:ncol], reads=[ssumB[gp]], writes=[dbcB[gp][0]])
            P.dma(rb_.rearrange("p (c n) -> p c n", c=2)[:, :, 0:ncol], bass.AP(dbc, gp * 3 * 512, [[0, 128], [512, 2], [1, ncol]]),
                  reads=[dbcB[gp][0]], writes=[rbcB[gp]])
            for comp in range(2):
                r_ = rb_[:, comp * 512:comp * 512 + ncol]
                self.ts("dve", r_, r_, self.tiny1[:, 0:1], None, ALU.add, None, [rbcB[gp], self.cB], [rbcB[gp]])
                self.rcp(r_, r_, [rbcB[gp]], [rbcB[gp]])
                self.tt("dve", dd_[:, comp * 512: comp * 512 + ncol], o_ap(comp), r_, ALU.mult, [obufs[comp], rbcB[gp]], [ddB[gp]])
            self.stt("dve", dd_[:, 0:ncol], dd_[:, 512:512 + ncol], self.neglam[:, 0:1], dd_[:, 0:ncol], ALU.mult, ALU.add, [ddB[gp], self.sB], [ddB[gp]])
            self.tt("pool", dq_[:, 0:ncol], dd_[:, 0:ncol], dd_[:, 0:ncol], ALU.mult, [ddB[gp]], [dsqB[gp]])
            P.strict = False

            def stage1():
                P.strict = small
                self.mm(ps[64:96, 7, 0:ncol], ones32, dq_[:, 0:ncol], True, True, [dsqB[gp], self.cB], [q7B[2]])
                self.act(rw_[64:96, 0:ncol], ps[64:96, 7, 0:ncol], AF.Ln, [q7B[2], self.cB], [rrowB[gp]], bias=self.eps1[64:96, :], scale=1.0 / 128)
                P.strict = True
                self.act(rw_[64:96, 0:ncol], rw_[64:96, 0:ncol], AF.Exp, [rrowB[gp], self.cB], [rrowB[gp]], bias=self.zero1[64:96, :], scale=-0.5)
                P.strict = small
                P.dma(dbc.ap()[gp, 2:3, 0:ncol], rw_[64:65, 0:ncol], reads=[rrowB[gp]], writes=[dbcB[gp][1]])
                P.dma(rrb_[:, 0:ncol], bass.AP(dbc, (gp * 3 + 2) * 512, [[0, 128], [1, ncol]]), reads=[dbcB[gp][1]], writes=[rrbcB[gp]])
                self.stt("dve", ybt[:, ycols:ycols + ncol], dd_[:, 0:ncol], self.gsub[:, 0:1], rrb_[:, 0:ncol], ALU.mult, ALU.mult,
                         [ddB[gp], rrbcB[gp], self.sB], [ybb])
                P.strict = False
                if out_dma is not None:
                    out_dma()
            pending.append(stage1)

        def group(nsteps, qk, av, near_off, ncol, ncols_sum, sum_rhs, pre_done=False, next_qk0=None):
            def pe_tail(t):
                p_, pB_ = pt[t % NPB], ptB[t % NPB]
                for comp in range(2):
                    rl = sum_rhs(p_, comp)
                    for n_, r_ in enumerate(rl):
                        self.mm(ps[32 * comp:32 * comp + 32, 7, 0:ncol], ones32, r_, t == 0 and n_ == 0, t == nsteps - 1 and n_ == len(rl) - 1,
                                [pB_, self.cB], [q7B[comp]])
                av(t, p_, pB_)

            b0 = sbase[0]
            sbase[0] += nsteps
            if not pre_done:
                qk(0, b0 % 2)
            for t in range(nsteps):
                if t + 1 < nsteps:
                    qk(t + 1, (b0 + t + 1) % 2)
                elif next_qk0 is not None:
                    next_qk0((b0 + nsteps) % 2)
                sb = SB[(b0 + t) % 2]
                p_, pB_ = pt[t % NPB], ptB[t % NPB]
                self.act(p_.rearrange("p (c n) -> p c n", c=2), ps[:, sb[0]:sb[0] + 2, :], AF.Exp, [psB[sb[0]], psB[sb[1]], self.cB], [pB_], scale=0.125)
                off = near_off(t)
                if off is not None:
                    for comp in range(2):
                        self.tt("dve", p_[:, comp * 512:(comp + 1) * 512], p_[:, comp * 512:(comp + 1) * 512],
                                self.strip_h[:, off:off + 512], ALU.mult, [pB_, self.stripB], [pB_])
                if t >= 1:
                    pe_tail(t - 1)
                if t == min(10, nsteps - 1):
                    flush()
            pe_tail(nsteps - 1)

        def group_h(nsteps, qk, av, near, batches, HQ, pre_done=False, next_qk0=None):
            def pe_tail(t):
                kb0, nseg = batches[t]
                wd = nseg * HQ
                p_, pB_ = pt[t % NPB], ptB[t % NPB]
                for comp in range(2):
                    self.mm(ps[32 * comp:32 * comp + 32, 7, 0:wd], ones32, p_[:, comp * 512:comp * 512 + wd], t == 0, t == nsteps - 1,
                            [pB_, self.cB], [q7B[comp]])
                av(t, p_, pB_)

            b0 = sbase[0]
            sbase[0] += nsteps
            if not pre_done:
                qk(0, b0 % 2)
            for t in range(nsteps):
                if t + 1 < nsteps:
                    qk(t + 1, (b0 + t + 1) % 2)
                elif next_qk0 is not None:
                    next_qk0((b0 + nsteps) % 2)
                sb = SB[(b0 + t) % 2]
                kb0, nseg = batches[t]
                wd = nseg * HQ
                p_, pB_ = pt[t % NPB], ptB[t % NPB]
                self.act(p_.rearrange("p (c n) -> p c n", c=2)[:, :, 0:wd], ps[:, sb[0]:sb[0] + 2, 0:wd], AF.Exp,
                         [psB[sb[0]], psB[sb[1]], self.cB], [pB_], scale=0.125)
                nr = near(t)
                if nr is not None:
                    c0, ns_, off = nr
                    fa = self.strip_h[:, off:off + ns_ * 128].rearrange("p (u c) -> p u c", c=128)[:, :, 0:HQ]
                    for comp in range(2):
                        pv = p_[:, comp * 512 + c0 * HQ: comp * 512 + (c0 + ns_) * HQ].rearrange("p (u c) -> p u c", c=HQ)
                        self.tt("dve", pv, pv, fa, ALU.mult, [pB_, self.stripB], [pB_])
                if t >= 1:
                    pe_tail(t - 1)
            pe_tail(nsteps - 1)

        def make_slot(h, i, qq):
            pp = h % 2
            KT, VS = KTs[pp], VSs[pp]
            kB, vB = kvB[pp]
            q, qb_ = qt[qq], qB[qq]
            ybt, ybb = yb[qq], ybB[qq]
            strip_h = self.strip[:, h, :]
            nkb = 16 * i + 16
            near0 = 16 * i - 1
            HQ = 32
            batches = [(16 * b, 16) for b in range(i)] + [(16 * i, 12)]
            nb = len(batches)

            def qk(t, par):
                sb = SB[par]
                for comp in range(2):
                    self.mm(ps[:, sb[comp], :], KT[64 * comp:64 * comp + 64, t * 128:(t + 1) * 128], q[64 * comp:64 * comp + 64, 128:640],
                            True, True, [kB[t // 32], qb_], [psB[sb[comp]]])

            def av(t, p_, pB_):
                for comp in range(2):
                    self.mm(ps[:, 4 + comp, :], VS[:, t, :], p_[:, comp * 512:(comp + 1) * 512], t == 0, t == nkb - 1,
                            [vB[t // 32], pB_], [psB[4 + comp]])

            def qkh(t, par):
                sb = SB[par]
                kb0, nseg = batches[t]
                for comp in range(2):
                    for u in range(nseg):
                        kb = kb0 + nseg - 1 - u
                        self.mm(ps[:, sb[comp], u * HQ:(u + 1) * HQ], KT[64 * comp:64 * comp + 64, kb * 128:(kb + 1) * 128],
                                q[64 * comp:64 * comp + 64, 128 - HQ:128], True, True, [kB[kb // 32], qb_], [psB[sb[comp]]])

            def avh(t, p_, pB_):
                kb0, nseg = batches[t]
                for comp in range(2):
                    for u in range(nseg):
                        kb = kb0 + nseg - 1 - u
                        self.mm(ps[:, 6, comp * HQ:(comp + 1) * HQ], VS[:, kb, :], p_[:, comp * 512 + u * HQ: comp * 512 + (u + 1) * HQ],
                                t == 0 and u == 0 and comp == 0, t == nb - 1 and u == nseg - 1, [vB[kb // 32], pB_], [psB[6]])

            def nearh(t):
                kb0, nseg = batches[t]
                if kb0 == 16 * i:
                    return (0, 12, 2048 - 128 * 13 + (128 - HQ))
                if kb0 == 16 * i - 16:
                    return (0, 4, 2048 - 128 + (128 - HQ))
                return None

            def odma():
                P.dma(self.d_YB.ap()[i, :, h, :], ybt, reads=[ybb])

            def run_main(pre_done, next_qk0):
                self.strip_h = strip_h
                group(nkb, qk, av, lambda t: (2048 - 128 * (t - near0)) if t >= near0 else None, 512, 512,
                      lambda p_, comp: [p_[:, comp * 512:(comp + 1) * 512]], pre_done, next_qk0)
                finalize(512, lambda comp: ps[:, 4 + comp, :], [psB[4], psB[5]], 128, ybt, ybb, None)

            def run_halo(pre_done, next_qk0):
                self.strip_h = strip_h
                group_h(nb, qkh, avh, nearh, batches, HQ, pre_done, next_qk0)
                finalize(HQ, lambda comp: ps[:, 6, comp * HQ:(comp + 1) * HQ], [psB[6], psB[6]], 128 - HQ, ybt, ybb, odma,
                         nred=(16 if i > 0 else 12))

            return dict(run_main=run_main, run_halo=run_halo, qk0=lambda par: qk(0, par), qkh0=lambda par: qkh(0, par))

        slots = [(h, i) for h in range(4) for i in range(NSLOT)]
        descs = {}

        def get(k):
            if k not in descs:
                descs[k] = make_slot(slots[k][0], slots[k][1], k % 2)
            return descs[k]

        load_kv(0)
        P.dma(qt[0], self.d_QB.ap()[0, :, 0, :], writes=[qB[0]])
        pre = False
        for k, (h, i) in enumerate(slots):
            if i == 0 and h + 1 < 4:
                load_kv(h + 1)
            if k + 1 < len(slots):
                nh, ni = slots[k + 1]
                P.dma(qt[(k + 1) % 2], self.d_QB.ap()[ni, :, nh, :], writes=[qB[(k + 1) % 2]])
            d = get(k)
            d["run_main"](pre, d["qkh0"])
            nxt = get(k + 1)["qk0"] if k + 1 < len(slots) else None
            d["run_halo"](True, nxt)
            pre = nxt is not None
            descs.pop(k, None)
        flush()
        P.barrier()
        A.release(m)

    def phase4a(self):
        P, A = self.P, self.A
        ps, psB = self.ps, self.psB
        A.release(self.pre_strip_mark)
        m = A.mark()
        win = self.d_win.ap()
        wgl = A.alloc(KC * 2048, BF16).rearrange("p (c n) -> p c n", c=KC)
        wbra = A.alloc(4 * D, BF16).rearrange("p (c n) -> p c n", c=4)
        wbrb = A.alloc(4 * D, BF16).rearrange("p (c n) -> p c n", c=4)
        wo = A.alloc(KC * D, BF16).rearrange("p (c n) -> p c n", c=KC)
        self.load_w(wgl, win[:, :, C_GL:C_GL + 2048], KC, 2048, gain=self.gmix)
        self.load_w(wbra, self.d_wbra.ap(), 4, D)
        self.load_w(wbrb, self.d_wbrb.ap(), 4, D)
        self.load_w(wo, self.d_wo.ap(), KC, D)
        wB = P.buf("w4a")
        xts = [A.alloc(KC * HW, F32).rearrange("p (c n) -> p c n", c=KC) for _ in range(2)]
        xBs = P.bufs(2, "x")
        hTs = [A.alloc(KC * HW, BF16).rearrange("p (c n) -> p c n", c=KC) for _ in range(2)]
        hBs = P.bufs(2, "h")
        yas = [A.alloc(4 * HW, BF16).rearrange("p (c n) -> p c n", c=4) for _ in range(2)]
        ybs = [A.alloc(4 * HW, BF16).rearrange("p (c n) -> p c n", c=4) for _ in range(2)]
        yBs = [P.bufs(2, "y") for _ in range(2)]
        gates = A.alloc(16 * HW, BF16).rearrange("p (c n) -> p c n", c=16)
        gB = P.bufs(16, "g")
        mixed = A.alloc(KC * HW, BF16).rearrange("p (c n) -> p c n", c=KC)
        mxB = P.bufs(KC, "mx")
        t1 = [A.alloc(512, BF16) for _ in range(2)]
        t2 = [A.alloc(512, BF16) for _ in range(2)]
        tB = [P.bufs(2, "t") for _ in range(2)]
        x2 = [A.alloc(512, F32) for _ in range(2)]
        x2B = P.bufs(2, "x2")
        xo = self.d_xo.ap()
        PIECES = ((96, 512), (608, 32))
        cnt = 0
        def loads(i):
            pp = i % 2
            P.dma(hTs[pp], self.d_H.ap()[i], writes=[hBs[pp]])
            P.dma(yas[pp], self.d_YA.ap()[i], writes=[yBs[pp][0]])
            P.dma(ybs[pp], self.d_YB.ap()[i], writes=[yBs[pp][1]])
            P.dma(xts[pp], xo[:, :, i * SLOTW + 128:(i + 1) * SLOTW], writes=[xBs[pp]])

        loads(0)
        for i in range(NSLOT):
            pp = i % 2
            if i + 1 < NSLOT:
                loads(i + 1)
            xt, xB, hT, ya, yb, yB = xts[pp], xBs[pp], hTs[pp], yas[pp], ybs[pp], yBs[pp]
            hB = [hBs[pp]] * KC
            for (o, w) in PIECES:
                for gc in range(16):
                    bk = 1 + gc % 2
                    for c in range(KC):
                        self.mm(ps[:, bk, 0:w], wgl[:, c, gc * 128:(gc + 1) * 128], hT[:, c, o:o + w], c == 0, c == KC - 1, [wB, hB[c]], [psB[bk]])
                    self.act(gates[:, gc, o:o + w], ps[:, bk, 0:w], AF.Sigmoid, [psB[bk], self.cB], [gB[gc]])
            for (o, w) in PIECES:
                for mc in range(KC):
                    k2 = cnt % 2
                    cnt += 1
                    ba, bb = 3 + 2 * k2, 4 + 2 * k2
                    for r in range(4):
                        self.mm(ps[:, ba, 0:w], wbra[:, r, mc * 128:(mc + 1) * 128], ya[:, r, o:o + w], r == 0, r == 3, [wB, yB[0]], [psB[ba]])
                    for r in range(4):
                        self.mm(ps[:, bb, 0:w], wbrb[:, r, mc * 128:(mc + 1) * 128], yb[:, r, o:o + w], r == 0, r == 3, [wB, yB[1]], [psB[bb]])
                    self.tt("dve", t1[k2][:, 0:w], ps[:, ba, 0:w], gates[:, mc, o:o + w], ALU.mult, [psB[ba], gB[mc]], [tB[k2][0]])
                    self.tt("dve", t2[k2][:, 0:w], ps[:, bb, 0:w], gates[:, 8 + mc, o:o + w], ALU.mult, [psB[bb], gB[8 + mc]], [tB[k2][1]])
                    self.tt("pool", mixed[:, mc, o:o + w], t1[k2][:, 0:w], t2[k2][:, 0:w], ALU.add, [tB[k2][0], tB[k2][1]], [mxB[mc]])
            for (o, w) in PIECES:
                for oc in range(KC):
                    k2 = cnt % 2
                    cnt += 1
                    bk = 1 + k2
                    for mc in range(KC):
                        self.mm(ps[:, bk, 0:w], wo[:, mc, oc * 128:(oc + 1) * 128], mixed[:, mc, o:o + w], mc == 0, mc == KC - 1, [wB, mxB[mc]], [psB[bk]])
                    self.tt("dve", x2[k2][:, 0:w], ps[:, bk, 0:w], xt[:, oc, o:o + w], ALU.add, [psB[bk], xB], [x2B[k2]])
                    P.dma(self.d_X2.ap()[i, :, oc, o:o + w], x2[k2][:, 0:w], reads=[x2B[k2]], key=f"x2o{k2}", eng=STORE_ENG)
        P.barrier()
        A.release(m)

    def phase4b(self):
        P, A = self.P, self.A
        ps, psB = self.ps, self.psB
        A.release(self.pre_strip_mark)
        m = A.mark()
        wup = A.alloc(KC * 2 * DFF, BF16).rearrange("p (c n) -> p c n", c=KC)
        wdn = A.alloc(NFC * D, BF16).rearrange("p (c n) -> p c n", c=NFC)
        self.load_w(wup, self.d_wup.ap(), KC, 2 * DFF, gain=self.gffn)
        self.load_w(wdn, self.d_wdn.ap(), NFC, D)
        wB = P.buf("w4b")
        W2 = 514
        xa_off = A.alloc_raw(NFC * 512 * 2)
        xaB = P.buf("xa")
        sq = [A.alloc(W2, BF16) for _ in range(2)]
        sqB = P.bufs(2, "sq")
        hT = A.alloc(KC * W2, BF16).rearrange("p (c n) -> p c n", c=KC)
        hB = P.bufs(KC, "h")
        rstd = A.alloc(W2, F32)
        rB = P.buf("r")
        uh = A.alloc(44 * 2, F32).rearrange("p (c n) -> p c n", c=44)
        uhB = P.buf("uh")
        U = [A.alloc(W2, F32) for _ in range(4)]
        UB = P.bufs(4, "U")
        tg = [A.alloc(512, F32) for _ in range(3)]
        tv = [A.alloc(512, F32) for _ in range(3)]
        tgB = P.bufs(3, "tg")
        tvB = P.bufs(3, "tv")
        ot = [A.alloc(512, F32) for _ in range(2)]
        otB = P.bufs(2, "ot")
        xr = [A.alloc(512, F32) for _ in range(2)]
        xrB = P.bufs(2, "xr")
        cw = self.cw.rearrange("p (a b) -> p a b", a=3)
        outT = self.d_out.ap()
        xt = A.at(xa_off, KC * W2, F32).rearrange("p (c n) -> p c n", c=KC)
        aT = A.at(xa_off, NFC * 512, BF16).rearrange("p (c n) -> p c n", c=NFC)
        for i in range(NSLOT):
            P.dma(xt, self.d_X2.ap()[i, :, :, 126:640], writes=[xaB])
            self.norm_tile(xt, xaB, W2, sq, sqB, hT, hB, rstd, rB, [(0, 2, 7), (2, 512, 0)])
            for fc in range(44):
                for c in range(KC):
                    self.mm(ps[:, 7, fc * 2:fc * 2 + 2], wup[:, c, fc * 128:(fc + 1) * 128], hT[:, c, 0:2], c == 0, c == KC - 1, [wB, hB[c]], [psB[7]])
            self.cp("dve", uh, ps[:, 7, 0:88].rearrange("p (c n) -> p c n", c=44), [psB[7]], [uhB])
            def gate_tail(f):
                k2 = f % 3
                self.act(tg[k2], tg[k2], AF.Silu, [tgB[k2], self.cB], [tgB[k2]])
                self.tt("pool", aT[:, f, :], tg[k2], tv[k2], ALU.mult, [tgB[k2], tvB[k2]], [xaB])

            for f in range(NFC):
                k2 = f % 2
                k3 = f % 3
                for half, fc in enumerate((f, NFC + f)):
                    bk = 1 + 2 * k2 + half
                    ui = 2 * k2 + half
                    for c in range(KC):
                        self.mm(ps[:, bk, :], wup[:, c, fc * 128:(fc + 1) * 128], hT[:, c, 2:W2], c == 0, c == KC - 1, [wB, hB[c]], [psB[bk]])
                    self.cp("pool", U[ui][:, 0:2], uh[:, fc, :], [uhB], [UB[ui]])
                    self.cp("act", U[ui][:, 2:W2], ps[:, bk, :], [psB[bk]], [UB[ui]])
                    t_, tb_ = (tg[k3], tgB[k3]) if half == 0 else (tv[k3], tvB[k3])
                    eng = "dve"
                    self.ts(eng, t_, U[ui][:, 0:512], cw[:, 0, fc:fc + 1], self.cb[:, fc:fc + 1], ALU.mult, ALU.add, [UB[ui], self.sB], [tb_])
                    self.stt(eng, t_, U[ui][:, 1:513], cw[:, 1, fc:fc + 1], t_, ALU.mult, ALU.add, [UB[ui], self.sB, tb_], [tb_])
                    self.stt(eng, t_, U[ui][:, 2:514], cw[:, 2, fc:fc + 1], t_, ALU.mult, ALU.add, [UB[ui], self.sB, tb_], [tb_])
                if f >= 1:
                    gate_tail(f - 1)
            gate_tail(NFC - 1)
            for oc in range(KC):
                k2 = oc % 2
                bk = 5 + k2
                P.dma(xr[k2], self.d_X2.ap()[i, :, oc, 128:640], writes=[xrB[k2]])
                for f in range(NFC):
                    self.mm(ps[:, bk, :], wdn[:, f, oc * 128:(oc + 1) * 128], aT[:, f, :], f == 0, f == NFC - 1, [wB, xaB], [psB[bk]])
                self.tt("dve", ot[k2], ps[:, bk, :], xr[k2], ALU.add, [psB[bk], xrB[k2]], [otB[k2]])
                P.dma(outT[:, oc, i * 512:(i + 1) * 512], ot[k2], reads=[otB[k2]], eng=STORE_ENG)
        P.barrier()
        A.release(m)

    def build(self, nph=N_PHASES):
        self.declare()
        self.P.strict = True
        self.setup()
        self.P.strict = False
        self.P.barrier()
        phases = [self.phase1, self.phase2, self.phase3, self.phase4a, self.phase4b]
        for n_, ph in enumerate(phases[:nph]):
            if n_ == 0 and SKIP_P1:
                continue
            ph()
        self.P.barrier()
        self.P.emit()
        return self.nc


def _host_inputs(inputs):
    f = np.float32
    x = np.asarray(inputs["x"], dtype=f)

    def pc(w, kc):
        n = w.shape[1]
        return np.ascontiguousarray(w.reshape(kc, 128, n).transpose(1, 0, 2))

    w_br_a = np.asarray(inputs["w_br_a"][0], dtype=f)
    wa = w_br_a.reshape(2, 4, 64, D)
    wbra = np.ascontiguousarray(wa.transpose(0, 2, 1, 3).reshape(128, 4, D))
    common = {
        "w_in": pc(np.asarray(inputs["w_in"][0], dtype=f), KC),
        "w_br_a": wbra,
        "w_br_b": pc(np.asarray(inputs["w_br_b"][0], dtype=f), 4),
        "w_o": pc(np.asarray(inputs["w_o"][0], dtype=f), KC),
        "w_up": pc(np.asarray(inputs["w_up"][0], dtype=f), KC),
        "w_down": pc(np.asarray(inputs["w_down"][0], dtype=f), NFC),
        "g_mix": np.ascontiguousarray(np.asarray(inputs["g_mix"][0], dtype=f).reshape(KC, 128).T),
        "g_ffn": np.ascontiguousarray(np.asarray(inputs["g_ffn"][0], dtype=f).reshape(KC, 128).T),
        "conv_w": np.ascontiguousarray(np.asarray(inputs["conv_w"][0], dtype=f).reshape(3, 44, 128).transpose(2, 0, 1)),
        "conv_b": np.ascontiguousarray(np.asarray(inputs["conv_b"][0], dtype=f).reshape(44, 128).T),
        "rel_bias": np.ascontiguousarray(np.asarray(inputs["rel_bias"], dtype=f)),
        "qn_a": np.asarray(inputs["qn_a"], dtype=f).reshape(1, 64),
        "kn_a": np.asarray(inputs["kn_a"], dtype=f).reshape(1, 64),
        "qn_b": np.asarray(inputs["qn_b"], dtype=f).reshape(1, 64),
        "kn_b": np.asarray(inputs["kn_b"], dtype=f).reshape(1, 64),
        "sinks": np.asarray(inputs["sinks"], dtype=f).reshape(1, 8),
        "lam_q1": np.asarray(inputs["lam_q1"], dtype=f).reshape(1, 64),
        "lam_k1": np.asarray(inputs["lam_k1"], dtype=f).reshape(1, 64),
        "lam_q2": np.asarray(inputs["lam_q2"], dtype=f).reshape(1, 64),
        "lam_k2": np.asarray(inputs["lam_k2"], dtype=f).reshape(1, 64),
        "subln_b": np.asarray(inputs["subln_b"], dtype=f).reshape(128, 1),
        "Jmat": np.ascontiguousarray(np.eye(128, dtype=f)[::-1]),
        "bdones": np.kron(np.eye(2, dtype=f), np.ones((64, 64), dtype=f)),
    }
    oha = np.zeros((33, 384), dtype=f)
    for mm_ in range(384):
        d = mm_ - 127
        if 0 <= d < 128:
            oha[int(_t5_bucket_np(np.array(d))), mm_] = 1
        else:
            oha[32, mm_] = 1
    common["oh_a"] = oha
    in_maps = []
    for core in range(8):
        b, j = core // 4, core % 4
        xTb = np.ascontiguousarray(x[b].T.reshape(KC, 128, S).transpose(1, 0, 2))
        xo = np.zeros((128, KC, NSLOT, SLOTW), dtype=f)
        for i in range(NSLOT):
            G = 4 * i + j
            t0 = 512 * G - 256
            lo = max(t0, 0)
            xo[:, :, i, lo - t0:] = xTb[:, :, lo:t0 + SLOTW]
        ohd = np.zeros((33, VECD), dtype=f)
        d = np.arange(VECD) + 512 * j - 2047
        bk = _t5_bucket_np(d)
        for mm_ in range(VECD):
            if d[mm_] >= 0:
                ohd[bk[mm_], mm_] = 1
            else:
                ohd[32, mm_] = 1
        mp = dict(common)
        mp["xT"] = xTb
        mp["xo"] = xo.reshape(128, KC, NSLOT * SLOTW)
        mp["oh_d"] = ohd
        mp["m0"] = np.full((128, 1), 0.0 if j == 0 else 1.0, dtype=f)
        in_maps.append(mp)
    return in_maps


_NC_CACHE = {}


def kernel(**inputs):
    in_maps = _host_inputs(inputs)
    if "nc" not in _NC_CACHE:
        _NC_CACHE["nc"] = K().build()
    nc = _NC_CACHE["nc"]
    res = run_bass_kernel_spmd(nc, in_maps, core_ids=list(range(8)))
    out = np.zeros((2, S, D), dtype=np.float32)
    for core in range(8):
        b, j = core // 4, core % 4
        o = res.results[core]["outT"].reshape(128, KC, NSLOT, 512)
        for i in range(NSLOT):
            G = 4 * i + j
            out[b, 512 * G:512 * (G + 1), :] = o[:, :, i, :].transpose(2, 1, 0).reshape(512, D)
    return out
```

```python
import contextlib
import math
import numpy as np
import concourse.bass as bass
import concourse.mybir as mybir
from concourse.bass_utils import run_bass_kernel_spmd

F32 = mybir.dt.float32
BF16 = mybir.dt.bfloat16
AF = mybir.ActivationFunctionType
ALU = mybir.AluOpType

ENGS = ("pe", "act", "dve", "pool", "sp")
SEM_ROT = 3000


class Buf:
    __slots__ = ("name", "w", "rs")

    def __init__(self, name):
        self.name = name
        self.w = None
        self.rs = []


class Op:
    __slots__ = ("eng", "fn", "waits", "signal", "dma_key", "dma_sem", "dma_val", "sig_sem", "sig_val")

    def __init__(self, eng, fn, dma_key=None):
        self.eng = eng
        self.fn = fn
        self.waits = []
        self.signal = False
        self.dma_key = dma_key
        self.dma_sem = None
        self.dma_val = None
        self.sig_sem = None
        self.sig_val = None


class Prog:
    def __init__(self, nc):
        self.nc = nc
        self.ops = {e: [] for e in ENGS}
        self.dma_cnt = {}
        self.all_dma_last = {}
        self.stack = contextlib.ExitStack()
        self.nbufs = 0
        self.strict = False

    def buf(self, name=None):
        self.nbufs += 1
        return Buf(f"{name or 'b'}#{self.nbufs}")

    def bufs(self, n, name="b"):
        return [self.buf(f"{name}{i}") for i in range(n)]

    def _dep(self, op, y):
        if y is None or y is op:
            return
        if y.dma_key is None and y.eng == op.eng and op.dma_key is None and not self.strict:
            return
        if y.dma_key is None:
            y.signal = True
        if y not in op.waits:
            op.waits.append(y)

    def op(self, eng, fn, reads=(), writes=(), dma_key=None):
        o = Op(eng, fn, dma_key)
        for b in reads:
            self._dep(o, b.w)
        for b in writes:
            self._dep(o, b.w)
            for r in b.rs:
                self._dep(o, r)
        for b in reads:
            b.rs.append(o)
        for b in writes:
            b.w = o
            b.rs = []
        if dma_key is not None:
            st = self.dma_cnt.setdefault(dma_key, [0, 0])
            if st[1] + 16 > 4000:
                st[0] += 1
                st[1] = 0
            st[1] += 16
            o.dma_sem = (dma_key, st[0])
            o.dma_val = st[1]
            self.all_dma_last[dma_key] = o
        self.ops[eng].append(o)
        return o

    def dma(self, out, in_, reads=(), writes=(), key=None, eng="sp", **kw):
        prim = writes[0] if len(writes) else reads[0]
        key = key if (key is not None and key.startswith("SH_")) else prim.name
        return self.op(eng, lambda e: e.dma_start(out=out, in_=in_, **kw), reads, writes, dma_key=key)

    def barrier(self):
        lasts = [self.ops[e][-1] for e in ENGS if self.ops[e]]
        dmas = list(self.all_dma_last.values())
        news = []
        for e in ENGS:
            o = Op(e, None)
            for y in lasts:
                self._dep(o, y)
            for y in dmas:
                self._dep(o, y)
            news.append(o)
        for o in news:
            self.ops[o.eng].append(o)

    def emit(self):
        nc = self.nc
        semkeys = set()
        for e in ENGS:
            gen, cnt = 0, 0
            for o in self.ops[e]:
                if o.dma_key is not None:
                    semkeys.add(o.dma_sem)
                    continue
                if o.signal:
                    if cnt >= SEM_ROT:
                        gen += 1
                        cnt = 0
                    cnt += 1
                    o.sig_sem = ("eng", e, gen)
                    o.sig_val = cnt
                    semkeys.add(o.sig_sem)
        sems = {}
        for n, k in enumerate(sorted(semkeys, key=str)):
            sems[k] = self.stack.enter_context(nc.semaphore(f"sm{n}"))
        self.nsems = len(sems)
        block = self.stack.enter_context(nc.Block())
        engmap = {"pe": "tensor", "act": "scalar", "dve": "vector", "pool": "gpsimd", "sp": "sync"}

        def make(e):
            def body(eng):
                waited = {}
                for o in self.ops[e]:
                    for y in o.waits:
                        if y.dma_key is not None:
                            sk, v = y.dma_sem, y.dma_val
                        else:
                            sk, v = y.sig_sem, y.sig_val
                        if waited.get(sk, 0) >= v:
                            continue
                        waited[sk] = v
                        eng.wait_ge(sems[sk], v)
                    if o.fn is None:
                        if o.signal:
                            eng.nop().then_inc(sems[o.sig_sem], 1)
                        continue
                    ins = o.fn(eng)
                    if o.dma_key is not None:
                        ins.then_inc(sems[o.dma_sem], 16)
                    elif o.signal:
                        ins.then_inc(sems[o.sig_sem], 1)
            return body

        for e in ENGS:
            getattr(block, engmap[e])(make(e))
        self.stack.close()


class Arena:
    def __init__(self, prog, nbytes, name="arena"):
        nc = prog.nc
        self.t8 = prog.stack.enter_context(nc.sbuf_tensor(name, [128, nbytes], mybir.dt.uint8))
        self.views = {}
        self.nbytes = nbytes
        self.off = 0

    def view(self, dt):
        if dt not in self.views:
            self.views[dt] = self.t8.bitcast(dt)
        return self.views[dt]

    def alloc(self, nelem, dt):
        sz = mybir.dt.size(dt)
        self.off = (self.off + 63) // 64 * 64
        o = self.off
        self.off += nelem * sz
        assert self.off <= self.nbytes, f"arena overflow {self.off} > {self.nbytes}"
        return self.view(dt)[:, o // sz: o // sz + nelem]

    def alloc_raw(self, nbytes):
        self.off = (self.off + 63) // 64 * 64
        o = self.off
        self.off += nbytes
        assert self.off <= self.nbytes, f"arena overflow {self.off} > {self.nbytes}"
        return o

    def at(self, o, nelem, dt):
        sz = mybir.dt.size(dt)
        return self.view(dt)[:, o // sz: o // sz + nelem]

    def mark(self):
        return self.off

    def release(self, m):
        self.off = m


D = 1024
S = 16384
KC = 8
NSLOT = 8
SLOTW = 768
HW = 640
DFF = 2816
NFC = 22
EPS = 1e-6
LAM_INIT = 0.8 - 0.6 * math.exp(-0.3 * 0)
VECD = 2688
STRIPW = 2560
C_QA, C_KA, C_VA, C_QB, C_KB, C_VB, C_GL = 0, 512, 640, 768, 1280, 1792, 2304

DEBUG_SCRATCH = False
STORE_ENG = "pool"
P2_STAGE = 9
P2_SLOTS = 8
SKIP_P1 = False
N_PHASES = 5


def _t5_bucket_np(rel):
    n = np.maximum(rel, 0)
    nf = np.maximum(n, 1).astype(np.float32)
    large = 16 + (np.log(nf / np.float32(16)) / np.float32(math.log(8.0)) * np.float32(16)).astype(np.int32)
    large = np.minimum(large, 31)
    return np.where(n < 16, n, large)


class K:
    def __init__(self):
        nc = bass.Bass("TRN2", target_bir_lowering=False)
        self.nc = nc
        self.P = Prog(nc)
        self.A = Arena(self.P, 209920)
        self.ps = self.P.stack.enter_context(nc.psum_tensor("ps", [128, 8, 512], F32))
        self.psB = self.P.bufs(8, "psb")

    def mm(self, out, lhsT, rhs, start, stop, R, W):
        self.P.op("pe", lambda e: e.matmul(out, lhsT=lhsT, rhs=rhs, start=start, stop=stop), R, W)

    def act(self, out, in_, func, R, W, bias=None, scale=1.0):
        b = self.zero1 if bias is None else bias
        npart = out.shape[0]
        if npart != 128 and b.shape[0] == 128:
            b = b[0:npart, :]
        self.P.op("act", lambda e: e.activation(out=out, in_=in_, func=func, bias=b, scale=scale), R, W)

    def tt(self, eng, out, a, b, op, R, W):
        self.P.op(eng, lambda e: e.tensor_tensor(out=out, in0=a, in1=b, op=op), R, W)

    def ts(self, eng, out, a, s1, s2, op0, op1, R, W):
        if op1 is None:
            self.P.op(eng, lambda e: e.tensor_scalar(out=out, in0=a, scalar1=s1, scalar2=None, op0=op0), R, W)
        else:
            self.P.op(eng, lambda e: e.tensor_scalar(out=out, in0=a, scalar1=s1, scalar2=s2, op0=op0, op1=op1), R, W)

    def stt(self, eng, out, a, s, b, op0, op1, R, W):
        self.P.op(eng, lambda e: e.scalar_tensor_tensor(out=out, in0=a, scalar=s, in1=b, op0=op0, op1=op1), R, W)

    def cp(self, eng, out, in_, R, W):
        if eng == "act":
            self.act(out, in_, AF.Identity, R, W)
        else:
            self.P.op(eng, lambda e: e.tensor_copy(out=out, in_=in_), R, W)

    def rcp(self, out, in_, R, W):
        self.P.op("dve", lambda e: e.reciprocal(out=out, in_=in_), R, W)

    def rsqrt_act(self, out, in_, scale, R, W, power=-0.5, bias=None):
        self.act(out, in_, AF.Ln, R, W, bias=self.eps1 if bias is None else bias, scale=scale)
        old = self.P.strict
        if out.shape[-1] <= 256:
            self.P.strict = True
        self.act(out, out, AF.Exp, list(W) + [self.cB], W, scale=power)
        self.P.strict = old

    def memset(self, eng, ap, val, W):
        self.P.op(eng, lambda e: e.memset(ap, val), (), W)

    def declare(self):
        nc = self.nc

        def din(name, shape, dt=F32):
            return nc.dram_tensor(name, list(shape), dt, kind="ExternalInput")

        def dscr(name, shape, dt):
            return nc.dram_tensor(name, list(shape), dt, kind="ExternalOutput" if DEBUG_SCRATCH else "Internal")

        self.d_xT = din("xT", [128, KC, S])
        self.d_xo = din("xo", [128, KC, NSLOT * SLOTW])
        self.d_win = din("w_in", [128, KC, 4352])
        self.d_wbra = din("w_br_a", [128, 4, D])
        self.d_wbrb = din("w_br_b", [128, 4, D])
        self.d_wo = din("w_o", [128, KC, D])
        self.d_wup = din("w_up", [128, KC, 2 * DFF])
        self.d_wdn = din("w_down", [128, NFC, D])
        self.d_gmix = din("g_mix", [128, KC])
        self.d_gffn = din("g_ffn", [128, KC])
        self.d_cw = din("conv_w", [128, 3, 44])
        self.d_cb = din("conv_b", [128, 44])
        self.d_relb = din("rel_bias", [32, 12])
        self.d_qna = din("qn_a", [1, 64])
        self.d_kna = din("kn_a", [1, 64])
        self.d_qnb = din("qn_b", [1, 64])
        self.d_knb = din("kn_b", [1, 64])
        self.d_sinks = din("sinks", [1, 8])
        self.d_lq1 = din("lam_q1", [1, 64])
        self.d_lk1 = din("lam_k1", [1, 64])
        self.d_lq2 = din("lam_q2", [1, 64])
        self.d_lk2 = din("lam_k2", [1, 64])
        self.d_subln = din("subln_b", [128, 1])
        self.d_oha = din("oh_a", [33, 384])
        self.d_ohd = din("oh_d", [33, VECD])
        self.d_m0 = din("m0", [128, 1])
        self.d_J = din("Jmat", [128, 128])
        self.d_bd = din("bdones", [128, 128])
        self.d_out = nc.dram_tensor("outT", [128, KC, NSLOT * 512], F32, kind="ExternalOutput")
        self.d_KT = dscr("s_KT", [4, 128, S], BF16)
        self.d_VS = dscr("s_VS", [4, 128, 128, 128], BF16)
        self.d_QB = dscr("s_QB", [NSLOT, 128, 4, HW], BF16)
        self.d_YA = dscr("s_YA", [NSLOT, 128, 4, HW], BF16)
        self.d_YB = dscr("s_YB", [NSLOT, 128, 4, HW], BF16)
        self.d_X2 = dscr("s_X2", [NSLOT, 128, KC, HW], F32)
        self.d_H = dscr("s_H", [NSLOT, 128, KC, HW], BF16)
        if DEBUG_SCRATCH:
            self.d_dstrip = dscr("s_strip", [128, 4 * STRIPW], BF16)
            self.d_dsa = dscr("s_sa", [128, 2 * 8 * 128], BF16)
            self.d_dsmall = dscr("s_small", [128, 8], F32)
        self.d_bc = nc.dram_tensor("s_bc", [2, 3, 512], F32, kind="Internal")
        self.d_VA = dscr("s_VECA", [8, 384], F32)
        self.d_VD = dscr("s_VECD", [4, VECD], F32)

    def setup(self):
        P, A, nc = self.P, self.A, self.nc
        ps, psB = self.ps, self.psB
        cB = P.buf("consts")
        self.cB = cB
        self.zero1 = A.alloc(1, F32)
        self.eps1 = A.alloc(1, F32)
        self.tiny1 = A.alloc(1, F32)
        self.ones_bf = A.alloc(128, BF16)
        self.ones_f = A.alloc(128, F32)
        self.bd_bf = A.alloc(128, BF16)
        self.J = A.alloc(128, F32)
        bd_f = A.alloc(128, F32)
        self.memset("pool", self.zero1, 0.0, [cB])
        self.memset("pool", self.eps1, EPS, [cB])
        self.memset("pool", self.tiny1, 1e-30, [cB])
        self.memset("pool", self.ones_bf, 1.0, [cB])
        self.memset("pool", self.ones_f, 1.0, [cB])
        jB = P.buf("J")
        P.dma(self.J, self.d_J.ap(), writes=[jB], key="c0")
        P.dma(bd_f, self.d_bd.ap(), writes=[jB], key="c0")
        self.cp("dve", self.bd_bf, bd_f, [jB], [cB])
        sB = P.buf("small")
        self.gmix = A.alloc(KC, F32)
        self.gffn = A.alloc(KC, F32)
        self.cw = A.alloc(3 * 44, F32)
        self.cb = A.alloc(44, F32)
        self.m0 = A.alloc(1, F32)
        P.dma(self.gmix, self.d_gmix.ap(), writes=[sB], key="c1")
        P.dma(self.gffn, self.d_gffn.ap(), writes=[sB], key="c1")
        P.dma(self.cw, self.d_cw.ap().rearrange("p a b -> p (a b)"), writes=[sB], key="c1")
        P.dma(self.cb, self.d_cb.ap(), writes=[sB], key="c1")
        P.dma(self.m0, self.d_m0.ap(), writes=[sB], key="c1")
        self.gq_a = A.alloc(1, F32)
        self.gk_a = A.alloc(1, F32)
        self.gq_b = A.alloc(1, F32)
        self.gk_b = A.alloc(1, F32)
        for dst, src in ((self.gq_a, self.d_qna), (self.gk_a, self.d_kna), (self.gq_b, self.d_qnb), (self.gk_b, self.d_knb)):
            for hlf in range(2):
                P.dma(dst[64 * hlf:64 * hlf + 64, :], bass.AP(src, 0, [[1, 64], [1, 1]]), writes=[sB], key="c1")
        self.gsub = A.alloc(1, F32)
        P.dma(self.gsub, self.d_subln.ap(), writes=[sB], key="c1")
        self.ts("dve", self.gsub, self.gsub, 1.0 - LAM_INIT, None, ALU.mult, None, [sB], [sB])
        lam4 = A.alloc(4 * 64, F32)
        for n_, src in enumerate((self.d_lq1, self.d_lk1, self.d_lq2, self.d_lk2)):
            P.dma(lam4[:, n_ * 64:(n_ + 1) * 64], bass.AP(src, 0, [[0, 128], [1, 64]]), writes=[sB], key="c1")
        lp = A.alloc(2 * 64, F32)
        ls = A.alloc(2, F32)
        self.neglam = A.alloc(1, F32)
        self.tt("dve", lp[:, 0:64], lam4[:, 0:64], lam4[:, 64:128], ALU.mult, [sB], [sB])
        self.tt("dve", lp[:, 64:128], lam4[:, 128:192], lam4[:, 192:256], ALU.mult, [sB], [sB])
        P.op("dve", lambda e: e.reduce_sum(out=ls[:, 0:1], in_=lp[:, 0:64], axis=mybir.AxisListType.X), [sB], [sB])
        P.op("dve", lambda e: e.reduce_sum(out=ls[:, 1:2], in_=lp[:, 64:128], axis=mybir.AxisListType.X), [sB], [sB])
        self.act(ls, ls, AF.Exp, [sB, cB], [sB])
        self.tt("dve", self.neglam, ls[:, 1:2], ls[:, 0:1], ALU.subtract, [sB], [sB])
        self.ts("dve", self.neglam, self.neglam, -LAM_INIT, None, ALU.add, None, [sB], [sB])
        self.sB = sB
        self.esrow = A.alloc(2 * 512, F32)
        sk = A.alloc(8, F32)
        P.dma(sk[0:1, :], self.d_sinks.ap(), writes=[sB], key="c1")
        self.act(sk[0:1, :], sk[0:1, :], AF.Exp, [sB, cB], [sB])
        for hq in range(8):
            g_, r_ = hq // 4, hq % 4
            col = g_ * 512 + ((r_ % 2) * 2 + r_ // 2) * 128
            self.ts("dve", self.esrow[0:1, col:col + 128], self.ones_f[0:1, 0:128], sk[0:1, hq:hq + 1], None, ALU.mult, None, [sB, cB], [sB])

        self.esk = A.alloc(4, F32)
        for g_ in range(2):
            P.dma(self.esk[64 * g_:64 * g_ + 64, :], bass.AP(self.d_sinks, 4 * g_, [[0, 64], [1, 4]]), writes=[sB], key="c1")
        self.act(self.esk, self.esk, AF.Exp, [sB, cB], [sB])

        tB = P.buf("tab")
        tabp = A.alloc(12, F32)
        tab31 = A.alloc(4, F32)
        self.memset("pool", tabp[32:33, :], -30000.0, [tB])
        P.dma(tabp[0:32, :], self.d_relb.ap(), writes=[tB], key="c2")
        P.dma(tab31[0:32, :], bass.AP(self.d_relb, 31 * 12 + 8, [[0, 32], [1, 4]]), writes=[tB], key="c2")
        self.tt("dve", tabp[0:32, 8:12], tabp[0:32, 8:12], tab31[0:32, :], ALU.subtract, [tB], [tB])
        m = A.mark()
        oha = A.alloc(384, F32)
        ohd = A.alloc(VECD, F32)
        veca = A.alloc(384, F32)
        vecd = A.alloc(VECD, F32)
        ohB = P.buf("oh")
        P.dma(oha[0:33, :], self.d_oha.ap(), writes=[ohB], key="c2")
        P.dma(ohd[0:33, :], self.d_ohd.ap(), writes=[ohB], key="c2")
        vB = P.buf("vec")
        self.mm(ps[0:8, 0, 0:384], tabp[0:33, 0:8], oha[0:33, :], True, True, [tB, ohB], [psB[0]])
        self.act(veca[0:8, :], ps[0:8, 0, 0:384], AF.Exp, [psB[0], cB], [vB])
        for pc in range(6):
            w = 512 if pc < 5 else VECD - 2560
            bk = 1 + pc % 2
            self.mm(ps[0:4, bk, 0:w], tabp[0:33, 8:12], ohd[0:33, pc * 512: pc * 512 + w], True, True, [tB, ohB], [psB[bk]])
            self.act(vecd[0:4, pc * 512: pc * 512 + w], ps[0:4, bk, 0:w], AF.Exp, [psB[bk], cB], [vB])
        dvB = P.buf("dvec")
        P.dma(self.d_VA.ap(), veca[0:8, :], reads=[vB], writes=[dvB], key="c3")
        P.dma(self.d_VD.ap(), vecd[0:4, :], reads=[vB], writes=[dvB], key="c3")
        A.release(m)
        self.pre_strip_mark = A.mark()
        self.strip = A.alloc(4 * STRIPW, BF16).rearrange("p (h u) -> p h u", h=4)
        self.sa = A.alloc(2 * 8 * 128, BF16).rearrange("p (t h q) -> p t h q", t=2, h=8)
        self.stripB = P.buf("strip")
        self.persist_mark = A.mark()
        m = A.mark()
        rev = A.alloc(STRIPW, F32)
        revB = P.bufs(2, "rev")
        P.dma(rev[:, 0:2048].rearrange("p (h u) -> p h u", h=8), bass.AP(self.d_VA, 0, [[1, 128], [384, 8], [1, 256]]),
              reads=[dvB], writes=[revB[0]], key="c4")
        for pi in range(4):
            bk = pi % 2
            self.mm(ps[:, bk, :], self.J, rev[:, pi * 512:(pi + 1) * 512], True, True, [jB, revB[0]], [psB[bk]])
            src = ps[:, bk, :].rearrange("p (h t q) -> p h t q", h=2, t=2)
            for ty in range(2):
                self.cp("dve" if ty == 0 else "act", self.sa[:, ty, 2 * pi:2 * pi + 2, :], src[:, :, ty, :], [psB[bk]], [self.stripB])
        for h in range(4):
            rb = revB[(h + 1) % 2]
            P.dma(rev, bass.AP(self.d_VD, h * VECD, [[1, 128], [1, STRIPW]]), reads=[dvB], writes=[revB[0], revB[1]], key="c4")
            for pc in range(5):
                bk = pc % 2
                self.mm(ps[:, bk, :], self.J, rev[:, pc * 512:(pc + 1) * 512], True, True, [jB, revB[0], revB[1]], [psB[bk]])
                self.cp("dve" if pc % 2 == 0 else "act", self.strip[:, h, pc * 512:(pc + 1) * 512], ps[:, bk, :], [psB[bk]], [self.stripB])
        A.release(m)
        if DEBUG_SCRATCH:
            P.dma(self.d_dstrip.ap(), self.strip.rearrange("p h u -> p (h u)"), reads=[self.stripB])
            P.dma(self.d_dsa.ap(), self.sa.rearrange("p t h q -> p (t h q)"), reads=[self.stripB])
            for n_, t_ in enumerate((self.neglam, self.gsub, self.gq_b, self.gk_b)):
                P.dma(self.d_dsmall.ap()[:, n_:n_ + 1], t_, reads=[self.sB], allow_slow_non_contiguous=True)

    def load_w(self, dst, src, kc, ncols, gain=None, dup=None, key="w"):
        P, A = self.P, self.A
        m = A.mark()
        pw = 512 if kc <= 8 else 192
        stg = [A.alloc(kc * pw, F32).rearrange("p (c n) -> p c n", c=kc) for _ in range(2)]
        sb = P.bufs(2, "stg")
        wB = P.buf("wdst")
        engs = ("dve", "act")
        n = 0
        for i, c0 in enumerate(range(0, ncols, pw)):
            w = min(pw, ncols - c0)
            s, b = stg[i % 2], sb[i % 2]
            P.dma(s[:, :, 0:w], src[:, :, c0:c0 + w], writes=[b], key=f"SH_stg{i % 2}")
            for c in range(kc):
                eng = engs[n % 2]
                n += 1
                if gain is None:
                    self.cp(eng, dst[:, c, c0:c0 + w], s[:, c, 0:w], [b], [wB])
                elif eng == "act":
                    self.act(dst[:, c, c0:c0 + w], s[:, c, 0:w], AF.Identity, [b, self.sB, self.cB], [wB], scale=gain[:, c:c + 1])
                else:
                    self.ts(eng, dst[:, c, c0:c0 + w], s[:, c, 0:w], gain[:, c:c + 1], None, ALU.mult, None, [b, self.sB], [wB])
        self.P.barrier()
        A.release(m)
        return wB

    def norm_tile(self, xt, xB, n, sq, sqB, hT, hB, rstd, rB, pieces):
        ps, psB = self.ps, self.psB
        for c in range(KC):
            self.act(sq[c % 2][:, 0:n], xt[:, c, :], AF.Square, [xB, self.cB], [sqB[c % 2]])
            for (o, w, bank) in pieces:
                self.mm(ps[:, bank, 0:w], self.ones_bf, sq[c % 2][:, o:o + w], c == 0, c == KC - 1, [sqB[c % 2], self.cB], [psB[bank]])
        for (o, w, bank) in pieces:
            small = w < 128
            self.P.strict = small
            self.rsqrt_act(rstd[:, o:o + w], ps[:, bank, 0:w], 1.0 / D, [psB[bank], self.cB], [rB])
            self.P.strict = False
        for c in range(KC):
            self.tt("dve" if c % 2 == 0 else "pool", hT[:, c, 0:n], xt[:, c, :], rstd[:, 0:n], ALU.mult, [xB, rB], [hB[c]])

    def ph_tasks(self, tasks, hT, hB, wB, tmp, tmpB, pbanks, nbanks, mid=None):
        ps, psB = self.ps, self.psB
        n = len(tasks)

        def proj(k):
            wfn, o, w, out, outB, gain = tasks[k]
            bk = pbanks[k % len(pbanks)]
            ksq, _ = tmp[k % len(tmp)]
            for c in range(KC):
                self.mm(ps[:, bk, 0:w], wfn(c), hT[:, c, o:o + w], c == 0, c == KC - 1, [wB, hB[c]], [psB[bk]])
            self.act(ksq[:, 0:w], ps[:, bk, 0:w], AF.Square, [psB[bk], self.cB], [tmpB[k % len(tmp)][0]])

        def norm(k):
            wfn, o, w, out, outB, gain = tasks[k]
            bk = pbanks[k % len(pbanks)]
            bn = nbanks[k % len(nbanks)]
            ksq, rk = tmp[k % len(tmp)]
            tb = tmpB[k % len(tmp)]
            self.mm(ps[:, bn, 0:w], self.bd_bf, ksq[:, 0:w], True, True, [tb[0], self.cB], [psB[bn]])
            self.rsqrt_act(rk[:, 0:w], ps[:, bn, 0:w], 1.0 / 64, [psB[bn], self.cB], [tb[1]])
            self.stt("dve", out, ps[:, bk, 0:w], gain, rk[:, 0:w], ALU.mult, ALU.mult, [psB[bk], tb[1], self.sB], [outB])

        for k in range(n + 1):
            if k < n:
                proj(k)
            if k == n and mid is not None:
                mid()
            if k >= 1:
                norm(k - 1)

    def phase1(self):
        P, A = self.P, self.A
        ps, psB = self.ps, self.psB
        m = A.mark()
        wkb = A.alloc(KC * 512, BF16).rearrange("p (c n) -> p c n", c=KC)
        wvb = A.alloc(KC * 512, BF16).rearrange("p (c n) -> p c n", c=KC)
        win = self.d_win.ap()
        wB1 = self.load_w(wkb, win[:, :, C_KB:C_KB + 512], KC, 512, gain=self.gmix)
        wB2 = self.load_w(wvb, win[:, :, C_VB:C_VB + 512], KC, 512, gain=self.gmix)
        xt = [A.alloc(KC * 512, F32).rearrange("p (c n) -> p c n", c=KC) for _ in range(2)]
        xB = P.bufs(2, "x")
        sq = [A.alloc(512, BF16) for _ in range(2)]
        sqB = P.bufs(2, "sq")
        hT = [A.alloc(KC * 512, BF16).rearrange("p (c n) -> p c n", c=KC) for _ in range(2)]
        hB = [P.bufs(KC, "h") for _ in range(2)]
        rstd = [A.alloc(512, F32) for _ in range(2)]
        rB = P.bufs(2, "r")
        tmp = [(A.alloc(512, BF16), A.alloc(512, F32)) for _ in range(3)]
        tmpB = [P.bufs(2, "tmp") for _ in range(3)]
        kout = [A.alloc(4 * 512, BF16).rearrange("p (h n) -> p h n", h=4) for _ in range(2)]
        koB = [P.bufs(4, "ko") for _ in range(2)]
        vout = [A.alloc(4 * 512, BF16).rearrange("p (b n) -> p b n", b=4) for _ in range(2)]
        voB = [P.bufs(4, "vo") for _ in range(2)]
        xT = self.d_xT.ap()
        NT = S // 512

        def stats(T):
            pp = T % 2
            P.dma(xt[pp], xT[:, :, T * 512:(T + 1) * 512], writes=[xB[pp]], key=f"x{pp}")
            self.norm_tile(xt[pp], xB[pp], 512, sq, sqB, hT[pp], hB[pp], rstd[pp], rB[pp], [(0, 512, 0)])

        stats(0)
        for T in range(NT):
            pp = T % 2
            if T + 1 < NT:
                stats(T + 1)
            tasks = [((lambda c, hh=hh: wkb[:, c, hh * 128:(hh + 1) * 128]), 0, 512, kout[pp][:, hh, :], koB[pp][hh], self.gk_b) for hh in range(4)]
            self.ph_tasks(tasks, hT[pp], hB[pp], wB1, tmp, tmpB, (1, 2, 3), (6, 7))
            for hh in range(4):
                P.dma(self.d_KT.ap()[hh, :, T * 512:(T + 1) * 512], kout[pp][:, hh, :], reads=[koB[pp][hh]], key=f"ko{pp}", eng=STORE_ENG)
            for blk in range(4):
                bk = 4 + blk % 2
                for c in range(KC):
                    self.mm(ps[:, bk, :], hT[pp][:, c, blk * 128:(blk + 1) * 128], wvb[:, c, :], c == 0, c == KC - 1,
                            [hB[pp][c], wB2], [psB[bk]])
                self.cp("act" if blk % 2 == 0 else "dve", vout[pp][:, blk, :], ps[:, bk, :], [psB[bk]], [voB[pp][blk]])
            for hh in range(4):
                P.dma(self.d_VS.ap()[hh, :, 4 * T:4 * T + 4, :], vout[pp][:, :, hh * 128:(hh + 1) * 128],
                      reads=voB[pp], key=f"vo{pp}", eng=STORE_ENG)
        P.barrier()
        A.release(m)

    def phase2(self):
        P, A = self.P, self.A
        ps, psB = self.ps, self.psB
        m = A.mark()
        win = self.d_win.ap()

        def walloc(n):
            return A.alloc(KC * n, BF16).rearrange("p (c n) -> p c n", c=KC)

        wqa, wka2, wva, wqb = walloc(512), walloc(256), walloc(128), walloc(512)
        wBq = self.load_w(wqa, win[:, :, C_QA:C_QA + 512], KC, 512, gain=self.gmix)
        for g in range(2):
            for hlf in range(2):
                self.load_w(wka2[:, :, g * 128 + hlf * 64: g * 128 + hlf * 64 + 64], win[:, :, C_KA + 64 * g:C_KA + 64 * g + 64], KC, 64,
                            gain=self.gmix)
        self.load_w(wva, win[:, :, C_VA:C_VA + 128], KC, 128, gain=self.gmix)
        self.load_w(wqb, win[:, :, C_QB:C_QB + 512], KC, 512, gain=self.gmix)
        wB = P.buf("w2")
        xts = [A.alloc(KC * SLOTW, F32).rearrange("p (c n) -> p c n", c=KC) for _ in range(2)]
        xBs = P.bufs(2, "x")
        sq = [A.alloc(SLOTW, BF16) for _ in range(2)]
        sqB = P.bufs(2, "sq")
        hTs = [A.alloc(KC * SLOTW, BF16).rearrange("p (c n) -> p c n", c=KC) for _ in range(2)]
        hBs = [P.bufs(KC, "h") for _ in range(2)]
        rstds = [A.alloc(SLOTW, F32) for _ in range(2)]
        rBs = P.bufs(2, "r")
        tmp = [(A.alloc(512, BF16), A.alloc(512, F32)) for _ in range(3)]
        tmpB = [P.bufs(2, "tmp") for _ in range(3)]
        qaT = A.alloc(4 * HW, BF16).rearrange("p (c n) -> p c n", c=4)
        qaB = P.bufs(4, "qa")
        kaT = A.alloc(2 * SLOTW, BF16).rearrange("p (g n) -> p g n", g=2)
        kaB = P.bufs(2, "ka")
        va = A.alloc(6 * 128, BF16).rearrange("p (b n) -> p b n", b=6)
        vaB = P.buf("va")
        qbT = A.alloc(4 * HW, BF16).rearrange("p (c n) -> p c n", c=4)
        qbB = P.buf("qb")
        yaT = A.alloc(4 * HW, BF16).rearrange("p (c n) -> p c n", c=4)
        yaB = P.buf("ya")
        pt = [A.alloc(512, BF16) for _ in range(8)]
        ptB = P.bufs(8, "pt")
        den = [A.alloc(512, F32) for _ in range(2)]
        denB = P.bufs(2, "den")
        xo = self.d_xo.ap()
        nsl = min(NSLOT, P2_SLOTS)

        def stats(i):
            pp = i % 2
            P.dma(xts[pp], xo[:, :, i * SLOTW:(i + 1) * SLOTW], writes=[xBs[pp]], key="x2")
            self.norm_tile(xts[pp], xBs[pp], SLOTW, sq, sqB, hTs[pp], hBs[pp], rstds[pp], rBs[pp], [(0, 512, 0), (512, 256, 7)])

        stats(0)
        for i in range(nsl):
            hT, hB = hTs[i % 2], hBs[i % 2]
            tasks = []
            for (o, w) in ((128, 512), (640, 128)):
                for cm in range(4):
                    tasks.append(((lambda c, cm=cm: wqa[:, c, cm * 128:(cm + 1) * 128]), o, w, qaT[:, cm, o - 128:o - 128 + w], qaB[cm], self.gq_a))
                for cm in range(4):
                    tasks.append(((lambda c, cm=cm: wqb[:, c, cm * 128:(cm + 1) * 128]), o, w, qbT[:, cm, o - 128:o - 128 + w], qbB, self.gq_b))
            for (o, w) in ((0, 512), (512, 256)):
                for g in range(2):
                    tasks.append(((lambda c, g=g: wka2[:, c, g * 128:(g + 1) * 128]), o, w, kaT[:, g, o:o + w], kaB[g], self.gk_a))
            P.dma(self.d_H.ap()[i], hT[:, :, 128:SLOTW], reads=hB, eng=STORE_ENG)
            self.ph_tasks(tasks, hT, hB, wB, tmp, tmpB, (1, 2, 3), (5, 6))
            P.dma(self.d_QB.ap()[i], qbT, reads=[qbB], key="qbo", eng=STORE_ENG)
            for half in range(2):
                bk = 4 + half
                for bl in range(3):
                    blk = half * 3 + bl
                    for c in range(KC):
                        self.mm(ps[:, bk, bl * 128:(bl + 1) * 128], hT[:, c, blk * 128:(blk + 1) * 128], wva[:, c, :], c == 0, c == KC - 1,
                                [hB[c], wB], [psB[bk]])
                self.cp("act", va[:, half * 3:half * 3 + 3, :], ps[:, bk, 0:384].rearrange("p (b n) -> p b n", b=3), [psB[bk]], [vaB])
            if i + 1 < nsl:
                stats(i + 1)
            def swa_front(n, i=i):
                qo = (n - 1) * 128
                for g in range(2):
                    for kk, kblk in enumerate((n - 1, n)):
                        idx = g * 2 + kk
                        pi = (n % 2) * 4 + idx
                        b0 = (2, 6)[idx % 2]
                        for r in range(4):
                            par, rr = r % 2, r // 2
                            pb = 64 * par
                            self.mm(ps[:, b0 + par, rr * 128:(rr + 1) * 128], kaT[pb:pb + 64, g, kblk * 128:(kblk + 1) * 128],
                                    qaT[pb:pb + 64, 2 * g + rr, qo:qo + 128], True, True,
                                    [kaB[g], qaB[2 * g + rr]], [psB[b0 + par]])
                        self.act(pt[pi].rearrange("p (a n) -> p a n", a=2), ps[:, b0:b0 + 2, 0:256], AF.Exp,
                                 [psB[b0], psB[b0 + 1], self.cB], [ptB[pi]], scale=0.125)
                        ty = 1 if kk == 0 else 0
                        fa = self.sa[:, ty, 4 * g:4 * g + 4, :].rearrange("p (rr par) q -> p par rr q", par=2)
                        p4 = pt[pi].rearrange("p (par rr q) -> p par rr q", par=2, rr=2)
                        meng = "dve" if idx % 2 == 0 else "pool"
                        self.tt(meng, p4, p4, fa, ALU.mult, [ptB[pi], self.stripB], [ptB[pi]])
                        if i == 0 and n == 2 and kk == 0:
                            self.ts(meng, pt[pi], pt[pi], self.m0[:, 0:1], None, ALU.mult, None, [ptB[pi], self.sB], [ptB[pi]])

            def swa_back(n):
                qo = (n - 1) * 128
                for g in range(2):
                    for kk, kblk in enumerate((n - 1, n)):
                        pi = (n % 2) * 4 + g * 2 + kk
                        self.mm(ps[64 * g:64 * g + 64, 4, :], va[:, kblk, 64 * g:64 * g + 64], pt[pi], kk == 0, kk == 1,
                                [vaB, ptB[pi]], [psB[4]])
                    for kk in range(2):
                        pi = (n % 2) * 4 + g * 2 + kk
                        self.mm(ps[64 * g:64 * g + 64, 5, :], self.ones_bf[:, 0:64], pt[pi], kk == 0, kk == 1,
                                [ptB[pi], self.cB], [psB[5]])
                dn = den[n % 2]
                for par in range(2):
                    for rr in range(2):
                        cb_ = (par * 2 + rr) * 128
                        r_ = 2 * rr + par
                        self.act(dn[:, cb_:cb_ + 128], ps[:, 5, cb_:cb_ + 128], AF.Ln, [psB[5], self.cB, self.sB], [denB[n % 2]],
                                 bias=self.esk[:, r_:r_ + 1], scale=1.0)
                P.strict = True
                self.act(dn, dn, AF.Exp, [denB[n % 2], self.cB], [denB[n % 2]], scale=-1.0)
                P.strict = False
                self.tt("dve", yaT[:, :, qo:qo + 128].rearrange("p (rr par) q -> p par rr q", par=2),
                        ps[:, 4, :].rearrange("p (par rr q) -> p par rr q", par=2, rr=2),
                        dn.rearrange("p (par rr q) -> p par rr q", par=2, rr=2), ALU.mult, [psB[4], denB[n % 2]], [yaB])

            swa_front(1)
            for n in range(1, 6):
                if n < 5:
                    swa_front(n + 1)
                swa_back(n)
            P.dma(self.d_YA.ap()[i], yaT, reads=[yaB], key="yao", eng=STORE_ENG)
        P.barrier()
        A.release(m)

    def phase3(self):
        P, A = self.P, self.A
        ps, psB = self.ps, self.psB
        m = A.mark()
        KTs = [A.alloc(S, BF16) for _ in range(2)]
        VSs = [A.alloc(128 * 128, BF16).rearrange("p (b e) -> p b e", b=128) for _ in range(2)]
        kvB = [(P.bufs(4, "kt"), P.bufs(4, "vs")) for _ in range(2)]
        qt = [A.alloc(HW, BF16) for _ in range(2)]
        qB = P.bufs(2, "q")
        NPB = 4
        pt = [A.alloc(1024, BF16) for _ in range(NPB)]
        ptB = P.bufs(NPB, "pt")
        ssum = [A.alloc(512, F32) for _ in range(2)]
        ssumB = P.bufs(2, "ssum")
        rbc = [A.alloc(1024, F32) for _ in range(2)]
        rbcB = P.bufs(2, "rbc")
        dd = [A.alloc(1024, F32) for _ in range(2)]
        ddB = P.bufs(2, "dd")
        dsq = [A.alloc(512, BF16) for _ in range(2)]
        dsqB = P.bufs(2, "dsq")
        rrow = [A.alloc(512, F32) for _ in range(2)]
        rrowB = P.bufs(2, "rrow")
        rrbc = [A.alloc(512, F32) for _ in range(2)]
        rrbcB = P.bufs(2, "rrbc")
        yb = [A.alloc(HW, BF16) for _ in range(2)]
        ybB = P.bufs(2, "yb")
        dbcB = [P.bufs(2, "dbc") for _ in range(2)]
        q7B = P.bufs(4, "ps7q")
        ones32 = self.ones_bf[:, 0:32]
        SB = [(0, 1), (2, 3)]
        dbc = self.d_bc

        def load_kv(h):
            pp = h % 2
            for q4 in range(4):
                P.dma(KTs[pp][:, q4 * 4096:(q4 + 1) * 4096], self.d_KT.ap()[h, :, q4 * 4096:(q4 + 1) * 4096], writes=[kvB[pp][0][q4]])
            for q4 in range(4):
                P.dma(VSs[pp][:, q4 * 32:(q4 + 1) * 32, :], self.d_VS.ap()[h, :, q4 * 32:(q4 + 1) * 32, :], writes=[kvB[pp][1][q4]])

        pending = []
        gcount = [0]
        sbase = [0]

        def flush():
            while pending:
                pending.pop(0)()

        def finalize(ncol, o_ap, obufs, ycols, ybt, ybb, out_dma, nred=1):
            gp = gcount[0] % 2
            gcount[0] += 1
            small = ncol < 128
            P.strict = small
            ss_, rb_, dd_, dq_, rw_, rrb_ = ssum[gp], rbc[gp], dd[gp], dsq[gp], rrow[gp], rrbc[gp]
            if nred == 1:
                self.cp("dve", ss_[0:64, 0:ncol], ps[0:64, 7, 0:ncol], [q7B[0], q7B[1]], [ssumB[gp]])
            else:
                wtot = nred * ncol
                self.cp("dve", ss_[0:64, 0:wtot], ps[0:64, 7, 0:wtot], [q7B[0], q7B[1]], [ssumB[gp]])
                P.strict = True
                if nred == 12:
                    steps = [(4 * ncol, 8 * ncol, 4 * ncol), (4 * ncol, 4 * ncol, 4 * ncol), (2 * ncol, 2 * ncol, 2 * ncol), (ncol, ncol, ncol)]
                else:
                    steps = [(8 * ncol, 8 * ncol, 8 * ncol), (4 * ncol, 4 * ncol, 4 * ncol), (2 * ncol, 2 * ncol, 2 * ncol), (ncol, ncol, ncol)]
                for (wd_, src_, _) in steps:
                    self.tt("dve", ss_[0:64, 0:wd_], ss_[0:64, 0:wd_], ss_[0:64, src_:src_ + wd_], ALU.add, [ssumB[gp]], [ssumB[gp]])
                P.strict = small
            for comp in range(2):
                P.dma(dbc.ap()[gp, comp:comp + 1, 0:ncol], ss_[32 * comp:32 * comp + 1, 0:ncol], reads=[ssumB[gp]], writes=[dbcB[gp][0]])
            P.dma(rb_.rearrange("p (c n) -> p c n", c=2)[:, :, 0:ncol], bass.AP(dbc, gp * 3 * 512, [[0, 128], [512, 2], [1, ncol]]),
                  reads=[dbcB[gp][0]], writes=[rbcB[gp]])
            for comp in range(2):
                r_ = rb_[:, comp * 512:comp * 512 + ncol]
                self.ts("dve", r_, r_, self.tiny1[:, 0:1], None, ALU.add, None, [rbcB[gp], self.cB], [rbcB[gp]])
                self.rcp(r_, r_, [rbcB[gp]], [rbcB[gp]])
                self.tt("dve", dd_[:, comp * 512: comp * 512 + ncol], o_ap(comp), r_, ALU.mult, [obufs[comp], rbcB[gp]], [ddB[gp]])
            self.stt("dve", dd_[:, 0:ncol], dd_[:, 512:512 + ncol], self.neglam[:, 0:1], dd_[:, 0:ncol], ALU.mult, ALU.add, [ddB[gp], self.sB], [ddB[gp]])
            self.tt("pool", dq_[:, 0:ncol], dd_[:, 0:ncol], dd_[:, 0:ncol], ALU.mult, [ddB[gp]], [dsqB[gp]])
            P.strict = False

            def stage1():
                P.strict = small
                self.mm(ps[64:96, 7, 0:ncol], ones32, dq_[:, 0:ncol], True, True, [dsqB[gp], self.cB], [q7B[2]])
                self.act(rw_[64:96, 0:ncol], ps[64:96, 7, 0:ncol], AF.Ln, [q7B[2], self.cB], [rrowB[gp]], bias=self.eps1[64:96, :], scale=1.0 / 128)
                P.strict = True
                self.act(rw_[64:96, 0:ncol], rw_[64:96, 0:ncol], AF.Exp, [rrowB[gp], self.cB], [rrowB[gp]], bias=self.zero1[64:96, :], scale=-0.5)
                P.strict = small
                P.dma(dbc.ap()[gp, 2:3, 0:ncol], rw_[64:65, 0:ncol], reads=[rrowB[gp]], writes=[dbcB[gp][1]])
                P.dma(rrb_[:, 0:ncol], bass.AP(dbc, (gp * 3 + 2) * 512, [[0, 128], [1, ncol]]), reads=[dbcB[gp][1]], writes=[rrbcB[gp]])
                self.stt("dve", ybt[:, ycols:ycols + ncol], dd_[:, 0:ncol], self.gsub[:, 0:1], rrb_[:, 0:ncol], ALU.mult, ALU.mult,
                         [ddB[gp], rrbcB[gp], self.sB], [ybb])
                P.strict = False
                if out_dma is not None:
                    out_dma()
            pending.append(stage1)

        def group(nsteps, qk, av, near_off, ncol, ncols_sum, sum_rhs, pre_done=False, next_qk0=None):
            def pe_tail(t):
                p_, pB_ = pt[t % NPB], ptB[t % NPB]
                for comp in range(2):
                    rl = sum_rhs(p_, comp)
                    for n_, r_ in enumerate(rl):
                        self.mm(ps[32 * comp:32 * comp + 32, 7, 0:ncol], ones32, r_, t == 0 and n_ == 0, t == nsteps - 1 and n_ == len(rl) - 1,
                                [pB_, self.cB], [q7B[comp]])
                av(t, p_, pB_)

            b0 = sbase[0]
            sbase[0] += nsteps
            if not pre_done:
                qk(0, b0 % 2)
            for t in range(nsteps):
                if t + 1 < nsteps:
                    qk(t + 1, (b0 + t + 1) % 2)
                elif next_qk0 is not None:
                    next_qk0((b0 + nsteps) % 2)
                sb = SB[(b0 + t) % 2]
                p_, pB_ = pt[t % NPB], ptB[t % NPB]
                self.act(p_.rearrange("p (c n) -> p c n", c=2), ps[:, sb[0]:sb[0] + 2, :], AF.Exp, [psB[sb[0]], psB[sb[1]], self.cB], [pB_], scale=0.125)
                off = near_off(t)
                if off is not None:
                    for comp in range(2):
                        self.tt("dve", p_[:, comp * 512:(comp + 1) * 512], p_[:, comp * 512:(comp + 1) * 512],
                                self.strip_h[:, off:off + 512], ALU.mult, [pB_, self.stripB], [pB_])
                if t >= 1:
                    pe_tail(t - 1)
                if t == min(10, nsteps - 1):
                    flush()
            pe_tail(nsteps - 1)

        def group_h(nsteps, qk, av, near, batches, HQ, pre_done=False, next_qk0=None):
            def pe_tail(t):
                kb0, nseg = batches[t]
                wd = nseg * HQ
                p_, pB_ = pt[t % NPB], ptB[t % NPB]
                for comp in range(2):
                    self.mm(ps[32 * comp:32 * comp + 32, 7, 0:wd], ones32, p_[:, comp * 512:comp * 512 + wd], t == 0, t == nsteps - 1,
                            [pB_, self.cB], [q7B[comp]])
                av(t, p_, pB_)

            b0 = sbase[0]
            sbase[0] += nsteps
            if not pre_done:
                qk(0, b0 % 2)
            for t in range(nsteps):
                if t + 1 < nsteps:
                    qk(t + 1, (b0 + t + 1) % 2)
                elif next_qk0 is not None:
                    next_qk0((b0 + nsteps) % 2)
                sb = SB[(b0 + t) % 2]
                kb0, nseg = batches[t]
                wd = nseg * HQ
                p_, pB_ = pt[t % NPB], ptB[t % NPB]
                self.act(p_.rearrange("p (c n) -> p c n", c=2)[:, :, 0:wd], ps[:, sb[0]:sb[0] + 2, 0:wd], AF.Exp,
                         [psB[sb[0]], psB[sb[1]], self.cB], [pB_], scale=0.125)
                nr = near(t)
                if nr is not None:
                    c0, ns_, off = nr
                    fa = self.strip_h[:, off:off + ns_ * 128].rearrange("p (u c) -> p u c", c=128)[:, :, 0:HQ]
                    for comp in range(2):
                        pv = p_[:, comp * 512 + c0 * HQ: comp * 512 + (c0 + ns_) * HQ].rearrange("p (u c) -> p u c", c=HQ)
                        self.tt("dve", pv, pv, fa, ALU.mult, [pB_, self.stripB], [pB_])
                if t >= 1:
                    pe_tail(t - 1)
            pe_tail(nsteps - 1)

        def make_slot(h, i, qq):
            pp = h % 2
            KT, VS = KTs[pp], VSs[pp]
            kB, vB = kvB[pp]
            q, qb_ = qt[qq], qB[qq]
            ybt, ybb = yb[qq], ybB[qq]
            strip_h = self.strip[:, h, :]
            nkb = 16 * i + 16
            near0 = 16 * i - 1
            HQ = 32
            batches = [(16 * b, 16) for b in range(i)] + [(16 * i, 12)]
            nb = len(batches)

            def qk(t, par):
                sb = SB[par]
                for comp in range(2):
                    self.mm(ps[:, sb[comp], :], KT[64 * comp:64 * comp + 64, t * 128:(t + 1) * 128], q[64 * comp:64 * comp + 64, 128:640],
                            True, True, [kB[t // 32], qb_], [psB[sb[comp]]])

            def av(t, p_, pB_):
                for comp in range(2):
                    self.mm(ps[:, 4 + comp, :], VS[:, t, :], p_[:, comp * 512:(comp + 1) * 512], t == 0, t == nkb - 1,
                            [vB[t // 32], pB_], [psB[4 + comp]])

            def qkh(t, par):
                sb = SB[par]
                kb0, nseg = batches[t]
                for comp in range(2):
                    for u in range(nseg):
                        kb = kb0 + nseg - 1 - u
                        self.mm(ps[:, sb[comp], u * HQ:(u + 1) * HQ], KT[64 * comp:64 * comp + 64, kb * 128:(kb + 1) * 128],
                                q[64 * comp:64 * comp + 64, 128 - HQ:128], True, True, [kB[kb // 32], qb_], [psB[sb[comp]]])

            def avh(t, p_, pB_):
                kb0, nseg = batches[t]
                for comp in range(2):
                    for u in range(nseg):
                        kb = kb0 + nseg - 1 - u
                        self.mm(ps[:, 6, comp * HQ:(comp + 1) * HQ], VS[:, kb, :], p_[:, comp * 512 + u * HQ: comp * 512 + (u + 1) * HQ],
                                t == 0 and u == 0 and comp == 0, t == nb - 1 and u == nseg - 1, [vB[kb // 32], pB_], [psB[6]])

            def nearh(t):
                kb0, nseg = batches[t]
                if kb0 == 16 * i:
                    return (0, 12, 2048 - 128 * 13 + (128 - HQ))
                if kb0 == 16 * i - 16:
                    return (0, 4, 2048 - 128 + (128 - HQ))
                return None

            def odma():
                P.dma(self.d_YB.ap()[i, :, h, :], ybt, reads=[ybb])

            def run_main(pre_done, next_qk0):
                self.strip_h = strip_h
                group(nkb, qk, av, lambda t: (2048 - 128 * (t - near0)) if t >= near0 else None, 512, 512,
                      lambda p_, comp: [p_[:, comp * 512:(comp + 1) * 512]], pre_done, next_qk0)
                finalize(512, lambda comp: ps[:, 4 + comp, :], [psB[4], psB[5]], 128, ybt, ybb, None)

            def run_halo(pre_done, next_qk0):
                self.strip_h = strip_h
                group_h(nb, qkh, avh, nearh, batches, HQ, pre_done, next_qk0)
                finalize(HQ, lambda comp: ps[:, 6, comp * HQ:(comp + 1) * HQ], [psB[6], psB[6]], 128 - HQ, ybt, ybb, odma,
                         nred=(16 if i > 0 else 12))

            return dict(run_main=run_main, run_halo=run_halo, qk0=lambda par: qk(0, par), qkh0=lambda par: qkh(0, par))

        slots = [(h, i) for h in range(4) for i in range(NSLOT)]
        descs = {}

        def get(k):
            if k not in descs:
                descs[k] = make_slot(slots[k][0], slots[k][1], k % 2)
            return descs[k]

        load_kv(0)
        P.dma(qt[0], self.d_QB.ap()[0, :, 0, :], writes=[qB[0]])
        pre = False
        for k, (h, i) in enumerate(slots):
            if i == 0 and h + 1 < 4:
                load_kv(h + 1)
            if k + 1 < len(slots):
                nh, ni = slots[k + 1]
                P.dma(qt[(k + 1) % 2], self.d_QB.ap()[ni, :, nh, :], writes=[qB[(k + 1) % 2]])
            d = get(k)
            d["run_main"](pre, d["qkh0"])
            nxt = get(k + 1)["qk0"] if k + 1 < len(slots) else None
            d["run_halo"](True, nxt)
            pre = nxt is not None
            descs.pop(k, None)
        flush()
        P.barrier()
        A.release(m)

    def phase4a(self):
        P, A = self.P, self.A
        ps, psB = self.ps, self.psB
        A.release(self.pre_strip_mark)
        m = A.mark()
        win = self.d_win.ap()
        wgl = A.alloc(KC * 2048, BF16).rearrange("p (c n) -> p c n", c=KC)
        wbra = A.alloc(4 * D, BF16).rearrange("p (c n) -> p c n", c=4)
        wbrb = A.alloc(4 * D, BF16).rearrange("p (c n) -> p c n", c=4)
        wo = A.alloc(KC * D, BF16).rearrange("p (c n) -> p c n", c=KC)
        self.load_w(wgl, win[:, :, C_GL:C_GL + 2048], KC, 2048, gain=self.gmix)
        self.load_w(wbra, self.d_wbra.ap(), 4, D)
        self.load_w(wbrb, self.d_wbrb.ap(), 4, D)
        self.load_w(wo, self.d_wo.ap(), KC, D)
        wB = P.buf("w4a")
        xts = [A.alloc(KC * HW, F32).rearrange("p (c n) -> p c n", c=KC) for _ in range(2)]
        xBs = P.bufs(2, "x")
        hTs = [A.alloc(KC * HW, BF16).rearrange("p (c n) -> p c n", c=KC) for _ in range(2)]
        hBs = P.bufs(2, "h")
        yas = [A.alloc(4 * HW, BF16).rearrange("p (c n) -> p c n", c=4) for _ in range(2)]
        ybs = [A.alloc(4 * HW, BF16).rearrange("p (c n) -> p c n", c=4) for _ in range(2)]
        yBs = [P.bufs(2, "y") for _ in range(2)]
        gates = A.alloc(16 * HW, BF16).rearrange("p (c n) -> p c n", c=16)
        gB = P.bufs(16, "g")
        mixed = A.alloc(KC * HW, BF16).rearrange("p (c n) -> p c n", c=KC)
        mxB = P.bufs(KC, "mx")
        t1 = [A.alloc(512, BF16) for _ in range(2)]
        t2 = [A.alloc(512, BF16) for _ in range(2)]
        tB = [P.bufs(2, "t") for _ in range(2)]
        x2 = [A.alloc(512, F32) for _ in range(2)]
        x2B = P.bufs(2, "x2")
        xo = self.d_xo.ap()
        PIECES = ((96, 512), (608, 32))
        cnt = 0
        def loads(i):
            pp = i % 2
            P.dma(hTs[pp], self.d_H.ap()[i], writes=[hBs[pp]])
            P.dma(yas[pp], self.d_YA.ap()[i], writes=[yBs[pp][0]])
            P.dma(ybs[pp], self.d_YB.ap()[i], writes=[yBs[pp][1]])
            P.dma(xts[pp], xo[:, :, i * SLOTW + 128:(i + 1) * SLOTW], writes=[xBs[pp]])

        loads(0)
        for i in range(NSLOT):
            pp = i % 2
            if i + 1 < NSLOT:
                loads(i + 1)
            xt, xB, hT, ya, yb, yB = xts[pp], xBs[pp], hTs[pp], yas[pp], ybs[pp], yBs[pp]
            hB = [hBs[pp]] * KC
            for (o, w) in PIECES:
                for gc in range(16):
                    bk = 1 + gc % 2
                    for c in range(KC):
                        self.mm(ps[:, bk, 0:w], wgl[:, c, gc * 128:(gc + 1) * 128], hT[:, c, o:o + w], c == 0, c == KC - 1, [wB, hB[c]], [psB[bk]])
                    self.act(gates[:, gc, o:o + w], ps[:, bk, 0:w], AF.Sigmoid, [psB[bk], self.cB], [gB[gc]])
            for (o, w) in PIECES:
                for mc in range(KC):
                    k2 = cnt % 2
                    cnt += 1
                    ba, bb = 3 + 2 * k2, 4 + 2 * k2
                    for r in range(4):
                        self.mm(ps[:, ba, 0:w], wbra[:, r, mc * 128:(mc + 1) * 128], ya[:, r, o:o + w], r == 0, r == 3, [wB, yB[0]], [psB[ba]])
                    for r in range(4):
                        self.mm(ps[:, bb, 0:w], wbrb[:, r, mc * 128:(mc + 1) * 128], yb[:, r, o:o + w], r == 0, r == 3, [wB, yB[1]], [psB[bb]])
                    self.tt("dve", t1[k2][:, 0:w], ps[:, ba, 0:w], gates[:, mc, o:o + w], ALU.mult, [psB[ba], gB[mc]], [tB[k2][0]])
                    self.tt("dve", t2[k2][:, 0:w], ps[:, bb, 0:w], gates[:, 8 + mc, o:o + w], ALU.mult, [psB[bb], gB[8 + mc]], [tB[k2][1]])
                    self.tt("pool", mixed[:, mc, o:o + w], t1[k2][:, 0:w], t2[k2][:, 0:w], ALU.add, [tB[k2][0], tB[k2][1]], [mxB[mc]])
            for (o, w) in PIECES:
                for oc in range(KC):
                    k2 = cnt % 2
                    cnt += 1
                    bk = 1 + k2
                    for mc in range(KC):
                        self.mm(ps[:, bk, 0:w], wo[:, mc, oc * 128:(oc + 1) * 128], mixed[:, mc, o:o + w], mc == 0, mc == KC - 1, [wB, mxB[mc]], [psB[bk]])
                    self.tt("dve", x2[k2][:, 0:w], ps[:, bk, 0:w], xt[:, oc, o:o + w], ALU.add, [psB[bk], xB], [x2B[k2]])
                    P.dma(self.d_X2.ap()[i, :, oc, o:o + w], x2[k2][:, 0:w], reads=[x2B[k2]], key=f"x2o{k2}", eng=STORE_ENG)
        P.barrier()
        A.release(m)

    def phase4b(self):
        P, A = self.P, self.A
        ps, psB = self.ps, self.psB
        A.release(self.pre_strip_mark)
        m = A.mark()
        wup = A.alloc(KC * 2 * DFF, BF16).rearrange("p (c n) -> p c n", c=KC)
        wdn = A.alloc(NFC * D, BF16).rearrange("p (c n) -> p c n", c=NFC)
        self.load_w(wup, self.d_wup.ap(), KC, 2 * DFF, gain=self.gffn)
        self.load_w(wdn, self.d_wdn.ap(), NFC, D)
        wB = P.buf("w4b")
        W2 = 514
        xa_off = A.alloc_raw(NFC * 512 * 2)
        xaB = P.buf("xa")
        sq = [A.alloc(W2, BF16) for _ in range(2)]
        sqB = P.bufs(2, "sq")
        hT = A.alloc(KC * W2, BF16).rearrange("p (c n) -> p c n", c=KC)
        hB = P.bufs(KC, "h")
        rstd = A.alloc(W2, F32)
        rB = P.buf("r")
        uh = A.alloc(44 * 2, F32).rearrange("p (c n) -> p c n", c=44)
        uhB = P.buf("uh")
        U = [A.alloc(W2, F32) for _ in range(4)]
        UB = P.bufs(4, "U")
        tg = [A.alloc(512, F32) for _ in range(3)]
        tv = [A.alloc(512, F32) for _ in range(3)]
        tgB = P.bufs(3, "tg")
        tvB = P.bufs(3, "tv")
        ot = [A.alloc(512, F32) for _ in range(2)]
        otB = P.bufs(2, "ot")
        xr = [A.alloc(512, F32) for _ in range(2)]
        xrB = P.bufs(2, "xr")
        cw = self.cw.rearrange("p (a b) -> p a b", a=3)
        outT = self.d_out.ap()
        xt = A.at(xa_off, KC * W2, F32).rearrange("p (c n) -> p c n", c=KC)
        aT = A.at(xa_off, NFC * 512, BF16).rearrange("p (c n) -> p c n", c=NFC)
        for i in range(NSLOT):
            P.dma(xt, self.d_X2.ap()[i, :, :, 126:640], writes=[xaB])
            self.norm_tile(xt, xaB, W2, sq, sqB, hT, hB, rstd, rB, [(0, 2, 7), (2, 512, 0)])
            for fc in range(44):
                for c in range(KC):
                    self.mm(ps[:, 7, fc * 2:fc * 2 + 2], wup[:, c, fc * 128:(fc + 1) * 128], hT[:, c, 0:2], c == 0, c == KC - 1, [wB, hB[c]], [psB[7]])
            self.cp("dve", uh, ps[:, 7, 0:88].rearrange("p (c n) -> p c n", c=44), [psB[7]], [uhB])
            def gate_tail(f):
                k2 = f % 3
                self.act(tg[k2], tg[k2], AF.Silu, [tgB[k2], self.cB], [tgB[k2]])
                self.tt("pool", aT[:, f, :], tg[k2], tv[k2], ALU.mult, [tgB[k2], tvB[k2]], [xaB])

            for f in range(NFC):
                k2 = f % 2
                k3 = f % 3
                for half, fc in enumerate((f, NFC + f)):
                    bk = 1 + 2 * k2 + half
                    ui = 2 * k2 + half
                    for c in range(KC):
                        self.mm(ps[:, bk, :], wup[:, c, fc * 128:(fc + 1) * 128], hT[:, c, 2:W2], c == 0, c == KC - 1, [wB, hB[c]], [psB[bk]])
                    self.cp("pool", U[ui][:, 0:2], uh[:, fc, :], [uhB], [UB[ui]])
                    self.cp("act", U[ui][:, 2:W2], ps[:, bk, :], [psB[bk]], [UB[ui]])
                    t_, tb_ = (tg[k3], tgB[k3]) if half == 0 else (tv[k3], tvB[k3])
                    eng = "dve"
                    self.ts(eng, t_, U[ui][:, 0:512], cw[:, 0, fc:fc + 1], self.cb[:, fc:fc + 1], ALU.mult, ALU.add, [UB[ui], self.sB], [tb_])
                    self.stt(eng, t_, U[ui][:, 1:513], cw[:, 1, fc:fc + 1], t_, ALU.mult, ALU.add, [UB[ui], self.sB, tb_], [tb_])
                    self.stt(eng, t_, U[ui][:, 2:514], cw[:, 2, fc:fc + 1], t_, ALU.mult, ALU.add, [UB[ui], self.sB, tb_], [tb_])
                if f >= 1:
                    gate_tail(f - 1)
            gate_tail(NFC - 1)
            for oc in range(KC):
                k2 = oc % 2
                bk = 5 + k2
                P.dma(xr[k2], self.d_X2.ap()[i, :, oc, 128:640], writes=[xrB[k2]])
                for f in range(NFC):
                    self.mm(ps[:, bk, :], wdn[:, f, oc * 128:(oc + 1) * 128], aT[:, f, :], f == 0, f == NFC - 1, [wB, xaB], [psB[bk]])
                self.tt("dve", ot[k2], ps[:, bk, :], xr[k2], ALU.add, [psB[bk], xrB[k2]], [otB[k2]])
                P.dma(outT[:, oc, i * 512:(i + 1) * 512], ot[k2], reads=[otB[k2]], eng=STORE_ENG)
        P.barrier()
        A.release(m)

    def build(self, nph=N_PHASES):
        self.declare()
        self.P.strict = True
        self.setup()
        self.P.strict = False
        self.P.barrier()
        phases = [self.phase1, self.phase2, self.phase3, self.phase4a, self.phase4b]
        for n_, ph in enumerate(phases[:nph]):
            if n_ == 0 and SKIP_P1:
                continue
            ph()
        self.P.barrier()
        self.P.emit()
        return self.nc


def _host_inputs(inputs):
    f = np.float32
    x = np.asarray(inputs["x"], dtype=f)

    def pc(w, kc):
        n = w.shape[1]
        return np.ascontiguousarray(w.reshape(kc, 128, n).transpose(1, 0, 2))

    w_br_a = np.asarray(inputs["w_br_a"][0], dtype=f)
    wa = w_br_a.reshape(2, 4, 64, D)
    wbra = np.ascontiguousarray(wa.transpose(0, 2, 1, 3).reshape(128, 4, D))
    common = {
        "w_in": pc(np.asarray(inputs["w_in"][0], dtype=f), KC),
        "w_br_a": wbra,
        "w_br_b": pc(np.asarray(inputs["w_br_b"][0], dtype=f), 4),
        "w_o": pc(np.asarray(inputs["w_o"][0], dtype=f), KC),
        "w_up": pc(np.asarray(inputs["w_up"][0], dtype=f), KC),
        "w_down": pc(np.asarray(inputs["w_down"][0], dtype=f), NFC),
        "g_mix": np.ascontiguousarray(np.asarray(inputs["g_mix"][0], dtype=f).reshape(KC, 128).T),
        "g_ffn": np.ascontiguousarray(np.asarray(inputs["g_ffn"][0], dtype=f).reshape(KC, 128).T),
        "conv_w": np.ascontiguousarray(np.asarray(inputs["conv_w"][0], dtype=f).reshape(3, 44, 128).transpose(2, 0, 1)),
        "conv_b": np.ascontiguousarray(np.asarray(inputs["conv_b"][0], dtype=f).reshape(44, 128).T),
        "rel_bias": np.ascontiguousarray(np.asarray(inputs["rel_bias"], dtype=f)),
        "qn_a": np.asarray(inputs["qn_a"], dtype=f).reshape(1, 64),
        "kn_a": np.asarray(inputs["kn_a"], dtype=f).reshape(1, 64),
        "qn_b": np.asarray(inputs["qn_b"], dtype=f).reshape(1, 64),
        "kn_b": np.asarray(inputs["kn_b"], dtype=f).reshape(1, 64),
        "sinks": np.asarray(inputs["sinks"], dtype=f).reshape(1, 8),
        "lam_q1": np.asarray(inputs["lam_q1"], dtype=f).reshape(1, 64),
        "lam_k1": np.asarray(inputs["lam_k1"], dtype=f).reshape(1, 64),
        "lam_q2": np.asarray(inputs["lam_q2"], dtype=f).reshape(1, 64),
        "lam_k2": np.asarray(inputs["lam_k2"], dtype=f).reshape(1, 64),
        "subln_b": np.asarray(inputs["subln_b"], dtype=f).reshape(128, 1),
        "Jmat": np.ascontiguousarray(np.eye(128, dtype=f)[::-1]),
        "bdones": np.kron(np.eye(2, dtype=f), np.ones((64, 64), dtype=f)),
    }
    oha = np.zeros((33, 384), dtype=f)
    for mm_ in range(384):
        d = mm_ - 127
        if 0 <= d < 128:
            oha[int(_t5_bucket_np(np.array(d))), mm_] = 1
        else:
            oha[32, mm_] = 1
    common["oh_a"] = oha
    in_maps = []
    for core in range(8):
        b, j = core // 4, core % 4
        xTb = np.ascontiguousarray(x[b].T.reshape(KC, 128, S).transpose(1, 0, 2))
        xo = np.zeros((128, KC, NSLOT, SLOTW), dtype=f)
        for i in range(NSLOT):
            G = 4 * i + j
            t0 = 512 * G - 256
            lo = max(t0, 0)
            xo[:, :, i, lo - t0:] = xTb[:, :, lo:t0 + SLOTW]
        ohd = np.zeros((33, VECD), dtype=f)
        d = np.arange(VECD) + 512 * j - 2047
        bk = _t5_bucket_np(d)
        for mm_ in range(VECD):
            if d[mm_] >= 0:
                ohd[bk[mm_], mm_] = 1
            else:
                ohd[32, mm_] = 1
        mp = dict(common)
        mp["xT"] = xTb
        mp["xo"] = xo.reshape(128, KC, NSLOT * SLOTW)
        mp["oh_d"] = ohd
        mp["m0"] = np.full((128, 1), 0.0 if j == 0 else 1.0, dtype=f)
        in_maps.append(mp)
    return in_maps


_NC_CACHE = {}


def kernel(**inputs):
    in_maps = _host_inputs(inputs)
    if "nc" not in _NC_CACHE:
        _NC_CACHE["nc"] = K().build()
    nc = _NC_CACHE["nc"]
    res = run_bass_kernel_spmd(nc, in_maps, core_ids=list(range(8)))
    out = np.zeros((2, S, D), dtype=np.float32)
    for core in range(8):
        b, j = core // 4, core % 4
        o = res.results[core]["outT"].reshape(128, KC, NSLOT, 512)
        for i in range(NSLOT):
            G = 4 * i + j
            out[b, 512 * G:512 * (G + 1), :] = o[:, :, i, :].transpose(2, 1, 0).reshape(512, D)
    return out
```

```python
import contextlib
import math
import numpy as np
import concourse.bass as bass
import concourse.mybir as mybir
from concourse.bass_utils import run_bass_kernel_spmd

F32 = mybir.dt.float32
BF16 = mybir.dt.bfloat16
AF = mybir.ActivationFunctionType
ALU = mybir.AluOpType

ENGS = ("pe", "act", "dve", "pool", "sp")
SEM_ROT = 3000


class Buf:
    __slots__ = ("name", "w", "rs")

    def __init__(self, name):
        self.name = name
        self.w = None
        self.rs = []


class Op:
    __slots__ = ("eng", "fn", "waits", "signal", "dma_key", "dma_sem", "dma_val", "sig_sem", "sig_val")

    def __init__(self, eng, fn, dma_key=None):
        self.eng = eng
        self.fn = fn
        self.waits = []
        self.signal = False
        self.dma_key = dma_key
        self.dma_sem = None
        self.dma_val = None
        self.sig_sem = None
        self.sig_val = None


class Prog:
    def __init__(self, nc):
        self.nc = nc
        self.ops = {e: [] for e in ENGS}
        self.dma_cnt = {}
        self.all_dma_last = {}
        self.stack = contextlib.ExitStack()
        self.nbufs = 0
        self.strict = False

    def buf(self, name=None):
        self.nbufs += 1
        return Buf(f"{name or 'b'}#{self.nbufs}")

    def bufs(self, n, name="b"):
        return [self.buf(f"{name}{i}") for i in range(n)]

    def _dep(self, op, y):
        if y is None or y is op:
            return
        if y.dma_key is None and y.eng == op.eng and op.dma_key is None and not self.strict:
            return
        if y.dma_key is None:
            y.signal = True
        if y not in op.waits:
            op.waits.append(y)

    def op(self, eng, fn, reads=(), writes=(), dma_key=None):
        o = Op(eng, fn, dma_key)
        for b in reads:
            self._dep(o, b.w)
        for b in writes:
            self._dep(o, b.w)
            for r in b.rs:
                self._dep(o, r)
        for b in reads:
            b.rs.append(o)
        for b in writes:
            b.w = o
            b.rs = []
        if dma_key is not None:
            st = self.dma_cnt.setdefault(dma_key, [0, 0])
            if st[1] + 16 > 4000:
                st[0] += 1
                st[1] = 0
            st[1] += 16
            o.dma_sem = (dma_key, st[0])
            o.dma_val = st[1]
            self.all_dma_last[dma_key] = o
        self.ops[eng].append(o)
        return o

    def dma(self, out, in_, reads=(), writes=(), key=None, eng="sp", **kw):
        prim = writes[0] if len(writes) else reads[0]
        key = key if (key is not None and key.startswith("SH_")) else prim.name
        return self.op(eng, lambda e: e.dma_start(out=out, in_=in_, **kw), reads, writes, dma_key=key)

    def barrier(self):
        lasts = [self.ops[e][-1] for e in ENGS if self.ops[e]]
        dmas = list(self.all_dma_last.values())
        news = []
        for e in ENGS:
            o = Op(e, None)
            for y in lasts:
                self._dep(o, y)
            for y in dmas:
                self._dep(o, y)
            news.append(o)
        for o in news:
            self.ops[o.eng].append(o)

    def emit(self):
        nc = self.nc
        semkeys = set()
        for e in ENGS:
            gen, cnt = 0, 0
            for o in self.ops[e]:
                if o.dma_key is not None:
                    semkeys.add(o.dma_sem)
                    continue
                if o.signal:
                    if cnt >= SEM_ROT:
                        gen += 1
                        cnt = 0
                    cnt += 1
                    o.sig_sem = ("eng", e, gen)
                    o.sig_val = cnt
                    semkeys.add(o.sig_sem)
        sems = {}
        for n, k in enumerate(sorted(semkeys, key=str)):
            sems[k] = self.stack.enter_context(nc.semaphore(f"sm{n}"))
        self.nsems = len(sems)
        block = self.stack.enter_context(nc.Block())
        engmap = {"pe": "tensor", "act": "scalar", "dve": "vector", "pool": "gpsimd", "sp": "sync"}

        def make(e):
            def body(eng):
                waited = {}
                for o in self.ops[e]:
                    for y in o.waits:
                        if y.dma_key is not None:
                            sk, v = y.dma_sem, y.dma_val
                        else:
                            sk, v = y.sig_sem, y.sig_val
                        if waited.get(sk, 0) >= v:
                            continue
                        waited[sk] = v
                        eng.wait_ge(sems[sk], v)
                    if o.fn is None:
                        if o.signal:
                            eng.nop().then_inc(sems[o.sig_sem], 1)
                        continue
                    ins = o.fn(eng)
                    if o.dma_key is not None:
                        ins.then_inc(sems[o.dma_sem], 16)
                    elif o.signal:
                        ins.then_inc(sems[o.sig_sem], 1)
            return body

        for e in ENGS:
            getattr(block, engmap[e])(make(e))
        self.stack.close()


class Arena:
    def __init__(self, prog, nbytes, name="arena"):
        nc = prog.nc
        self.t8 = prog.stack.enter_context(nc.sbuf_tensor(name, [128, nbytes], mybir.dt.uint8))
        self.views = {}
        self.nbytes = nbytes
        self.off = 0

    def view(self, dt):
        if dt not in self.views:
            self.views[dt] = self.t8.bitcast(dt)
        return self.views[dt]

    def alloc(self, nelem, dt):
        sz = mybir.dt.size(dt)
        self.off = (self.off + 63) // 64 * 64
        o = self.off
        self.off += nelem * sz
        assert self.off <= self.nbytes, f"arena overflow {self.off} > {self.nbytes}"
        return self.view(dt)[:, o // sz: o // sz + nelem]

    def alloc_raw(self, nbytes):
        self.off = (self.off + 63) // 64 * 64
        o = self.off
        self.off += nbytes
        assert self.off <= self.nbytes, f"arena overflow {self.off} > {self.nbytes}"
        return o

    def at(self, o, nelem, dt):
        sz = mybir.dt.size(dt)
        return self.view(dt)[:, o // sz: o // sz + nelem]

    def mark(self):
        return self.off

    def release(self, m):
        self.off = m


D = 1024
S = 16384
KC = 8
NSLOT = 8
SLOTW = 768
HW = 640
DFF = 2816
NFC = 22
EPS = 1e-6
LAM_INIT = 0.8 - 0.6 * math.exp(-0.3 * 0)
VECD = 2688
STRIPW = 2560
C_QA, C_KA, C_VA, C_QB, C_KB, C_VB, C_GL = 0, 512, 640, 768, 1280, 1792, 2304

DEBUG_SCRATCH = False
STORE_ENG = "pool"
P2_STAGE = 9
P2_SLOTS = 8
SKIP_P1 = False
N_PHASES = 5


def _t5_bucket_np(rel):
    n = np.maximum(rel, 0)
    nf = np.maximum(n, 1).astype(np.float32)
    large = 16 + (np.log(nf / np.float32(16)) / np.float32(math.log(8.0)) * np.float32(16)).astype(np.int32)
    large = np.minimum(large, 31)
    return np.where(n < 16, n, large)


class K:
    def __init__(self):
        nc = bass.Bass("TRN2", target_bir_lowering=False)
        self.nc = nc
        self.P = Prog(nc)
        self.A = Arena(self.P, 209920)
        self.ps = self.P.stack.enter_context(nc.psum_tensor("ps", [128, 8, 512], F32))
        self.psB = self.P.bufs(8, "psb")

    def mm(self, out, lhsT, rhs, start, stop, R, W):
        self.P.op("pe", lambda e: e.matmul(out, lhsT=lhsT, rhs=rhs, start=start, stop=stop), R, W)

    def act(self, out, in_, func, R, W, bias=None, scale=1.0):
        b = self.zero1 if bias is None else bias
        npart = out.shape[0]
        if npart != 128 and b.shape[0] == 128:
            b = b[0:npart, :]
        self.P.op("act", lambda e: e.activation(out=out, in_=in_, func=func, bias=b, scale=scale), R, W)

    def tt(self, eng, out, a, b, op, R, W):
        self.P.op(eng, lambda e: e.tensor_tensor(out=out, in0=a, in1=b, op=op), R, W)

    def ts(self, eng, out, a, s1, s2, op0, op1, R, W):
        if op1 is None:
            self.P.op(eng, lambda e: e.tensor_scalar(out=out, in0=a, scalar1=s1, scalar2=None, op0=op0), R, W)
        else:
            self.P.op(eng, lambda e: e.tensor_scalar(out=out, in0=a, scalar1=s1, scalar2=s2, op0=op0, op1=op1), R, W)

    def stt(self, eng, out, a, s, b, op0, op1, R, W):
        self.P.op(eng, lambda e: e.scalar_tensor_tensor(out=out, in0=a, scalar=s, in1=b, op0=op0, op1=op1), R, W)

    def cp(self, eng, out, in_, R, W):
        if eng == "act":
            self.act(out, in_, AF.Identity, R, W)
        else:
            self.P.op(eng, lambda e: e.tensor_copy(out=out, in_=in_), R, W)

    def rcp(self, out, in_, R, W):
        self.P.op("dve", lambda e: e.reciprocal(out=out, in_=in_), R, W)

    def rsqrt_act(self, out, in_, scale, R, W, power=-0.5, bias=None):
        self.act(out, in_, AF.Ln, R, W, bias=self.eps1 if bias is None else bias, scale=scale)
        old = self.P.strict
        if out.shape[-1] <= 256:
            self.P.strict = True
        self.act(out, out, AF.Exp, list(W) + [self.cB], W, scale=power)
        self.P.strict = old

    def memset(self, eng, ap, val, W):
        self.P.op(eng, lambda e: e.memset(ap, val), (), W)

    def declare(self):
        nc = self.nc

        def din(name, shape, dt=F32):
            return nc.dram_tensor(name, list(shape), dt, kind="ExternalInput")

        def dscr(name, shape, dt):
            return nc.dram_tensor(name, list(shape), dt, kind="ExternalOutput" if DEBUG_SCRATCH else "Internal")

        self.d_xT = din("xT", [128, KC, S])
        self.d_xo = din("xo", [128, KC, NSLOT * SLOTW])
        self.d_win = din("w_in", [128, KC, 4352])
        self.d_wbra = din("w_br_a", [128, 4, D])
        self.d_wbrb = din("w_br_b", [128, 4, D])
        self.d_wo = din("w_o", [128, KC, D])
        self.d_wup = din("w_up", [128, KC, 2 * DFF])
        self.d_wdn = din("w_down", [128, NFC, D])
        self.d_gmix = din("g_mix", [128, KC])
        self.d_gffn = din("g_ffn", [128, KC])
        self.d_cw = din("conv_w", [128, 3, 44])
        self.d_cb = din("conv_b", [128, 44])
        self.d_relb = din("rel_bias", [32, 12])
        self.d_qna = din("qn_a", [1, 64])
        self.d_kna = din("kn_a", [1, 64])
        self.d_qnb = din("qn_b", [1, 64])
        self.d_knb = din("kn_b", [1, 64])
        self.d_sinks = din("sinks", [1, 8])
        self.d_lq1 = din("lam_q1", [1, 64])
        self.d_lk1 = din("lam_k1", [1, 64])
        self.d_lq2 = din("lam_q2", [1, 64])
        self.d_lk2 = din("lam_k2", [1, 64])
        self.d_subln = din("subln_b", [128, 1])
        self.d_oha = din("oh_a", [33, 384])
        self.d_ohd = din("oh_d", [33, VECD])
        self.d_m0 = din("m0", [128, 1])
        self.d_J = din("Jmat", [128, 128])
        self.d_bd = din("bdones", [128, 128])
        self.d_out = nc.dram_tensor("outT", [128, KC, NSLOT * 512], F32, kind="ExternalOutput")
        self.d_KT = dscr("s_KT", [4, 128, S], BF16)
        self.d_VS = dscr("s_VS", [4, 128, 128, 128], BF16)
        self.d_QB = dscr("s_QB", [NSLOT, 128, 4, HW], BF16)
        self.d_YA = dscr("s_YA", [NSLOT, 128, 4, HW], BF16)
        self.d_YB = dscr("s_YB", [NSLOT, 128, 4, HW], BF16)
        self.d_X2 = dscr("s_X2", [NSLOT, 128, KC, HW], F32)
        self.d_H = dscr("s_H", [NSLOT, 128, KC, HW], BF16)
        if DEBUG_SCRATCH:
            self.d_dstrip = dscr("s_strip", [128, 4 * STRIPW], BF16)
            self.d_dsa = dscr("s_sa", [128, 2 * 8 * 128], BF16)
            self.d_dsmall = dscr("s_small", [128, 8], F32)
        self.d_bc = nc.dram_tensor("s_bc", [2, 3, 512], F32, kind="Internal")
        self.d_VA = dscr("s_VECA", [8, 384], F32)
        self.d_VD = dscr("s_VECD", [4, VECD], F32)

    def setup(self):
        P, A, nc = self.P, self.A, self.nc
        ps, psB = self.ps, self.psB
        cB = P.buf("consts")
        self.cB = cB
        self.zero1 = A.alloc(1, F32)
        self.eps1 = A.alloc(1, F32)
        self.tiny1 = A.alloc(1, F32)
        self.ones_bf = A.alloc(128, BF16)
        self.ones_f = A.alloc(128, F32)
        self.bd_bf = A.alloc(128, BF16)
        self.J = A.alloc(128, F32)
        bd_f = A.alloc(128, F32)
        self.memset("pool", self.zero1, 0.0, [cB])
        self.memset("pool", self.eps1, EPS, [cB])
        self.memset("pool", self.tiny1, 1e-30, [cB])
        self.memset("pool", self.ones_bf, 1.0, [cB])
        self.memset("pool", self.ones_f, 1.0, [cB])
        jB = P.buf("J")
        P.dma(self.J, self.d_J.ap(), writes=[jB], key="c0")
        P.dma(bd_f, self.d_bd.ap(), writes=[jB], key="c0")
        self.cp("dve", self.bd_bf, bd_f, [jB], [cB])
        sB = P.buf("small")
        self.gmix = A.alloc(KC, F32)
        self.gffn = A.alloc(KC, F32)
        self.cw = A.alloc(3 * 44, F32)
        self.cb = A.alloc(44, F32)
        self.m0 = A.alloc(1, F32)
        P.dma(self.gmix, self.d_gmix.ap(), writes=[sB], key="c1")
        P.dma(self.gffn, self.d_gffn.ap(), writes=[sB], key="c1")
        P.dma(self.cw, self.d_cw.ap().rearrange("p a b -> p (a b)"), writes=[sB], key="c1")
        P.dma(self.cb, self.d_cb.ap(), writes=[sB], key="c1")
        P.dma(self.m0, self.d_m0.ap(), writes=[sB], key="c1")
        self.gq_a = A.alloc(1, F32)
        self.gk_a = A.alloc(1, F32)
        self.gq_b = A.alloc(1, F32)
        self.gk_b = A.alloc(1, F32)
        for dst, src in ((self.gq_a, self.d_qna), (self.gk_a, self.d_kna), (self.gq_b, self.d_qnb), (self.gk_b, self.d_knb)):
            for hlf in range(2):
                P.dma(dst[64 * hlf:64 * hlf + 64, :], bass.AP(src, 0, [[1, 64], [1, 1]]), writes=[sB], key="c1")
        self.gsub = A.alloc(1, F32)
        P.dma(self.gsub, self.d_subln.ap(), writes=[sB], key="c1")
        self.ts("dve", self.gsub, self.gsub, 1.0 - LAM_INIT, None, ALU.mult, None, [sB], [sB])
        lam4 = A.alloc(4 * 64, F32)
        for n_, src in enumerate((self.d_lq1, self.d_lk1, self.d_lq2, self.d_lk2)):
            P.dma(lam4[:, n_ * 64:(n_ + 1) * 64], bass.AP(src, 0, [[0, 128], [1, 64]]), writes=[sB], key="c1")
        lp = A.alloc(2 * 64, F32)
        ls = A.alloc(2, F32)
        self.neglam = A.alloc(1, F32)
        self.tt("dve", lp[:, 0:64], lam4[:, 0:64], lam4[:, 64:128], ALU.mult, [sB], [sB])
        self.tt("dve", lp[:, 64:128], lam4[:, 128:192], lam4[:, 192:256], ALU.mult, [sB], [sB])
        P.op("dve", lambda e: e.reduce_sum(out=ls[:, 0:1], in_=lp[:, 0:64], axis=mybir.AxisListType.X), [sB], [sB])
        P.op("dve", lambda e: e.reduce_sum(out=ls[:, 1:2], in_=lp[:, 64:128], axis=mybir.AxisListType.X), [sB], [sB])
        self.act(ls, ls, AF.Exp, [sB, cB], [sB])
        self.tt("dve", self.neglam, ls[:, 1:2], ls[:, 0:1], ALU.subtract, [sB], [sB])
        self.ts("dve", self.neglam, self.neglam, -LAM_INIT, None, ALU.add, None, [sB], [sB])
        self.sB = sB
        self.esrow = A.alloc(2 * 512, F32)
        sk = A.alloc(8, F32)
        P.dma(sk[0:1, :], self.d_sinks.ap(), writes=[sB], key="c1")
        self.act(sk[0:1, :], sk[0:1, :], AF.Exp, [sB, cB], [sB])
        for hq in range(8):
            g_, r_ = hq // 4, hq % 4
            col = g_ * 512 + ((r_ % 2) * 2 + r_ // 2) * 128
            self.ts("dve", self.esrow[0:1, col:col + 128], self.ones_f[0:1, 0:128], sk[0:1, hq:hq + 1], None, ALU.mult, None, [sB, cB], [sB])

        self.esk = A.alloc(4, F32)
        for g_ in range(2):
            P.dma(self.esk[64 * g_:64 * g_ + 64, :], bass.AP(self.d_sinks, 4 * g_, [[0, 64], [1, 4]]), writes=[sB], key="c1")
        self.act(self.esk, self.esk, AF.Exp, [sB, cB], [sB])

        tB = P.buf("tab")
        tabp = A.alloc(12, F32)
        tab31 = A.alloc(4, F32)
        self.memset("pool", tabp[32:33, :], -30000.0, [tB])
        P.dma(tabp[0:32, :], self.d_relb.ap(), writes=[tB], key="c2")
        P.dma(tab31[0:32, :], bass.AP(self.d_relb, 31 * 12 + 8, [[0, 32], [1, 4]]), writes=[tB], key="c2")
        self.tt("dve", tabp[0:32, 8:12], tabp[0:32, 8:12], tab31[0:32, :], ALU.subtract, [tB], [tB])
        m = A.mark()
        oha = A.alloc(384, F32)
        ohd = A.alloc(VECD, F32)
        veca = A.alloc(384, F32)
        vecd = A.alloc(VECD, F32)
        ohB = P.buf("oh")
        P.dma(oha[0:33, :], self.d_oha.ap(), writes=[ohB], key="c2")
        P.dma(ohd[0:33, :], self.d_ohd.ap(), writes=[ohB], key="c2")
        vB = P.buf("vec")
        self.mm(ps[0:8, 0, 0:384], tabp[0:33, 0:8], oha[0:33, :], True, True, [tB, ohB], [psB[0]])
        self.act(veca[0:8, :], ps[0:8, 0, 0:384], AF.Exp, [psB[0], cB], [vB])
        for pc in range(6):
            w = 512 if pc < 5 else VECD - 2560
            bk = 1 + pc % 2
            self.mm(ps[0:4, bk, 0:w], tabp[0:33, 8:12], ohd[0:33, pc * 512: pc * 512 + w], True, True, [tB, ohB], [psB[bk]])
            self.act(vecd[0:4, pc * 512: pc * 512 + w], ps[0:4, bk, 0:w], AF.Exp, [psB[bk], cB], [vB])
        dvB = P.buf("dvec")
        P.dma(self.d_VA.ap(), veca[0:8, :], reads=[vB], writes=[dvB], key="c3")
        P.dma(self.d_VD.ap(), vecd[0:4, :], reads=[vB], writes=[dvB], key="c3")
        A.release(m)
        self.pre_strip_mark = A.mark()
        self.strip = A.alloc(4 * STRIPW, BF16).rearrange("p (h u) -> p h u", h=4)
        self.sa = A.alloc(2 * 8 * 128, BF16).rearrange("p (t h q) -> p t h q", t=2, h=8)
        self.stripB = P.buf("strip")
        self.persist_mark = A.mark()
        m = A.mark()
        rev = A.alloc(STRIPW, F32)
        revB = P.bufs(2, "rev")
        P.dma(rev[:, 0:2048].rearrange("p (h u) -> p h u", h=8), bass.AP(self.d_VA, 0, [[1, 128], [384, 8], [1, 256]]),
              reads=[dvB], writes=[revB[0]], key="c4")
        for pi in range(4):
            bk = pi % 2
            self.mm(ps[:, bk, :], self.J, rev[:, pi * 512:(pi + 1) * 512], True, True, [jB, revB[0]], [psB[bk]])
            src = ps[:, bk, :].rearrange("p (h t q) -> p h t q", h=2, t=2)
            for ty in range(2):
                self.cp("dve" if ty == 0 else "act", self.sa[:, ty, 2 * pi:2 * pi + 2, :], src[:, :, ty, :], [psB[bk]], [self.stripB])
        for h in range(4):
            rb = revB[(h + 1) % 2]
            P.dma(rev, bass.AP(self.d_VD, h * VECD, [[1, 128], [1, STRIPW]]), reads=[dvB], writes=[revB[0], revB[1]], key="c4")
            for pc in range(5):
                bk = pc % 2
                self.mm(ps[:, bk, :], self.J, rev[:, pc * 512:(pc + 1) * 512], True, True, [jB, revB[0], revB[1]], [psB[bk]])
                self.cp("dve" if pc % 2 == 0 else "act", self.strip[:, h, pc * 512:(pc + 1) * 512], ps[:, bk, :], [psB[bk]], [self.stripB])
        A.release(m)
        if DEBUG_SCRATCH:
            P.dma(self.d_dstrip.ap(), self.strip.rearrange("p h u -> p (h u)"), reads=[self.stripB])
            P.dma(self.d_dsa.ap(), self.sa.rearrange("p t h q -> p (t h q)"), reads=[self.stripB])
            for n_, t_ in enumerate((self.neglam, self.gsub, self.gq_b, self.gk_b)):
                P.dma(self.d_dsmall.ap()[:, n_:n_ + 1], t_, reads=[self.sB], allow_slow_non_contiguous=True)

    def wl_begin(self, nst, nelem=KC * 512):
        P, A = self.P, self.A
        self._wl = dict(mark=A.mark(), nst=nst, n=0, en=0,
                        stg=[A.alloc(nelem, F32) for _ in range(nst)], sb=P.bufs(nst, "stg"),
                        wB={"dve": P.buf("wd"), "act": P.buf("wa")})

    def load_w(self, dst, src, kc, ncols, gain=None):
        P = self.P
        W = self._wl
        pw = 512 if kc <= 8 else 192
        engs = ("dve", "act")
        for c0 in range(0, ncols, pw):
            w = min(pw, ncols - c0)
            j = W["n"] % W["nst"]
            W["n"] += 1
            s_ = W["stg"][j][:, 0:kc * pw].rearrange("p (c n) -> p c n", c=kc)
            b = W["sb"][j]
            P.dma(s_[:, :, 0:w], src[:, :, c0:c0 + w], writes=[b], key=f"SH_stg{j}")
            for c in range(kc):
                eng = engs[W["en"] % 2]
                W["en"] += 1
                wB = W["wB"][eng]
                if gain is None:
                    self.cp(eng, dst[:, c, c0:c0 + w], s_[:, c, 0:w], [b], [wB])
                elif eng == "act":
                    self.act(dst[:, c, c0:c0 + w], s_[:, c, 0:w], AF.Identity, [b, self.sB, self.cB], [wB], scale=gain[:, c:c + 1])
                else:
                    self.ts(eng, dst[:, c, c0:c0 + w], s_[:, c, 0:w], gain[:, c:c + 1], None, ALU.mult, None, [b, self.sB], [wB])

    def wl_end(self):
        self.P.barrier()
        self.A.release(self._wl["mark"])
        self._wl = None

    def norm_tile(self, xt, xB, n, sq, sqB, hT, hB, rstd, rB, pieces):
        ps, psB = self.ps, self.psB
        for c in range(KC):
            self.act(sq[c % 2][:, 0:n], xt[:, c, :], AF.Square, [xB, self.cB], [sqB[c % 2]])
            for (o, w, bank) in pieces:
                self.mm(ps[:, bank, 0:w], self.ones_bf, sq[c % 2][:, o:o + w], c == 0, c == KC - 1, [sqB[c % 2], self.cB], [psB[bank]])
        for (o, w, bank) in pieces:
            small = w < 128
            self.P.strict = small
            self.rsqrt_act(rstd[:, o:o + w], ps[:, bank, 0:w], 1.0 / D, [psB[bank], self.cB], [rB])
            self.P.strict = False
        for c in range(KC):
            self.tt("dve" if c % 2 == 0 else "pool", hT[:, c, 0:n], xt[:, c, :], rstd[:, 0:n], ALU.mult, [xB, rB], [hB[c]])

    def ph_tasks(self, tasks, hT, hB, wB, tmp, tmpB, pbanks, nbanks, mid=None):
        ps, psB = self.ps, self.psB
        n = len(tasks)

        def proj(k):
            wfn, o, w, out, outB, gain = tasks[k]
            bk = pbanks[k % len(pbanks)]
            ksq, _ = tmp[k % len(tmp)]
            for c in range(KC):
                self.mm(ps[:, bk, 0:w], wfn(c), hT[:, c, o:o + w], c == 0, c == KC - 1, [wB, hB[c]], [psB[bk]])
            self.act(ksq[:, 0:w], ps[:, bk, 0:w], AF.Square, [psB[bk], self.cB], [tmpB[k % len(tmp)][0]])

        def norm(k):
            wfn, o, w, out, outB, gain = tasks[k]
            bk = pbanks[k % len(pbanks)]
            bn = nbanks[k % len(nbanks)]
            ksq, rk = tmp[k % len(tmp)]
            tb = tmpB[k % len(tmp)]
            self.mm(ps[:, bn, 0:w], self.bd_bf, ksq[:, 0:w], True, True, [tb[0], self.cB], [psB[bn]])
            self.rsqrt_act(rk[:, 0:w], ps[:, bn, 0:w], 1.0 / 64, [psB[bn], self.cB], [tb[1]])
            self.stt("dve", out, ps[:, bk, 0:w], gain, rk[:, 0:w], ALU.mult, ALU.mult, [psB[bk], tb[1], self.sB], [outB])

        for k in range(n + 1):
            if k < n:
                proj(k)
            if k == n and mid is not None:
                mid()
            if k >= 1:
                norm(k - 1)

    def phase1(self):
        P, A = self.P, self.A
        ps, psB = self.ps, self.psB
        m = A.mark()
        wkb = A.alloc(KC * 512, BF16).rearrange("p (c n) -> p c n", c=KC)
        wvb = A.alloc(KC * 512, BF16).rearrange("p (c n) -> p c n", c=KC)
        win = self.d_win.ap()
        self.wl_begin(4)
        self.load_w(wkb, win[:, :, C_KB:C_KB + 512], KC, 512, gain=self.gmix)
        self.load_w(wvb, win[:, :, C_VB:C_VB + 512], KC, 512, gain=self.gmix)
        self.wl_end()
        wB1 = P.buf("w1a")
        wB2 = P.buf("w1b")
        xt = [A.alloc(KC * 512, F32).rearrange("p (c n) -> p c n", c=KC) for _ in range(2)]
        xB = P.bufs(2, "x")
        sq = [A.alloc(512, BF16) for _ in range(2)]
        sqB = P.bufs(2, "sq")
        hT = [A.alloc(KC * 512, BF16).rearrange("p (c n) -> p c n", c=KC) for _ in range(2)]
        hB = [P.bufs(KC, "h") for _ in range(2)]
        rstd = [A.alloc(512, F32) for _ in range(2)]
        rB = P.bufs(2, "r")
        tmp = [(A.alloc(512, BF16), A.alloc(512, F32)) for _ in range(3)]
        tmpB = [P.bufs(2, "tmp") for _ in range(3)]
        kout = [A.alloc(4 * 512, BF16).rearrange("p (h n) -> p h n", h=4) for _ in range(2)]
        koB = [P.bufs(4, "ko") for _ in range(2)]
        vout = [A.alloc(4 * 512, BF16).rearrange("p (b n) -> p b n", b=4) for _ in range(2)]
        voB = [P.bufs(4, "vo") for _ in range(2)]
        xT = self.d_xT.ap()
        NT = S // 512

        def stats(T):
            pp = T % 2
            P.dma(xt[pp], xT[:, :, T * 512:(T + 1) * 512], writes=[xB[pp]], key=f"x{pp}")
            self.norm_tile(xt[pp], xB[pp], 512, sq, sqB, hT[pp], hB[pp], rstd[pp], rB[pp], [(0, 512, 0)])

        stats(0)
        for T in range(NT):
            pp = T % 2
            if T + 1 < NT:
                stats(T + 1)
            tasks = [((lambda c, hh=hh: wkb[:, c, hh * 128:(hh + 1) * 128]), 0, 512, kout[pp][:, hh, :], koB[pp][hh], self.gk_b) for hh in range(4)]
            self.ph_tasks(tasks, hT[pp], hB[pp], wB1, tmp, tmpB, (1, 2, 3), (6, 7))
            for hh in range(4):
                P.dma(self.d_KT.ap()[hh, :, T * 512:(T + 1) * 512], kout[pp][:, hh, :], reads=[koB[pp][hh]], key=f"ko{pp}", eng=STORE_ENG)
            for blk in range(4):
                bk = 4 + blk % 2
                for c in range(KC):
                    self.mm(ps[:, bk, :], hT[pp][:, c, blk * 128:(blk + 1) * 128], wvb[:, c, :], c == 0, c == KC - 1,
                            [hB[pp][c], wB2], [psB[bk]])
                self.cp("act" if blk % 2 == 0 else "dve", vout[pp][:, blk, :], ps[:, bk, :], [psB[bk]], [voB[pp][blk]])
            for hh in range(4):
                P.dma(self.d_VS.ap()[hh, :, 4 * T:4 * T + 4, :], vout[pp][:, :, hh * 128:(hh + 1) * 128],
                      reads=voB[pp], key=f"vo{pp}", eng=STORE_ENG)
        P.barrier()
        A.release(m)

    def phase2(self):
        P, A = self.P, self.A
        ps, psB = self.ps, self.psB
        m = A.mark()
        win = self.d_win.ap()

        def walloc(n):
            return A.alloc(KC * n, BF16).rearrange("p (c n) -> p c n", c=KC)

        wqa, wka2, wva, wqb = walloc(512), walloc(256), walloc(128), walloc(512)
        self.wl_begin(4)
        self.load_w(wqa, win[:, :, C_QA:C_QA + 512], KC, 512, gain=self.gmix)
        for g in range(2):
            for hlf in range(2):
                self.load_w(wka2[:, :, g * 128 + hlf * 64: g * 128 + hlf * 64 + 64], win[:, :, C_KA + 64 * g:C_KA + 64 * g + 64], KC, 64,
                            gain=self.gmix)
        self.load_w(wva, win[:, :, C_VA:C_VA + 128], KC, 128, gain=self.gmix)
        self.load_w(wqb, win[:, :, C_QB:C_QB + 512], KC, 512, gain=self.gmix)
        self.wl_end()
        wB = P.buf("w2")
        xts = [A.alloc(KC * SLOTW, F32).rearrange("p (c n) -> p c n", c=KC) for _ in range(2)]
        xBs = P.bufs(2, "x")
        sq = [A.alloc(SLOTW, BF16) for _ in range(2)]
        sqB = P.bufs(2, "sq")
        hTs = [A.alloc(KC * SLOTW, BF16).rearrange("p (c n) -> p c n", c=KC) for _ in range(2)]
        hBs = [P.bufs(KC, "h") for _ in range(2)]
        rstds = [A.alloc(SLOTW, F32) for _ in range(2)]
        rBs = P.bufs(2, "r")
        tmp = [(A.alloc(512, BF16), A.alloc(512, F32)) for _ in range(3)]
        tmpB = [P.bufs(2, "tmp") for _ in range(3)]
        qaT = A.alloc(4 * HW, BF16).rearrange("p (c n) -> p c n", c=4)
        qaB = P.bufs(4, "qa")
        kaT = A.alloc(2 * SLOTW, BF16).rearrange("p (g n) -> p g n", g=2)
        kaB = P.bufs(2, "ka")
        va = A.alloc(6 * 128, BF16).rearrange("p (b n) -> p b n", b=6)
        vaB = P.buf("va")
        qbT = A.alloc(4 * HW, BF16).rearrange("p (c n) -> p c n", c=4)
        qbB = P.buf("qb")
        yaT = A.alloc(4 * HW, BF16).rearrange("p (c n) -> p c n", c=4)
        yaB = P.buf("ya")
        pt = [A.alloc(512, BF16) for _ in range(8)]
        ptB = P.bufs(8, "pt")
        den = [A.alloc(512, F32) for _ in range(2)]
        denB = P.bufs(2, "den")
        xo = self.d_xo.ap()
        nsl = min(NSLOT, P2_SLOTS)

        def stats(i):
            pp = i % 2
            P.dma(xts[pp], xo[:, :, i * SLOTW:(i + 1) * SLOTW], writes=[xBs[pp]], key="x2")
            self.norm_tile(xts[pp], xBs[pp], SLOTW, sq, sqB, hTs[pp], hBs[pp], rstds[pp], rBs[pp], [(0, 512, 0), (512, 256, 7)])

        stats(0)
        for i in range(nsl):
            hT, hB = hTs[i % 2], hBs[i % 2]
            tasks = []
            for (o, w) in ((128, 512), (640, 128)):
                for cm in range(4):
                    tasks.append(((lambda c, cm=cm: wqa[:, c, cm * 128:(cm + 1) * 128]), o, w, qaT[:, cm, o - 128:o - 128 + w], qaB[cm], self.gq_a))
                for cm in range(4):
                    tasks.append(((lambda c, cm=cm: wqb[:, c, cm * 128:(cm + 1) * 128]), o, w, qbT[:, cm, o - 128:o - 128 + w], qbB, self.gq_b))
            for (o, w) in ((0, 512), (512, 256)):
                for g in range(2):
                    tasks.append(((lambda c, g=g: wka2[:, c, g * 128:(g + 1) * 128]), o, w, kaT[:, g, o:o + w], kaB[g], self.gk_a))
            P.dma(self.d_H.ap()[i], hT[:, :, 128:SLOTW], reads=hB, eng=STORE_ENG)
            self.ph_tasks(tasks, hT, hB, wB, tmp, tmpB, (1, 2, 3), (5, 6))
            P.dma(self.d_QB.ap()[i], qbT, reads=[qbB], key="qbo", eng=STORE_ENG)
            for half in range(2):
                bk = 4 + half
                for bl in range(3):
                    blk = half * 3 + bl
                    for c in range(KC):
                        self.mm(ps[:, bk, bl * 128:(bl + 1) * 128], hT[:, c, blk * 128:(blk + 1) * 128], wva[:, c, :], c == 0, c == KC - 1,
                                [hB[c], wB], [psB[bk]])
                self.cp("act", va[:, half * 3:half * 3 + 3, :], ps[:, bk, 0:384].rearrange("p (b n) -> p b n", b=3), [psB[bk]], [vaB])
            if i + 1 < nsl:
                stats(i + 1)
            def swa_front(n, i=i):
                qo = (n - 1) * 128
                for g in range(2):
                    for kk, kblk in enumerate((n - 1, n)):
                        idx = g * 2 + kk
                        pi = (n % 2) * 4 + idx
                        b0 = (2, 6)[idx % 2]
                        for r in range(4):
                            par, rr = r % 2, r // 2
                            pb = 64 * par
                            self.mm(ps[:, b0 + par, rr * 128:(rr + 1) * 128], kaT[pb:pb + 64, g, kblk * 128:(kblk + 1) * 128],
                                    qaT[pb:pb + 64, 2 * g + rr, qo:qo + 128], True, True,
                                    [kaB[g], qaB[2 * g + rr]], [psB[b0 + par]])
                        self.act(pt[pi].rearrange("p (a n) -> p a n", a=2), ps[:, b0:b0 + 2, 0:256], AF.Exp,
                                 [psB[b0], psB[b0 + 1], self.cB], [ptB[pi]], scale=0.125)
                        ty = 1 if kk == 0 else 0
                        fa = self.sa[:, ty, 4 * g:4 * g + 4, :].rearrange("p (rr par) q -> p par rr q", par=2)
                        p4 = pt[pi].rearrange("p (par rr q) -> p par rr q", par=2, rr=2)
                        meng = "dve" if idx % 2 == 0 else "pool"
                        self.tt(meng, p4, p4, fa, ALU.mult, [ptB[pi], self.stripB], [ptB[pi]])
                        if i == 0 and n == 2 and kk == 0:
                            self.ts(meng, pt[pi], pt[pi], self.m0[:, 0:1], None, ALU.mult, None, [ptB[pi], self.sB], [ptB[pi]])

            def swa_back(n):
                qo = (n - 1) * 128
                for g in range(2):
                    for kk, kblk in enumerate((n - 1, n)):
                        pi = (n % 2) * 4 + g * 2 + kk
                        self.mm(ps[64 * g:64 * g + 64, 4, :], va[:, kblk, 64 * g:64 * g + 64], pt[pi], kk == 0, kk == 1,
                                [vaB, ptB[pi]], [psB[4]])
                    for kk in range(2):
                        pi = (n % 2) * 4 + g * 2 + kk
                        self.mm(ps[64 * g:64 * g + 64, 5, :], self.ones_bf[:, 0:64], pt[pi], kk == 0, kk == 1,
                                [ptB[pi], self.cB], [psB[5]])
                dn = den[n % 2]
                for par in range(2):
                    for rr in range(2):
                        cb_ = (par * 2 + rr) * 128
                        r_ = 2 * rr + par
                        self.act(dn[:, cb_:cb_ + 128], ps[:, 5, cb_:cb_ + 128], AF.Ln, [psB[5], self.cB, self.sB], [denB[n % 2]],
                                 bias=self.esk[:, r_:r_ + 1], scale=1.0)
                P.strict = True
                self.act(dn, dn, AF.Exp, [denB[n % 2], self.cB], [denB[n % 2]], scale=-1.0)
                P.strict = False
                self.tt("dve", yaT[:, :, qo:qo + 128].rearrange("p (rr par) q -> p par rr q", par=2),
                        ps[:, 4, :].rearrange("p (par rr q) -> p par rr q", par=2, rr=2),
                        dn.rearrange("p (par rr q) -> p par rr q", par=2, rr=2), ALU.mult, [psB[4], denB[n % 2]], [yaB])

            swa_front(1)
            for n in range(1, 6):
                if n < 5:
                    swa_front(n + 1)
                swa_back(n)
            P.dma(self.d_YA.ap()[i], yaT, reads=[yaB], key="yao", eng=STORE_ENG)
        P.barrier()
        A.release(m)

    def phase3(self):
        P, A = self.P, self.A
        ps, psB = self.ps, self.psB
        m = A.mark()
        KTs = [A.alloc(S, BF16) for _ in range(2)]
        VSs = [A.alloc(128 * 128, BF16).rearrange("p (b e) -> p b e", b=128) for _ in range(2)]
        kvB = [(P.bufs(4, "kt"), P.bufs(4, "vs")) for _ in range(2)]
        qt = [A.alloc(HW, BF16) for _ in range(2)]
        qB = P.bufs(2, "q")
        NPB = 4
        pt = [A.alloc(1024, BF16) for _ in range(NPB)]
        ptB = P.bufs(NPB, "pt")
        ssum = [A.alloc(512, F32) for _ in range(2)]
        ssumB = P.bufs(2, "ssum")
        rbc = [A.alloc(1024, F32) for _ in range(2)]
        rbcB = P.bufs(2, "rbc")
        dd = [A.alloc(1024, F32) for _ in range(2)]
        ddB = P.bufs(2, "dd")
        dsq = [A.alloc(512, BF16) for _ in range(2)]
        dsqB = P.bufs(2, "dsq")
        rrow = [A.alloc(512, F32) for _ in range(2)]
        rrowB = P.bufs(2, "rrow")
        rrbc = [A.alloc(512, F32) for _ in range(2)]
        rrbcB = P.bufs(2, "rrbc")
        yb = [A.alloc(HW, BF16) for _ in range(2)]
        ybB = P.bufs(2, "yb")
        dbcB = [P.bufs(2, "dbc") for _ in range(2)]
        q7B = P.bufs(4, "ps7q")
        ones32 = self.ones_bf[:, 0:32]
        SB = [(0, 1), (2, 3)]
        dbc = self.d_bc

        def load_kv(h):
            pp = h % 2
            for q4 in range(4):
                P.dma(KTs[pp][:, q4 * 4096:(q4 + 1) * 4096], self.d_KT.ap()[h, :, q4 * 4096:(q4 + 1) * 4096], writes=[kvB[pp][0][q4]])
            for q4 in range(4):
                P.dma(VSs[pp][:, q4 * 32:(q4 + 1) * 32, :], self.d_VS.ap()[h, :, q4 * 32:(q4 + 1) * 32, :], writes=[kvB[pp][1][q4]])

        pending = []
        gcount = [0]
        sbase = [0]

        def flush():
            while pending:
                pending.pop(0)()

        def finalize(ncol, o_ap, obufs, ycols, ybt, ybb, out_dma, nred=1):
            gp = gcount[0] % 2
            gcount[0] += 1
            small = ncol < 128
            P.strict = small
            ss_, rb_, dd_, dq_, rw_, rrb_ = ssum[gp], rbc[gp], dd[gp], dsq[gp], rrow[gp], rrbc[gp]
            if nred == 1:
                self.cp("dve", ss_[0:64, 0:ncol], ps[0:64, 7, 0:ncol], [q7B[0], q7B[1]], [ssumB[gp]])
            else:
                wtot = nred * ncol
                self.cp("dve", ss_[0:64, 0:wtot], ps[0:64, 7, 0:wtot], [q7B[0], q7B[1]], [ssumB[gp]])
                P.strict = True
                if nred == 12:
                    steps = [(4 * ncol, 8 * ncol, 4 * ncol), (4 * ncol, 4 * ncol, 4 * ncol), (2 * ncol, 2 * ncol, 2 * ncol), (ncol, ncol, ncol)]
                else:
                    steps = [(8 * ncol, 8 * ncol, 8 * ncol), (4 * ncol, 4 * ncol, 4 * ncol), (2 * ncol, 2 * ncol, 2 * ncol), (ncol, ncol, ncol)]
                for (wd_, src_, _) in steps:
                    self.tt("dve", ss_[0:64, 0:wd_], ss_[0:64, 0:wd_], ss_[0:64, src_:src_ + wd_], ALU.add, [ssumB[gp]], [ssumB[gp]])
                P.strict = small
            for comp in range(2):
                P.dma(dbc.ap()[gp, comp:comp + 1, 0:ncol], ss_[32 * comp:32 * comp + 1, 0:ncol], reads=[ssumB[gp]], writes=[dbcB[gp][0]])
            P.dma(rb_.rearrange("p (c n) -> p c n", c=2)[:, :, 0:ncol], bass.AP(dbc, gp * 3 * 512, [[0, 128], [512, 2], [1, ncol]]),
                  reads=[dbcB[gp][0]], writes=[rbcB[gp]])
            for comp in range(2):
                r_ = rb_[:, comp * 512:comp * 512 + ncol]
                self.ts("dve", r_, r_, self.tiny1[:, 0:1], None, ALU.add, None, [rbcB[gp], self.cB], [rbcB[gp]])
                self.rcp(r_, r_, [rbcB[gp]], [rbcB[gp]])
                self.tt("dve", dd_[:, comp * 512: comp * 512 + ncol], o_ap(comp), r_, ALU.mult, [obufs[comp], rbcB[gp]], [ddB[gp]])
            self.stt("dve", dd_[:, 0:ncol], dd_[:, 512:512 + ncol], self.neglam[:, 0:1], dd_[:, 0:ncol], ALU.mult, ALU.add, [ddB[gp], self.sB], [ddB[gp]])
            self.tt("pool", dq_[:, 0:ncol], dd_[:, 0:ncol], dd_[:, 0:ncol], ALU.mult, [ddB[gp]], [dsqB[gp]])
            P.strict = False

            def stage1():
                P.strict = small
                self.mm(ps[64:96, 7, 0:ncol], ones32, dq_[:, 0:ncol], True, True, [dsqB[gp], self.cB], [q7B[2]])
                self.act(rw_[64:96, 0:ncol], ps[64:96, 7, 0:ncol], AF.Ln, [q7B[2], self.cB], [rrowB[gp]], bias=self.eps1[64:96, :], scale=1.0 / 128)
                P.strict = True
                self.act(rw_[64:96, 0:ncol], rw_[64:96, 0:ncol], AF.Exp, [rrowB[gp], self.cB], [rrowB[gp]], bias=self.zero1[64:96, :], scale=-0.5)
                P.strict = small
                P.dma(dbc.ap()[gp, 2:3, 0:ncol], rw_[64:65, 0:ncol], reads=[rrowB[gp]], writes=[dbcB[gp][1]])
                P.dma(rrb_[:, 0:ncol], bass.AP(dbc, (gp * 3 + 2) * 512, [[0, 128], [1, ncol]]), reads=[dbcB[gp][1]], writes=[rrbcB[gp]])
                self.stt("dve", ybt[:, ycols:ycols + ncol], dd_[:, 0:ncol], self.gsub[:, 0:1], rrb_[:, 0:ncol], ALU.mult, ALU.mult,
                         [ddB[gp], rrbcB[gp], self.sB], [ybb])
                P.strict = False
                if out_dma is not None:
                    out_dma()
            pending.append(stage1)

        def group(nsteps, qk, av, near_off, ncol, ncols_sum, sum_rhs, pre_done=False, next_qk0=None):
            def pe_tail(t):
                p_, pB_ = pt[t % NPB], ptB[t % NPB]
                for comp in range(2):
                    rl = sum_rhs(p_, comp)
                    for n_, r_ in enumerate(rl):
                        self.mm(ps[32 * comp:32 * comp + 32, 7, 0:ncol], ones32, r_, t == 0 and n_ == 0, t == nsteps - 1 and n_ == len(rl) - 1,
                                [pB_, self.cB], [q7B[comp]])
                av(t, p_, pB_)

            b0 = sbase[0]
            sbase[0] += nsteps
            if not pre_done:
                qk(0, b0 % 2)
            for t in range(nsteps):
                if t + 1 < nsteps:
                    qk(t + 1, (b0 + t + 1) % 2)
                elif next_qk0 is not None:
                    next_qk0((b0 + nsteps) % 2)
                sb = SB[(b0 + t) % 2]
                p_, pB_ = pt[t % NPB], ptB[t % NPB]
                self.act(p_.rearrange("p (c n) -> p c n", c=2), ps[:, sb[0]:sb[0] + 2, :], AF.Exp, [psB[sb[0]], psB[sb[1]], self.cB], [pB_], scale=0.125)
                off = near_off(t)
                if off is not None:
                    for comp in range(2):
                        self.tt("dve", p_[:, comp * 512:(comp + 1) * 512], p_[:, comp * 512:(comp + 1) * 512],
                                self.strip_h[:, off:off + 512], ALU.mult, [pB_, self.stripB], [pB_])
                if t >= 1:
                    pe_tail(t - 1)
                if t == min(10, nsteps - 1):
                    flush()
            pe_tail(nsteps - 1)

        def group_h(nsteps, qk, av, near, batches, HQ, pre_done=False, next_qk0=None):
            def pe_tail(t):
                kb0, nseg = batches[t]
                wd = nseg * HQ
                p_, pB_ = pt[t % NPB], ptB[t % NPB]
                for comp in range(2):
                    self.mm(ps[32 * comp:32 * comp + 32, 7, 0:wd], ones32, p_[:, comp * 512:comp * 512 + wd], t == 0, t == nsteps - 1,
                            [pB_, self.cB], [q7B[comp]])
                av(t, p_, pB_)

            b0 = sbase[0]
            sbase[0] += nsteps
            if not pre_done:
                qk(0, b0 % 2)
            for t in range(nsteps):
                if t + 1 < nsteps:
                    qk(t + 1, (b0 + t + 1) % 2)
                elif next_qk0 is not None:
                    next_qk0((b0 + nsteps) % 2)
                sb = SB[(b0 + t) % 2]
                kb0, nseg = batches[t]
                wd = nseg * HQ
                p_, pB_ = pt[t % NPB], ptB[t % NPB]
                self.act(p_.rearrange("p (c n) -> p c n", c=2)[:, :, 0:wd], ps[:, sb[0]:sb[0] + 2, 0:wd], AF.Exp,
                         [psB[sb[0]], psB[sb[1]], self.cB], [pB_], scale=0.125)
                nr = near(t)
                if nr is not None:
                    c0, ns_, off = nr
                    fa = self.strip_h[:, off:off + ns_ * 128].rearrange("p (u c) -> p u c", c=128)[:, :, 0:HQ]
                    for comp in range(2):
                        pv = p_[:, comp * 512 + c0 * HQ: comp * 512 + (c0 + ns_) * HQ].rearrange("p (u c) -> p u c", c=HQ)
                        self.tt("dve", pv, pv, fa, ALU.mult, [pB_, self.stripB], [pB_])
                if t >= 1:
                    pe_tail(t - 1)
            pe_tail(nsteps - 1)

        def make_slot(h, i, qq):
            pp = h % 2
            KT, VS = KTs[pp], VSs[pp]
            kB, vB = kvB[pp]
            q, qb_ = qt[qq], qB[qq]
            ybt, ybb = yb[qq], ybB[qq]
            strip_h = self.strip[:, h, :]
            nkb = 16 * i + 16
            near0 = 16 * i - 1
            HQ = 32
            batches = [(16 * b, 16) for b in range(i)] + [(16 * i, 12)]
            nb = len(batches)

            def qk(t, par):
                sb = SB[par]
                for comp in range(2):
                    self.mm(ps[:, sb[comp], :], KT[64 * comp:64 * comp + 64, t * 128:(t + 1) * 128], q[64 * comp:64 * comp + 64, 128:640],
                            True, True, [kB[t // 32], qb_], [psB[sb[comp]]])

            def av(t, p_, pB_):
                for comp in range(2):
                    self.mm(ps[:, 4 + comp, :], VS[:, t, :], p_[:, comp * 512:(comp + 1) * 512], t == 0, t == nkb - 1,
                            [vB[t // 32], pB_], [psB[4 + comp]])

            def qkh(t, par):
                sb = SB[par]
                kb0, nseg = batches[t]
                for comp in range(2):
                    for u in range(nseg):
                        kb = kb0 + nseg - 1 - u
                        self.mm(ps[:, sb[comp], u * HQ:(u + 1) * HQ], KT[64 * comp:64 * comp + 64, kb * 128:(kb + 1) * 128],
                                q[64 * comp:64 * comp + 64, 128 - HQ:128], True, True, [kB[kb // 32], qb_], [psB[sb[comp]]])

            def avh(t, p_, pB_):
                kb0, nseg = batches[t]
                for comp in range(2):
                    for u in range(nseg):
                        kb = kb0 + nseg - 1 - u
                        self.mm(ps[:, 6, comp * HQ:(comp + 1) * HQ], VS[:, kb, :], p_[:, comp * 512 + u * HQ: comp * 512 + (u + 1) * HQ],
                                t == 0 and u == 0 and comp == 0, t == nb - 1 and u == nseg - 1, [vB[kb // 32], pB_], [psB[6]])

            def nearh(t):
                kb0, nseg = batches[t]
                if kb0 == 16 * i:
                    return (0, 12, 2048 - 128 * 13 + (128 - HQ))
                if kb0 == 16 * i - 16:
                    return (0, 4, 2048 - 128 + (128 - HQ))
                return None

            def odma():
                P.dma(self.d_YB.ap()[i, :, h, :], ybt, reads=[ybb])

            def run_main(pre_done, next_qk0):
                self.strip_h = strip_h
                group(nkb, qk, av, lambda t: (2048 - 128 * (t - near0)) if t >= near0 else None, 512, 512,
                      lambda p_, comp: [p_[:, comp * 512:(comp + 1) * 512]], pre_done, next_qk0)
                finalize(512, lambda comp: ps[:, 4 + comp, :], [psB[4], psB[5]], 128, ybt, ybb, None)

            def run_halo(pre_done, next_qk0):
                self.strip_h = strip_h
                group_h(nb, qkh, avh, nearh, batches, HQ, pre_done, next_qk0)
                finalize(HQ, lambda comp: ps[:, 6, comp * HQ:(comp + 1) * HQ], [psB[6], psB[6]], 128 - HQ, ybt, ybb, odma,
                         nred=(16 if i > 0 else 12))

            return dict(run_main=run_main, run_halo=run_halo, qk0=lambda par: qk(0, par), qkh0=lambda par: qkh(0, par))

        slots = [(h, i) for h in range(4) for i in range(NSLOT)]
        descs = {}

        def get(k):
            if k not in descs:
                descs[k] = make_slot(slots[k][0], slots[k][1], k % 2)
            return descs[k]

        load_kv(0)
        P.dma(qt[0], self.d_QB.ap()[0, :, 0, :], writes=[qB[0]])
        pre = False
        for k, (h, i) in enumerate(slots):
            if i == 0 and h + 1 < 4:
                load_kv(h + 1)
            if k + 1 < len(slots):
                nh, ni = slots[k + 1]
                P.dma(qt[(k + 1) % 2], self.d_QB.ap()[ni, :, nh, :], writes=[qB[(k + 1) % 2]])
            d = get(k)
            d["run_main"](pre, d["qkh0"])
            nxt = get(k + 1)["qk0"] if k + 1 < len(slots) else None
            d["run_halo"](True, nxt)
            pre = nxt is not None
            descs.pop(k, None)
        flush()
        P.barrier()
        A.release(m)

    def phase4a(self):
        P, A = self.P, self.A
        ps, psB = self.ps, self.psB
        A.release(self.pre_strip_mark)
        m = A.mark()
        win = self.d_win.ap()
        wgl = A.alloc(KC * 2048, BF16).rearrange("p (c n) -> p c n", c=KC)
        wbra = A.alloc(4 * D, BF16).rearrange("p (c n) -> p c n", c=4)
        wbrb = A.alloc(4 * D, BF16).rearrange("p (c n) -> p c n", c=4)
        wo = A.alloc(KC * D, BF16).rearrange("p (c n) -> p c n", c=KC)
        self.wl_begin(4)
        self.load_w(wgl, win[:, :, C_GL:C_GL + 2048], KC, 2048, gain=self.gmix)
        self.load_w(wbra, self.d_wbra.ap(), 4, D)
        self.load_w(wbrb, self.d_wbrb.ap(), 4, D)
        self.load_w(wo, self.d_wo.ap(), KC, D)
        self.wl_end()
        wB = P.buf("w4a")
        xts = [A.alloc(KC * HW, F32).rearrange("p (c n) -> p c n", c=KC) for _ in range(2)]
        xBs = P.bufs(2, "x")
        hTs = [A.alloc(KC * HW, BF16).rearrange("p (c n) -> p c n", c=KC) for _ in range(2)]
        hBs = P.bufs(2, "h")
        yas = [A.alloc(4 * HW, BF16).rearrange("p (c n) -> p c n", c=4) for _ in range(2)]
        ybs = [A.alloc(4 * HW, BF16).rearrange("p (c n) -> p c n", c=4) for _ in range(2)]
        yBs = [P.bufs(2, "y") for _ in range(2)]
        gates = A.alloc(16 * HW, BF16).rearrange("p (c n) -> p c n", c=16)
        gB = P.bufs(16, "g")
        mixed = A.alloc(KC * HW, BF16).rearrange("p (c n) -> p c n", c=KC)
        mxB = P.bufs(KC, "mx")
        t1 = [A.alloc(512, BF16) for _ in range(2)]
        t2 = [A.alloc(512, BF16) for _ in range(2)]
        tB = [P.bufs(2, "t") for _ in range(2)]
        x2 = [A.alloc(512, F32) for _ in range(2)]
        x2B = P.bufs(2, "x2")
        xo = self.d_xo.ap()
        PIECES = ((96, 512), (608, 32))
        cnt = 0
        def loads(i):
            pp = i % 2
            P.dma(hTs[pp], self.d_H.ap()[i], writes=[hBs[pp]])
            P.dma(yas[pp], self.d_YA.ap()[i], writes=[yBs[pp][0]])
            P.dma(ybs[pp], self.d_YB.ap()[i], writes=[yBs[pp][1]])
            P.dma(xts[pp], xo[:, :, i * SLOTW + 128:(i + 1) * SLOTW], writes=[xBs[pp]])

        loads(0)
        for i in range(NSLOT):
            pp = i % 2
            if i + 1 < NSLOT:
                loads(i + 1)
            xt, xB, hT, ya, yb, yB = xts[pp], xBs[pp], hTs[pp], yas[pp], ybs[pp], yBs[pp]
            hB = [hBs[pp]] * KC
            for (o, w) in PIECES:
                for gc in range(16):
                    bk = 1 + gc % 2
                    for c in range(KC):
                        self.mm(ps[:, bk, 0:w], wgl[:, c, gc * 128:(gc + 1) * 128], hT[:, c, o:o + w], c == 0, c == KC - 1, [wB, hB[c]], [psB[bk]])
                    self.act(gates[:, gc, o:o + w], ps[:, bk, 0:w], AF.Sigmoid, [psB[bk], self.cB], [gB[gc]])
            for (o, w) in PIECES:
                for mc in range(KC):
                    k2 = cnt % 2
                    cnt += 1
                    ba, bb = 3 + 2 * k2, 4 + 2 * k2
                    for r in range(4):
                        self.mm(ps[:, ba, 0:w], wbra[:, r, mc * 128:(mc + 1) * 128], ya[:, r, o:o + w], r == 0, r == 3, [wB, yB[0]], [psB[ba]])
                    for r in range(4):
                        self.mm(ps[:, bb, 0:w], wbrb[:, r, mc * 128:(mc + 1) * 128], yb[:, r, o:o + w], r == 0, r == 3, [wB, yB[1]], [psB[bb]])
                    self.tt("dve", t1[k2][:, 0:w], ps[:, ba, 0:w], gates[:, mc, o:o + w], ALU.mult, [psB[ba], gB[mc]], [tB[k2][0]])
                    self.tt("dve", t2[k2][:, 0:w], ps[:, bb, 0:w], gates[:, 8 + mc, o:o + w], ALU.mult, [psB[bb], gB[8 + mc]], [tB[k2][1]])
                    self.tt("pool", mixed[:, mc, o:o + w], t1[k2][:, 0:w], t2[k2][:, 0:w], ALU.add, [tB[k2][0], tB[k2][1]], [mxB[mc]])
            for (o, w) in PIECES:
                for oc in range(KC):
                    k2 = cnt % 2
                    cnt += 1
                    bk = 1 + k2
                    for mc in range(KC):
                        self.mm(ps[:, bk, 0:w], wo[:, mc, oc * 128:(oc + 1) * 128], mixed[:, mc, o:o + w], mc == 0, mc == KC - 1, [wB, mxB[mc]], [psB[bk]])
                    self.tt("dve", x2[k2][:, 0:w], ps[:, bk, 0:w], xt[:, oc, o:o + w], ALU.add, [psB[bk], xB], [x2B[k2]])
                    P.dma(self.d_X2.ap()[i, :, oc, o:o + w], x2[k2][:, 0:w], reads=[x2B[k2]], key=f"x2o{k2}", eng=STORE_ENG)
        P.barrier()
        A.release(m)

    def phase4b(self):
        P, A = self.P, self.A
        ps, psB = self.ps, self.psB
        A.release(self.pre_strip_mark)
        m = A.mark()
        wup = A.alloc(KC * 2 * DFF, BF16).rearrange("p (c n) -> p c n", c=KC)
        wdn = A.alloc(NFC * D, BF16).rearrange("p (c n) -> p c n", c=NFC)
        self.wl_begin(3, NFC * 192)
        self.load_w(wup, self.d_wup.ap(), KC, 2 * DFF, gain=self.gffn)
        self.load_w(wdn, self.d_wdn.ap(), NFC, D)
        self.wl_end()
        wB = P.buf("w4b")
        W2 = 514
        xa_off = A.alloc_raw(NFC * 512 * 2)
        xaB = P.buf("xa")
        sq = [A.alloc(W2, BF16) for _ in range(2)]
        sqB = P.bufs(2, "sq")
        hT = A.alloc(KC * W2, BF16).rearrange("p (c n) -> p c n", c=KC)
        hB = P.bufs(KC, "h")
        rstd = A.alloc(W2, F32)
        rB = P.buf("r")
        uh = A.alloc(44 * 2, F32).rearrange("p (c n) -> p c n", c=44)
        uhB = P.buf("uh")
        U = [A.alloc(W2, F32) for _ in range(4)]
        UB = P.bufs(4, "U")
        tg = [A.alloc(512, F32) for _ in range(3)]
        tv = [A.alloc(512, F32) for _ in range(3)]
        tgB = P.bufs(3, "tg")
        tvB = P.bufs(3, "tv")
        ot = [A.alloc(512, F32) for _ in range(2)]
        otB = P.bufs(2, "ot")
        xr = [A.alloc(512, F32) for _ in range(2)]
        xrB = P.bufs(2, "xr")
        cw = self.cw.rearrange("p (a b) -> p a b", a=3)
        outT = self.d_out.ap()
        xt = A.at(xa_off, KC * W2, F32).rearrange("p (c n) -> p c n", c=KC)
        aT = A.at(xa_off, NFC * 512, BF16).rearrange("p (c n) -> p c n", c=NFC)
        for i in range(NSLOT):
            P.dma(xt, self.d_X2.ap()[i, :, :, 126:640], writes=[xaB])
            self.norm_tile(xt, xaB, W2, sq, sqB, hT, hB, rstd, rB, [(0, 2, 7), (2, 512, 0)])
            for fc in range(44):
                for c in range(KC):
                    self.mm(ps[:, 7, fc * 2:fc * 2 + 2], wup[:, c, fc * 128:(fc + 1) * 128], hT[:, c, 0:2], c == 0, c == KC - 1, [wB, hB[c]], [psB[7]])
            self.cp("dve", uh, ps[:, 7, 0:88].rearrange("p (c n) -> p c n", c=44), [psB[7]], [uhB])
            def gate_tail(f):
                k2 = f % 3
                self.act(tg[k2], tg[k2], AF.Silu, [tgB[k2], self.cB], [tgB[k2]])
                self.tt("pool", aT[:, f, :], tg[k2], tv[k2], ALU.mult, [tgB[k2], tvB[k2]], [xaB])

            for f in range(NFC):
                k2 = f % 2
                k3 = f % 3
                for half, fc in enumerate((f, NFC + f)):
                    bk = 1 + 2 * k2 + half
                    ui = 2 * k2 + half
                    for c in range(KC):
                        self.mm(ps[:, bk, :], wup[:, c, fc * 128:(fc + 1) * 128], hT[:, c, 2:W2], c == 0, c == KC - 1, [wB, hB[c]], [psB[bk]])
                    self.cp("pool", U[ui][:, 0:2], uh[:, fc, :], [uhB], [UB[ui]])
                    self.cp("act", U[ui][:, 2:W2], ps[:, bk, :], [psB[bk]], [UB[ui]])
                    t_, tb_ = (tg[k3], tgB[k3]) if half == 0 else (tv[k3], tvB[k3])
                    eng = "dve"
                    self.ts(eng, t_, U[ui][:, 0:512], cw[:, 0, fc:fc + 1], self.cb[:, fc:fc + 1], ALU.mult, ALU.add, [UB[ui], self.sB], [tb_])
                    self.stt(eng, t_, U[ui][:, 1:513], cw[:, 1, fc:fc + 1], t_, ALU.mult, ALU.add, [UB[ui], self.sB, tb_], [tb_])
                    self.stt(eng, t_, U[ui][:, 2:514], cw[:, 2, fc:fc + 1], t_, ALU.mult, ALU.add, [UB[ui], self.sB, tb_], [tb_])
                if f >= 1:
                    gate_tail(f - 1)
            gate_tail(NFC - 1)
            for oc in range(KC):
                k2 = oc % 2
                bk = 5 + k2
                P.dma(xr[k2], self.d_X2.ap()[i, :, oc, 128:640], writes=[xrB[k2]])
                for f in range(NFC):
                    self.mm(ps[:, bk, :], wdn[:, f, oc * 128:(oc + 1) * 128], aT[:, f, :], f == 0, f == NFC - 1, [wB, xaB], [psB[bk]])
                self.tt("dve", ot[k2], ps[:, bk, :], xr[k2], ALU.add, [psB[bk], xrB[k2]], [otB[k2]])
                P.dma(outT[:, oc, i * 512:(i + 1) * 512], ot[k2], reads=[otB[k2]], eng=STORE_ENG)
        P.barrier()
        A.release(m)

    def build(self, nph=N_PHASES):
        self.declare()
        self.P.strict = True
        self.setup()
        self.P.strict = False
        self.P.barrier()
        phases = [self.phase1, self.phase2, self.phase3, self.phase4a, self.phase4b]
        for n_, ph in enumerate(phases[:nph]):
            if n_ == 0 and SKIP_P1:
                continue
            ph()
        self.P.barrier()
        self.P.emit()
        return self.nc


def _host_inputs(inputs):
    f = np.float32
    x = np.asarray(inputs["x"], dtype=f)

    def pc(w, kc):
        n = w.shape[1]
        return np.ascontiguousarray(w.reshape(kc, 128, n).transpose(1, 0, 2))

    w_br_a = np.asarray(inputs["w_br_a"][0], dtype=f)
    wa = w_br_a.reshape(2, 4, 64, D)
    wbra = np.ascontiguousarray(wa.transpose(0, 2, 1, 3).reshape(128, 4, D))
    common = {
        "w_in": pc(np.asarray(inputs["w_in"][0], dtype=f), KC),
        "w_br_a": wbra,
        "w_br_b": pc(np.asarray(inputs["w_br_b"][0], dtype=f), 4),
        "w_o": pc(np.asarray(inputs["w_o"][0], dtype=f), KC),
        "w_up": pc(np.asarray(inputs["w_up"][0], dtype=f), KC),
        "w_down": pc(np.asarray(inputs["w_down"][0], dtype=f), NFC),
        "g_mix": np.ascontiguousarray(np.asarray(inputs["g_mix"][0], dtype=f).reshape(KC, 128).T),
        "g_ffn": np.ascontiguousarray(np.asarray(inputs["g_ffn"][0], dtype=f).reshape(KC, 128).T),
        "conv_w": np.ascontiguousarray(np.asarray(inputs["conv_w"][0], dtype=f).reshape(3, 44, 128).transpose(2, 0, 1)),
        "conv_b": np.ascontiguousarray(np.asarray(inputs["conv_b"][0], dtype=f).reshape(44, 128).T),
        "rel_bias": np.ascontiguousarray(np.asarray(inputs["rel_bias"], dtype=f)),
        "qn_a": np.asarray(inputs["qn_a"], dtype=f).reshape(1, 64),
        "kn_a": np.asarray(inputs["kn_a"], dtype=f).reshape(1, 64),
        "qn_b": np.asarray(inputs["qn_b"], dtype=f).reshape(1, 64),
        "kn_b": np.asarray(inputs["kn_b"], dtype=f).reshape(1, 64),
        "sinks": np.asarray(inputs["sinks"], dtype=f).reshape(1, 8),
        "lam_q1": np.asarray(inputs["lam_q1"], dtype=f).reshape(1, 64),
        "lam_k1": np.asarray(inputs["lam_k1"], dtype=f).reshape(1, 64),
        "lam_q2": np.asarray(inputs["lam_q2"], dtype=f).reshape(1, 64),
        "lam_k2": np.asarray(inputs["lam_k2"], dtype=f).reshape(1, 64),
        "subln_b": np.asarray(inputs["subln_b"], dtype=f).reshape(128, 1),
        "Jmat": np.ascontiguousarray(np.eye(128, dtype=f)[::-1]),
        "bdones": np.kron(np.eye(2, dtype=f), np.ones((64, 64), dtype=f)),
    }
    oha = np.zeros((33, 384), dtype=f)
    for mm_ in range(384):
        d = mm_ - 127
        if 0 <= d < 128:
            oha[int(_t5_bucket_np(np.array(d))), mm_] = 1
        else:
            oha[32, mm_] = 1
    common["oh_a"] = oha
    in_maps = []
    for core in range(8):
        b, j = core // 4, core % 4
        xTb = np.ascontiguousarray(x[b].T.reshape(KC, 128, S).transpose(1, 0, 2))
        xo = np.zeros((128, KC, NSLOT, SLOTW), dtype=f)
        for i in range(NSLOT):
            G = 4 * i + j
            t0 = 512 * G - 256
            lo = max(t0, 0)
            xo[:, :, i, lo - t0:] = xTb[:, :, lo:t0 + SLOTW]
        ohd = np.zeros((33, VECD), dtype=f)
        d = np.arange(VECD) + 512 * j - 2047
        bk = _t5_bucket_np(d)
        for mm_ in range(VECD):
            if d[mm_] >= 0:
                ohd[bk[mm_], mm_] = 1
            else:
                ohd[32, mm_] = 1
        mp = dict(common)
        mp["xT"] = xTb
        mp["xo"] = xo.reshape(128, KC, NSLOT * SLOTW)
        mp["oh_d"] = ohd
        mp["m0"] = np.full((128, 1), 0.0 if j == 0 else 1.0, dtype=f)
        in_maps.append(mp)
    return in_maps


_NC_CACHE = {}


def kernel(**inputs):
    in_maps = _host_inputs(inputs)
    if "nc" not in _NC_CACHE:
        _NC_CACHE["nc"] = K().build()
    nc = _NC_CACHE["nc"]
    res = run_bass_kernel_spmd(nc, in_maps, core_ids=list(range(8)))
    out = np.zeros((2, S, D), dtype=np.float32)
    for core in range(8):
        b, j = core // 4, core % 4
        o = res.results[core]["outT"].reshape(128, KC, NSLOT, 512)
        for i in range(NSLOT):
            G = 4 * i + j
            out[b, 512 * G:512 * (G + 1), :] = o[:, :, i, :].transpose(2, 1, 0).reshape(512, D)
    return out
```
